# Optimizing a Trainium2 kernel written in Bass

```python
import jax, jax.numpy as jnp
from jax import lax
import numpy as np

D_MODEL = 1024
BATCH = 8
SEQ = 2048
DEPTH = 4

N_MIXERS = 3
N_LAYERS_A = (DEPTH + 2) // 3
N_LAYERS_B = (DEPTH + 1) // 3
N_LAYERS_C = DEPTH // 3

DILATED_CFG = ((128, 1), (512, 4), (2048, 16))
N_GROUPS_A = len(DILATED_CFG)
HEADS_PER_GROUP = 8
HEAD_DIM = 64
N_HEADS_A = N_GROUPS_A * HEADS_PER_GROUP
ATTN_WIDTH = HEADS_PER_GROUP * HEAD_DIM
QKV_WIDTH = N_GROUPS_A * 3 * ATTN_WIDTH
BLOCK = 128
NEG_INF = -1e30

SHORT_CONV_WIDTH = 3

POOL_WINDOWS = (2, 4, 8, 16)
N_POOL_GROUPS = len(POOL_WINDOWS)
POOL_GROUP_DIM = D_MODEL // N_POOL_GROUPS

D_FF = 2816
FFN_CONV_WIDTH = 3

RMS_EPS = 1e-6

kernel_name = "interleaved_dilated_attn_shortconv_pool_convffn"


def rms_norm(x, g):
    xf = x.astype(jnp.float32)
    y = xf * lax.rsqrt(jnp.mean(xf * xf, axis=-1, keepdims=True) + RMS_EPS)
    return (y * g.astype(jnp.float32)).astype(x.dtype)


def causal_depthwise_conv(x, w):
    k_w, c = w.shape
    return lax.conv_general_dilated(
        x, w.astype(x.dtype)[:, None, :], window_strides=(1,), padding=[(k_w - 1, 0)],
        dimension_numbers=("NWC", "WIO", "NWC"), feature_group_count=c)


def alibi_slopes():
    return jnp.asarray(2.0 ** (-8.0 * np.arange(1, N_HEADS_A + 1) / N_HEADS_A), jnp.float32)


def dilated_group_attention(q, k, v, slopes, window, dilation):
    b, s, h, dh = q.shape
    span = window // dilation
    length = s // dilation
    nb = -(-length // BLOCK)
    lp = nb * BLOCK

    def residues(t):
        return t.reshape(b, length, dilation, h, dh).transpose(0, 2, 3, 1, 4)

    qb = jnp.pad(residues(q), ((0, 0), (0, 0), (0, 0), (0, lp - length), (0, 0)))
    qb = qb.reshape(b, dilation, h, nb, BLOCK, dh)

    def band(t):
        tb = jnp.pad(residues(t), ((0, 0), (0, 0), (0, 0), (BLOCK, lp - length), (0, 0)))
        tb = tb.reshape(b, dilation, h, nb + 1, BLOCK, dh)
        return jnp.concatenate([tb[:, :, :, :-1], tb[:, :, :, 1:]], axis=4)

    kb, vb = band(k), band(v)
    scores = jnp.einsum("brhnqc,brhnkc->brhnqk", qb, kb).astype(jnp.float32) * (HEAD_DIM ** -0.5)

    delta = BLOCK + np.arange(BLOCK)[:, None] - np.arange(2 * BLOCK)[None, :]
    valid = (delta >= 0) & (delta <= span)
    valid = valid[None] & ((np.arange(nb)[:, None, None] > 0) | (np.arange(2 * BLOCK)[None, None, :] >= BLOCK))
    bias = -slopes[:, None, None] * jnp.asarray(delta * dilation, jnp.float32)[None]
    scores = jnp.where(valid, scores + bias[:, None], NEG_INF)

    lse = jax.nn.logsumexp(scores, axis=-1)
    p = jnp.exp(scores - lse[..., None])
    o = jnp.einsum("brhnqk,brhnkc->brhnqc", p.astype(vb.dtype), vb)
    o = o.reshape(b, dilation, h, lp, dh)[:, :, :, :length].transpose(0, 3, 1, 2, 4).reshape(b, s, h, dh)
    lse = lse.reshape(b, dilation, h, lp)[..., :length].transpose(0, 3, 1, 2).reshape(b, s, h)
    return o, lse


def dilated_attention_mixer(x, w_qkv, w_o):
    b, s, _ = x.shape
    qkv = (x @ w_qkv).reshape(b, s, N_GROUPS_A, 3, HEADS_PER_GROUP, HEAD_DIM)
    slopes = alibi_slopes()
    outs, lses = [], []
    for g, (window, dilation) in enumerate(DILATED_CFG):
        o, lse = dilated_group_attention(
            qkv[:, :, g, 0], qkv[:, :, g, 1], qkv[:, :, g, 2],
            slopes[g * HEADS_PER_GROUP:(g + 1) * HEADS_PER_GROUP], window, dilation)
        outs.append(o)
        lses.append(lse)
    wts = jax.nn.softmax(jnp.stack(lses, 0), axis=0)
    merged = jnp.einsum("gbsh,gbshc->bshc", wts, jnp.stack(outs, 0).astype(jnp.float32))
    return merged.reshape(b, s, ATTN_WIDTH).astype(x.dtype) @ w_o


def short_conv_mixer(x, w_in, w_dw, w_out):
    gate_b, gate_c, h = jnp.split(x @ w_in, 3, axis=-1)
    return (gate_b * causal_depthwise_conv(gate_c * h, w_dw)) @ w_out


def pooling_mixer(x, w_in, w_grp, scale, w_out):
    b, s, _ = x.shape
    u = (x @ w_in).astype(jnp.float32).reshape(b, s, N_POOL_GROUPS, POOL_GROUP_DIM)
    cs = jnp.cumsum(u, axis=1)
    pos = jnp.arange(1, s + 1, dtype=jnp.float32)[None, :, None]
    pooled = []
    for g, w in enumerate(POOL_WINDOWS):
        c = cs[:, :, g]
        prev = jnp.pad(c, ((0, 0), (w, 0), (0, 0)))[:, :s]
        pooled.append((c - prev) / jnp.minimum(pos, float(w)) - u[:, :, g])
    p = jnp.stack(pooled, axis=2).astype(x.dtype)
    y = jnp.einsum("bsgc,gcd->bsgd", p, w_grp).reshape(b, s, D_MODEL) * scale
    return y @ w_out


def conv_ffn(x, w_up, w_dw, w_down):
    h = causal_depthwise_conv(x @ w_up, w_dw)
    g, u = jnp.split(h, 2, axis=-1)
    return (jax.nn.silu(g) * u) @ w_down


def setup_inputs(seed: int = 0) -> dict:
    key = jax.random.key(seed)
    ks = jax.random.split(key, 16)
    f32 = jnp.float32

    def nrm(k, shape, fan_in):
        return jax.random.normal(k, shape, f32) * (fan_in ** -0.5)

    return {
        "x": jax.random.normal(ks[0], (BATCH, SEQ, D_MODEL), f32),
        "norm_g": 1.0 + 0.05 * jax.random.normal(ks[1], (DEPTH, 4, D_MODEL), f32),
        "attn_w_qkv": nrm(ks[2], (N_LAYERS_A, D_MODEL, QKV_WIDTH), D_MODEL),
        "attn_w_o": nrm(ks[3], (N_LAYERS_A, ATTN_WIDTH, D_MODEL), ATTN_WIDTH),
        "conv_w_in": nrm(ks[4], (N_LAYERS_B, D_MODEL, 3 * D_MODEL), D_MODEL),
        "conv_w_dw": nrm(ks[5], (N_LAYERS_B, SHORT_CONV_WIDTH, D_MODEL), SHORT_CONV_WIDTH),
        "conv_w_out": nrm(ks[6], (N_LAYERS_B, D_MODEL, D_MODEL), D_MODEL),
        "pool_w_in": nrm(ks[7], (N_LAYERS_C, D_MODEL, D_MODEL), D_MODEL),
        "pool_w_grp": nrm(ks[8], (N_LAYERS_C, N_POOL_GROUPS, POOL_GROUP_DIM, POOL_GROUP_DIM), POOL_GROUP_DIM),
        "pool_scale": 1.0 + 0.1 * jax.random.normal(ks[9], (N_LAYERS_C, D_MODEL), f32),
        "pool_w_out": nrm(ks[10], (N_LAYERS_C, D_MODEL, D_MODEL), D_MODEL),
        "ffn_w_up": nrm(ks[11], (DEPTH, D_MODEL, 2 * D_FF), D_MODEL),
        "ffn_w_dw": nrm(ks[12], (DEPTH, FFN_CONV_WIDTH, 2 * D_FF), FFN_CONV_WIDTH),
        "ffn_w_down": nrm(ks[13], (DEPTH, D_FF, D_MODEL), D_FF),
    }


def reference(x, norm_g, attn_w_qkv, attn_w_o, conv_w_in, conv_w_dw, conv_w_out,
              pool_w_in, pool_w_grp, pool_scale, pool_w_out, ffn_w_up, ffn_w_dw, ffn_w_down):
    ia = ib = ic = 0
    for i in range(DEPTH):
        h = rms_norm(x, norm_g[i, 0])
        kind = i % N_MIXERS
        if kind == 0:
            h = dilated_attention_mixer(h, attn_w_qkv[ia], attn_w_o[ia])
            ia += 1
        elif kind == 1:
            h = short_conv_mixer(h, conv_w_in[ib], conv_w_dw[ib], conv_w_out[ib])
            ib += 1
        else:
            h = pooling_mixer(h, pool_w_in[ic], pool_w_grp[ic], pool_scale[ic], pool_w_out[ic])
            ic += 1
        x = x + rms_norm(h, norm_g[i, 1])
        h = conv_ffn(rms_norm(x, norm_g[i, 2]), ffn_w_up[i], ffn_w_dw[i], ffn_w_down[i])
        x = x + rms_norm(h, norm_g[i, 3])
    return x
```

```python
import numpy as np
import concourse.bass as bass
import concourse.mybir as mybir
from concourse.bass_utils import run_bass_kernel_spmd
from contextlib import ExitStack

F32 = mybir.dt.float32
BF16 = mybir.dt.bfloat16
AF = mybir.ActivationFunctionType
ALU = mybir.AluOpType

S = 2048
EPS = 1e-6
DIL = (1, 4, 16)
NSLOT = 3
SLOT = 3072
ST = 520
NT = 12
FUSED = True
EXEC_LAYERS = None


class Prog:
    ENG = ("pe", "act", "dve", "pool", "sp")

    def __init__(self):
        self.q = {e: [] for e in self.ENG}
        self.cnt = {}
        self.waited = {e: {} for e in self.ENG}
        self.recs = {}
        self.unordered = set()
        self.pe_pending = False
        self.nops = 0

    def _overlaps(self, name, lo, hi):
        return [r for r in self.recs.get(name, ()) if r[0] < hi and lo < r[1]]

    def op(self, eng, fn, reads=(), writes=(), inc=True, stream=None, amt=1):
        own = stream if stream is not None else eng
        if self.pe_pending and eng != "pe":
            raise RuntimeError("non-PE op constructed inside an open PE group")
        deps = {}

        def add(dep):
            if dep is None:
                return
            s, c = dep
            if s in self.unordered:
                c = self.cnt.get(s, 0)
            if eng == "pe" and s == "pe":
                return
            if deps.get(s, 0) < c:
                deps[s] = c
        for (name, lo, hi) in reads:
            for r in self._overlaps(name, lo, hi):
                add(r[2])
        for (name, lo, hi) in writes:
            for r in self._overlaps(name, lo, hi):
                add(r[2])
                for s, c in r[3].items():
                    add((s, c))
        for s, c in deps.items():
            if self.waited[eng].get(s, 0) < c:
                self.waited[eng][s] = c
                self.q[eng].append(("wait", s, c))
        after = self.cnt.get(own, 0) + amt
        if inc:
            self.cnt[own] = after
            if eng == "pe":
                self.pe_pending = False
        else:
            assert eng == "pe"
            self.pe_pending = True
        self.q[eng].append(("op", fn, own if inc else None, amt))
        self.nops += 1
        for (name, lo, hi) in reads:
            lst = self.recs.setdefault(name, [])
            for r in lst:
                if r[0] == lo and r[1] == hi and r[2] is None:
                    r[3][own] = after
                    break
            else:
                lst.append([lo, hi, None, {own: after}])
        for (name, lo, hi) in writes:
            lst = self.recs.setdefault(name, [])
            lst[:] = [r for r in lst if not (lo <= r[0] and r[1] <= hi)]
            lst.append([lo, hi, (own, after), {}])

    def wait_all(self, eng, streams):
        for s in streams:
            c = self.cnt.get(s, 0)
            if c and self.waited[eng].get(s, 0) < c:
                self.waited[eng][s] = c
                self.q[eng].append(("wait", s, c))

    def emit(self, nc, stack):
        assert not self.pe_pending
        sems = {s: stack.enter_context(nc.semaphore("s_" + s)) for s in self.cnt}
        block = stack.enter_context(nc.Block())

        def replay(e, name):
            for it in self.q[name]:
                if it[0] == "wait":
                    e.wait_ge(sems[it[1]], it[2])
                else:
                    ins = it[1](e)
                    if it[2] is not None:
                        ins.then_inc(sems[it[2]], it[3])

        @block.tensor
        def _(e):
            replay(e, "pe")

        @block.scalar
        def _(e):
            replay(e, "act")

        @block.vector
        def _(e):
            replay(e, "dve")

        @block.gpsimd
        def _(e):
            replay(e, "pool")

        @block.sync
        def _(e):
            replay(e, "sp")


class DummyProg:
    def op(self, *a, **k):
        pass

    def wait_all(self, *a, **k):
        pass


def I(fn, *a, **k):
    return lambda e: getattr(e, fn)(*a, **k)


class WMgr:
    def __init__(self, P, dram, WB, sched):
        self.P, self.dram, self.WB, self.sched = P, dram, WB, sched
        self.rec = []
        self.issued = 0
        self.used = 0

    def next(self, name, blk, nelem):
        if self.sched is None:
            self.rec.append((name, blk, nelem))
            return 0
        i = self.used
        assert self.sched[i] == (name, blk, nelem), (self.sched[i], name, blk, nelem)
        while self.issued < min(i + NSLOT, len(self.sched)):
            self._issue(self.issued)
            self.issued += 1
        self.used += 1
        return (i % NSLOT) * SLOT

    def _issue(self, j):
        name, blk, nelem = self.sched[j]
        slot = j % NSLOT
        off = slot * SLOT
        self.P.op("pool", I("dma_start", out=self.WB[:, off:off + nelem], in_=self.dram[name][blk]),
                  writes=[("WB", off, off + nelem)], stream="w%d" % slot, amt=16)


def build_program(layers):
    nc = bass.Bass("TRN2", target_bir_lowering=False)
    dram = {}

    def din(name, shape):
        dram[name] = nc.dram_tensor(name, list(shape), F32, kind="ExternalInput").ap()
    din("xT", (1024, S))
    din("gvec", (128, 128))
    din("dwv", (128, 528))
    din("cdw", (128, 24))
    din("psc", (128, 8))
    for l in layers:
        din("wup%d" % l, (22, 128, 2048))
        din("wdn%d" % l, (8, 128, 2816))
        kind = l % 3
        if kind == 0:
            din("wqkv%d" % l, (12, 128, 3072))
            din("wo%d" % l, (2, 128, 2048))
        elif kind == 1:
            din("cwin", (8, 128, 3072))
            din("cwout", (4, 128, 2048))
        else:
            din("pwin", (4, 128, 2048))
            din("pwgrp", (4, 128, 512))
            din("pwout", (4, 128, 2048))
    yT = nc.dram_tensor("yT", [1024, S], F32, kind="ExternalOutput").ap()

    with ExitStack() as st:
        def sb(name, n, dt):
            return st.enter_context(nc.sbuf_tensor(name, [128, n], dt))
        X = sb("X", 8 * S, F32)
        XN = sb("XN", 8 * S, BF16)
        BIG = sb("BIG", 22528, BF16)
        SCR = sb("SCR", NT * ST, F32)
        SB16 = sb("SB16", 6 * 512, BF16)
        WB = sb("WB", NSLOT * SLOT, BF16)
        MASK = sb("MASK", 12 * 512, BF16)
        GV = sb("GV", 128, F32)
        DWV = sb("DWV", 528, F32)
        CDW = sb("CDW", 24, F32)
        PSC = sb("PSC", 8, F32)
        ONES = sb("ONES", 128, BF16)
        ONESAB = sb("ONESAB", 256, BF16)
        HALO = sb("HALO", 88, F32)
        INVT = sb("INVT", 16, F32)
        PS = [st.enter_context(nc.psum_tensor("PS%d" % b, [128, 512], F32)) for b in range(8)]

        def construct(P, W):
            mmi = [0]
            auxi = [0]
            sqi = [0]
            rsi = [0]
            ei = [0]

            def ps_mm():
                b = mmi[0] % 5
                mmi[0] += 1
                return PS[b], "PS%d" % b

            upi = [0]

            def ps_up():
                b = upi[0] % 8
                upi[0] += 1
                return PS[b], "PS%d" % b

            def ps_aux():
                b = (6, 7, 5)[auxi[0] % 3]
                auxi[0] += 1
                return PS[b], "PS%d" % b
            SS, SSn = PS[5], "PS5"

            def scr(i, lo=0, n=512):
                o = i * ST + lo
                return SCR[:, o:o + n], ("SCR", o, o + n)

            def sb16(i, n=512):
                return SB16[:, i * 512:i * 512 + n], ("SB16", i * 512, i * 512 + n)

            for (h0, h1, strm) in ((0, 512, "xld0"), (512, 1024, "xld1"), (1024, 2048, "xld2")):
                for c in range(8):
                    P.op("sp",
                         I("dma_start", out=X[:, c * S + h0:c * S + h1], in_=dram["xT"][c * 128:(c + 1) * 128, h0:h1]),
                         writes=[("X", c * S + h0, c * S + h1)], stream=strm, amt=16)
            for (t, nm, n) in ((GV, "gvec", 128), (DWV, "dwv", 528), (CDW, "cdw", 24), (PSC, "psc", 8)):
                P.op("sp", I("dma_start", out=t[:, :], in_=dram[nm][:, :]), writes=[(t_name(t, nm), 0, n)], stream="cst", amt=16)
            P.op("dve", I("memset", ONES[:, :], 1.0), writes=[("ONES", 0, 128)])
            P.op("dve", I("memset", ONESAB[:, :], 0.0), writes=[("ONESAB", 0, 256)])
            P.op("dve", I("memset", ONESAB[:, 0:64], 1.0), writes=[("ONESAB", 0, 64)])
            P.op("dve", I("memset", ONESAB[:, 192:256], 1.0), writes=[("ONESAB", 192, 256)])

            XNf = XN.bitcast(F32)
            MASKf = MASK.bitcast(F32)

            def add_sq(src_ap, src_rg, first, last, SS=SS, SSn=SSn, defer=False):
                k = sqi[0] % 2
                sqi[0] += 1
                sq_ap, sq_rg = sb16(k)
                P.op("act", I("activation", out=sq_ap, in_=src_ap, func=AF.Square), reads=[src_rg], writes=[sq_rg])

                def pe_part():
                    P.op("pe", I("matmul", SS[:, :], lhsT=ONES[:, :], rhs=sq_ap, start=first, stop=last),
                         reads=[sq_rg, ("ONES", 0, 128)], writes=[(SSn, 0, 512)])
                if defer:
                    return pe_part
                pe_part()
                return None

            def stats_to_rs(SS=SS, SSn=SSn):
                r = 10 + rsi[0] % 2
                rsi[0] += 1
                rs_ap, rs_rg = scr(r)
                P.op("act", I("activation", out=rs_ap, in_=SS[:, :], func=AF.Ln, scale=1.0 / 1024.0, bias=EPS),
                     reads=[(SSn, 0, 512)], writes=[rs_rg])
                P.op("act", I("activation", out=rs_ap, in_=rs_ap, func=AF.Exp, scale=-0.5), reads=[rs_rg], writes=[rs_rg])
                return rs_ap, rs_rg

            def prenorm_units(l, j, tok0, ntiles, sqt=(0, 1)):
                go = (l * 4 + j) * 8
                units = []
                for tt in range(ntiles):
                    t0 = tok0 + 512 * tt
                    rs_box = []

                    def sq(c, t0=t0):
                        o = c * S + t0
                        sq_ap, sq_rg = sb16(sqt[c % len(sqt)])
                        P.op("act", I("activation", out=sq_ap, in_=X[:, o:o + 512], func=AF.Square),
                             reads=[("X", o, o + 512)], writes=[sq_rg])

                    def stat(c):
                        sq_ap, sq_rg = sb16(sqt[c % len(sqt)])
                        P.op("pe", I("matmul", SS[:, :], lhsT=ONES[:, :], rhs=sq_ap, start=(c == 0), stop=(c == 7)),
                             reads=[sq_rg, ("ONES", 0, 128)], writes=[(SSn, 0, 512)])

                    def rs(rs_box=rs_box):
                        rs_box.append(stats_to_rs())

                    def apply(c0, c1, t0=t0, rs_box=rs_box):
                        rs_ap, rs_rg = rs_box[0]
                        for c in range(c0, c1):
                            o = c * S + t0
                            P.op("dve", I("scalar_tensor_tensor", out=XN[:, o:o + 512], in0=X[:, o:o + 512],
                                          scalar=GV[:, go + c:go + c + 1], in1=rs_ap, op0=ALU.mult, op1=ALU.mult),
                                 reads=[("X", o, o + 512), rs_rg, ("GV", 0, 128)], writes=[("XN", o, o + 512)])
                    nb_ = len(sqt)
                    units.append(lambda sq=sq: sq(0))
                    if nb_ >= 3:
                        units.append(lambda sq=sq: sq(1))
                        for c in range(2, 8):
                            units.append(lambda c=c, sq=sq, stat=stat: (sq(c), stat(c - 2)))
                        units.append(lambda stat=stat, rs=rs: (stat(6), stat(7), rs()))
                    else:
                        for c in range(1, 8):
                            units.append(lambda c=c, sq=sq, stat=stat: (sq(c), stat(c - 1)))
                        units.append(lambda stat=stat, rs=rs: (stat(7), rs()))
                    units.append(lambda apply=apply: apply(0, 4))
                    units.append(lambda apply=apply: apply(4, 8))
                return units

            def prenorm(l, j, tok0, ntiles):
                for u in prenorm_units(l, j, tok0, ntiles):
                    u()

            def proj_postnorm(l, j, wname, nk, mpb, in_off, in_cs, in_t0, tok0, ntiles, bg=None, bg_after=1):
                go = (l * 4 + j) * 8
                ncols = mpb * 128
                woff = 0
                pend = None
                bg_list = []
                SSp = (PS[6], "PS6")

                def hs(tt, m):
                    if tt % 2 == 0:
                        return scr(m)
                    o = m * S + 1024
                    return XNf[:, o // 2:o // 2 + 512], ("XN", o, o + 1024)
                for tt in range(ntiles):
                    t0 = tok0 + 512 * tt
                    it0 = in_t0 + 512 * tt
                    for m in range(8):
                        if m % mpb == 0:
                            woff = W.next(wname, m // mpb, nk * ncols)
                        ps, psn = ps_mm()
                        for kc in range(nk):
                            lo = woff + kc * ncols + (m % mpb) * 128
                            io = in_off + kc * in_cs + it0
                            P.op("pe", I("matmul", ps[:, :], lhsT=WB[:, lo:lo + 128], rhs=BIG[:, io:io + 512],
                                         start=(kc == 0), stop=(kc == nk - 1)),
                                 reads=[("WB", lo, lo + 128), ("BIG", io, io + 512)], writes=[(psn, 0, 512)],
                                 inc=(kc == nk - 1))
                        if pend is not None:
                            pend()
                        hs_ap, hs_rg = hs(tt, m)
                        P.op("act", I("activation", out=hs_ap, in_=ps[:, :], func=AF.Copy),
                             reads=[(psn, 0, 512)], writes=[hs_rg])
                        pend = add_sq(ps[:, :], (psn, 0, 512), m == 0, m == 7, SSp[0], SSp[1], defer=True)
                        if bg_list and tt > bg_after:
                            for _ in range(2):
                                if bg_list:
                                    bg_list.pop(0)()
                    pend()
                    pend = None
                    rs_ap, rs_rg = stats_to_rs(SSp[0], SSp[1])
                    for m in range(8):
                        hs_ap, hs_rg = hs(tt, m)
                        P.op("dve", I("scalar_tensor_tensor", out=hs_ap, in0=hs_ap, scalar=GV[:, go + m:go + m + 1],
                                      in1=rs_ap, op0=ALU.mult, op1=ALU.mult),
                             reads=[hs_rg, rs_rg, ("GV", 0, 128)], writes=[hs_rg])
                        xo = m * S + t0
                        P.op("dve",
                             I("tensor_tensor", out=X[:, xo:xo + 512], in0=X[:, xo:xo + 512], in1=hs_ap, op=ALU.add),
                             reads=[("X", xo, xo + 512), hs_rg], writes=[("X", xo, xo + 512)])
                    if tt == bg_after and bg is not None:
                        bg_list.extend(bg())
                while bg_list:
                    bg_list.pop(0)()

            def ffn(l, pre_done=False, next_l=None):
                for half in range(2):
                    tok0 = 1024 * half
                    if half == 0 and not pre_done:
                        prenorm(l, 2, 0, 2)
                    if half == 0:
                        P.op("dve", I("memset", HALO[:, :], 0.0), writes=[("HALO", 0, 88)])
                    pend = []
                    ptc = [0]
                    for i in range(22):
                        woff = W.next("wup%d" % l, i, 2048)
                        for tt in range(2):
                            t0 = tok0 + 512 * tt
                            pss = []
                            for wh in range(2):
                                ps, psn = ps_up()
                                pss.append((ps, psn))
                                for kc in range(8):
                                    lo = woff + kc * 256 + wh * 128
                                    xo = kc * S + t0
                                    P.op("pe", I("matmul", ps[:, :], lhsT=WB[:, lo:lo + 128], rhs=XN[:, xo:xo + 512],
                                                 start=(kc == 0), stop=(kc == 7)),
                                         reads=[("WB", lo, lo + 128), ("XN", xo, xo + 512)], writes=[(psn, 0, 512)],
                                         inc=(kc == 7))
                            tos = []
                            for wh in range(2):
                                ps, psn = pss[wh]
                                ch = 2 * i + wh
                                dwo = l * 132 + ch * 3
                                zo = (wh * 2 + (tt % 2)) * ST
                                to = (4 + wh * 3 + (ptc[0] % 3)) * ST
                                tos.append(to)
                                if tt == 0 and wh == 0:
                                    zv = SCR[:, zo:zo + 4 * ST].rearrange("p (w r) -> p w r", w=2)[:, :, 0:2]
                                    hv = HALO[:, ch * 2:ch * 2 + 4].rearrange("p (w r) -> p w r", w=2)
                                    P.op("dve", I("tensor_copy", out=zv, in_=hv),
                                         reads=[("HALO", ch * 2, ch * 2 + 4)],
                                         writes=[("SCR", zo, zo + 2), ("SCR", zo + 2 * ST, zo + 2 * ST + 2)])
                                P.op("act", I("activation", out=SCR[:, zo + 2:zo + 514], in_=ps[:, :], func=AF.Copy),
                                     reads=[(psn, 0, 512)], writes=[("SCR", zo + 2, zo + 514)])
                                P.op("act", I("activation", out=SCR[:, to:to + 512], in_=ps[:, :], func=AF.Copy,
                                              scale=DWV[:, dwo + 2:dwo + 3]),
                                     reads=[(psn, 0, 512), ("DWV", 0, 528)], writes=[("SCR", to, to + 512)])
                            for f in pend:
                                f()
                            pend = []
                            ch = 2 * i
                            zo = (tt % 2) * ST
                            zno = ((tt + 1) % 2) * ST
                            tv = SCR[:, zo:zo + 4 * ST].rearrange("p (w r) -> p w r", w=2)[:, :, 512:514]
                            t_rg = [("SCR", zo + 512, zo + 514), ("SCR", zo + 2 * ST + 512, zo + 2 * ST + 514)]
                            if tt == 0:
                                nv = SCR[:, zno:zno + 4 * ST].rearrange("p (w r) -> p w r", w=2)[:, :, 0:2]
                                P.op("dve", I("tensor_copy", out=nv, in_=tv), reads=t_rg,
                                     writes=[("SCR", zno, zno + 2), ("SCR", zno + 2 * ST, zno + 2 * ST + 2)])
                            elif half == 0:
                                hv = HALO[:, ch * 2:ch * 2 + 4].rearrange("p (w r) -> p w r", w=2)
                                P.op("dve", I("tensor_copy", out=hv, in_=tv), reads=t_rg,
                                     writes=[("HALO", ch * 2, ch * 2 + 4)])
                            for wh in range(2):
                                for k in (1, 0):
                                    ch = 2 * i + wh
                                    dwo = l * 132 + ch * 3
                                    zo = (wh * 2 + (tt % 2)) * ST
                                    to = tos[wh]
                                    P.op("dve", I("scalar_tensor_tensor", out=SCR[:, to:to + 512], in0=SCR[:, zo + k:zo + k + 512],
                                                  scalar=DWV[:, dwo + k:dwo + k + 1], in1=SCR[:, to:to + 512],
                                                  op0=ALU.mult, op1=ALU.add),
                                         reads=[("SCR", zo + k, zo + k + 512), ("SCR", to, to + 512), ("DWV", 0, 528)],
                                         writes=[("SCR", to, to + 512)])

                            def fin(tg=tos[0], tu=tos[1], ho=i * 1024 + 512 * tt):
                                P.op("act", I("activation", out=SCR[:, tg:tg + 512], in_=SCR[:, tg:tg + 512], func=AF.Silu),
                                     reads=[("SCR", tg, tg + 512)], writes=[("SCR", tg, tg + 512)])
                                P.op("dve", I("tensor_tensor", out=BIG[:, ho:ho + 512], in0=SCR[:, tg:tg + 512],
                                              in1=SCR[:, tu:tu + 512], op=ALU.mult),
                                     reads=[("SCR", tg, tg + 512), ("SCR", tu, tu + 512)], writes=[("BIG", ho, ho + 512)])
                            pend.append(fin)
                            ptc[0] += 1
                    for f in pend:
                        f()
                    if half == 0:
                        ffn_down(l, 0, bg=prenorm_units(l, 2, 1024, 2, sqt=(2, 3, 4)))
                        if next_l is None:
                            store_out(0, 1024, "ost0")
                    elif next_l is not None:
                        ffn_down(l, 1, bg=prenorm_units(next_l, 0, 0, 2, sqt=(2, 3, 4)))
                    else:
                        ffn_down(l, 1)

            def ffn_down(l, half, bg=()):
                tok0 = 1024 * half
                go = (l * 4 + 3) * 8
                SSb = [(PS[6], "PS6"), (PS[7], "PS7")]
                bg = list(bg)

                def hs(tt, m):
                    if tt == 0:
                        return scr(m)
                    o = m * S + 1024 * half
                    return XNf[:, o // 2:o // 2 + 512], ("XN", o, o + 1024)
                pend = None
                for m in range(8):
                    woff = W.next("wdn%d" % l, m, 2816)
                    for tt in range(2):
                        ps, psn = ps_mm()
                        for kc in range(22):
                            lo = woff + kc * 128
                            io = kc * 1024 + 512 * tt
                            P.op("pe", I("matmul", ps[:, :], lhsT=WB[:, lo:lo + 128], rhs=BIG[:, io:io + 512],
                                         start=(kc == 0), stop=(kc == 21)),
                                 reads=[("WB", lo, lo + 128), ("BIG", io, io + 512)], writes=[(psn, 0, 512)],
                                 inc=(kc == 21))
                        if pend is not None:
                            pend()
                        hs_ap, hs_rg = hs(tt, m)
                        P.op("act", I("activation", out=hs_ap, in_=ps[:, :], func=AF.Copy),
                             reads=[(psn, 0, 512)], writes=[hs_rg])
                        pend = add_sq(ps[:, :], (psn, 0, 512), m == 0, m == 7, SSb[tt][0], SSb[tt][1], defer=True)
                        for _ in range(2):
                            if bg:
                                bg.pop(0)()
                pend()
                while bg:
                    bg.pop(0)()
                for tt in (0, 1):
                    t0 = tok0 + 512 * tt
                    rs_ap, rs_rg = stats_to_rs(SSb[tt][0], SSb[tt][1])
                    for m in range(8):
                        hs_ap, hs_rg = hs(tt, m)
                        P.op("dve", I("scalar_tensor_tensor", out=hs_ap, in0=hs_ap, scalar=GV[:, go + m:go + m + 1],
                                      in1=rs_ap, op0=ALU.mult, op1=ALU.mult),
                             reads=[hs_rg, rs_rg, ("GV", 0, 128)], writes=[hs_rg])
                        xo = m * S + t0
                        eng = "dve"
                        P.op(eng, I("tensor_tensor", out=X[:, xo:xo + 512], in0=X[:, xo:xo + 512], in1=hs_ap, op=ALU.add),
                             reads=[("X", xo, xo + 512), hs_rg], writes=[("X", xo, xo + 512)])

            def mask_setup():
                io_ap, io_rg = scr(9, 0, 128)
                P.op("pool", I("iota", io_ap, [[1, 128]], base=0, channel_multiplier=-1,
                               allow_small_or_imprecise_dtypes=True), writes=[io_rg])
                vc, vc_rg = scr(8, 0, 128)
                vp, vp_rg = scr(8, 128, 128)
                dc, dc_rg = scr(8, 256, 128)
                dp, dp_rg = scr(8, 384, 128)
                P.op("dve", I("tensor_single_scalar", out=vc, in_=io_ap, scalar=0.0, op=ALU.is_ge), reads=[io_rg], writes=[vc_rg])
                P.op("dve", I("tensor_single_scalar", out=vp, in_=io_ap, scalar=0.0, op=ALU.is_le), reads=[io_rg], writes=[vp_rg])
                P.op("dve", I("tensor_single_scalar", out=dc, in_=io_ap, scalar=0.0, op=ALU.max), reads=[io_rg], writes=[dc_rg])
                P.op("dve", I("tensor_scalar", out=dp, in0=io_ap, scalar1=0.0, scalar2=128.0, op0=ALU.min, op1=ALU.add),
                     reads=[io_rg], writes=[dp_rg])

            mk = [0]

            def build_mask(g, c):
                vc, vc_rg = scr(8, 0, 128)
                vp, vp_rg = scr(8, 128, 128)
                dc, dc_rg = scr(8, 256, 128)
                dp, dp_rg = scr(8, 384, 128)
                for s_ in range(2):
                    h = 2 * c + s_
                    sd = float(2.0 ** (-8.0 * (g * 8 + h + 1) / 24.0)) * DIL[g]
                    mo = (g * 4 + c) * 512
                    for (d_ap, d_rg, v_ap, v_rg, co) in ((dc, dc_rg, vc, vc_rg, s_ * 128), (dp, dp_rg, vp, vp_rg, 256 + s_ * 128)):
                        t_ap, t_rg = scr(9, 128 + 128 * (mk[0] % 2), 128)
                        mk[0] += 1
                        P.op("act", I("activation", out=t_ap, in_=d_ap, func=AF.Exp, scale=-sd), reads=[d_rg], writes=[t_rg])
                        P.op("dve", I("tensor_tensor", out=MASK[:, mo + co:mo + co + 128], in0=t_ap, in1=v_ap, op=ALU.mult),
                             reads=[t_rg, v_rg], writes=[("MASK", mo + co, mo + co + 128)])

            QO, KAO, KBO, VO, MO = 0, 2048, 4096, 6144, 10240

            def attention(l, pre_tiles=0):
                prenorm(l, 0, 512 * pre_tiles, 4 - pre_tiles)
                first_attn = not masks_built[0]
                if first_attn:
                    mask_setup()
                    masks_built[0] = True
                P.op("dve", I("memset", BIG[:, KAO:MO], 0.0), writes=[("BIG", KAO, MO)])
                Vv = BIG[:, VO:MO].rearrange("p (b s c) -> p b s c", b=16, s=2)
                for c in range(4):
                    for g in range(3):
                        Dl = DIL[g]
                        nb = (S // Dl) // 128
                        woff = W.next("wqkv%d" % l, g * 4 + c, 3072)
                        if first_attn:
                            build_mask(g, c)
                        for wh in range(2):
                            for tt in range(4):
                                ps, psn = ps_mm()
                                for kc in range(8):
                                    lo = woff + kc * 384 + wh * 128
                                    xo = kc * S + 512 * tt
                                    P.op("pe", I("matmul", ps[:, :], lhsT=WB[:, lo:lo + 128], rhs=XN[:, xo:xo + 512],
                                                 start=(kc == 0), stop=(kc == 7)),
                                         reads=[("WB", lo, lo + 128), ("XN", xo, xo + 512)], writes=[(psn, 0, 512)],
                                         inc=(kc == 7))
                                nu = 512 // Dl
                                u0 = tt * nu

                                def views(dst_rows, base):
                                    if Dl == 1:
                                        return (BIG[dst_rows, base + 512 * tt:base + 512 * tt + 512], ps[dst_rows, :])
                                    ov = BIG[dst_rows, base:base + S].rearrange("p (r u) -> p u r", r=Dl)[:, u0:u0 + nu, :]
                                    iv = ps[dst_rows, :].rearrange("p (u r) -> p u r", r=Dl)
                                    return ov, iv
                                if wh == 0:
                                    ov, iv = views(slice(0, 128), QO)
                                    P.op("act", I("activation", out=ov, in_=iv, func=AF.Copy),
                                         reads=[(psn, 0, 512)], writes=[("BIG", QO, QO + S)])
                                else:
                                    ov, iv = views(slice(0, 64), KAO)
                                    P.op("act", I("activation", out=ov, in_=iv, func=AF.Copy), reads=[(psn, 0, 512)], writes=[("BIG", KAO, KAO + S)])
                                    ov, iv = views(slice(64, 128), KBO)
                                    P.op("dve", I("tensor_copy", out=ov, in_=iv), reads=[(psn, 0, 512)], writes=[("BIG", KBO, KBO + S)])
                        for b0 in range(0, 16, 4):
                            ps, psn = ps_mm()
                            for bb in range(4):
                                b = b0 + bb
                                r, n = b // nb, b % nb
                                stok = r + Dl * 128 * n
                                for kc in range(8):
                                    a0 = kc * S + stok
                                    a1 = a0 + Dl * 127 + 1
                                    lo = woff + kc * 384 + 256
                                    P.op("pe", I("matmul", ps[:, bb * 128:(bb + 1) * 128], lhsT=XN[:, a0:a1:Dl], rhs=WB[:, lo:lo + 128],
                                                 start=(kc == 0), stop=(kc == 7)),
                                         reads=[("XN", a0, a1), ("WB", lo, lo + 128)], writes=[(psn, bb * 128, (bb + 1) * 128)],
                                         inc=(bb == 3 and kc == 7))
                            psv = ps[:, :].rearrange("p (b c) -> p b c", b=4)
                            vlo = VO + b0 * 256
                            P.op("act", I("activation", out=Vv[:, b0:b0 + 4, 0, 0:64], in_=psv[:, :, 0:64], func=AF.Copy),
                                 reads=[(psn, 0, 512)], writes=[("BIG", vlo, vlo + 1024)])
                            P.op("dve", I("tensor_copy", out=Vv[:, b0:b0 + 4, 1, 64:128], in_=psv[:, :, 64:128]),
                                 reads=[(psn, 0, 512)], writes=[("BIG", vlo, vlo + 1024)])
                        mo = (g * 4 + c) * 512

                        def stage1(b):
                            r, n = b // nb, b % nb
                            with_prev = (n != 0)
                            ncols = 512 if with_prev else 256
                            ps, psn = ps_mm()
                            mms = [(0, KAO + 128 * b), (128, KBO + 128 * b)]
                            if with_prev:
                                mms += [(256, KAO + 128 * (b - 1)), (384, KBO + 128 * (b - 1))]
                            qo = QO + 128 * b
                            for idx, (co, ko) in enumerate(mms):
                                P.op("pe", I("matmul", ps[:, co:co + 128], lhsT=BIG[:, ko:ko + 128], rhs=BIG[:, qo:qo + 128],
                                             start=True, stop=True),
                                     reads=[("BIG", ko, ko + 128), ("BIG", qo, qo + 128)], writes=[(psn, co, co + 128)],
                                     inc=(idx == len(mms) - 1))
                            k2 = ei[0] % 3
                            ei[0] += 1
                            e_ap, e_rg = sb16(k2, ncols)
                            pt_ap, pt_rg = sb16(3 + k2, ncols)
                            P.op("act", I("activation", out=e_ap, in_=ps[:, 0:ncols], func=AF.Exp, scale=0.125),
                                 reads=[(psn, 0, ncols)], writes=[e_rg])
                            P.op("dve", I("tensor_tensor", out=pt_ap, in0=e_ap, in1=MASK[:, mo:mo + ncols], op=ALU.mult),
                                 reads=[e_rg, ("MASK", mo, mo + ncols)], writes=[pt_rg])
                            return (b, r, n, with_prev, (3 + k2) * 512, pt_rg)

                        def stage2(st1):
                            b, r, n, with_prev, ptb, pt_rg = st1
                            puz, puzn = ps_aux()
                            terms = [(b, 0, 0), (b, 1, 128)]
                            if with_prev:
                                terms += [(b - 1, 0, 256), (b - 1, 1, 384)]
                            for k, (vb, s_, pco) in enumerate(terms):
                                vlo = VO + vb * 256 + s_ * 128
                                P.op("pe", I("matmul", puz[:, 0:128], lhsT=Vv[:, vb, s_, :], rhs=SB16[:, ptb + pco:ptb + pco + 128],
                                             start=(k == 0), stop=(k == len(terms) - 1)),
                                     reads=[("BIG", vlo, vlo + 128), pt_rg], writes=[(puzn, 0, 128)], inc=False)
                            for k, (vb, s_, pco) in enumerate(terms):
                                P.op("pe", I("matmul", puz[:, 128:256], lhsT=ONESAB[:, s_ * 128:(s_ + 1) * 128],
                                             rhs=SB16[:, ptb + pco:ptb + pco + 128],
                                             start=(k == 0), stop=(k == len(terms) - 1)),
                                     reads=[("ONESAB", 0, 256), pt_rg], writes=[(puzn, 128, 256)],
                                     inc=(k == len(terms) - 1))
                            stok = r + Dl * 128 * n
                            e1 = stok + Dl * 127 + 1
                            dst = SCR[:, 0:4096].rearrange("p (z t) -> p z t", z=2)[:, :, stok:e1:Dl]
                            src = puz[:, 0:256].rearrange("p (z q) -> p z q", z=2)
                            rgs = [("SCR", stok, e1), ("SCR", 2048 + stok, 2048 + e1)]
                            if g == 0:
                                P.op("act", I("activation", out=dst, in_=src, func=AF.Copy), reads=[(puzn, 0, 256)], writes=rgs)
                            else:
                                P.op("dve", I("tensor_tensor", out=dst, in0=src, in1=dst, op=ALU.add),
                                     reads=[(puzn, 0, 256)] + rgs, writes=rgs)
                        sts = [stage1(0), stage1(1)]
                        for b in range(2, 16):
                            sts.append(stage1(b))
                            stage2(sts.pop(0))
                        stage2(sts.pop(0))
                        stage2(sts.pop(0))
                    mgo = MO + c * S
                    for q4 in range(4):
                        a0, a1 = 512 * q4, 512 * q4 + 512
                        P.op("act", I("activation", out=SCR[:, 2048 + a0:2048 + a1], in_=SCR[:, 2048 + a0:2048 + a1], func=AF.Ln),
                             reads=[("SCR", 2048 + a0, 2048 + a1)], writes=[("SCR", 2048 + a0, 2048 + a1)])
                        P.op("act", I("activation", out=SCR[:, 2048 + a0:2048 + a1], in_=SCR[:, 2048 + a0:2048 + a1], func=AF.Exp, scale=-1.0),
                             reads=[("SCR", 2048 + a0, 2048 + a1)], writes=[("SCR", 2048 + a0, 2048 + a1)])
                        P.op("dve", I("tensor_tensor", out=BIG[:, mgo + a0:mgo + a1], in0=SCR[:, a0:a1],
                                      in1=SCR[:, 2048 + a0:2048 + a1], op=ALU.mult),
                             reads=[("SCR", a0, a1), ("SCR", 2048 + a0, 2048 + a1)], writes=[("BIG", mgo + a0, mgo + a1)])
                proj_postnorm(l, 1, "wo%d" % l, 4, 4, MO, S, 0, 0, 4, bg=lambda: prenorm_units(l, 2, 0, 2, sqt=(2, 3, 4)))

            def convmix(l, pre_tiles=0):
                prenorm(l, 0, 512 * pre_tiles, 4 - pre_tiles)
                for i in range(8):
                    woff = W.next("cwin", i, 3072)
                    P.op("dve", I("memset", SCR[:, 0:2], 0.0), writes=[("SCR", 0, 2)])
                    for tt in range(4):
                        t0 = 512 * tt
                        pss = []
                        for j in range(3):
                            ps, psn = ps_mm()
                            pss.append((ps, psn))
                            for kc in range(8):
                                lo = woff + kc * 384 + j * 128
                                xo = kc * S + t0
                                P.op("pe", I("matmul", ps[:, :], lhsT=WB[:, lo:lo + 128], rhs=XN[:, xo:xo + 512],
                                             start=(kc == 0), stop=(kc == 7)),
                                     reads=[("WB", lo, lo + 128), ("XN", xo, xo + 512)], writes=[(psn, 0, 512)],
                                     inc=(kc == 7))
                        (pB, pBn), (pC, pCn), (pH, pHn) = pss
                        hb, hb_rg = scr(8)
                        bb_, bb_rg = scr(9)
                        P.op("act", I("activation", out=hb, in_=pH[:, :], func=AF.Copy), reads=[(pHn, 0, 512)], writes=[hb_rg])
                        P.op("act", I("activation", out=bb_, in_=pB[:, :], func=AF.Copy), reads=[(pBn, 0, 512)], writes=[bb_rg])
                        zo = (tt % 2) * ST
                        zno = ((tt + 1) % 2) * ST
                        P.op("dve", I("tensor_tensor", out=SCR[:, zo + 2:zo + 514], in0=pC[:, :], in1=hb, op=ALU.mult),
                             reads=[(pCn, 0, 512), hb_rg], writes=[("SCR", zo + 2, zo + 514)])
                        if tt < 3:
                            P.op("dve", I("tensor_copy", out=SCR[:, zno:zno + 2], in_=SCR[:, zo + 512:zo + 514]),
                                 reads=[("SCR", zo + 512, zo + 514)], writes=[("SCR", zno, zno + 2)])
                        to = (2 + tt % 2) * ST
                        P.op("dve", I("tensor_scalar", out=SCR[:, to:to + 512], in0=SCR[:, zo + 2:zo + 514],
                                      scalar1=CDW[:, i * 3 + 2:i * 3 + 3], scalar2=None, op0=ALU.mult),
                             reads=[("SCR", zo + 2, zo + 514), ("CDW", 0, 24)], writes=[("SCR", to, to + 512)])
                        for k in (1, 0):
                            P.op("dve", I("scalar_tensor_tensor", out=SCR[:, to:to + 512], in0=SCR[:, zo + k:zo + k + 512],
                                          scalar=CDW[:, i * 3 + k:i * 3 + k + 1], in1=SCR[:, to:to + 512],
                                          op0=ALU.mult, op1=ALU.add),
                                 reads=[("SCR", zo + k, zo + k + 512), ("SCR", to, to + 512), ("CDW", 0, 24)],
                                 writes=[("SCR", to, to + 512)])
                        yo = i * S + t0
                        P.op("dve", I("tensor_tensor", out=BIG[:, yo:yo + 512], in0=SCR[:, to:to + 512], in1=bb_, op=ALU.mult),
                             reads=[("SCR", to, to + 512), bb_rg], writes=[("BIG", yo, yo + 512)])
                proj_postnorm(l, 1, "cwout", 8, 2, 0, S, 0, 0, 4, bg=lambda: prenorm_units(l, 2, 0, 2, sqt=(2, 3, 4)))

            def poolmix(l, pre_tiles=0):
                prenorm(l, 0, 512 * pre_tiles, 4 - pre_tiles)
                CS0 = 2096
                PO = 16384
                SBf = SB16.bitcast(F32)
                ZT = SBf[:, 0:512]
                ZT_rg = ("SB16", 0, 1024)
                TM = HALO[:, 0:16]
                TM_rg = ("HALO", 0, 16)
                P.op("dve", I("memset", SCR[:, 2080:2096], 0.0), writes=[("SCR", 2080, 2096)])
                P.op("dve", I("memset", ZT, 0.0), writes=[ZT_rg])
                chunk_i = [0]
                for t in range(16):
                    P.op("dve", I("memset", INVT[:, t:t + 1], 1.0 / (t + 1)), writes=[("INVT", t, t + 1)])
                PSETS = [[(BIG, "BIG", 16384), (BIG, "BIG", 18432)], [(BIG, "BIG", 20480), (SB16, "SB16", 1024)]]

                def inproj(gi):
                    w = 2 ** (gi + 1)
                    woff = W.next("pwin", gi, 2048)
                    for cc in range(2):
                        UB0 = 0 if chunk_i[0] % 2 == 0 else 4160
                        chunk_i[0] += 1
                        for tt in range(4):
                            ps, psn = ps_mm()
                            for kc in range(8):
                                lo = woff + kc * 256 + cc * 128
                                xo = kc * S + 512 * tt
                                P.op("pe", I("matmul", ps[:, :], lhsT=WB[:, lo:lo + 128], rhs=XN[:, xo:xo + 512],
                                             start=(kc == 0), stop=(kc == 7)),
                                     reads=[("WB", lo, lo + 128), ("XN", xo, xo + 512)], writes=[(psn, 0, 512)],
                                     inc=(kc == 7))
                            P.op("act", I("activation", out=SCR[:, UB0 + 512 * tt:UB0 + 512 * tt + 512], in_=ps[:, :], func=AF.Copy),
                                 reads=[(psn, 0, 512)], writes=[("SCR", UB0 + 512 * tt, UB0 + 512 * tt + 512)])
                        for q in range(4):
                            co = CS0 + 512 * q
                            init = 0.0 if q == 0 else SCR[:, co - 1:co]
                            P.op("dve", I("tensor_tensor_scan", out=SCR[:, co:co + 512], data0=SCR[:, UB0 + 512 * q:UB0 + 512 * q + 512],
                                          data1=ZT, initial=init, op0=ALU.add, op1=ALU.add),
                                 reads=[("SCR", UB0 + 512 * q, UB0 + 512 * q + 512), ZT_rg, ("SCR", co - 1, co)],
                                 writes=[("SCR", co, co + 512)])
                        P.op("dve", I("tensor_tensor", out=TM, in0=SCR[:, CS0:CS0 + 16], in1=INVT[:, 0:16], op=ALU.mult),
                             reads=[("SCR", CS0, CS0 + 16), ("INVT", 0, 16)], writes=[TM_rg])
                        P.op("dve", I("tensor_tensor", out=TM, in0=TM, in1=SCR[:, UB0:UB0 + 16], op=ALU.subtract),
                             reads=[TM_rg, ("SCR", UB0, UB0 + 16)], writes=[TM_rg])
                        P.op("dve", I("scalar_tensor_tensor", out=SCR[:, UB0:UB0 + S], in0=SCR[:, CS0:CS0 + S], scalar=1.0 / w,
                                      in1=SCR[:, UB0:UB0 + S], op0=ALU.mult, op1=ALU.subtract),
                             reads=[("SCR", CS0, CS0 + S), ("SCR", UB0, UB0 + S)], writes=[("SCR", UB0, UB0 + S)])
                        pt_, pn_, po = PSETS[gi % 2][cc]
                        P.op("dve", I("scalar_tensor_tensor", out=pt_[:, po:po + S], in0=SCR[:, CS0 - w:CS0 - w + S], scalar=-1.0 / w,
                                      in1=SCR[:, UB0:UB0 + S], op0=ALU.mult, op1=ALU.add),
                             reads=[("SCR", CS0 - w, CS0 - w + S), ("SCR", UB0, UB0 + S)], writes=[(pn_, po, po + S)])
                        P.op("dve", I("tensor_copy", out=pt_[:, po:po + w - 1], in_=HALO[:, 0:w - 1]),
                             reads=[TM_rg], writes=[(pn_, po, po + w - 1)])

                def grp(gi):
                    goff = W.next("pwgrp", gi, 512)
                    for mm in range(2):
                        for tt in range(4):
                            ps, psn = ps_mm()
                            for kc in range(2):
                                lo = goff + kc * 256 + mm * 128
                                pt_, pn_, po = PSETS[gi % 2][kc]
                                io = po + 512 * tt
                                P.op("pe", I("matmul", ps[:, :], lhsT=WB[:, lo:lo + 128], rhs=pt_[:, io:io + 512],
                                             start=(kc == 0), stop=(kc == 1)),
                                     reads=[("WB", lo, lo + 128), (pn_, io, io + 512)], writes=[(psn, 0, 512)],
                                     inc=(kc == 1))
                            ch = 2 * gi + mm
                            yo = ch * S + 512 * tt
                            P.op("act", I("activation", out=BIG[:, yo:yo + 512], in_=ps[:, :], func=AF.Copy, scale=PSC[:, ch:ch + 1]),
                                 reads=[(psn, 0, 512), ("PSC", 0, 8)], writes=[("BIG", yo, yo + 512)])
                inproj(0)
                for gi in range(4):
                    if gi + 1 < 4:
                        inproj(gi + 1)
                    grp(gi)
                proj_postnorm(l, 1, "pwout", 8, 2, 0, S, 0, 0, 4, bg=lambda: prenorm_units(l, 2, 0, 2, sqt=(2, 3, 4)))

            def store_out(h0, h1, strm):
                for c in range(8):
                    P.op("sp", I("dma_start", out=yT[c * 128:(c + 1) * 128, h0:h1], in_=X[:, c * S + h0:c * S + h1]),
                         reads=[("X", c * S + h0, c * S + h1)], stream=strm, amt=16)

            run_layers = EXEC_LAYERS if EXEC_LAYERS is not None else layers
            masks_built = [False]
            for li, l in enumerate(run_layers):
                kind = l % 3
                pre_tiles = 2 if li > 0 else 0
                if kind == 0:
                    attention(l, pre_tiles)
                elif kind == 1:
                    convmix(l, pre_tiles)
                else:
                    poolmix(l, pre_tiles)
                ffn(l, pre_done=True, next_l=(run_layers[li + 1] if li + 1 < len(run_layers) else None))
            store_out(1024, 1536, "ost1")
            store_out(1536, 2048, "ost2")
            P.wait_all("sp", ["ost0", "ost1", "ost2"])

        def t_name(t, nm):
            return {"gvec": "GV", "dwv": "DWV", "cdw": "CDW", "psc": "PSC"}[nm]

        Wd = WMgr(DummyProg(), dram, WB, None)
        construct(DummyProg(), Wd)
        P = Prog()
        P.unordered.update(["xld0", "xld1", "xld2", "cst", "ost0", "ost1", "ost2"])
        W = WMgr(P, dram, WB, Wd.rec)
        construct(P, W)
        assert W.used == len(Wd.rec) and W.issued == len(Wd.rec)
        P.emit(nc, st)
    return nc


def chunkmajor(Wm, cols):
    sub = Wm[:, cols]
    nk = Wm.shape[0] // 128
    return np.ascontiguousarray(sub.reshape(nk, 128, -1).transpose(1, 0, 2).reshape(128, -1))


def host_layout(inp):
    f = lambda a: np.asarray(a, dtype=np.float32)
    out = {}
    ng = f(inp["norm_g"])
    out["gvec"] = np.ascontiguousarray(ng.reshape(16, 8, 128).transpose(2, 0, 1).reshape(128, 128))
    dw = f(inp["ffn_w_dw"])
    out["dwv"] = np.ascontiguousarray(dw.reshape(4, 3, 2, 22, 128).transpose(4, 0, 3, 2, 1).reshape(128, 528))
    cdw = f(inp["conv_w_dw"])[0]
    out["cdw"] = np.ascontiguousarray(cdw.reshape(3, 8, 128).transpose(2, 1, 0).reshape(128, 24))
    out["psc"] = np.ascontiguousarray(f(inp["pool_scale"])[0].reshape(8, 128).T)
    ar = np.arange
    for l in range(4):
        wu = f(inp["ffn_w_up"])[l]
        out["wup%d" % l] = np.stack([chunkmajor(wu, np.concatenate([ar(128 * i, 128 * i + 128), ar(2816 + 128 * i, 2816 + 128 * i + 128)]))
                                     for i in range(22)])
        wd = f(inp["ffn_w_down"])[l]
        out["wdn%d" % l] = np.stack([chunkmajor(wd, ar(128 * m, 128 * m + 128)) for m in range(8)])
    for ia, l in enumerate((0, 3)):
        wq = f(inp["attn_w_qkv"])[ia]
        blks = []
        for g in range(3):
            for c in range(4):
                base = g * 1536 + c * 128
                blks.append(chunkmajor(wq, np.concatenate([ar(base, base + 128), ar(base + 512, base + 640), ar(base + 1024, base + 1152)])))
        out["wqkv%d" % l] = np.stack(blks)
        wo = f(inp["attn_w_o"])[ia]
        out["wo%d" % l] = np.stack([chunkmajor(wo, ar(512 * b, 512 * b + 512)) for b in range(2)])
    cw = f(inp["conv_w_in"])[0]
    out["cwin"] = np.stack([chunkmajor(cw, np.concatenate([ar(128 * i, 128 * i + 128), ar(1024 + 128 * i, 1152 + 128 * i), ar(2048 + 128 * i, 2176 + 128 * i)]))
                            for i in range(8)])
    for nm, key in (("cwout", "conv_w_out"), ("pwin", "pool_w_in"), ("pwout", "pool_w_out")):
        wm = f(inp[key])[0]
        out[nm] = np.stack([chunkmajor(wm, ar(256 * b, 256 * b + 256)) for b in range(4)])
    wg = f(inp["pool_w_grp"])[0]
    out["pwgrp"] = np.stack([chunkmajor(wg[g], ar(0, 256)) for g in range(4)])
    return out


_PROG_CACHE = {}


def _run(layers, xT_list, lay):
    key = tuple(layers)
    if key not in _PROG_CACHE:
        _PROG_CACHE[key] = build_program(list(layers))
    nc = _PROG_CACHE[key]
    names = ["gvec", "dwv", "cdw", "psc"]
    for l in layers:
        names += ["wup%d" % l, "wdn%d" % l]
        kind = l % 3
        if kind == 0:
            names += ["wqkv%d" % l, "wo%d" % l]
        elif kind == 1:
            names += ["cwin", "cwout"]
        else:
            names += ["pwin", "pwgrp", "pwout"]
    in_maps = []
    for b in range(8):
        m = {n: lay[n] for n in names}
        m["xT"] = xT_list[b]
        in_maps.append(m)
    res = run_bass_kernel_spmd(nc, in_maps, core_ids=list(range(8)))
    return [np.asarray(r["yT"]) for r in res.results]


def kernel(**inputs):
    lay = host_layout(inputs)
    x = np.asarray(inputs["x"], dtype=np.float32)
    xT = [np.ascontiguousarray(x[b].T) for b in range(8)]
    if FUSED:
        yT = _run((0, 1, 2, 3), xT, lay)
    else:
        yT = xT
        for l in range(4):
            yT = _run((l,), yT, lay)
    return np.ascontiguousarray(np.stack([y.T for y in yT]).astype(np.float32))
```

```python
import numpy as np
import concourse.bass as bass
import concourse.mybir as mybir
from concourse.bass_utils import run_bass_kernel_spmd
from contextlib import ExitStack

F32 = mybir.dt.float32
BF16 = mybir.dt.bfloat16
AF = mybir.ActivationFunctionType
ALU = mybir.AluOpType

S = 2048
EPS = 1e-6
DIL = (1, 4, 16)
NSLOT = 3
SLOT = 3072
ST = 520
NT = 12
FUSED = True
EXEC_LAYERS = None


class Prog:
    ENG = ("pe", "act", "dve", "pool", "sp")

    def __init__(self):
        self.q = {e: [] for e in self.ENG}
        self.cnt = {}
        self.waited = {e: {} for e in self.ENG}
        self.recs = {}
        self.unordered = set()
        self.pe_pending = False
        self.nops = 0

    def _overlaps(self, name, lo, hi):
        return [r for r in self.recs.get(name, ()) if r[0] < hi and lo < r[1]]

    def op(self, eng, fn, reads=(), writes=(), inc=True, stream=None, amt=1):
        own = stream if stream is not None else eng
        if self.pe_pending and eng != "pe":
            raise RuntimeError("non-PE op constructed inside an open PE group")
        deps = {}

        def add(dep):
            if dep is None:
                return
            s, c = dep
            if s in self.unordered:
                c = self.cnt.get(s, 0)
            if eng == "pe" and s == "pe":
                return
            if deps.get(s, 0) < c:
                deps[s] = c
        for (name, lo, hi) in reads:
            for r in self._overlaps(name, lo, hi):
                add(r[2])
        for (name, lo, hi) in writes:
            for r in self._overlaps(name, lo, hi):
                add(r[2])
                for s, c in r[3].items():
                    add((s, c))
        for s, c in deps.items():
            if self.waited[eng].get(s, 0) < c:
                self.waited[eng][s] = c
                self.q[eng].append(("wait", s, c))
        after = self.cnt.get(own, 0) + amt
        if inc:
            self.cnt[own] = after
            if eng == "pe":
                self.pe_pending = False
        else:
            assert eng == "pe"
            self.pe_pending = True
        self.q[eng].append(("op", fn, own if inc else None, amt))
        self.nops += 1
        for (name, lo, hi) in reads:
            lst = self.recs.setdefault(name, [])
            for r in lst:
                if r[0] == lo and r[1] == hi and r[2] is None:
                    r[3][own] = after
                    break
            else:
                lst.append([lo, hi, None, {own: after}])
        for (name, lo, hi) in writes:
            lst = self.recs.setdefault(name, [])
            lst[:] = [r for r in lst if not (lo <= r[0] and r[1] <= hi)]
            lst.append([lo, hi, (own, after), {}])

    def wait_all(self, eng, streams):
        for s in streams:
            c = self.cnt.get(s, 0)
            if c and self.waited[eng].get(s, 0) < c:
                self.waited[eng][s] = c
                self.q[eng].append(("wait", s, c))

    def emit(self, nc, stack):
        assert not self.pe_pending
        sems = {s: stack.enter_context(nc.semaphore("s_" + s)) for s in self.cnt}
        block = stack.enter_context(nc.Block())

        def replay(e, name):
            for it in self.q[name]:
                if it[0] == "wait":
                    e.wait_ge(sems[it[1]], it[2])
                else:
                    ins = it[1](e)
                    if it[2] is not None:
                        ins.then_inc(sems[it[2]], it[3])

        @block.tensor
        def _(e):
            replay(e, "pe")

        @block.scalar
        def _(e):
            replay(e, "act")

        @block.vector
        def _(e):
            replay(e, "dve")

        @block.gpsimd
        def _(e):
            replay(e, "pool")

        @block.sync
        def _(e):
            replay(e, "sp")


class DummyProg:
    def op(self, *a, **k):
        pass

    def wait_all(self, *a, **k):
        pass


def I(fn, *a, **k):
    return lambda e: getattr(e, fn)(*a, **k)


class WMgr:
    def __init__(self, P, dram, WB, sched):
        self.P, self.dram, self.WB, self.sched = P, dram, WB, sched
        self.rec = []
        self.issued = 0
        self.used = 0

    def next(self, name, blk, nelem):
        if self.sched is None:
            self.rec.append((name, blk, nelem))
            return 0
        i = self.used
        assert self.sched[i] == (name, blk, nelem), (self.sched[i], name, blk, nelem)
        while self.issued < min(i + NSLOT, len(self.sched)):
            self._issue(self.issued)
            self.issued += 1
        self.used += 1
        return (i % NSLOT) * SLOT

    def _issue(self, j):
        name, blk, nelem = self.sched[j]
        slot = j % NSLOT
        off = slot * SLOT
        self.P.op("pool", I("dma_start", out=self.WB[:, off:off + nelem], in_=self.dram[name][blk]),
                  writes=[("WB", off, off + nelem)], stream="w%d" % slot, amt=16)


def build_program(layers):
    nc = bass.Bass("TRN2", target_bir_lowering=False)
    dram = {}

    def din(name, shape):
        dram[name] = nc.dram_tensor(name, list(shape), F32, kind="ExternalInput").ap()
    din("xT", (1024, S))
    din("gvec", (128, 128))
    din("dwv", (128, 528))
    din("cdw", (128, 24))
    din("psc", (128, 8))
    for l in layers:
        din("wup%d" % l, (22, 128, 2048))
        din("wdn%d" % l, (8, 128, 2816))
        kind = l % 3
        if kind == 0:
            din("wqkv%d" % l, (12, 128, 3072))
            din("wo%d" % l, (2, 128, 2048))
        elif kind == 1:
            din("cwin", (8, 128, 3072))
            din("cwout", (4, 128, 2048))
        else:
            din("pwin", (4, 128, 2048))
            din("pwgrp", (4, 128, 512))
            din("pwout", (4, 128, 2048))
    yT = nc.dram_tensor("yT", [1024, S], F32, kind="ExternalOutput").ap()

    with ExitStack() as st:
        def sb(name, n, dt):
            return st.enter_context(nc.sbuf_tensor(name, [128, n], dt))
        X = sb("X", 8 * S, F32)
        XN = sb("XN", 8 * S, BF16)
        BIG = sb("BIG", 22528, BF16)
        SCR = sb("SCR", NT * ST, F32)
        SB16 = sb("SB16", 6 * 512, BF16)
        WB = sb("WB", NSLOT * SLOT, BF16)
        MASK = sb("MASK", 12 * 512, BF16)
        GV = sb("GV", 128, F32)
        DWV = sb("DWV", 528, F32)
        CDW = sb("CDW", 24, F32)
        PSC = sb("PSC", 8, F32)
        ONES = sb("ONES", 128, BF16)
        ONESAB = sb("ONESAB", 256, BF16)
        HALO = sb("HALO", 88, F32)
        INVT = sb("INVT", 16, F32)
        PS = [st.enter_context(nc.psum_tensor("PS%d" % b, [128, 512], F32)) for b in range(8)]

        def construct(P, W):
            mmi = [0]
            auxi = [0]
            sqi = [0]
            rsi = [0]
            ei = [0]

            def ps_mm():
                b = mmi[0] % 5
                mmi[0] += 1
                return PS[b], "PS%d" % b

            upi = [0]

            def ps_up():
                b = upi[0] % 8
                upi[0] += 1
                return PS[b], "PS%d" % b

            def ps_aux():
                b = (6, 7, 5)[auxi[0] % 3]
                auxi[0] += 1
                return PS[b], "PS%d" % b
            SS, SSn = PS[5], "PS5"

            def scr(i, lo=0, n=512):
                o = i * ST + lo
                return SCR[:, o:o + n], ("SCR", o, o + n)

            def sb16(i, n=512):
                return SB16[:, i * 512:i * 512 + n], ("SB16", i * 512, i * 512 + n)

            for (h0, h1, strm) in ((0, 512, "xld0"), (512, 1024, "xld1"), (1024, 2048, "xld2")):
                for c in range(8):
                    P.op("sp",
                         I("dma_start", out=X[:, c * S + h0:c * S + h1], in_=dram["xT"][c * 128:(c + 1) * 128, h0:h1]),
                         writes=[("X", c * S + h0, c * S + h1)], stream=strm, amt=16)
            for (t, nm, n) in ((GV, "gvec", 128), (DWV, "dwv", 528), (CDW, "cdw", 24), (PSC, "psc", 8)):
                P.op("sp", I("dma_start", out=t[:, :], in_=dram[nm][:, :]), writes=[(t_name(t, nm), 0, n)], stream="cst", amt=16)
            P.op("dve", I("memset", ONES[:, :], 1.0), writes=[("ONES", 0, 128)])
            P.op("dve", I("memset", ONESAB[:, :], 0.0), writes=[("ONESAB", 0, 256)])
            P.op("dve", I("memset", ONESAB[:, 0:64], 1.0), writes=[("ONESAB", 0, 64)])
            P.op("dve", I("memset", ONESAB[:, 192:256], 1.0), writes=[("ONESAB", 192, 256)])

            XNf = XN.bitcast(F32)
            MASKf = MASK.bitcast(F32)

            def add_sq(src_ap, src_rg, first, last, SS=SS, SSn=SSn, defer=False):
                k = sqi[0] % 2
                sqi[0] += 1
                sq_ap, sq_rg = sb16(k)
                P.op("act", I("activation", out=sq_ap, in_=src_ap, func=AF.Square), reads=[src_rg], writes=[sq_rg])

                def pe_part():
                    P.op("pe", I("matmul", SS[:, :], lhsT=ONES[:, :], rhs=sq_ap, start=first, stop=last),
                         reads=[sq_rg, ("ONES", 0, 128)], writes=[(SSn, 0, 512)])
                if defer:
                    return pe_part
                pe_part()
                return None

            def stats_to_rs(SS=SS, SSn=SSn):
                r = 10 + rsi[0] % 2
                rsi[0] += 1
                rs_ap, rs_rg = scr(r)
                P.op("act", I("activation", out=rs_ap, in_=SS[:, :], func=AF.Ln, scale=1.0 / 1024.0, bias=EPS),
                     reads=[(SSn, 0, 512)], writes=[rs_rg])
                P.op("act", I("activation", out=rs_ap, in_=rs_ap, func=AF.Exp, scale=-0.5), reads=[rs_rg], writes=[rs_rg])
                return rs_ap, rs_rg

            def prenorm_units(l, j, tok0, ntiles, sqt=(0, 1)):
                go = (l * 4 + j) * 8
                units = []
                for tt in range(ntiles):
                    t0 = tok0 + 512 * tt
                    rs_box = []

                    def sq(c, t0=t0):
                        o = c * S + t0
                        sq_ap, sq_rg = sb16(sqt[c % len(sqt)])
                        P.op("act", I("activation", out=sq_ap, in_=X[:, o:o + 512], func=AF.Square),
                             reads=[("X", o, o + 512)], writes=[sq_rg])

                    def stat(c):
                        sq_ap, sq_rg = sb16(sqt[c % len(sqt)])
                        P.op("pe", I("matmul", SS[:, :], lhsT=ONES[:, :], rhs=sq_ap, start=(c == 0), stop=(c == 7)),
                             reads=[sq_rg, ("ONES", 0, 128)], writes=[(SSn, 0, 512)])

                    def rs(rs_box=rs_box):
                        rs_box.append(stats_to_rs())

                    def apply(c0, c1, t0=t0, rs_box=rs_box):
                        rs_ap, rs_rg = rs_box[0]
                        for c in range(c0, c1):
                            o = c * S + t0
                            P.op("dve", I("scalar_tensor_tensor", out=XN[:, o:o + 512], in0=X[:, o:o + 512],
                                          scalar=GV[:, go + c:go + c + 1], in1=rs_ap, op0=ALU.mult, op1=ALU.mult),
                                 reads=[("X", o, o + 512), rs_rg, ("GV", 0, 128)], writes=[("XN", o, o + 512)])
                    nb_ = len(sqt)
                    units.append(lambda sq=sq: sq(0))
                    if nb_ >= 3:
                        units.append(lambda sq=sq: sq(1))
                        for c in range(2, 8):
                            units.append(lambda c=c, sq=sq, stat=stat: (sq(c), stat(c - 2)))
                        units.append(lambda stat=stat, rs=rs: (stat(6), stat(7), rs()))
                    else:
                        for c in range(1, 8):
                            units.append(lambda c=c, sq=sq, stat=stat: (sq(c), stat(c - 1)))
                        units.append(lambda stat=stat, rs=rs: (stat(7), rs()))
                    units.append(lambda apply=apply: apply(0, 4))
                    units.append(lambda apply=apply: apply(4, 8))
                return units

            def prenorm(l, j, tok0, ntiles):
                for u in prenorm_units(l, j, tok0, ntiles):
                    u()

            def proj_postnorm(l, j, wname, nk, mpb, in_off, in_cs, in_t0, tok0, ntiles, bg=None, bg_after=1):
                go = (l * 4 + j) * 8
                ncols = mpb * 128
                woff = 0
                pend = None
                bg_list = []
                SSp = (PS[6], "PS6")

                def hs(tt, m):
                    if tt % 2 == 0:
                        return scr(m)
                    o = m * S + 1024
                    return XNf[:, o // 2:o // 2 + 512], ("XN", o, o + 1024)
                for tt in range(ntiles):
                    t0 = tok0 + 512 * tt
                    it0 = in_t0 + 512 * tt
                    for m in range(8):
                        if m % mpb == 0:
                            woff = W.next(wname, m // mpb, nk * ncols)
                        ps, psn = ps_mm()
                        for kc in range(nk):
                            lo = woff + kc * ncols + (m % mpb) * 128
                            io = in_off + kc * in_cs + it0
                            P.op("pe", I("matmul", ps[:, :], lhsT=WB[:, lo:lo + 128], rhs=BIG[:, io:io + 512],
                                         start=(kc == 0), stop=(kc == nk - 1)),
                                 reads=[("WB", lo, lo + 128), ("BIG", io, io + 512)], writes=[(psn, 0, 512)],
                                 inc=(kc == nk - 1))
                        if pend is not None:
                            pend()
                        hs_ap, hs_rg = hs(tt, m)
                        P.op("act", I("activation", out=hs_ap, in_=ps[:, :], func=AF.Copy),
                             reads=[(psn, 0, 512)], writes=[hs_rg])
                        pend = add_sq(ps[:, :], (psn, 0, 512), m == 0, m == 7, SSp[0], SSp[1], defer=True)
                        if bg_list and tt > bg_after:
                            for _ in range(2):
                                if bg_list:
                                    bg_list.pop(0)()
                    pend()
                    pend = None
                    rs_ap, rs_rg = stats_to_rs(SSp[0], SSp[1])
                    for m in range(8):
                        hs_ap, hs_rg = hs(tt, m)
                        P.op("dve", I("scalar_tensor_tensor", out=hs_ap, in0=hs_ap, scalar=GV[:, go + m:go + m + 1],
                                      in1=rs_ap, op0=ALU.mult, op1=ALU.mult),
                             reads=[hs_rg, rs_rg, ("GV", 0, 128)], writes=[hs_rg])
                        xo = m * S + t0
                        P.op("dve",
                             I("tensor_tensor", out=X[:, xo:xo + 512], in0=X[:, xo:xo + 512], in1=hs_ap, op=ALU.add),
                             reads=[("X", xo, xo + 512), hs_rg], writes=[("X", xo, xo + 512)])
                    if tt == bg_after and bg is not None:
                        bg_list.extend(bg())
                while bg_list:
                    bg_list.pop(0)()

            def ffn(l, pre_done=False, next_l=None):
                for half in range(2):
                    tok0 = 1024 * half
                    if half == 0 and not pre_done:
                        prenorm(l, 2, 0, 2)
                    if half == 0:
                        P.op("dve", I("memset", HALO[:, :], 0.0), writes=[("HALO", 0, 88)])
                    pend = []
                    ptc = [0]
                    for i in range(22):
                        woff = W.next("wup%d" % l, i, 2048)
                        for tt in range(2):
                            t0 = tok0 + 512 * tt
                            pss = []
                            for wh in range(2):
                                ps, psn = ps_up()
                                pss.append((ps, psn))
                                for kc in range(8):
                                    lo = woff + kc * 256 + wh * 128
                                    xo = kc * S + t0
                                    P.op("pe", I("matmul", ps[:, :], lhsT=WB[:, lo:lo + 128], rhs=XN[:, xo:xo + 512],
                                                 start=(kc == 0), stop=(kc == 7)),
                                         reads=[("WB", lo, lo + 128), ("XN", xo, xo + 512)], writes=[(psn, 0, 512)],
                                         inc=(kc == 7))
                            tos = []
                            for wh in range(2):
                                ps, psn = pss[wh]
                                ch = 2 * i + wh
                                dwo = l * 132 + ch * 3
                                zo = (wh * 2 + (tt % 2)) * ST
                                to = (4 + wh * 3 + (ptc[0] % 3)) * ST
                                tos.append(to)
                                if tt == 0 and wh == 0:
                                    zv = SCR[:, zo:zo + 4 * ST].rearrange("p (w r) -> p w r", w=2)[:, :, 0:2]
                                    hv = HALO[:, ch * 2:ch * 2 + 4].rearrange("p (w r) -> p w r", w=2)
                                    P.op("act", I("activation", out=zv, in_=hv, func=AF.Copy),
                                         reads=[("HALO", ch * 2, ch * 2 + 4)],
                                         writes=[("SCR", zo, zo + 2), ("SCR", zo + 2 * ST, zo + 2 * ST + 2)])
                                P.op("act", I("activation", out=SCR[:, zo + 2:zo + 514], in_=ps[:, :], func=AF.Copy),
                                     reads=[(psn, 0, 512)], writes=[("SCR", zo + 2, zo + 514)])
                                P.op("act", I("activation", out=SCR[:, to:to + 512], in_=ps[:, :], func=AF.Copy,
                                              scale=DWV[:, dwo + 2:dwo + 3]),
                                     reads=[(psn, 0, 512), ("DWV", 0, 528)], writes=[("SCR", to, to + 512)])
                            for f in pend:
                                f()
                            pend = []
                            ch = 2 * i
                            zo = (tt % 2) * ST
                            zno = ((tt + 1) % 2) * ST
                            tv = SCR[:, zo:zo + 4 * ST].rearrange("p (w r) -> p w r", w=2)[:, :, 512:514]
                            t_rg = [("SCR", zo + 512, zo + 514), ("SCR", zo + 2 * ST + 512, zo + 2 * ST + 514)]
                            if tt == 0:
                                nv = SCR[:, zno:zno + 4 * ST].rearrange("p (w r) -> p w r", w=2)[:, :, 0:2]
                                P.op("act", I("activation", out=nv, in_=tv, func=AF.Copy), reads=t_rg,
                                     writes=[("SCR", zno, zno + 2), ("SCR", zno + 2 * ST, zno + 2 * ST + 2)])
                            elif half == 0:
                                hv = HALO[:, ch * 2:ch * 2 + 4].rearrange("p (w r) -> p w r", w=2)
                                P.op("act", I("activation", out=hv, in_=tv, func=AF.Copy), reads=t_rg,
                                     writes=[("HALO", ch * 2, ch * 2 + 4)])
                            for wh in range(2):
                                for k in (1, 0):
                                    ch = 2 * i + wh
                                    dwo = l * 132 + ch * 3
                                    zo = (wh * 2 + (tt % 2)) * ST
                                    to = tos[wh]
                                    P.op("dve", I("scalar_tensor_tensor", out=SCR[:, to:to + 512], in0=SCR[:, zo + k:zo + k + 512],
                                                  scalar=DWV[:, dwo + k:dwo + k + 1], in1=SCR[:, to:to + 512],
                                                  op0=ALU.mult, op1=ALU.add),
                                         reads=[("SCR", zo + k, zo + k + 512), ("SCR", to, to + 512), ("DWV", 0, 528)],
                                         writes=[("SCR", to, to + 512)])

                            def fin(tg=tos[0], tu=tos[1], ho=i * 1024 + 512 * tt):
                                P.op("act", I("activation", out=SCR[:, tg:tg + 512], in_=SCR[:, tg:tg + 512], func=AF.Silu),
                                     reads=[("SCR", tg, tg + 512)], writes=[("SCR", tg, tg + 512)])
                                P.op("dve", I("tensor_tensor", out=BIG[:, ho:ho + 512], in0=SCR[:, tg:tg + 512],
                                              in1=SCR[:, tu:tu + 512], op=ALU.mult),
                                     reads=[("SCR", tg, tg + 512), ("SCR", tu, tu + 512)], writes=[("BIG", ho, ho + 512)])
                            pend.append(fin)
                            ptc[0] += 1
                    for f in pend:
                        f()
                    if half == 0:
                        ffn_down(l, 0, bg=prenorm_units(l, 2, 1024, 2, sqt=(2, 3, 4)))
                        if next_l is None:
                            store_out(0, 1024, "ost0")
                    elif next_l is not None:
                        ffn_down(l, 1, bg=prenorm_units(next_l, 0, 0, 2, sqt=(2, 3, 4)))
                    else:
                        ffn_down(l, 1)

            def ffn_down(l, half, bg=()):
                tok0 = 1024 * half
                go = (l * 4 + 3) * 8
                SSb = [(PS[6], "PS6"), (PS[7], "PS7")]
                bg = list(bg)

                def hs(tt, m):
                    if tt == 0:
                        return scr(m)
                    o = m * S + 1024 * half
                    return XNf[:, o // 2:o // 2 + 512], ("XN", o, o + 1024)
                pend = None
                for m in range(8):
                    woff = W.next("wdn%d" % l, m, 2816)
                    for tt in range(2):
                        ps, psn = ps_mm()
                        for kc in range(22):
                            lo = woff + kc * 128
                            io = kc * 1024 + 512 * tt
                            P.op("pe", I("matmul", ps[:, :], lhsT=WB[:, lo:lo + 128], rhs=BIG[:, io:io + 512],
                                         start=(kc == 0), stop=(kc == 21)),
                                 reads=[("WB", lo, lo + 128), ("BIG", io, io + 512)], writes=[(psn, 0, 512)],
                                 inc=(kc == 21))
                        if pend is not None:
                            pend()
                        hs_ap, hs_rg = hs(tt, m)
                        P.op("act", I("activation", out=hs_ap, in_=ps[:, :], func=AF.Copy),
                             reads=[(psn, 0, 512)], writes=[hs_rg])
                        pend = add_sq(ps[:, :], (psn, 0, 512), m == 0, m == 7, SSb[tt][0], SSb[tt][1], defer=True)
                        for _ in range(2):
                            if bg:
                                bg.pop(0)()
                pend()
                while bg:
                    bg.pop(0)()
                for tt in (0, 1):
                    t0 = tok0 + 512 * tt
                    rs_ap, rs_rg = stats_to_rs(SSb[tt][0], SSb[tt][1])
                    for m in range(8):
                        hs_ap, hs_rg = hs(tt, m)
                        P.op("dve", I("scalar_tensor_tensor", out=hs_ap, in0=hs_ap, scalar=GV[:, go + m:go + m + 1],
                                      in1=rs_ap, op0=ALU.mult, op1=ALU.mult),
                             reads=[hs_rg, rs_rg, ("GV", 0, 128)], writes=[hs_rg])
                        xo = m * S + t0
                        eng = "dve"
                        P.op(eng, I("tensor_tensor", out=X[:, xo:xo + 512], in0=X[:, xo:xo + 512], in1=hs_ap, op=ALU.add),
                             reads=[("X", xo, xo + 512), hs_rg], writes=[("X", xo, xo + 512)])

            def mask_setup():
                io_ap, io_rg = scr(9, 0, 128)
                P.op("pool", I("iota", io_ap, [[1, 128]], base=0, channel_multiplier=-1,
                               allow_small_or_imprecise_dtypes=True), writes=[io_rg])
                vc, vc_rg = scr(8, 0, 128)
                vp, vp_rg = scr(8, 128, 128)
                dc, dc_rg = scr(8, 256, 128)
                dp, dp_rg = scr(8, 384, 128)
                P.op("dve", I("tensor_single_scalar", out=vc, in_=io_ap, scalar=0.0, op=ALU.is_ge), reads=[io_rg], writes=[vc_rg])
                P.op("dve", I("tensor_single_scalar", out=vp, in_=io_ap, scalar=0.0, op=ALU.is_le), reads=[io_rg], writes=[vp_rg])
                P.op("dve", I("tensor_single_scalar", out=dc, in_=io_ap, scalar=0.0, op=ALU.max), reads=[io_rg], writes=[dc_rg])
                P.op("dve", I("tensor_scalar", out=dp, in0=io_ap, scalar1=0.0, scalar2=128.0, op0=ALU.min, op1=ALU.add),
                     reads=[io_rg], writes=[dp_rg])

            mk = [0]

            def build_mask(g, c):
                vc, vc_rg = scr(8, 0, 128)
                vp, vp_rg = scr(8, 128, 128)
                dc, dc_rg = scr(8, 256, 128)
                dp, dp_rg = scr(8, 384, 128)
                for s_ in range(2):
                    h = 2 * c + s_
                    sd = float(2.0 ** (-8.0 * (g * 8 + h + 1) / 24.0)) * DIL[g]
                    mo = (g * 4 + c) * 512
                    for (d_ap, d_rg, v_ap, v_rg, co) in ((dc, dc_rg, vc, vc_rg, s_ * 128), (dp, dp_rg, vp, vp_rg, 256 + s_ * 128)):
                        t_ap, t_rg = scr(9, 128 + 128 * (mk[0] % 2), 128)
                        mk[0] += 1
                        P.op("act", I("activation", out=t_ap, in_=d_ap, func=AF.Exp, scale=-sd), reads=[d_rg], writes=[t_rg])
                        P.op("dve", I("tensor_tensor", out=MASK[:, mo + co:mo + co + 128], in0=t_ap, in1=v_ap, op=ALU.mult),
                             reads=[t_rg, v_rg], writes=[("MASK", mo + co, mo + co + 128)])

            QO, KAO, KBO, VO, MO = 0, 2048, 4096, 6144, 10240

            def attention(l, pre_tiles=0):
                prenorm(l, 0, 512 * pre_tiles, 4 - pre_tiles)
                first_attn = not masks_built[0]
                if first_attn:
                    mask_setup()
                    masks_built[0] = True
                P.op("dve", I("memset", BIG[:, KAO:MO], 0.0), writes=[("BIG", KAO, MO)])
                Vv = BIG[:, VO:MO].rearrange("p (b s c) -> p b s c", b=16, s=2)
                for c in range(4):
                    for g in range(3):
                        Dl = DIL[g]
                        nb = (S // Dl) // 128
                        woff = W.next("wqkv%d" % l, g * 4 + c, 3072)
                        if first_attn:
                            build_mask(g, c)
                        for wh in range(2):
                            for tt in range(4):
                                ps, psn = ps_mm()
                                for kc in range(8):
                                    lo = woff + kc * 384 + wh * 128
                                    xo = kc * S + 512 * tt
                                    P.op("pe", I("matmul", ps[:, :], lhsT=WB[:, lo:lo + 128], rhs=XN[:, xo:xo + 512],
                                                 start=(kc == 0), stop=(kc == 7)),
                                         reads=[("WB", lo, lo + 128), ("XN", xo, xo + 512)], writes=[(psn, 0, 512)],
                                         inc=(kc == 7))
                                nu = 512 // Dl
                                u0 = tt * nu

                                def views(dst_rows, base):
                                    if Dl == 1:
                                        return (BIG[dst_rows, base + 512 * tt:base + 512 * tt + 512], ps[dst_rows, :])
                                    ov = BIG[dst_rows, base:base + S].rearrange("p (r u) -> p u r", r=Dl)[:, u0:u0 + nu, :]
                                    iv = ps[dst_rows, :].rearrange("p (u r) -> p u r", r=Dl)
                                    return ov, iv
                                if wh == 0:
                                    ov, iv = views(slice(0, 128), QO)
                                    P.op("act", I("activation", out=ov, in_=iv, func=AF.Copy),
                                         reads=[(psn, 0, 512)], writes=[("BIG", QO, QO + S)])
                                else:
                                    ov, iv = views(slice(0, 64), KAO)
                                    P.op("act", I("activation", out=ov, in_=iv, func=AF.Copy), reads=[(psn, 0, 512)], writes=[("BIG", KAO, KAO + S)])
                                    ov, iv = views(slice(64, 128), KBO)
                                    P.op("dve", I("tensor_copy", out=ov, in_=iv), reads=[(psn, 0, 512)], writes=[("BIG", KBO, KBO + S)])
                        for b0 in range(0, 16, 4):
                            ps, psn = ps_mm()
                            for bb in range(4):
                                b = b0 + bb
                                r, n = b // nb, b % nb
                                stok = r + Dl * 128 * n
                                for kc in range(8):
                                    a0 = kc * S + stok
                                    a1 = a0 + Dl * 127 + 1
                                    lo = woff + kc * 384 + 256
                                    P.op("pe", I("matmul", ps[:, bb * 128:(bb + 1) * 128], lhsT=XN[:, a0:a1:Dl], rhs=WB[:, lo:lo + 128],
                                                 start=(kc == 0), stop=(kc == 7)),
                                         reads=[("XN", a0, a1), ("WB", lo, lo + 128)], writes=[(psn, bb * 128, (bb + 1) * 128)],
                                         inc=(bb == 3 and kc == 7))
                            psv = ps[:, :].rearrange("p (b c) -> p b c", b=4)
                            vlo = VO + b0 * 256
                            P.op("act", I("activation", out=Vv[:, b0:b0 + 4, 0, 0:64], in_=psv[:, :, 0:64], func=AF.Copy),
                                 reads=[(psn, 0, 512)], writes=[("BIG", vlo, vlo + 1024)])
                            P.op("dve", I("tensor_copy", out=Vv[:, b0:b0 + 4, 1, 64:128], in_=psv[:, :, 64:128]),
                                 reads=[(psn, 0, 512)], writes=[("BIG", vlo, vlo + 1024)])
                        mo = (g * 4 + c) * 512

                        def stage1(b):
                            r, n = b // nb, b % nb
                            with_prev = (n != 0)
                            ncols = 512 if with_prev else 256
                            ps, psn = ps_mm()
                            mms = [(0, KAO + 128 * b), (128, KBO + 128 * b)]
                            if with_prev:
                                mms += [(256, KAO + 128 * (b - 1)), (384, KBO + 128 * (b - 1))]
                            qo = QO + 128 * b
                            for idx, (co, ko) in enumerate(mms):
                                P.op("pe", I("matmul", ps[:, co:co + 128], lhsT=BIG[:, ko:ko + 128], rhs=BIG[:, qo:qo + 128],
                                             start=True, stop=True),
                                     reads=[("BIG", ko, ko + 128), ("BIG", qo, qo + 128)], writes=[(psn, co, co + 128)],
                                     inc=(idx == len(mms) - 1))
                            k2 = ei[0] % 3
                            ei[0] += 1
                            e_ap, e_rg = sb16(k2, ncols)
                            pt_ap, pt_rg = sb16(3 + k2, ncols)
                            P.op("act", I("activation", out=e_ap, in_=ps[:, 0:ncols], func=AF.Exp, scale=0.125),
                                 reads=[(psn, 0, ncols)], writes=[e_rg])
                            P.op("dve", I("tensor_tensor", out=pt_ap, in0=e_ap, in1=MASK[:, mo:mo + ncols], op=ALU.mult),
                                 reads=[e_rg, ("MASK", mo, mo + ncols)], writes=[pt_rg])
                            return (b, r, n, with_prev, (3 + k2) * 512, pt_rg)

                        def stage2(st1):
                            b, r, n, with_prev, ptb, pt_rg = st1
                            puz, puzn = ps_aux()
                            terms = [(b, 0, 0), (b, 1, 128)]
                            if with_prev:
                                terms += [(b - 1, 0, 256), (b - 1, 1, 384)]
                            for k, (vb, s_, pco) in enumerate(terms):
                                vlo = VO + vb * 256 + s_ * 128
                                P.op("pe", I("matmul", puz[:, 0:128], lhsT=Vv[:, vb, s_, :], rhs=SB16[:, ptb + pco:ptb + pco + 128],
                                             start=(k == 0), stop=(k == len(terms) - 1)),
                                     reads=[("BIG", vlo, vlo + 128), pt_rg], writes=[(puzn, 0, 128)], inc=False)
                            for k, (vb, s_, pco) in enumerate(terms):
                                P.op("pe", I("matmul", puz[:, 128:256], lhsT=ONESAB[:, s_ * 128:(s_ + 1) * 128],
                                             rhs=SB16[:, ptb + pco:ptb + pco + 128],
                                             start=(k == 0), stop=(k == len(terms) - 1)),
                                     reads=[("ONESAB", 0, 256), pt_rg], writes=[(puzn, 128, 256)],
                                     inc=(k == len(terms) - 1))
                            stok = r + Dl * 128 * n
                            e1 = stok + Dl * 127 + 1
                            dst = SCR[:, 0:4096].rearrange("p (z t) -> p z t", z=2)[:, :, stok:e1:Dl]
                            src = puz[:, 0:256].rearrange("p (z q) -> p z q", z=2)
                            rgs = [("SCR", stok, e1), ("SCR", 2048 + stok, 2048 + e1)]
                            if g == 0:
                                P.op("act", I("activation", out=dst, in_=src, func=AF.Copy), reads=[(puzn, 0, 256)], writes=rgs)
                            else:
                                P.op("dve", I("tensor_tensor", out=dst, in0=src, in1=dst, op=ALU.add),
                                     reads=[(puzn, 0, 256)] + rgs, writes=rgs)
                        sts = [stage1(0), stage1(1)]
                        for b in range(2, 16):
                            sts.append(stage1(b))
                            stage2(sts.pop(0))
                        stage2(sts.pop(0))
                        stage2(sts.pop(0))
                    mgo = MO + c * S
                    for q4 in range(4):
                        a0, a1 = 512 * q4, 512 * q4 + 512
                        P.op("act", I("activation", out=SCR[:, 2048 + a0:2048 + a1], in_=SCR[:, 2048 + a0:2048 + a1], func=AF.Ln),
                             reads=[("SCR", 2048 + a0, 2048 + a1)], writes=[("SCR", 2048 + a0, 2048 + a1)])
                        P.op("act", I("activation", out=SCR[:, 2048 + a0:2048 + a1], in_=SCR[:, 2048 + a0:2048 + a1], func=AF.Exp, scale=-1.0),
                             reads=[("SCR", 2048 + a0, 2048 + a1)], writes=[("SCR", 2048 + a0, 2048 + a1)])
                        P.op("dve", I("tensor_tensor", out=BIG[:, mgo + a0:mgo + a1], in0=SCR[:, a0:a1],
                                      in1=SCR[:, 2048 + a0:2048 + a1], op=ALU.mult),
                             reads=[("SCR", a0, a1), ("SCR", 2048 + a0, 2048 + a1)], writes=[("BIG", mgo + a0, mgo + a1)])
                proj_postnorm(l, 1, "wo%d" % l, 4, 4, MO, S, 0, 0, 4, bg=lambda: prenorm_units(l, 2, 0, 2, sqt=(2, 3, 4)))

            def convmix(l, pre_tiles=0):
                prenorm(l, 0, 512 * pre_tiles, 4 - pre_tiles)
                for i in range(8):
                    woff = W.next("cwin", i, 3072)
                    P.op("dve", I("memset", SCR[:, 0:2], 0.0), writes=[("SCR", 0, 2)])
                    for tt in range(4):
                        t0 = 512 * tt
                        pss = []
                        for j in range(3):
                            ps, psn = ps_mm()
                            pss.append((ps, psn))
                            for kc in range(8):
                                lo = woff + kc * 384 + j * 128
                                xo = kc * S + t0
                                P.op("pe", I("matmul", ps[:, :], lhsT=WB[:, lo:lo + 128], rhs=XN[:, xo:xo + 512],
                                             start=(kc == 0), stop=(kc == 7)),
                                     reads=[("WB", lo, lo + 128), ("XN", xo, xo + 512)], writes=[(psn, 0, 512)],
                                     inc=(kc == 7))
                        (pB, pBn), (pC, pCn), (pH, pHn) = pss
                        hb, hb_rg = scr(8)
                        bb_, bb_rg = scr(9)
                        P.op("act", I("activation", out=hb, in_=pH[:, :], func=AF.Copy), reads=[(pHn, 0, 512)], writes=[hb_rg])
                        P.op("act", I("activation", out=bb_, in_=pB[:, :], func=AF.Copy), reads=[(pBn, 0, 512)], writes=[bb_rg])
                        zo = (tt % 2) * ST
                        zno = ((tt + 1) % 2) * ST
                        P.op("dve", I("tensor_tensor", out=SCR[:, zo + 2:zo + 514], in0=pC[:, :], in1=hb, op=ALU.mult),
                             reads=[(pCn, 0, 512), hb_rg], writes=[("SCR", zo + 2, zo + 514)])
                        if tt < 3:
                            P.op("dve", I("tensor_copy", out=SCR[:, zno:zno + 2], in_=SCR[:, zo + 512:zo + 514]),
                                 reads=[("SCR", zo + 512, zo + 514)], writes=[("SCR", zno, zno + 2)])
                        to = (2 + tt % 2) * ST
                        P.op("dve", I("tensor_scalar", out=SCR[:, to:to + 512], in0=SCR[:, zo + 2:zo + 514],
                                      scalar1=CDW[:, i * 3 + 2:i * 3 + 3], scalar2=None, op0=ALU.mult),
                             reads=[("SCR", zo + 2, zo + 514), ("CDW", 0, 24)], writes=[("SCR", to, to + 512)])
                        for k in (1, 0):
                            P.op("dve", I("scalar_tensor_tensor", out=SCR[:, to:to + 512], in0=SCR[:, zo + k:zo + k + 512],
                                          scalar=CDW[:, i * 3 + k:i * 3 + k + 1], in1=SCR[:, to:to + 512],
                                          op0=ALU.mult, op1=ALU.add),
                                 reads=[("SCR", zo + k, zo + k + 512), ("SCR", to, to + 512), ("CDW", 0, 24)],
                                 writes=[("SCR", to, to + 512)])
                        yo = i * S + t0
                        P.op("dve", I("tensor_tensor", out=BIG[:, yo:yo + 512], in0=SCR[:, to:to + 512], in1=bb_, op=ALU.mult),
                             reads=[("SCR", to, to + 512), bb_rg], writes=[("BIG", yo, yo + 512)])
                proj_postnorm(l, 1, "cwout", 8, 2, 0, S, 0, 0, 4, bg=lambda: prenorm_units(l, 2, 0, 2, sqt=(2, 3, 4)))

            def poolmix(l, pre_tiles=0):
                prenorm(l, 0, 512 * pre_tiles, 4 - pre_tiles)
                CS0 = 2096
                PO = 16384
                SBf = SB16.bitcast(F32)
                ZT = SBf[:, 0:512]
                ZT_rg = ("SB16", 0, 1024)
                TM = HALO[:, 0:16]
                TM_rg = ("HALO", 0, 16)
                P.op("dve", I("memset", SCR[:, 2080:2096], 0.0), writes=[("SCR", 2080, 2096)])
                P.op("dve", I("memset", ZT, 0.0), writes=[ZT_rg])
                chunk_i = [0]
                for t in range(16):
                    P.op("dve", I("memset", INVT[:, t:t + 1], 1.0 / (t + 1)), writes=[("INVT", t, t + 1)])
                PSETS = [[(BIG, "BIG", 16384), (BIG, "BIG", 18432)], [(BIG, "BIG", 20480), (SB16, "SB16", 1024)]]

                def inproj(gi):
                    w = 2 ** (gi + 1)
                    woff = W.next("pwin", gi, 2048)
                    for cc in range(2):
                        UB0 = 0 if chunk_i[0] % 2 == 0 else 4160
                        chunk_i[0] += 1
                        for tt in range(4):
                            ps, psn = ps_mm()
                            for kc in range(8):
                                lo = woff + kc * 256 + cc * 128
                                xo = kc * S + 512 * tt
                                P.op("pe", I("matmul", ps[:, :], lhsT=WB[:, lo:lo + 128], rhs=XN[:, xo:xo + 512],
                                             start=(kc == 0), stop=(kc == 7)),
                                     reads=[("WB", lo, lo + 128), ("XN", xo, xo + 512)], writes=[(psn, 0, 512)],
                                     inc=(kc == 7))
                            P.op("act", I("activation", out=SCR[:, UB0 + 512 * tt:UB0 + 512 * tt + 512], in_=ps[:, :], func=AF.Copy),
                                 reads=[(psn, 0, 512)], writes=[("SCR", UB0 + 512 * tt, UB0 + 512 * tt + 512)])
                        for q in range(4):
                            co = CS0 + 512 * q
                            init = 0.0 if q == 0 else SCR[:, co - 1:co]
                            P.op("dve", I("tensor_tensor_scan", out=SCR[:, co:co + 512], data0=SCR[:, UB0 + 512 * q:UB0 + 512 * q + 512],
                                          data1=ZT, initial=init, op0=ALU.add, op1=ALU.add),
                                 reads=[("SCR", UB0 + 512 * q, UB0 + 512 * q + 512), ZT_rg, ("SCR", co - 1, co)],
                                 writes=[("SCR", co, co + 512)])
                        P.op("dve", I("tensor_tensor", out=TM, in0=SCR[:, CS0:CS0 + 16], in1=INVT[:, 0:16], op=ALU.mult),
                             reads=[("SCR", CS0, CS0 + 16), ("INVT", 0, 16)], writes=[TM_rg])
                        P.op("dve", I("tensor_tensor", out=TM, in0=TM, in1=SCR[:, UB0:UB0 + 16], op=ALU.subtract),
                             reads=[TM_rg, ("SCR", UB0, UB0 + 16)], writes=[TM_rg])
                        P.op("dve", I("scalar_tensor_tensor", out=SCR[:, UB0:UB0 + S], in0=SCR[:, CS0:CS0 + S], scalar=1.0 / w,
                                      in1=SCR[:, UB0:UB0 + S], op0=ALU.mult, op1=ALU.subtract),
                             reads=[("SCR", CS0, CS0 + S), ("SCR", UB0, UB0 + S)], writes=[("SCR", UB0, UB0 + S)])
                        pt_, pn_, po = PSETS[gi % 2][cc]
                        P.op("dve", I("scalar_tensor_tensor", out=pt_[:, po:po + S], in0=SCR[:, CS0 - w:CS0 - w + S], scalar=-1.0 / w,
                                      in1=SCR[:, UB0:UB0 + S], op0=ALU.mult, op1=ALU.add),
                             reads=[("SCR", CS0 - w, CS0 - w + S), ("SCR", UB0, UB0 + S)], writes=[(pn_, po, po + S)])
                        P.op("dve", I("tensor_copy", out=pt_[:, po:po + w - 1], in_=HALO[:, 0:w - 1]),
                             reads=[TM_rg], writes=[(pn_, po, po + w - 1)])

                def grp(gi):
                    goff = W.next("pwgrp", gi, 512)
                    for mm in range(2):
                        for tt in range(4):
                            ps, psn = ps_mm()
                            for kc in range(2):
                                lo = goff + kc * 256 + mm * 128
                                pt_, pn_, po = PSETS[gi % 2][kc]
                                io = po + 512 * tt
                                P.op("pe", I("matmul", ps[:, :], lhsT=WB[:, lo:lo + 128], rhs=pt_[:, io:io + 512],
                                             start=(kc == 0), stop=(kc == 1)),
                                     reads=[("WB", lo, lo + 128), (pn_, io, io + 512)], writes=[(psn, 0, 512)],
                                     inc=(kc == 1))
                            ch = 2 * gi + mm
                            yo = ch * S + 512 * tt
                            P.op("act", I("activation", out=BIG[:, yo:yo + 512], in_=ps[:, :], func=AF.Copy, scale=PSC[:, ch:ch + 1]),
                                 reads=[(psn, 0, 512), ("PSC", 0, 8)], writes=[("BIG", yo, yo + 512)])
                inproj(0)
                for gi in range(4):
                    if gi + 1 < 4:
                        inproj(gi + 1)
                    grp(gi)
                proj_postnorm(l, 1, "pwout", 8, 2, 0, S, 0, 0, 4, bg=lambda: prenorm_units(l, 2, 0, 2, sqt=(2, 3, 4)))

            def store_out(h0, h1, strm):
                for c in range(8):
                    P.op("sp", I("dma_start", out=yT[c * 128:(c + 1) * 128, h0:h1], in_=X[:, c * S + h0:c * S + h1]),
                         reads=[("X", c * S + h0, c * S + h1)], stream=strm, amt=16)

            run_layers = EXEC_LAYERS if EXEC_LAYERS is not None else layers
            masks_built = [False]
            for li, l in enumerate(run_layers):
                kind = l % 3
                pre_tiles = 2 if li > 0 else 0
                if kind == 0:
                    attention(l, pre_tiles)
                elif kind == 1:
                    convmix(l, pre_tiles)
                else:
                    poolmix(l, pre_tiles)
                ffn(l, pre_done=True, next_l=(run_layers[li + 1] if li + 1 < len(run_layers) else None))
            store_out(1024, 1536, "ost1")
            store_out(1536, 2048, "ost2")
            P.wait_all("sp", ["ost0", "ost1", "ost2"])

        def t_name(t, nm):
            return {"gvec": "GV", "dwv": "DWV", "cdw": "CDW", "psc": "PSC"}[nm]

        Wd = WMgr(DummyProg(), dram, WB, None)
        construct(DummyProg(), Wd)
        P = Prog()
        P.unordered.update(["xld0", "xld1", "xld2", "cst", "ost0", "ost1", "ost2"])
        W = WMgr(P, dram, WB, Wd.rec)
        construct(P, W)
        assert W.used == len(Wd.rec) and W.issued == len(Wd.rec)
        P.emit(nc, st)
    return nc


def chunkmajor(Wm, cols):
    sub = Wm[:, cols]
    nk = Wm.shape[0] // 128
    return np.ascontiguousarray(sub.reshape(nk, 128, -1).transpose(1, 0, 2).reshape(128, -1))


def host_layout(inp):
    f = lambda a: np.asarray(a, dtype=np.float32)
    out = {}
    ng = f(inp["norm_g"])
    out["gvec"] = np.ascontiguousarray(ng.reshape(16, 8, 128).transpose(2, 0, 1).reshape(128, 128))
    dw = f(inp["ffn_w_dw"])
    out["dwv"] = np.ascontiguousarray(dw.reshape(4, 3, 2, 22, 128).transpose(4, 0, 3, 2, 1).reshape(128, 528))
    cdw = f(inp["conv_w_dw"])[0]
    out["cdw"] = np.ascontiguousarray(cdw.reshape(3, 8, 128).transpose(2, 1, 0).reshape(128, 24))
    out["psc"] = np.ascontiguousarray(f(inp["pool_scale"])[0].reshape(8, 128).T)
    ar = np.arange
    for l in range(4):
        wu = f(inp["ffn_w_up"])[l]
        out["wup%d" % l] = np.stack([chunkmajor(wu, np.concatenate([ar(128 * i, 128 * i + 128), ar(2816 + 128 * i, 2816 + 128 * i + 128)]))
                                     for i in range(22)])
        wd = f(inp["ffn_w_down"])[l]
        out["wdn%d" % l] = np.stack([chunkmajor(wd, ar(128 * m, 128 * m + 128)) for m in range(8)])
    for ia, l in enumerate((0, 3)):
        wq = f(inp["attn_w_qkv"])[ia]
        blks = []
        for g in range(3):
            for c in range(4):
                base = g * 1536 + c * 128
                blks.append(chunkmajor(wq, np.concatenate([ar(base, base + 128), ar(base + 512, base + 640), ar(base + 1024, base + 1152)])))
        out["wqkv%d" % l] = np.stack(blks)
        wo = f(inp["attn_w_o"])[ia]
        out["wo%d" % l] = np.stack([chunkmajor(wo, ar(512 * b, 512 * b + 512)) for b in range(2)])
    cw = f(inp["conv_w_in"])[0]
    out["cwin"] = np.stack([chunkmajor(cw, np.concatenate([ar(128 * i, 128 * i + 128), ar(1024 + 128 * i, 1152 + 128 * i), ar(2048 + 128 * i, 2176 + 128 * i)]))
                            for i in range(8)])
    for nm, key in (("cwout", "conv_w_out"), ("pwin", "pool_w_in"), ("pwout", "pool_w_out")):
        wm = f(inp[key])[0]
        out[nm] = np.stack([chunkmajor(wm, ar(256 * b, 256 * b + 256)) for b in range(4)])
    wg = f(inp["pool_w_grp"])[0]
    out["pwgrp"] = np.stack([chunkmajor(wg[g], ar(0, 256)) for g in range(4)])
    return out


_PROG_CACHE = {}


def _run(layers, xT_list, lay):
    key = tuple(layers)
    if key not in _PROG_CACHE:
        _PROG_CACHE[key] = build_program(list(layers))
    nc = _PROG_CACHE[key]
    names = ["gvec", "dwv", "cdw", "psc"]
    for l in layers:
        names += ["wup%d" % l, "wdn%d" % l]
        kind = l % 3
        if kind == 0:
            names += ["wqkv%d" % l, "wo%d" % l]
        elif kind == 1:
            names += ["cwin", "cwout"]
        else:
            names += ["pwin", "pwgrp", "pwout"]
    in_maps = []
    for b in range(8):
        m = {n: lay[n] for n in names}
        m["xT"] = xT_list[b]
        in_maps.append(m)
    res = run_bass_kernel_spmd(nc, in_maps, core_ids=list(range(8)))
    return [np.asarray(r["yT"]) for r in res.results]


def kernel(**inputs):
    lay = host_layout(inputs)
    x = np.asarray(inputs["x"], dtype=np.float32)
    xT = [np.ascontiguousarray(x[b].T) for b in range(8)]
    if FUSED:
        yT = _run((0, 1, 2, 3), xT, lay)
    else:
        yT = xT
        for l in range(4):
            yT = _run((l,), yT, lay)
    return np.ascontiguousarray(np.stack([y.T for y in yT]).astype(np.float32))
```

```python
import numpy as np
import concourse.bass as bass
import concourse.mybir as mybir
from concourse.bass_utils import run_bass_kernel_spmd
from contextlib import ExitStack

F32 = mybir.dt.float32
BF16 = mybir.dt.bfloat16
AF = mybir.ActivationFunctionType
ALU = mybir.AluOpType

S = 2048
EPS = 1e-6
DIL = (1, 4, 16)
NSLOT = 3
SLOT = 3072
ST = 520
NT = 12
FUSED = True
EXEC_LAYERS = None


class Prog:
    ENG = ("pe", "act", "dve", "pool", "sp")

    def __init__(self):
        self.q = {e: [] for e in self.ENG}
        self.cnt = {}
        self.waited = {e: {} for e in self.ENG}
        self.recs = {}
        self.unordered = set()
        self.pe_pending = False
        self.nops = 0

    def _overlaps(self, name, lo, hi):
        return [r for r in self.recs.get(name, ()) if r[0] < hi and lo < r[1]]

    def op(self, eng, fn, reads=(), writes=(), inc=True, stream=None, amt=1):
        own = stream if stream is not None else eng
        if self.pe_pending and eng != "pe":
            raise RuntimeError("non-PE op constructed inside an open PE group")
        deps = {}

        def add(dep):
            if dep is None:
                return
            s, c = dep
            if s in self.unordered:
                c = self.cnt.get(s, 0)
            if eng == "pe" and s == "pe":
                return
            if deps.get(s, 0) < c:
                deps[s] = c
        for (name, lo, hi) in reads:
            for r in self._overlaps(name, lo, hi):
                add(r[2])
        for (name, lo, hi) in writes:
            for r in self._overlaps(name, lo, hi):
                add(r[2])
                for s, c in r[3].items():
                    add((s, c))
        for s, c in deps.items():
            if self.waited[eng].get(s, 0) < c:
                self.waited[eng][s] = c
                self.q[eng].append(("wait", s, c))
        after = self.cnt.get(own, 0) + amt
        if inc:
            self.cnt[own] = after
            if eng == "pe":
                self.pe_pending = False
        else:
            assert eng == "pe"
            self.pe_pending = True
        self.q[eng].append(("op", fn, own if inc else None, amt))
        self.nops += 1
        for (name, lo, hi) in reads:
            lst = self.recs.setdefault(name, [])
            for r in lst:
                if r[0] == lo and r[1] == hi and r[2] is None:
                    r[3][own] = after
                    break
            else:
                lst.append([lo, hi, None, {own: after}])
        for (name, lo, hi) in writes:
            lst = self.recs.setdefault(name, [])
            lst[:] = [r for r in lst if not (lo <= r[0] and r[1] <= hi)]
            lst.append([lo, hi, (own, after), {}])

    def wait_all(self, eng, streams):
        for s in streams:
            c = self.cnt.get(s, 0)
            if c and self.waited[eng].get(s, 0) < c:
                self.waited[eng][s] = c
                self.q[eng].append(("wait", s, c))

    def emit(self, nc, stack):
        assert not self.pe_pending
        sems = {s: stack.enter_context(nc.semaphore("s_" + s)) for s in self.cnt}
        block = stack.enter_context(nc.Block())

        def replay(e, name):
            for it in self.q[name]:
                if it[0] == "wait":
                    e.wait_ge(sems[it[1]], it[2])
                else:
                    ins = it[1](e)
                    if it[2] is not None:
                        ins.then_inc(sems[it[2]], it[3])

        @block.tensor
        def _(e):
            replay(e, "pe")

        @block.scalar
        def _(e):
            replay(e, "act")

        @block.vector
        def _(e):
            replay(e, "dve")

        @block.gpsimd
        def _(e):
            replay(e, "pool")

        @block.sync
        def _(e):
            replay(e, "sp")


class DummyProg:
    def op(self, *a, **k):
        pass

    def wait_all(self, *a, **k):
        pass


def I(fn, *a, **k):
    return lambda e: getattr(e, fn)(*a, **k)


class WMgr:
    def __init__(self, P, dram, WB, sched):
        self.P, self.dram, self.WB, self.sched = P, dram, WB, sched
        self.rec = []
        self.issued = 0
        self.used = 0

    def next(self, name, blk, nelem):
        if self.sched is None:
            self.rec.append((name, blk, nelem))
            return 0
        i = self.used
        assert self.sched[i] == (name, blk, nelem), (self.sched[i], name, blk, nelem)
        while self.issued < min(i + NSLOT, len(self.sched)):
            self._issue(self.issued)
            self.issued += 1
        self.used += 1
        return (i % NSLOT) * SLOT

    def _issue(self, j):
        name, blk, nelem = self.sched[j]
        slot = j % NSLOT
        off = slot * SLOT
        self.P.op("pool", I("dma_start", out=self.WB[:, off:off + nelem], in_=self.dram[name][blk]),
                  writes=[("WB", off, off + nelem)], stream="w%d" % slot, amt=16)


def build_program(layers):
    nc = bass.Bass("TRN2", target_bir_lowering=False)
    dram = {}

    def din(name, shape):
        dram[name] = nc.dram_tensor(name, list(shape), F32, kind="ExternalInput").ap()
    din("xT", (1024, S))
    din("gvec", (128, 128))
    din("dwv", (128, 528))
    din("cdw", (128, 24))
    din("psc", (128, 8))
    for l in layers:
        din("wup%d" % l, (22, 128, 2048))
        din("wdn%d" % l, (8, 128, 2816))
        kind = l % 3
        if kind == 0:
            din("wqkv%d" % l, (12, 128, 3072))
            din("wo%d" % l, (2, 128, 2048))
        elif kind == 1:
            din("cwin", (8, 128, 3072))
            din("cwout", (4, 128, 2048))
        else:
            din("pwin", (4, 128, 2048))
            din("pwgrp", (4, 128, 512))
            din("pwout", (4, 128, 2048))
    yT = nc.dram_tensor("yT", [1024, S], F32, kind="ExternalOutput").ap()

    with ExitStack() as st:
        def sb(name, n, dt):
            return st.enter_context(nc.sbuf_tensor(name, [128, n], dt))
        X = sb("X", 8 * S, F32)
        XN = sb("XN", 8 * S, BF16)
        BIG = sb("BIG", 22528, BF16)
        SCR = sb("SCR", NT * ST, F32)
        SB16 = sb("SB16", 6 * 512, BF16)
        WB = sb("WB", NSLOT * SLOT, BF16)
        MASK = sb("MASK", 12 * 512, BF16)
        GV = sb("GV", 128, F32)
        DWV = sb("DWV", 528, F32)
        CDW = sb("CDW", 24, F32)
        PSC = sb("PSC", 8, F32)
        ONES = sb("ONES", 128, BF16)
        ONESAB = sb("ONESAB", 256, BF16)
        HALO = sb("HALO", 88, F32)
        INVT = sb("INVT", 16, F32)
        PS = [st.enter_context(nc.psum_tensor("PS%d" % b, [128, 512], F32)) for b in range(8)]

        def construct(P, W):
            mmi = [0]
            auxi = [0]
            sqi = [0]
            rsi = [0]
            ei = [0]

            def ps_mm():
                b = mmi[0] % 5
                mmi[0] += 1
                return PS[b], "PS%d" % b

            upi = [0]

            def ps_up():
                b = upi[0] % 8
                upi[0] += 1
                return PS[b], "PS%d" % b

            def ps_aux():
                b = (6, 7, 5)[auxi[0] % 3]
                auxi[0] += 1
                return PS[b], "PS%d" % b
            SS, SSn = PS[5], "PS5"

            def scr(i, lo=0, n=512):
                o = i * ST + lo
                return SCR[:, o:o + n], ("SCR", o, o + n)

            def sb16(i, n=512):
                return SB16[:, i * 512:i * 512 + n], ("SB16", i * 512, i * 512 + n)

            for (h0, h1, strm) in ((0, 512, "xld0"), (512, 1024, "xld1"), (1024, 2048, "xld2")):
                for c in range(8):
                    P.op("sp",
                         I("dma_start", out=X[:, c * S + h0:c * S + h1], in_=dram["xT"][c * 128:(c + 1) * 128, h0:h1]),
                         writes=[("X", c * S + h0, c * S + h1)], stream=strm, amt=16)
            for (t, nm, n) in ((GV, "gvec", 128), (DWV, "dwv", 528), (CDW, "cdw", 24), (PSC, "psc", 8)):
                P.op("sp", I("dma_start", out=t[:, :], in_=dram[nm][:, :]), writes=[(t_name(t, nm), 0, n)], stream="cst", amt=16)
            P.op("dve", I("memset", ONES[:, :], 1.0), writes=[("ONES", 0, 128)])
            P.op("dve", I("memset", ONESAB[:, :], 0.0), writes=[("ONESAB", 0, 256)])
            P.op("dve", I("memset", ONESAB[:, 0:64], 1.0), writes=[("ONESAB", 0, 64)])
            P.op("dve", I("memset", ONESAB[:, 192:256], 1.0), writes=[("ONESAB", 192, 256)])

            XNf = XN.bitcast(F32)
            MASKf = MASK.bitcast(F32)

            def add_sq(src_ap, src_rg, first, last, SS=SS, SSn=SSn, defer=False):
                k = sqi[0] % 2
                sqi[0] += 1
                sq_ap, sq_rg = sb16(k)
                P.op("act", I("activation", out=sq_ap, in_=src_ap, func=AF.Square), reads=[src_rg], writes=[sq_rg])

                def pe_part():
                    P.op("pe", I("matmul", SS[:, :], lhsT=ONES[:, :], rhs=sq_ap, start=first, stop=last),
                         reads=[sq_rg, ("ONES", 0, 128)], writes=[(SSn, 0, 512)])
                if defer:
                    return pe_part
                pe_part()
                return None

            def stats_to_rs(SS=SS, SSn=SSn):
                r = 10 + rsi[0] % 2
                rsi[0] += 1
                rs_ap, rs_rg = scr(r)
                P.op("act", I("activation", out=rs_ap, in_=SS[:, :], func=AF.Ln, scale=1.0 / 1024.0, bias=EPS),
                     reads=[(SSn, 0, 512)], writes=[rs_rg])
                P.op("act", I("activation", out=rs_ap, in_=rs_ap, func=AF.Exp, scale=-0.5), reads=[rs_rg], writes=[rs_rg])
                return rs_ap, rs_rg

            def prenorm_units(l, j, tok0, ntiles, sqt=(0, 1)):
                go = (l * 4 + j) * 8
                units = []
                for tt in range(ntiles):
                    t0 = tok0 + 512 * tt
                    rs_box = []

                    def sq(c, t0=t0):
                        o = c * S + t0
                        sq_ap, sq_rg = sb16(sqt[c % len(sqt)])
                        P.op("act", I("activation", out=sq_ap, in_=X[:, o:o + 512], func=AF.Square),
                             reads=[("X", o, o + 512)], writes=[sq_rg])

                    def stat(c):
                        sq_ap, sq_rg = sb16(sqt[c % len(sqt)])
                        P.op("pe", I("matmul", SS[:, :], lhsT=ONES[:, :], rhs=sq_ap, start=(c == 0), stop=(c == 7)),
                             reads=[sq_rg, ("ONES", 0, 128)], writes=[(SSn, 0, 512)])

                    def rs(rs_box=rs_box):
                        rs_box.append(stats_to_rs())

                    def apply(c0, c1, t0=t0, rs_box=rs_box):
                        rs_ap, rs_rg = rs_box[0]
                        for c in range(c0, c1):
                            o = c * S + t0
                            P.op("dve", I("scalar_tensor_tensor", out=XN[:, o:o + 512], in0=X[:, o:o + 512],
                                          scalar=GV[:, go + c:go + c + 1], in1=rs_ap, op0=ALU.mult, op1=ALU.mult),
                                 reads=[("X", o, o + 512), rs_rg, ("GV", 0, 128)], writes=[("XN", o, o + 512)])
                    nb_ = len(sqt)
                    units.append(lambda sq=sq: sq(0))
                    if nb_ >= 3:
                        units.append(lambda sq=sq: sq(1))
                        for c in range(2, 8):
                            units.append(lambda c=c, sq=sq, stat=stat: (sq(c), stat(c - 2)))
                        units.append(lambda stat=stat, rs=rs: (stat(6), stat(7), rs()))
                    else:
                        for c in range(1, 8):
                            units.append(lambda c=c, sq=sq, stat=stat: (sq(c), stat(c - 1)))
                        units.append(lambda stat=stat, rs=rs: (stat(7), rs()))
                    units.append(lambda apply=apply: apply(0, 4))
                    units.append(lambda apply=apply: apply(4, 8))
                return units

            def prenorm(l, j, tok0, ntiles):
                for u in prenorm_units(l, j, tok0, ntiles):
                    u()

            def proj_postnorm(l, j, wname, nk, mpb, in_off, in_cs, in_t0, tok0, ntiles, bg=None, bg_after=1):
                go = (l * 4 + j) * 8
                ncols = mpb * 128
                woff = 0
                pend = None
                bg_list = []
                SSp = (PS[6], "PS6")

                def hs(tt, m):
                    if tt % 2 == 0:
                        return scr(m)
                    o = m * S + 1024
                    return XNf[:, o // 2:o // 2 + 512], ("XN", o, o + 1024)
                for tt in range(ntiles):
                    t0 = tok0 + 512 * tt
                    it0 = in_t0 + 512 * tt
                    for m in range(8):
                        if m % mpb == 0:
                            woff = W.next(wname, m // mpb, nk * ncols)
                        ps, psn = ps_mm()
                        for kc in range(nk):
                            lo = woff + kc * ncols + (m % mpb) * 128
                            io = in_off + kc * in_cs + it0
                            P.op("pe", I("matmul", ps[:, :], lhsT=WB[:, lo:lo + 128], rhs=BIG[:, io:io + 512],
                                         start=(kc == 0), stop=(kc == nk - 1)),
                                 reads=[("WB", lo, lo + 128), ("BIG", io, io + 512)], writes=[(psn, 0, 512)],
                                 inc=(kc == nk - 1))
                        if pend is not None:
                            pend()
                        hs_ap, hs_rg = hs(tt, m)
                        P.op("act", I("activation", out=hs_ap, in_=ps[:, :], func=AF.Copy),
                             reads=[(psn, 0, 512)], writes=[hs_rg])
                        pend = add_sq(ps[:, :], (psn, 0, 512), m == 0, m == 7, SSp[0], SSp[1], defer=True)
                        if bg_list and tt > bg_after:
                            for _ in range(2):
                                if bg_list:
                                    bg_list.pop(0)()
                    pend()
                    pend = None
                    rs_ap, rs_rg = stats_to_rs(SSp[0], SSp[1])
                    for m in range(8):
                        hs_ap, hs_rg = hs(tt, m)
                        P.op("dve", I("scalar_tensor_tensor", out=hs_ap, in0=hs_ap, scalar=GV[:, go + m:go + m + 1],
                                      in1=rs_ap, op0=ALU.mult, op1=ALU.mult),
                             reads=[hs_rg, rs_rg, ("GV", 0, 128)], writes=[hs_rg])
                        xo = m * S + t0
                        P.op("dve",
                             I("tensor_tensor", out=X[:, xo:xo + 512], in0=X[:, xo:xo + 512], in1=hs_ap, op=ALU.add),
                             reads=[("X", xo, xo + 512), hs_rg], writes=[("X", xo, xo + 512)])
                    if tt == bg_after and bg is not None:
                        bg_list.extend(bg())
                while bg_list:
                    bg_list.pop(0)()

            def ffn(l, pre_done=False, next_l=None):
                for half in range(2):
                    tok0 = 1024 * half
                    if half == 0 and not pre_done:
                        prenorm(l, 2, 0, 2)
                    if half == 0:
                        P.op("dve", I("memset", HALO[:, :], 0.0), writes=[("HALO", 0, 88)])
                    pend = []
                    ptc = [0]
                    for i in range(22):
                        woff = W.next("wup%d" % l, i, 2048)
                        for tt in range(2):
                            t0 = tok0 + 512 * tt
                            pss = []
                            for wh in range(2):
                                ps, psn = ps_up()
                                pss.append((ps, psn))
                                for kc in range(8):
                                    lo = woff + kc * 256 + wh * 128
                                    xo = kc * S + t0
                                    P.op("pe", I("matmul", ps[:, :], lhsT=WB[:, lo:lo + 128], rhs=XN[:, xo:xo + 512],
                                                 start=(kc == 0), stop=(kc == 7)),
                                         reads=[("WB", lo, lo + 128), ("XN", xo, xo + 512)], writes=[(psn, 0, 512)],
                                         inc=(kc == 7))
                            tos = []
                            for wh in range(2):
                                ps, psn = pss[wh]
                                ch = 2 * i + wh
                                dwo = l * 132 + ch * 3
                                zo = (wh * 3 + (ptc[0] % 3)) * ST
                                to = (6 + wh * 3 + (ptc[0] % 3)) * ST
                                tos.append(to)
                                if tt == 0 and wh == 0:
                                    zv = SCR[:, zo:zo + 6 * ST].rearrange("p (w r) -> p w r", w=2)[:, :, 0:2]
                                    hv = HALO[:, ch * 2:ch * 2 + 4].rearrange("p (w r) -> p w r", w=2)
                                    P.op("act", I("activation", out=zv, in_=hv, func=AF.Copy),
                                         reads=[("HALO", ch * 2, ch * 2 + 4)],
                                         writes=[("SCR", zo, zo + 2), ("SCR", zo + 3 * ST, zo + 3 * ST + 2)])
                                P.op("act", I("activation", out=SCR[:, zo + 2:zo + 514], in_=ps[:, :], func=AF.Copy),
                                     reads=[(psn, 0, 512)], writes=[("SCR", zo + 2, zo + 514)])
                                P.op("act", I("activation", out=SCR[:, to:to + 512], in_=ps[:, :], func=AF.Copy,
                                              scale=DWV[:, dwo + 2:dwo + 3]),
                                     reads=[(psn, 0, 512), ("DWV", 0, 528)], writes=[("SCR", to, to + 512)])
                            for f in pend:
                                f()
                            pend = []
                            ch = 2 * i
                            zo = (ptc[0] % 3) * ST
                            zno = ((ptc[0] + 1) % 3) * ST
                            tv = SCR[:, zo:zo + 6 * ST].rearrange("p (w r) -> p w r", w=2)[:, :, 512:514]
                            t_rg = [("SCR", zo + 512, zo + 514), ("SCR", zo + 3 * ST + 512, zo + 3 * ST + 514)]
                            if tt == 0:
                                nv = SCR[:, zno:zno + 6 * ST].rearrange("p (w r) -> p w r", w=2)[:, :, 0:2]
                                P.op("act", I("activation", out=nv, in_=tv, func=AF.Copy), reads=t_rg,
                                     writes=[("SCR", zno, zno + 2), ("SCR", zno + 3 * ST, zno + 3 * ST + 2)])
                            elif half == 0:
                                hv = HALO[:, ch * 2:ch * 2 + 4].rearrange("p (w r) -> p w r", w=2)
                                P.op("act", I("activation", out=hv, in_=tv, func=AF.Copy), reads=t_rg,
                                     writes=[("HALO", ch * 2, ch * 2 + 4)])
                            for wh in range(2):
                                for k in (1, 0):
                                    ch = 2 * i + wh
                                    dwo = l * 132 + ch * 3
                                    zo = (wh * 3 + (ptc[0] % 3)) * ST
                                    to = tos[wh]
                                    P.op("dve", I("scalar_tensor_tensor", out=SCR[:, to:to + 512], in0=SCR[:, zo + k:zo + k + 512],
                                                  scalar=DWV[:, dwo + k:dwo + k + 1], in1=SCR[:, to:to + 512],
                                                  op0=ALU.mult, op1=ALU.add),
                                         reads=[("SCR", zo + k, zo + k + 512), ("SCR", to, to + 512), ("DWV", 0, 528)],
                                         writes=[("SCR", to, to + 512)])

                            def fin(tg=tos[0], tu=tos[1], ho=i * 1024 + 512 * tt):
                                P.op("act", I("activation", out=SCR[:, tg:tg + 512], in_=SCR[:, tg:tg + 512], func=AF.Silu),
                                     reads=[("SCR", tg, tg + 512)], writes=[("SCR", tg, tg + 512)])
                                P.op("dve", I("tensor_tensor", out=BIG[:, ho:ho + 512], in0=SCR[:, tg:tg + 512],
                                              in1=SCR[:, tu:tu + 512], op=ALU.mult),
                                     reads=[("SCR", tg, tg + 512), ("SCR", tu, tu + 512)], writes=[("BIG", ho, ho + 512)])
                            pend.append(fin)
                            ptc[0] += 1
                    for f in pend:
                        f()
                    if half == 0:
                        ffn_down(l, 0, bg=prenorm_units(l, 2, 1024, 2, sqt=(2, 3, 4)))
                        if next_l is None:
                            store_out(0, 1024, "ost0")
                    elif next_l is not None:
                        ffn_down(l, 1, bg=prenorm_units(next_l, 0, 0, 2, sqt=(2, 3, 4)))
                    else:
                        ffn_down(l, 1)

            def ffn_down(l, half, bg=()):
                tok0 = 1024 * half
                go = (l * 4 + 3) * 8
                SSb = [(PS[6], "PS6"), (PS[7], "PS7")]
                bg = list(bg)

                def hs(tt, m):
                    if tt == 0:
                        return scr(m)
                    o = m * S + 1024 * half
                    return XNf[:, o // 2:o // 2 + 512], ("XN", o, o + 1024)
                pend = None
                for m in range(8):
                    woff = W.next("wdn%d" % l, m, 2816)
                    for tt in range(2):
                        ps, psn = ps_mm()
                        for kc in range(22):
                            lo = woff + kc * 128
                            io = kc * 1024 + 512 * tt
                            P.op("pe", I("matmul", ps[:, :], lhsT=WB[:, lo:lo + 128], rhs=BIG[:, io:io + 512],
                                         start=(kc == 0), stop=(kc == 21)),
                                 reads=[("WB", lo, lo + 128), ("BIG", io, io + 512)], writes=[(psn, 0, 512)],
                                 inc=(kc == 21))
                        if pend is not None:
                            pend()
                        hs_ap, hs_rg = hs(tt, m)
                        P.op("act", I("activation", out=hs_ap, in_=ps[:, :], func=AF.Copy),
                             reads=[(psn, 0, 512)], writes=[hs_rg])
                        pend = add_sq(ps[:, :], (psn, 0, 512), m == 0, m == 7, SSb[tt][0], SSb[tt][1], defer=True)
                        for _ in range(2):
                            if bg:
                                bg.pop(0)()
                pend()
                while bg:
                    bg.pop(0)()
                for tt in (0, 1):
                    t0 = tok0 + 512 * tt
                    rs_ap, rs_rg = stats_to_rs(SSb[tt][0], SSb[tt][1])
                    for m in range(8):
                        hs_ap, hs_rg = hs(tt, m)
                        P.op("dve", I("scalar_tensor_tensor", out=hs_ap, in0=hs_ap, scalar=GV[:, go + m:go + m + 1],
                                      in1=rs_ap, op0=ALU.mult, op1=ALU.mult),
                             reads=[hs_rg, rs_rg, ("GV", 0, 128)], writes=[hs_rg])
                        xo = m * S + t0
                        eng = "dve"
                        P.op(eng, I("tensor_tensor", out=X[:, xo:xo + 512], in0=X[:, xo:xo + 512], in1=hs_ap, op=ALU.add),
                             reads=[("X", xo, xo + 512), hs_rg], writes=[("X", xo, xo + 512)])

            def mask_setup():
                io_ap, io_rg = scr(9, 0, 128)
                P.op("pool", I("iota", io_ap, [[1, 128]], base=0, channel_multiplier=-1,
                               allow_small_or_imprecise_dtypes=True), writes=[io_rg])
                vc, vc_rg = scr(8, 0, 128)
                vp, vp_rg = scr(8, 128, 128)
                dc, dc_rg = scr(8, 256, 128)
                dp, dp_rg = scr(8, 384, 128)
                P.op("dve", I("tensor_single_scalar", out=vc, in_=io_ap, scalar=0.0, op=ALU.is_ge), reads=[io_rg], writes=[vc_rg])
                P.op("dve", I("tensor_single_scalar", out=vp, in_=io_ap, scalar=0.0, op=ALU.is_le), reads=[io_rg], writes=[vp_rg])
                P.op("dve", I("tensor_single_scalar", out=dc, in_=io_ap, scalar=0.0, op=ALU.max), reads=[io_rg], writes=[dc_rg])
                P.op("dve", I("tensor_scalar", out=dp, in0=io_ap, scalar1=0.0, scalar2=128.0, op0=ALU.min, op1=ALU.add),
                     reads=[io_rg], writes=[dp_rg])

            mk = [0]

            def build_mask(g, c):
                vc, vc_rg = scr(8, 0, 128)
                vp, vp_rg = scr(8, 128, 128)
                dc, dc_rg = scr(8, 256, 128)
                dp, dp_rg = scr(8, 384, 128)
                for s_ in range(2):
                    h = 2 * c + s_
                    sd = float(2.0 ** (-8.0 * (g * 8 + h + 1) / 24.0)) * DIL[g]
                    mo = (g * 4 + c) * 512
                    for (d_ap, d_rg, v_ap, v_rg, co) in ((dc, dc_rg, vc, vc_rg, s_ * 128), (dp, dp_rg, vp, vp_rg, 256 + s_ * 128)):
                        t_ap, t_rg = scr(9, 128 + 128 * (mk[0] % 2), 128)
                        mk[0] += 1
                        P.op("act", I("activation", out=t_ap, in_=d_ap, func=AF.Exp, scale=-sd), reads=[d_rg], writes=[t_rg])
                        P.op("dve", I("tensor_tensor", out=MASK[:, mo + co:mo + co + 128], in0=t_ap, in1=v_ap, op=ALU.mult),
                             reads=[t_rg, v_rg], writes=[("MASK", mo + co, mo + co + 128)])

            QO, KAO, KBO, VO, MO = 0, 2048, 4096, 6144, 10240

            def attention(l, pre_tiles=0):
                prenorm(l, 0, 512 * pre_tiles, 4 - pre_tiles)
                first_attn = not masks_built[0]
                if first_attn:
                    mask_setup()
                    masks_built[0] = True
                P.op("dve", I("memset", BIG[:, KAO:MO], 0.0), writes=[("BIG", KAO, MO)])
                Vv = BIG[:, VO:MO].rearrange("p (b s c) -> p b s c", b=16, s=2)
                for c in range(4):
                    for g in range(3):
                        Dl = DIL[g]
                        nb = (S // Dl) // 128
                        woff = W.next("wqkv%d" % l, g * 4 + c, 3072)
                        if first_attn:
                            build_mask(g, c)
                        for wh in range(2):
                            for tt in range(4):
                                ps, psn = ps_mm()
                                for kc in range(8):
                                    lo = woff + kc * 384 + wh * 128
                                    xo = kc * S + 512 * tt
                                    P.op("pe", I("matmul", ps[:, :], lhsT=WB[:, lo:lo + 128], rhs=XN[:, xo:xo + 512],
                                                 start=(kc == 0), stop=(kc == 7)),
                                         reads=[("WB", lo, lo + 128), ("XN", xo, xo + 512)], writes=[(psn, 0, 512)],
                                         inc=(kc == 7))
                                nu = 512 // Dl
                                u0 = tt * nu

                                def views(dst_rows, base):
                                    if Dl == 1:
                                        return (BIG[dst_rows, base + 512 * tt:base + 512 * tt + 512], ps[dst_rows, :])
                                    ov = BIG[dst_rows, base:base + S].rearrange("p (r u) -> p u r", r=Dl)[:, u0:u0 + nu, :]
                                    iv = ps[dst_rows, :].rearrange("p (u r) -> p u r", r=Dl)
                                    return ov, iv
                                if wh == 0:
                                    ov, iv = views(slice(0, 128), QO)
                                    P.op("act", I("activation", out=ov, in_=iv, func=AF.Copy),
                                         reads=[(psn, 0, 512)], writes=[("BIG", QO, QO + S)])
                                else:
                                    ov, iv = views(slice(0, 64), KAO)
                                    P.op("act", I("activation", out=ov, in_=iv, func=AF.Copy), reads=[(psn, 0, 512)], writes=[("BIG", KAO, KAO + S)])
                                    ov, iv = views(slice(64, 128), KBO)
                                    P.op("dve", I("tensor_copy", out=ov, in_=iv), reads=[(psn, 0, 512)], writes=[("BIG", KBO, KBO + S)])
                        for b0 in range(0, 16, 4):
                            ps, psn = ps_mm()
                            for bb in range(4):
                                b = b0 + bb
                                r, n = b // nb, b % nb
                                stok = r + Dl * 128 * n
                                for kc in range(8):
                                    a0 = kc * S + stok
                                    a1 = a0 + Dl * 127 + 1
                                    lo = woff + kc * 384 + 256
                                    P.op("pe", I("matmul", ps[:, bb * 128:(bb + 1) * 128], lhsT=XN[:, a0:a1:Dl], rhs=WB[:, lo:lo + 128],
                                                 start=(kc == 0), stop=(kc == 7)),
                                         reads=[("XN", a0, a1), ("WB", lo, lo + 128)], writes=[(psn, bb * 128, (bb + 1) * 128)],
                                         inc=(bb == 3 and kc == 7))
                            psv = ps[:, :].rearrange("p (b c) -> p b c", b=4)
                            vlo = VO + b0 * 256
                            P.op("act", I("activation", out=Vv[:, b0:b0 + 4, 0, 0:64], in_=psv[:, :, 0:64], func=AF.Copy),
                                 reads=[(psn, 0, 512)], writes=[("BIG", vlo, vlo + 1024)])
                            P.op("dve", I("tensor_copy", out=Vv[:, b0:b0 + 4, 1, 64:128], in_=psv[:, :, 64:128]),
                                 reads=[(psn, 0, 512)], writes=[("BIG", vlo, vlo + 1024)])
                        mo = (g * 4 + c) * 512

                        def stage1(b):
                            r, n = b // nb, b % nb
                            with_prev = (n != 0)
                            ncols = 512 if with_prev else 256
                            ps, psn = ps_mm()
                            mms = [(0, KAO + 128 * b), (128, KBO + 128 * b)]
                            if with_prev:
                                mms += [(256, KAO + 128 * (b - 1)), (384, KBO + 128 * (b - 1))]
                            qo = QO + 128 * b
                            for idx, (co, ko) in enumerate(mms):
                                P.op("pe", I("matmul", ps[:, co:co + 128], lhsT=BIG[:, ko:ko + 128], rhs=BIG[:, qo:qo + 128],
                                             start=True, stop=True),
                                     reads=[("BIG", ko, ko + 128), ("BIG", qo, qo + 128)], writes=[(psn, co, co + 128)],
                                     inc=(idx == len(mms) - 1))
                            k2 = ei[0] % 3
                            ei[0] += 1
                            e_ap, e_rg = sb16(k2, ncols)
                            pt_ap, pt_rg = sb16(3 + k2, ncols)
                            P.op("act", I("activation", out=e_ap, in_=ps[:, 0:ncols], func=AF.Exp, scale=0.125),
                                 reads=[(psn, 0, ncols)], writes=[e_rg])
                            P.op("dve", I("tensor_tensor", out=pt_ap, in0=e_ap, in1=MASK[:, mo:mo + ncols], op=ALU.mult),
                                 reads=[e_rg, ("MASK", mo, mo + ncols)], writes=[pt_rg])
                            return (b, r, n, with_prev, (3 + k2) * 512, pt_rg)

                        def stage2(st1):
                            b, r, n, with_prev, ptb, pt_rg = st1
                            puz, puzn = ps_aux()
                            terms = [(b, 0, 0), (b, 1, 128)]
                            if with_prev:
                                terms += [(b - 1, 0, 256), (b - 1, 1, 384)]
                            for k, (vb, s_, pco) in enumerate(terms):
                                vlo = VO + vb * 256 + s_ * 128
                                P.op("pe", I("matmul", puz[:, 0:128], lhsT=Vv[:, vb, s_, :], rhs=SB16[:, ptb + pco:ptb + pco + 128],
                                             start=(k == 0), stop=(k == len(terms) - 1)),
                                     reads=[("BIG", vlo, vlo + 128), pt_rg], writes=[(puzn, 0, 128)], inc=False)
                            for k, (vb, s_, pco) in enumerate(terms):
                                P.op("pe", I("matmul", puz[:, 128:256], lhsT=ONESAB[:, s_ * 128:(s_ + 1) * 128],
                                             rhs=SB16[:, ptb + pco:ptb + pco + 128],
                                             start=(k == 0), stop=(k == len(terms) - 1)),
                                     reads=[("ONESAB", 0, 256), pt_rg], writes=[(puzn, 128, 256)],
                                     inc=(k == len(terms) - 1))
                            stok = r + Dl * 128 * n
                            e1 = stok + Dl * 127 + 1
                            dst = SCR[:, 0:4096].rearrange("p (z t) -> p z t", z=2)[:, :, stok:e1:Dl]
                            src = puz[:, 0:256].rearrange("p (z q) -> p z q", z=2)
                            rgs = [("SCR", stok, e1), ("SCR", 2048 + stok, 2048 + e1)]
                            if g == 0:
                                P.op("act", I("activation", out=dst, in_=src, func=AF.Copy), reads=[(puzn, 0, 256)], writes=rgs)
                            else:
                                P.op("dve", I("tensor_tensor", out=dst, in0=src, in1=dst, op=ALU.add),
                                     reads=[(puzn, 0, 256)] + rgs, writes=rgs)
                        sts = [stage1(0), stage1(1)]
                        for b in range(2, 16):
                            sts.append(stage1(b))
                            stage2(sts.pop(0))
                        stage2(sts.pop(0))
                        stage2(sts.pop(0))
                    mgo = MO + c * S
                    for q4 in range(4):
                        a0, a1 = 512 * q4, 512 * q4 + 512
                        P.op("act", I("activation", out=SCR[:, 2048 + a0:2048 + a1], in_=SCR[:, 2048 + a0:2048 + a1], func=AF.Ln),
                             reads=[("SCR", 2048 + a0, 2048 + a1)], writes=[("SCR", 2048 + a0, 2048 + a1)])
                        P.op("act", I("activation", out=SCR[:, 2048 + a0:2048 + a1], in_=SCR[:, 2048 + a0:2048 + a1], func=AF.Exp, scale=-1.0),
                             reads=[("SCR", 2048 + a0, 2048 + a1)], writes=[("SCR", 2048 + a0, 2048 + a1)])
                        P.op("dve", I("tensor_tensor", out=BIG[:, mgo + a0:mgo + a1], in0=SCR[:, a0:a1],
                                      in1=SCR[:, 2048 + a0:2048 + a1], op=ALU.mult),
                             reads=[("SCR", a0, a1), ("SCR", 2048 + a0, 2048 + a1)], writes=[("BIG", mgo + a0, mgo + a1)])
                proj_postnorm(l, 1, "wo%d" % l, 4, 4, MO, S, 0, 0, 4, bg=lambda: prenorm_units(l, 2, 0, 2, sqt=(2, 3, 4)))

            def convmix(l, pre_tiles=0):
                prenorm(l, 0, 512 * pre_tiles, 4 - pre_tiles)
                for i in range(8):
                    woff = W.next("cwin", i, 3072)
                    P.op("dve", I("memset", SCR[:, 0:2], 0.0), writes=[("SCR", 0, 2)])
                    for tt in range(4):
                        t0 = 512 * tt
                        pss = []
                        for j in range(3):
                            ps, psn = ps_mm()
                            pss.append((ps, psn))
                            for kc in range(8):
                                lo = woff + kc * 384 + j * 128
                                xo = kc * S + t0
                                P.op("pe", I("matmul", ps[:, :], lhsT=WB[:, lo:lo + 128], rhs=XN[:, xo:xo + 512],
                                             start=(kc == 0), stop=(kc == 7)),
                                     reads=[("WB", lo, lo + 128), ("XN", xo, xo + 512)], writes=[(psn, 0, 512)],
                                     inc=(kc == 7))
                        (pB, pBn), (pC, pCn), (pH, pHn) = pss
                        hb, hb_rg = scr(8)
                        bb_, bb_rg = scr(9)
                        P.op("act", I("activation", out=hb, in_=pH[:, :], func=AF.Copy), reads=[(pHn, 0, 512)], writes=[hb_rg])
                        P.op("act", I("activation", out=bb_, in_=pB[:, :], func=AF.Copy), reads=[(pBn, 0, 512)], writes=[bb_rg])
                        zo = (tt % 2) * ST
                        zno = ((tt + 1) % 2) * ST
                        P.op("dve", I("tensor_tensor", out=SCR[:, zo + 2:zo + 514], in0=pC[:, :], in1=hb, op=ALU.mult),
                             reads=[(pCn, 0, 512), hb_rg], writes=[("SCR", zo + 2, zo + 514)])
                        if tt < 3:
                            P.op("dve", I("tensor_copy", out=SCR[:, zno:zno + 2], in_=SCR[:, zo + 512:zo + 514]),
                                 reads=[("SCR", zo + 512, zo + 514)], writes=[("SCR", zno, zno + 2)])
                        to = (2 + tt % 2) * ST
                        P.op("dve", I("tensor_scalar", out=SCR[:, to:to + 512], in0=SCR[:, zo + 2:zo + 514],
                                      scalar1=CDW[:, i * 3 + 2:i * 3 + 3], scalar2=None, op0=ALU.mult),
                             reads=[("SCR", zo + 2, zo + 514), ("CDW", 0, 24)], writes=[("SCR", to, to + 512)])
                        for k in (1, 0):
                            P.op("dve", I("scalar_tensor_tensor", out=SCR[:, to:to + 512], in0=SCR[:, zo + k:zo + k + 512],
                                          scalar=CDW[:, i * 3 + k:i * 3 + k + 1], in1=SCR[:, to:to + 512],
                                          op0=ALU.mult, op1=ALU.add),
                                 reads=[("SCR", zo + k, zo + k + 512), ("SCR", to, to + 512), ("CDW", 0, 24)],
                                 writes=[("SCR", to, to + 512)])
                        yo = i * S + t0
                        P.op("dve", I("tensor_tensor", out=BIG[:, yo:yo + 512], in0=SCR[:, to:to + 512], in1=bb_, op=ALU.mult),
                             reads=[("SCR", to, to + 512), bb_rg], writes=[("BIG", yo, yo + 512)])
                proj_postnorm(l, 1, "cwout", 8, 2, 0, S, 0, 0, 4, bg=lambda: prenorm_units(l, 2, 0, 2, sqt=(2, 3, 4)))

            def poolmix(l, pre_tiles=0):
                prenorm(l, 0, 512 * pre_tiles, 4 - pre_tiles)
                CS0 = 2096
                PO = 16384
                SBf = SB16.bitcast(F32)
                ZT = SBf[:, 0:512]
                ZT_rg = ("SB16", 0, 1024)
                TM = HALO[:, 0:16]
                TM_rg = ("HALO", 0, 16)
                P.op("dve", I("memset", SCR[:, 2080:2096], 0.0), writes=[("SCR", 2080, 2096)])
                P.op("dve", I("memset", ZT, 0.0), writes=[ZT_rg])
                chunk_i = [0]
                for t in range(16):
                    P.op("dve", I("memset", INVT[:, t:t + 1], 1.0 / (t + 1)), writes=[("INVT", t, t + 1)])
                PSETS = [[(BIG, "BIG", 16384), (BIG, "BIG", 18432)], [(BIG, "BIG", 20480), (SB16, "SB16", 1024)]]

                def inproj(gi):
                    w = 2 ** (gi + 1)
                    woff = W.next("pwin", gi, 2048)
                    for cc in range(2):
                        UB0 = 0 if chunk_i[0] % 2 == 0 else 4160
                        chunk_i[0] += 1
                        for tt in range(4):
                            ps, psn = ps_mm()
                            for kc in range(8):
                                lo = woff + kc * 256 + cc * 128
                                xo = kc * S + 512 * tt
                                P.op("pe", I("matmul", ps[:, :], lhsT=WB[:, lo:lo + 128], rhs=XN[:, xo:xo + 512],
                                             start=(kc == 0), stop=(kc == 7)),
                                     reads=[("WB", lo, lo + 128), ("XN", xo, xo + 512)], writes=[(psn, 0, 512)],
                                     inc=(kc == 7))
                            P.op("act", I("activation", out=SCR[:, UB0 + 512 * tt:UB0 + 512 * tt + 512], in_=ps[:, :], func=AF.Copy),
                                 reads=[(psn, 0, 512)], writes=[("SCR", UB0 + 512 * tt, UB0 + 512 * tt + 512)])
                        for q in range(4):
                            co = CS0 + 512 * q
                            init = 0.0 if q == 0 else SCR[:, co - 1:co]
                            P.op("dve", I("tensor_tensor_scan", out=SCR[:, co:co + 512], data0=SCR[:, UB0 + 512 * q:UB0 + 512 * q + 512],
                                          data1=ZT, initial=init, op0=ALU.add, op1=ALU.add),
                                 reads=[("SCR", UB0 + 512 * q, UB0 + 512 * q + 512), ZT_rg, ("SCR", co - 1, co)],
                                 writes=[("SCR", co, co + 512)])
                        P.op("dve", I("tensor_tensor", out=TM, in0=SCR[:, CS0:CS0 + 16], in1=INVT[:, 0:16], op=ALU.mult),
                             reads=[("SCR", CS0, CS0 + 16), ("INVT", 0, 16)], writes=[TM_rg])
                        P.op("dve", I("tensor_tensor", out=TM, in0=TM, in1=SCR[:, UB0:UB0 + 16], op=ALU.subtract),
                             reads=[TM_rg, ("SCR", UB0, UB0 + 16)], writes=[TM_rg])
                        P.op("dve", I("scalar_tensor_tensor", out=SCR[:, UB0:UB0 + S], in0=SCR[:, CS0:CS0 + S], scalar=1.0 / w,
                                      in1=SCR[:, UB0:UB0 + S], op0=ALU.mult, op1=ALU.subtract),
                             reads=[("SCR", CS0, CS0 + S), ("SCR", UB0, UB0 + S)], writes=[("SCR", UB0, UB0 + S)])
                        pt_, pn_, po = PSETS[gi % 2][cc]
                        P.op("dve", I("scalar_tensor_tensor", out=pt_[:, po:po + S], in0=SCR[:, CS0 - w:CS0 - w + S], scalar=-1.0 / w,
                                      in1=SCR[:, UB0:UB0 + S], op0=ALU.mult, op1=ALU.add),
                             reads=[("SCR", CS0 - w, CS0 - w + S), ("SCR", UB0, UB0 + S)], writes=[(pn_, po, po + S)])
                        P.op("dve", I("tensor_copy", out=pt_[:, po:po + w - 1], in_=HALO[:, 0:w - 1]),
                             reads=[TM_rg], writes=[(pn_, po, po + w - 1)])

                def grp(gi):
                    goff = W.next("pwgrp", gi, 512)
                    for mm in range(2):
                        for tt in range(4):
                            ps, psn = ps_mm()
                            for kc in range(2):
                                lo = goff + kc * 256 + mm * 128
                                pt_, pn_, po = PSETS[gi % 2][kc]
                                io = po + 512 * tt
                                P.op("pe", I("matmul", ps[:, :], lhsT=WB[:, lo:lo + 128], rhs=pt_[:, io:io + 512],
                                             start=(kc == 0), stop=(kc == 1)),
                                     reads=[("WB", lo, lo + 128), (pn_, io, io + 512)], writes=[(psn, 0, 512)],
                                     inc=(kc == 1))
                            ch = 2 * gi + mm
                            yo = ch * S + 512 * tt
                            P.op("act", I("activation", out=BIG[:, yo:yo + 512], in_=ps[:, :], func=AF.Copy, scale=PSC[:, ch:ch + 1]),
                                 reads=[(psn, 0, 512), ("PSC", 0, 8)], writes=[("BIG", yo, yo + 512)])
                inproj(0)
                for gi in range(4):
                    if gi + 1 < 4:
                        inproj(gi + 1)
                    grp(gi)
                proj_postnorm(l, 1, "pwout", 8, 2, 0, S, 0, 0, 4, bg=lambda: prenorm_units(l, 2, 0, 2, sqt=(2, 3, 4)))

            def store_out(h0, h1, strm):
                for c in range(8):
                    P.op("sp", I("dma_start", out=yT[c * 128:(c + 1) * 128, h0:h1], in_=X[:, c * S + h0:c * S + h1]),
                         reads=[("X", c * S + h0, c * S + h1)], stream=strm, amt=16)

            run_layers = EXEC_LAYERS if EXEC_LAYERS is not None else layers
            masks_built = [False]
            for li, l in enumerate(run_layers):
                kind = l % 3
                pre_tiles = 2 if li > 0 else 0
                if kind == 0:
                    attention(l, pre_tiles)
                elif kind == 1:
                    convmix(l, pre_tiles)
                else:
                    poolmix(l, pre_tiles)
                ffn(l, pre_done=True, next_l=(run_layers[li + 1] if li + 1 < len(run_layers) else None))
            store_out(1024, 1536, "ost1")
            store_out(1536, 2048, "ost2")
            P.wait_all("sp", ["ost0", "ost1", "ost2"])

        def t_name(t, nm):
            return {"gvec": "GV", "dwv": "DWV", "cdw": "CDW", "psc": "PSC"}[nm]

        Wd = WMgr(DummyProg(), dram, WB, None)
        construct(DummyProg(), Wd)
        P = Prog()
        P.unordered.update(["xld0", "xld1", "xld2", "cst", "ost0", "ost1", "ost2"])
        W = WMgr(P, dram, WB, Wd.rec)
        construct(P, W)
        assert W.used == len(Wd.rec) and W.issued == len(Wd.rec)
        P.emit(nc, st)
    return nc


def chunkmajor(Wm, cols):
    sub = Wm[:, cols]
    nk = Wm.shape[0] // 128
    return np.ascontiguousarray(sub.reshape(nk, 128, -1).transpose(1, 0, 2).reshape(128, -1))


def host_layout(inp):
    f = lambda a: np.asarray(a, dtype=np.float32)
    out = {}
    ng = f(inp["norm_g"])
    out["gvec"] = np.ascontiguousarray(ng.reshape(16, 8, 128).transpose(2, 0, 1).reshape(128, 128))
    dw = f(inp["ffn_w_dw"])
    out["dwv"] = np.ascontiguousarray(dw.reshape(4, 3, 2, 22, 128).transpose(4, 0, 3, 2, 1).reshape(128, 528))
    cdw = f(inp["conv_w_dw"])[0]
    out["cdw"] = np.ascontiguousarray(cdw.reshape(3, 8, 128).transpose(2, 1, 0).reshape(128, 24))
    out["psc"] = np.ascontiguousarray(f(inp["pool_scale"])[0].reshape(8, 128).T)
    ar = np.arange
    for l in range(4):
        wu = f(inp["ffn_w_up"])[l]
        out["wup%d" % l] = np.stack([chunkmajor(wu, np.concatenate([ar(128 * i, 128 * i + 128), ar(2816 + 128 * i, 2816 + 128 * i + 128)]))
                                     for i in range(22)])
        wd = f(inp["ffn_w_down"])[l]
        out["wdn%d" % l] = np.stack([chunkmajor(wd, ar(128 * m, 128 * m + 128)) for m in range(8)])
    for ia, l in enumerate((0, 3)):
        wq = f(inp["attn_w_qkv"])[ia]
        blks = []
        for g in range(3):
            for c in range(4):
                base = g * 1536 + c * 128
                blks.append(chunkmajor(wq, np.concatenate([ar(base, base + 128), ar(base + 512, base + 640), ar(base + 1024, base + 1152)])))
        out["wqkv%d" % l] = np.stack(blks)
        wo = f(inp["attn_w_o"])[ia]
        out["wo%d" % l] = np.stack([chunkmajor(wo, ar(512 * b, 512 * b + 512)) for b in range(2)])
    cw = f(inp["conv_w_in"])[0]
    out["cwin"] = np.stack([chunkmajor(cw, np.concatenate([ar(128 * i, 128 * i + 128), ar(1024 + 128 * i, 1152 + 128 * i), ar(2048 + 128 * i, 2176 + 128 * i)]))
                            for i in range(8)])
    for nm, key in (("cwout", "conv_w_out"), ("pwin", "pool_w_in"), ("pwout", "pool_w_out")):
        wm = f(inp[key])[0]
        out[nm] = np.stack([chunkmajor(wm, ar(256 * b, 256 * b + 256)) for b in range(4)])
    wg = f(inp["pool_w_grp"])[0]
    out["pwgrp"] = np.stack([chunkmajor(wg[g], ar(0, 256)) for g in range(4)])
    return out


_PROG_CACHE = {}


def _run(layers, xT_list, lay):
    key = tuple(layers)
    if key not in _PROG_CACHE:
        _PROG_CACHE[key] = build_program(list(layers))
    nc = _PROG_CACHE[key]
    names = ["gvec", "dwv", "cdw", "psc"]
    for l in layers:
        names += ["wup%d" % l, "wdn%d" % l]
        kind = l % 3
        if kind == 0:
            names += ["wqkv%d" % l, "wo%d" % l]
        elif kind == 1:
            names += ["cwin", "cwout"]
        else:
            names += ["pwin", "pwgrp", "pwout"]
    in_maps = []
    for b in range(8):
        m = {n: lay[n] for n in names}
        m["xT"] = xT_list[b]
        in_maps.append(m)
    res = run_bass_kernel_spmd(nc, in_maps, core_ids=list(range(8)))
    return [np.asarray(r["yT"]) for r in res.results]


def kernel(**inputs):
    lay = host_layout(inputs)
    x = np.asarray(inputs["x"], dtype=np.float32)
    xT = [np.ascontiguousarray(x[b].T) for b in range(8)]
    if FUSED:
        yT = _run((0, 1, 2, 3), xT, lay)
    else:
        yT = xT
        for l in range(4):
            yT = _run((l,), yT, lay)
    return np.ascontiguousarray(np.stack([y.T for y in yT]).astype(np.float32))
```

```python
import numpy as np
import concourse.bass as bass
import concourse.mybir as mybir
from concourse.bass_utils import run_bass_kernel_spmd
from contextlib import ExitStack

F32 = mybir.dt.float32
BF16 = mybir.dt.bfloat16
AF = mybir.ActivationFunctionType
ALU = mybir.AluOpType

S = 2048
EPS = 1e-6
DIL = (1, 4, 16)
NSLOT = 3
SLOT = 3072
ST = 520
NT = 12
FUSED = True
EXEC_LAYERS = None


class Prog:
    ENG = ("pe", "act", "dve", "pool", "sp")

    def __init__(self):
        self.q = {e: [] for e in self.ENG}
        self.cnt = {}
        self.waited = {e: {} for e in self.ENG}
        self.recs = {}
        self.unordered = set()
        self.pe_pending = False
        self.nops = 0

    def _overlaps(self, name, lo, hi):
        return [r for r in self.recs.get(name, ()) if r[0] < hi and lo < r[1]]

    def op(self, eng, fn, reads=(), writes=(), inc=True, stream=None, amt=1):
        own = stream if stream is not None else eng
        if self.pe_pending and eng != "pe":
            raise RuntimeError("non-PE op constructed inside an open PE group")
        deps = {}

        def add(dep):
            if dep is None:
                return
            s, c = dep
            if s in self.unordered:
                c = self.cnt.get(s, 0)
            if eng == "pe" and s == "pe":
                return
            if deps.get(s, 0) < c:
                deps[s] = c
        for (name, lo, hi) in reads:
            for r in self._overlaps(name, lo, hi):
                add(r[2])
        for (name, lo, hi) in writes:
            for r in self._overlaps(name, lo, hi):
                add(r[2])
                for s, c in r[3].items():
                    add((s, c))
        for s, c in deps.items():
            if self.waited[eng].get(s, 0) < c:
                self.waited[eng][s] = c
                self.q[eng].append(("wait", s, c))
        after = self.cnt.get(own, 0) + amt
        if inc:
            self.cnt[own] = after
            if eng == "pe":
                self.pe_pending = False
        else:
            assert eng == "pe"
            self.pe_pending = True
        self.q[eng].append(("op", fn, own if inc else None, amt))
        self.nops += 1
        for (name, lo, hi) in reads:
            lst = self.recs.setdefault(name, [])
            for r in lst:
                if r[0] == lo and r[1] == hi and r[2] is None:
                    r[3][own] = after
                    break
            else:
                lst.append([lo, hi, None, {own: after}])
        for (name, lo, hi) in writes:
            lst = self.recs.setdefault(name, [])
            lst[:] = [r for r in lst if not (lo <= r[0] and r[1] <= hi)]
            lst.append([lo, hi, (own, after), {}])

    def wait_all(self, eng, streams):
        for s in streams:
            c = self.cnt.get(s, 0)
            if c and self.waited[eng].get(s, 0) < c:
                self.waited[eng][s] = c
                self.q[eng].append(("wait", s, c))

    def emit(self, nc, stack):
        assert not self.pe_pending
        sems = {s: stack.enter_context(nc.semaphore("s_" + s)) for s in self.cnt}
        block = stack.enter_context(nc.Block())

        def replay(e, name):
            for it in self.q[name]:
                if it[0] == "wait":
                    e.wait_ge(sems[it[1]], it[2])
                else:
                    ins = it[1](e)
                    if it[2] is not None:
                        ins.then_inc(sems[it[2]], it[3])

        @block.tensor
        def _(e):
            replay(e, "pe")

        @block.scalar
        def _(e):
            replay(e, "act")

        @block.vector
        def _(e):
            replay(e, "dve")

        @block.gpsimd
        def _(e):
            replay(e, "pool")

        @block.sync
        def _(e):
            replay(e, "sp")


class DummyProg:
    def op(self, *a, **k):
        pass

    def wait_all(self, *a, **k):
        pass


def I(fn, *a, **k):
    return lambda e: getattr(e, fn)(*a, **k)


class WMgr:
    def __init__(self, P, dram, WB, sched):
        self.P, self.dram, self.WB, self.sched = P, dram, WB, sched
        self.rec = []
        self.issued = 0
        self.used = 0

    def next(self, name, blk, nelem):
        if self.sched is None:
            self.rec.append((name, blk, nelem))
            return 0
        i = self.used
        assert self.sched[i] == (name, blk, nelem), (self.sched[i], name, blk, nelem)
        while self.issued < min(i + NSLOT, len(self.sched)):
            self._issue(self.issued)
            self.issued += 1
        self.used += 1
        return (i % NSLOT) * SLOT

    def _issue(self, j):
        name, blk, nelem = self.sched[j]
        slot = j % NSLOT
        off = slot * SLOT
        self.P.op("pool", I("dma_start", out=self.WB[:, off:off + nelem], in_=self.dram[name][blk]),
                  writes=[("WB", off, off + nelem)], stream="w%d" % slot, amt=16)


def build_program(layers):
    nc = bass.Bass("TRN2", target_bir_lowering=False)
    dram = {}

    def din(name, shape):
        dram[name] = nc.dram_tensor(name, list(shape), F32, kind="ExternalInput").ap()
    din("xT", (1024, S))
    din("gvec", (128, 128))
    din("dwv", (128, 528))
    din("cdw", (128, 24))
    din("psc", (128, 8))
    for l in layers:
        din("wup%d" % l, (22, 128, 2048))
        din("wdn%d" % l, (8, 128, 2816))
        kind = l % 3
        if kind == 0:
            din("wqkv%d" % l, (12, 128, 3072))
            din("wo%d" % l, (2, 128, 2048))
        elif kind == 1:
            din("cwin", (8, 128, 3072))
            din("cwout", (4, 128, 2048))
        else:
            din("pwin", (4, 128, 2048))
            din("pwgrp", (4, 128, 512))
            din("pwout", (4, 128, 2048))
    yT = nc.dram_tensor("yT", [1024, S], F32, kind="ExternalOutput").ap()

    with ExitStack() as st:
        def sb(name, n, dt):
            return st.enter_context(nc.sbuf_tensor(name, [128, n], dt))
        X = sb("X", 8 * S, F32)
        XN = sb("XN", 8 * S, BF16)
        BIG = sb("BIG", 22528, BF16)
        SCR = sb("SCR", NT * ST, F32)
        SB16 = sb("SB16", 6 * 512, BF16)
        WB = sb("WB", NSLOT * SLOT, BF16)
        MASK = sb("MASK", 12 * 512, BF16)
        GV = sb("GV", 128, F32)
        DWV = sb("DWV", 528, F32)
        CDW = sb("CDW", 24, F32)
        PSC = sb("PSC", 8, F32)
        ONES = sb("ONES", 128, BF16)
        ONESAB = sb("ONESAB", 256, BF16)
        HALO = sb("HALO", 88, F32)
        INVT = sb("INVT", 16, F32)
        PS = [st.enter_context(nc.psum_tensor("PS%d" % b, [128, 512], F32)) for b in range(8)]

        def construct(P, W):
            mmi = [0]
            auxi = [0]
            sqi = [0]
            rsi = [0]
            ei = [0]

            def ps_mm():
                b = mmi[0] % 5
                mmi[0] += 1
                return PS[b], "PS%d" % b

            upi = [0]

            def ps_up():
                b = upi[0] % 8
                upi[0] += 1
                return PS[b], "PS%d" % b

            def ps_aux():
                b = (6, 7, 5)[auxi[0] % 3]
                auxi[0] += 1
                return PS[b], "PS%d" % b
            SS, SSn = PS[5], "PS5"

            def scr(i, lo=0, n=512):
                o = i * ST + lo
                return SCR[:, o:o + n], ("SCR", o, o + n)

            def sb16(i, n=512):
                return SB16[:, i * 512:i * 512 + n], ("SB16", i * 512, i * 512 + n)

            for (h0, h1, strm) in ((0, 512, "xld0"), (512, 1024, "xld1"), (1024, 2048, "xld2")):
                for c in range(8):
                    P.op("sp",
                         I("dma_start", out=X[:, c * S + h0:c * S + h1], in_=dram["xT"][c * 128:(c + 1) * 128, h0:h1]),
                         writes=[("X", c * S + h0, c * S + h1)], stream=strm, amt=16)
            for (t, nm, n) in ((GV, "gvec", 128), (DWV, "dwv", 528), (CDW, "cdw", 24), (PSC, "psc", 8)):
                P.op("sp", I("dma_start", out=t[:, :], in_=dram[nm][:, :]), writes=[(t_name(t, nm), 0, n)], stream="cst", amt=16)
            P.op("dve", I("memset", ONES[:, :], 1.0), writes=[("ONES", 0, 128)])
            P.op("dve", I("memset", ONESAB[:, :], 0.0), writes=[("ONESAB", 0, 256)])
            P.op("dve", I("memset", ONESAB[:, 0:64], 1.0), writes=[("ONESAB", 0, 64)])
            P.op("dve", I("memset", ONESAB[:, 192:256], 1.0), writes=[("ONESAB", 192, 256)])

            XNf = XN.bitcast(F32)
            MASKf = MASK.bitcast(F32)

            def add_sq(src_ap, src_rg, first, last, SS=SS, SSn=SSn, defer=False):
                k = sqi[0] % 2
                sqi[0] += 1
                sq_ap, sq_rg = sb16(k)
                P.op("act", I("activation", out=sq_ap, in_=src_ap, func=AF.Square), reads=[src_rg], writes=[sq_rg])

                def pe_part():
                    P.op("pe", I("matmul", SS[:, :], lhsT=ONES[:, :], rhs=sq_ap, start=first, stop=last),
                         reads=[sq_rg, ("ONES", 0, 128)], writes=[(SSn, 0, 512)])
                if defer:
                    return pe_part
                pe_part()
                return None

            def stats_to_rs(SS=SS, SSn=SSn):
                r = 10 + rsi[0] % 2
                rsi[0] += 1
                rs_ap, rs_rg = scr(r)
                P.op("act", I("activation", out=rs_ap, in_=SS[:, :], func=AF.Ln, scale=1.0 / 1024.0, bias=EPS),
                     reads=[(SSn, 0, 512)], writes=[rs_rg])
                P.op("act", I("activation", out=rs_ap, in_=rs_ap, func=AF.Exp, scale=-0.5), reads=[rs_rg], writes=[rs_rg])
                return rs_ap, rs_rg

            def prenorm_units(l, j, tok0, ntiles, sqt=(0, 1)):
                go = (l * 4 + j) * 8
                units = []
                for tt in range(ntiles):
                    t0 = tok0 + 512 * tt
                    rs_box = []

                    def sq(c, t0=t0):
                        o = c * S + t0
                        sq_ap, sq_rg = sb16(sqt[c % len(sqt)])
                        P.op("act", I("activation", out=sq_ap, in_=X[:, o:o + 512], func=AF.Square),
                             reads=[("X", o, o + 512)], writes=[sq_rg])

                    def stat(c):
                        sq_ap, sq_rg = sb16(sqt[c % len(sqt)])
                        P.op("pe", I("matmul", SS[:, :], lhsT=ONES[:, :], rhs=sq_ap, start=(c == 0), stop=(c == 7)),
                             reads=[sq_rg, ("ONES", 0, 128)], writes=[(SSn, 0, 512)])

                    def rs(rs_box=rs_box):
                        rs_box.append(stats_to_rs())

                    def apply(c0, c1, t0=t0, rs_box=rs_box):
                        rs_ap, rs_rg = rs_box[0]
                        for c in range(c0, c1):
                            o = c * S + t0
                            P.op("dve", I("scalar_tensor_tensor", out=XN[:, o:o + 512], in0=X[:, o:o + 512],
                                          scalar=GV[:, go + c:go + c + 1], in1=rs_ap, op0=ALU.mult, op1=ALU.mult),
                                 reads=[("X", o, o + 512), rs_rg, ("GV", 0, 128)], writes=[("XN", o, o + 512)])
                    nb_ = len(sqt)
                    units.append(lambda sq=sq: sq(0))
                    if nb_ >= 3:
                        units.append(lambda sq=sq: sq(1))
                        for c in range(2, 8):
                            units.append(lambda c=c, sq=sq, stat=stat: (sq(c), stat(c - 2)))
                        units.append(lambda stat=stat, rs=rs: (stat(6), stat(7), rs()))
                    else:
                        for c in range(1, 8):
                            units.append(lambda c=c, sq=sq, stat=stat: (sq(c), stat(c - 1)))
                        units.append(lambda stat=stat, rs=rs: (stat(7), rs()))
                    units.append(lambda apply=apply: apply(0, 4))
                    units.append(lambda apply=apply: apply(4, 8))
                return units

            def prenorm(l, j, tok0, ntiles):
                for u in prenorm_units(l, j, tok0, ntiles):
                    u()

            def proj_postnorm(l, j, wname, nk, mpb, in_off, in_cs, in_t0, tok0, ntiles, bg=None, bg_after=1):
                go = (l * 4 + j) * 8
                ncols = mpb * 128
                woff = 0
                pend = None
                bg_list = []
                SSp = (PS[6], "PS6")

                def hs(tt, m):
                    if tt % 2 == 0:
                        return scr(m)
                    o = m * S + 1024
                    return XNf[:, o // 2:o // 2 + 512], ("XN", o, o + 1024)
                for tt in range(ntiles):
                    t0 = tok0 + 512 * tt
                    it0 = in_t0 + 512 * tt
                    for m in range(8):
                        if m % mpb == 0:
                            woff = W.next(wname, m // mpb, nk * ncols)
                        ps, psn = ps_mm()
                        for kc in range(nk):
                            lo = woff + kc * ncols + (m % mpb) * 128
                            io = in_off + kc * in_cs + it0
                            P.op("pe", I("matmul", ps[:, :], lhsT=WB[:, lo:lo + 128], rhs=BIG[:, io:io + 512],
                                         start=(kc == 0), stop=(kc == nk - 1)),
                                 reads=[("WB", lo, lo + 128), ("BIG", io, io + 512)], writes=[(psn, 0, 512)],
                                 inc=(kc == nk - 1))
                        if pend is not None:
                            pend()
                        hs_ap, hs_rg = hs(tt, m)
                        P.op("act", I("activation", out=hs_ap, in_=ps[:, :], func=AF.Copy),
                             reads=[(psn, 0, 512)], writes=[hs_rg])
                        pend = add_sq(ps[:, :], (psn, 0, 512), m == 0, m == 7, SSp[0], SSp[1], defer=True)
                        if bg_list and tt > bg_after:
                            for _ in range(2):
                                if bg_list:
                                    bg_list.pop(0)()
                    pend()
                    pend = None
                    rs_ap, rs_rg = stats_to_rs(SSp[0], SSp[1])
                    for m in range(8):
                        hs_ap, hs_rg = hs(tt, m)
                        P.op("dve", I("scalar_tensor_tensor", out=hs_ap, in0=hs_ap, scalar=GV[:, go + m:go + m + 1],
                                      in1=rs_ap, op0=ALU.mult, op1=ALU.mult),
                             reads=[hs_rg, rs_rg, ("GV", 0, 128)], writes=[hs_rg])
                        xo = m * S + t0
                        P.op("dve",
                             I("tensor_tensor", out=X[:, xo:xo + 512], in0=X[:, xo:xo + 512], in1=hs_ap, op=ALU.add),
                             reads=[("X", xo, xo + 512), hs_rg], writes=[("X", xo, xo + 512)])
                    if tt == bg_after and bg is not None:
                        bg_list.extend(bg())
                while bg_list:
                    bg_list.pop(0)()

            def ffn(l, pre_done=False, next_l=None):
                for half in range(2):
                    tok0 = 1024 * half
                    if half == 0 and not pre_done:
                        prenorm(l, 2, 0, 2)
                    if half == 0:
                        P.op("dve", I("memset", HALO[:, :], 0.0), writes=[("HALO", 0, 88)])
                    pend = []
                    ptc = [0]
                    for i in range(22):
                        woff = W.next("wup%d" % l, i, 2048)
                        for tt in range(2):
                            t0 = tok0 + 512 * tt
                            pss = []
                            for wh in range(2):
                                ps, psn = ps_up()
                                pss.append((ps, psn))
                                for kc in range(8):
                                    lo = woff + kc * 256 + wh * 128
                                    xo = kc * S + t0
                                    P.op("pe", I("matmul", ps[:, :], lhsT=WB[:, lo:lo + 128], rhs=XN[:, xo:xo + 512],
                                                 start=(kc == 0), stop=(kc == 7)),
                                         reads=[("WB", lo, lo + 128), ("XN", xo, xo + 512)], writes=[(psn, 0, 512)],
                                         inc=(kc == 7))
                            tos = []
                            for wh in range(2):
                                ps, psn = pss[wh]
                                ch = 2 * i + wh
                                dwo = l * 132 + ch * 3
                                zo = (wh * 3 + (ptc[0] % 3)) * ST
                                to = (6 + wh * 3 + (ptc[0] % 3)) * ST
                                tos.append(to)
                                if tt == 0 and wh == 0:
                                    zv = SCR[:, zo:zo + 6 * ST].rearrange("p (w r) -> p w r", w=2)[:, :, 0:2]
                                    hv = HALO[:, ch * 2:ch * 2 + 4].rearrange("p (w r) -> p w r", w=2)
                                    P.op("act", I("activation", out=zv, in_=hv, func=AF.Copy),
                                         reads=[("HALO", ch * 2, ch * 2 + 4)],
                                         writes=[("SCR", zo, zo + 2), ("SCR", zo + 3 * ST, zo + 3 * ST + 2)])
                                P.op("act", I("activation", out=SCR[:, zo + 2:zo + 514], in_=ps[:, :], func=AF.Copy),
                                     reads=[(psn, 0, 512)], writes=[("SCR", zo + 2, zo + 514)])
                                P.op("act", I("activation", out=SCR[:, to:to + 512], in_=ps[:, :], func=AF.Copy,
                                              scale=DWV[:, dwo + 2:dwo + 3]),
                                     reads=[(psn, 0, 512), ("DWV", 0, 528)], writes=[("SCR", to, to + 512)])
                            pend2 = []
                            for f in pend:
                                pend2.append(f())
                            pend = []
                            ch = 2 * i
                            zo = (ptc[0] % 3) * ST
                            zno = ((ptc[0] + 1) % 3) * ST
                            tv = SCR[:, zo:zo + 6 * ST].rearrange("p (w r) -> p w r", w=2)[:, :, 512:514]
                            t_rg = [("SCR", zo + 512, zo + 514), ("SCR", zo + 3 * ST + 512, zo + 3 * ST + 514)]
                            if tt == 0:
                                nv = SCR[:, zno:zno + 6 * ST].rearrange("p (w r) -> p w r", w=2)[:, :, 0:2]
                                P.op("act", I("activation", out=nv, in_=tv, func=AF.Copy), reads=t_rg,
                                     writes=[("SCR", zno, zno + 2), ("SCR", zno + 3 * ST, zno + 3 * ST + 2)])
                            elif half == 0:
                                hv = HALO[:, ch * 2:ch * 2 + 4].rearrange("p (w r) -> p w r", w=2)
                                P.op("act", I("activation", out=hv, in_=tv, func=AF.Copy), reads=t_rg,
                                     writes=[("HALO", ch * 2, ch * 2 + 4)])
                            for k in (1, 0):
                                for wh in range(2):
                                    ch = 2 * i + wh
                                    dwo = l * 132 + ch * 3
                                    zo = (wh * 3 + (ptc[0] % 3)) * ST
                                    to = tos[wh]
                                    P.op("dve", I("scalar_tensor_tensor", out=SCR[:, to:to + 512], in0=SCR[:, zo + k:zo + k + 512],
                                                  scalar=DWV[:, dwo + k:dwo + k + 1], in1=SCR[:, to:to + 512],
                                                  op0=ALU.mult, op1=ALU.add),
                                         reads=[("SCR", zo + k, zo + k + 512), ("SCR", to, to + 512), ("DWV", 0, 528)],
                                         writes=[("SCR", to, to + 512)])

                            for f2 in pend2:
                                f2()

                            def fin(tg=tos[0], tu=tos[1], ho=i * 1024 + 512 * tt):
                                P.op("act", I("activation", out=SCR[:, tg:tg + 512], in_=SCR[:, tg:tg + 512], func=AF.Silu),
                                     reads=[("SCR", tg, tg + 512)], writes=[("SCR", tg, tg + 512)])

                                def gate():
                                    P.op("dve", I("tensor_tensor", out=BIG[:, ho:ho + 512], in0=SCR[:, tg:tg + 512],
                                                  in1=SCR[:, tu:tu + 512], op=ALU.mult),
                                         reads=[("SCR", tg, tg + 512), ("SCR", tu, tu + 512)], writes=[("BIG", ho, ho + 512)])
                                return gate
                            pend.append(fin)
                            ptc[0] += 1
                    for f in pend:
                        f()()
                    if half == 0:
                        ffn_down(l, 0, bg=prenorm_units(l, 2, 1024, 2, sqt=(2, 3, 4)))
                        if next_l is None:
                            store_out(0, 1024, "ost0")
                    elif next_l is not None:
                        ffn_down(l, 1, bg=prenorm_units(next_l, 0, 0, 2, sqt=(2, 3, 4)))
                    else:
                        ffn_down(l, 1)

            def ffn_down(l, half, bg=()):
                tok0 = 1024 * half
                go = (l * 4 + 3) * 8
                SSb = [(PS[6], "PS6"), (PS[7], "PS7")]
                bg = list(bg)

                def hs(tt, m):
                    if tt == 0:
                        return scr(m)
                    o = m * S + 1024 * half
                    return XNf[:, o // 2:o // 2 + 512], ("XN", o, o + 1024)
                pend = None
                for m in range(8):
                    woff = W.next("wdn%d" % l, m, 2816)
                    for tt in range(2):
                        ps, psn = ps_mm()
                        for kc in range(22):
                            lo = woff + kc * 128
                            io = kc * 1024 + 512 * tt
                            P.op("pe", I("matmul", ps[:, :], lhsT=WB[:, lo:lo + 128], rhs=BIG[:, io:io + 512],
                                         start=(kc == 0), stop=(kc == 21)),
                                 reads=[("WB", lo, lo + 128), ("BIG", io, io + 512)], writes=[(psn, 0, 512)],
                                 inc=(kc == 21))
                        if pend is not None:
                            pend()
                        hs_ap, hs_rg = hs(tt, m)
                        P.op("act", I("activation", out=hs_ap, in_=ps[:, :], func=AF.Copy),
                             reads=[(psn, 0, 512)], writes=[hs_rg])
                        pend = add_sq(ps[:, :], (psn, 0, 512), m == 0, m == 7, SSb[tt][0], SSb[tt][1], defer=True)
                        for _ in range(2):
                            if bg:
                                bg.pop(0)()
                pend()
                while bg:
                    bg.pop(0)()
                for tt in (0, 1):
                    t0 = tok0 + 512 * tt
                    rs_ap, rs_rg = stats_to_rs(SSb[tt][0], SSb[tt][1])
                    for m in range(8):
                        hs_ap, hs_rg = hs(tt, m)
                        P.op("dve", I("scalar_tensor_tensor", out=hs_ap, in0=hs_ap, scalar=GV[:, go + m:go + m + 1],
                                      in1=rs_ap, op0=ALU.mult, op1=ALU.mult),
                             reads=[hs_rg, rs_rg, ("GV", 0, 128)], writes=[hs_rg])
                        xo = m * S + t0
                        eng = "dve"
                        P.op(eng, I("tensor_tensor", out=X[:, xo:xo + 512], in0=X[:, xo:xo + 512], in1=hs_ap, op=ALU.add),
                             reads=[("X", xo, xo + 512), hs_rg], writes=[("X", xo, xo + 512)])

            def mask_setup():
                io_ap, io_rg = scr(9, 0, 128)
                P.op("pool", I("iota", io_ap, [[1, 128]], base=0, channel_multiplier=-1,
                               allow_small_or_imprecise_dtypes=True), writes=[io_rg])
                vc, vc_rg = scr(8, 0, 128)
                vp, vp_rg = scr(8, 128, 128)
                dc, dc_rg = scr(8, 256, 128)
                dp, dp_rg = scr(8, 384, 128)
                P.op("dve", I("tensor_single_scalar", out=vc, in_=io_ap, scalar=0.0, op=ALU.is_ge), reads=[io_rg], writes=[vc_rg])
                P.op("dve", I("tensor_single_scalar", out=vp, in_=io_ap, scalar=0.0, op=ALU.is_le), reads=[io_rg], writes=[vp_rg])
                P.op("dve", I("tensor_single_scalar", out=dc, in_=io_ap, scalar=0.0, op=ALU.max), reads=[io_rg], writes=[dc_rg])
                P.op("dve", I("tensor_scalar", out=dp, in0=io_ap, scalar1=0.0, scalar2=128.0, op0=ALU.min, op1=ALU.add),
                     reads=[io_rg], writes=[dp_rg])

            mk = [0]

            def build_mask(g, c):
                vc, vc_rg = scr(8, 0, 128)
                vp, vp_rg = scr(8, 128, 128)
                dc, dc_rg = scr(8, 256, 128)
                dp, dp_rg = scr(8, 384, 128)
                for s_ in range(2):
                    h = 2 * c + s_
                    sd = float(2.0 ** (-8.0 * (g * 8 + h + 1) / 24.0)) * DIL[g]
                    mo = (g * 4 + c) * 512
                    for (d_ap, d_rg, v_ap, v_rg, co) in ((dc, dc_rg, vc, vc_rg, s_ * 128), (dp, dp_rg, vp, vp_rg, 256 + s_ * 128)):
                        t_ap, t_rg = scr(9, 128 + 128 * (mk[0] % 2), 128)
                        mk[0] += 1
                        P.op("act", I("activation", out=t_ap, in_=d_ap, func=AF.Exp, scale=-sd), reads=[d_rg], writes=[t_rg])
                        P.op("dve", I("tensor_tensor", out=MASK[:, mo + co:mo + co + 128], in0=t_ap, in1=v_ap, op=ALU.mult),
                             reads=[t_rg, v_rg], writes=[("MASK", mo + co, mo + co + 128)])

            QO, KAO, KBO, VO, MO = 0, 2048, 4096, 6144, 10240

            def attention(l, pre_tiles=0):
                prenorm(l, 0, 512 * pre_tiles, 4 - pre_tiles)
                first_attn = not masks_built[0]
                if first_attn:
                    mask_setup()
                    masks_built[0] = True
                P.op("dve", I("memset", BIG[:, KAO:MO], 0.0), writes=[("BIG", KAO, MO)])
                Vv = BIG[:, VO:MO].rearrange("p (b s c) -> p b s c", b=16, s=2)
                for c in range(4):
                    for g in range(3):
                        Dl = DIL[g]
                        nb = (S // Dl) // 128
                        woff = W.next("wqkv%d" % l, g * 4 + c, 3072)
                        if first_attn:
                            build_mask(g, c)
                        for wh in range(2):
                            for tt in range(4):
                                ps, psn = ps_mm()
                                for kc in range(8):
                                    lo = woff + kc * 384 + wh * 128
                                    xo = kc * S + 512 * tt
                                    P.op("pe", I("matmul", ps[:, :], lhsT=WB[:, lo:lo + 128], rhs=XN[:, xo:xo + 512],
                                                 start=(kc == 0), stop=(kc == 7)),
                                         reads=[("WB", lo, lo + 128), ("XN", xo, xo + 512)], writes=[(psn, 0, 512)],
                                         inc=(kc == 7))
                                nu = 512 // Dl
                                u0 = tt * nu

                                def views(dst_rows, base):
                                    if Dl == 1:
                                        return (BIG[dst_rows, base + 512 * tt:base + 512 * tt + 512], ps[dst_rows, :])
                                    ov = BIG[dst_rows, base:base + S].rearrange("p (r u) -> p u r", r=Dl)[:, u0:u0 + nu, :]
                                    iv = ps[dst_rows, :].rearrange("p (u r) -> p u r", r=Dl)
                                    return ov, iv
                                if wh == 0:
                                    ov, iv = views(slice(0, 128), QO)
                                    P.op("act", I("activation", out=ov, in_=iv, func=AF.Copy),
                                         reads=[(psn, 0, 512)], writes=[("BIG", QO, QO + S)])
                                else:
                                    ov, iv = views(slice(0, 64), KAO)
                                    P.op("act", I("activation", out=ov, in_=iv, func=AF.Copy), reads=[(psn, 0, 512)], writes=[("BIG", KAO, KAO + S)])
                                    ov, iv = views(slice(64, 128), KBO)
                                    P.op("dve", I("tensor_copy", out=ov, in_=iv), reads=[(psn, 0, 512)], writes=[("BIG", KBO, KBO + S)])
                        for b0 in range(0, 16, 4):
                            ps, psn = ps_mm()
                            for bb in range(4):
                                b = b0 + bb
                                r, n = b // nb, b % nb
                                stok = r + Dl * 128 * n
                                for kc in range(8):
                                    a0 = kc * S + stok
                                    a1 = a0 + Dl * 127 + 1
                                    lo = woff + kc * 384 + 256
                                    P.op("pe", I("matmul", ps[:, bb * 128:(bb + 1) * 128], lhsT=XN[:, a0:a1:Dl], rhs=WB[:, lo:lo + 128],
                                                 start=(kc == 0), stop=(kc == 7)),
                                         reads=[("XN", a0, a1), ("WB", lo, lo + 128)], writes=[(psn, bb * 128, (bb + 1) * 128)],
                                         inc=(bb == 3 and kc == 7))
                            psv = ps[:, :].rearrange("p (b c) -> p b c", b=4)
                            vlo = VO + b0 * 256
                            P.op("act", I("activation", out=Vv[:, b0:b0 + 4, 0, 0:64], in_=psv[:, :, 0:64], func=AF.Copy),
                                 reads=[(psn, 0, 512)], writes=[("BIG", vlo, vlo + 1024)])
                            P.op("dve", I("tensor_copy", out=Vv[:, b0:b0 + 4, 1, 64:128], in_=psv[:, :, 64:128]),
                                 reads=[(psn, 0, 512)], writes=[("BIG", vlo, vlo + 1024)])
                        mo = (g * 4 + c) * 512

                        def stage1(b):
                            r, n = b // nb, b % nb
                            with_prev = (n != 0)
                            ncols = 512 if with_prev else 256
                            ps, psn = ps_mm()
                            mms = [(0, KAO + 128 * b), (128, KBO + 128 * b)]
                            if with_prev:
                                mms += [(256, KAO + 128 * (b - 1)), (384, KBO + 128 * (b - 1))]
                            qo = QO + 128 * b
                            for idx, (co, ko) in enumerate(mms):
                                P.op("pe", I("matmul", ps[:, co:co + 128], lhsT=BIG[:, ko:ko + 128], rhs=BIG[:, qo:qo + 128],
                                             start=True, stop=True),
                                     reads=[("BIG", ko, ko + 128), ("BIG", qo, qo + 128)], writes=[(psn, co, co + 128)],
                                     inc=(idx == len(mms) - 1))
                            k2 = ei[0] % 3
                            ei[0] += 1
                            e_ap, e_rg = sb16(k2, ncols)
                            pt_ap, pt_rg = sb16(3 + k2, ncols)
                            P.op("act", I("activation", out=e_ap, in_=ps[:, 0:ncols], func=AF.Exp, scale=0.125),
                                 reads=[(psn, 0, ncols)], writes=[e_rg])
                            P.op("dve", I("tensor_tensor", out=pt_ap, in0=e_ap, in1=MASK[:, mo:mo + ncols], op=ALU.mult),
                                 reads=[e_rg, ("MASK", mo, mo + ncols)], writes=[pt_rg])
                            return (b, r, n, with_prev, (3 + k2) * 512, pt_rg)

                        def stage2(st1):
                            b, r, n, with_prev, ptb, pt_rg = st1
                            puz, puzn = ps_aux()
                            terms = [(b, 0, 0), (b, 1, 128)]
                            if with_prev:
                                terms += [(b - 1, 0, 256), (b - 1, 1, 384)]
                            for k, (vb, s_, pco) in enumerate(terms):
                                vlo = VO + vb * 256 + s_ * 128
                                P.op("pe", I("matmul", puz[:, 0:128], lhsT=Vv[:, vb, s_, :], rhs=SB16[:, ptb + pco:ptb + pco + 128],
                                             start=(k == 0), stop=(k == len(terms) - 1)),
                                     reads=[("BIG", vlo, vlo + 128), pt_rg], writes=[(puzn, 0, 128)], inc=False)
                            for k, (vb, s_, pco) in enumerate(terms):
                                P.op("pe", I("matmul", puz[:, 128:256], lhsT=ONESAB[:, s_ * 128:(s_ + 1) * 128],
                                             rhs=SB16[:, ptb + pco:ptb + pco + 128],
                                             start=(k == 0), stop=(k == len(terms) - 1)),
                                     reads=[("ONESAB", 0, 256), pt_rg], writes=[(puzn, 128, 256)],
                                     inc=(k == len(terms) - 1))
                            stok = r + Dl * 128 * n
                            e1 = stok + Dl * 127 + 1
                            dst = SCR[:, 0:4096].rearrange("p (z t) -> p z t", z=2)[:, :, stok:e1:Dl]
                            src = puz[:, 0:256].rearrange("p (z q) -> p z q", z=2)
                            rgs = [("SCR", stok, e1), ("SCR", 2048 + stok, 2048 + e1)]
                            if g == 0:
                                P.op("act", I("activation", out=dst, in_=src, func=AF.Copy), reads=[(puzn, 0, 256)], writes=rgs)
                            else:
                                P.op("dve", I("tensor_tensor", out=dst, in0=src, in1=dst, op=ALU.add),
                                     reads=[(puzn, 0, 256)] + rgs, writes=rgs)
                        sts = [stage1(0), stage1(1)]
                        for b in range(2, 16):
                            sts.append(stage1(b))
                            stage2(sts.pop(0))
                        stage2(sts.pop(0))
                        stage2(sts.pop(0))
                    mgo = MO + c * S
                    for q4 in range(4):
                        a0, a1 = 512 * q4, 512 * q4 + 512
                        P.op("act", I("activation", out=SCR[:, 2048 + a0:2048 + a1], in_=SCR[:, 2048 + a0:2048 + a1], func=AF.Ln),
                             reads=[("SCR", 2048 + a0, 2048 + a1)], writes=[("SCR", 2048 + a0, 2048 + a1)])
                        P.op("act", I("activation", out=SCR[:, 2048 + a0:2048 + a1], in_=SCR[:, 2048 + a0:2048 + a1], func=AF.Exp, scale=-1.0),
                             reads=[("SCR", 2048 + a0, 2048 + a1)], writes=[("SCR", 2048 + a0, 2048 + a1)])
                        P.op("dve", I("tensor_tensor", out=BIG[:, mgo + a0:mgo + a1], in0=SCR[:, a0:a1],
                                      in1=SCR[:, 2048 + a0:2048 + a1], op=ALU.mult),
                             reads=[("SCR", a0, a1), ("SCR", 2048 + a0, 2048 + a1)], writes=[("BIG", mgo + a0, mgo + a1)])
                proj_postnorm(l, 1, "wo%d" % l, 4, 4, MO, S, 0, 0, 4, bg=lambda: prenorm_units(l, 2, 0, 2, sqt=(2, 3, 4)))

            def convmix(l, pre_tiles=0):
                prenorm(l, 0, 512 * pre_tiles, 4 - pre_tiles)
                for i in range(8):
                    woff = W.next("cwin", i, 3072)
                    P.op("dve", I("memset", SCR[:, 0:2], 0.0), writes=[("SCR", 0, 2)])
                    for tt in range(4):
                        t0 = 512 * tt
                        pss = []
                        for j in range(3):
                            ps, psn = ps_mm()
                            pss.append((ps, psn))
                            for kc in range(8):
                                lo = woff + kc * 384 + j * 128
                                xo = kc * S + t0
                                P.op("pe", I("matmul", ps[:, :], lhsT=WB[:, lo:lo + 128], rhs=XN[:, xo:xo + 512],
                                             start=(kc == 0), stop=(kc == 7)),
                                     reads=[("WB", lo, lo + 128), ("XN", xo, xo + 512)], writes=[(psn, 0, 512)],
                                     inc=(kc == 7))
                        (pB, pBn), (pC, pCn), (pH, pHn) = pss
                        hb, hb_rg = scr(8)
                        bb_, bb_rg = scr(9)
                        P.op("act", I("activation", out=hb, in_=pH[:, :], func=AF.Copy), reads=[(pHn, 0, 512)], writes=[hb_rg])
                        P.op("act", I("activation", out=bb_, in_=pB[:, :], func=AF.Copy), reads=[(pBn, 0, 512)], writes=[bb_rg])
                        zo = (tt % 2) * ST
                        zno = ((tt + 1) % 2) * ST
                        P.op("dve", I("tensor_tensor", out=SCR[:, zo + 2:zo + 514], in0=pC[:, :], in1=hb, op=ALU.mult),
                             reads=[(pCn, 0, 512), hb_rg], writes=[("SCR", zo + 2, zo + 514)])
                        if tt < 3:
                            P.op("dve", I("tensor_copy", out=SCR[:, zno:zno + 2], in_=SCR[:, zo + 512:zo + 514]),
                                 reads=[("SCR", zo + 512, zo + 514)], writes=[("SCR", zno, zno + 2)])
                        to = (2 + tt % 2) * ST
                        P.op("dve", I("tensor_scalar", out=SCR[:, to:to + 512], in0=SCR[:, zo + 2:zo + 514],
                                      scalar1=CDW[:, i * 3 + 2:i * 3 + 3], scalar2=None, op0=ALU.mult),
                             reads=[("SCR", zo + 2, zo + 514), ("CDW", 0, 24)], writes=[("SCR", to, to + 512)])
                        for k in (1, 0):
                            P.op("dve", I("scalar_tensor_tensor", out=SCR[:, to:to + 512], in0=SCR[:, zo + k:zo + k + 512],
                                          scalar=CDW[:, i * 3 + k:i * 3 + k + 1], in1=SCR[:, to:to + 512],
                                          op0=ALU.mult, op1=ALU.add),
                                 reads=[("SCR", zo + k, zo + k + 512), ("SCR", to, to + 512), ("CDW", 0, 24)],
                                 writes=[("SCR", to, to + 512)])
                        yo = i * S + t0
                        P.op("dve", I("tensor_tensor", out=BIG[:, yo:yo + 512], in0=SCR[:, to:to + 512], in1=bb_, op=ALU.mult),
                             reads=[("SCR", to, to + 512), bb_rg], writes=[("BIG", yo, yo + 512)])
                proj_postnorm(l, 1, "cwout", 8, 2, 0, S, 0, 0, 4, bg=lambda: prenorm_units(l, 2, 0, 2, sqt=(2, 3, 4)))

            def poolmix(l, pre_tiles=0):
                prenorm(l, 0, 512 * pre_tiles, 4 - pre_tiles)
                CS0 = 2096
                PO = 16384
                SBf = SB16.bitcast(F32)
                ZT = SBf[:, 0:512]
                ZT_rg = ("SB16", 0, 1024)
                TM = HALO[:, 0:16]
                TM_rg = ("HALO", 0, 16)
                P.op("dve", I("memset", SCR[:, 2080:2096], 0.0), writes=[("SCR", 2080, 2096)])
                P.op("dve", I("memset", ZT, 0.0), writes=[ZT_rg])
                chunk_i = [0]
                for t in range(16):
                    P.op("dve", I("memset", INVT[:, t:t + 1], 1.0 / (t + 1)), writes=[("INVT", t, t + 1)])
                PSETS = [[(BIG, "BIG", 16384), (BIG, "BIG", 18432)], [(BIG, "BIG", 20480), (SB16, "SB16", 1024)]]

                def inproj(gi):
                    w = 2 ** (gi + 1)
                    woff = W.next("pwin", gi, 2048)
                    for cc in range(2):
                        UB0 = 0 if chunk_i[0] % 2 == 0 else 4160
                        chunk_i[0] += 1
                        for tt in range(4):
                            ps, psn = ps_mm()
                            for kc in range(8):
                                lo = woff + kc * 256 + cc * 128
                                xo = kc * S + 512 * tt
                                P.op("pe", I("matmul", ps[:, :], lhsT=WB[:, lo:lo + 128], rhs=XN[:, xo:xo + 512],
                                             start=(kc == 0), stop=(kc == 7)),
                                     reads=[("WB", lo, lo + 128), ("XN", xo, xo + 512)], writes=[(psn, 0, 512)],
                                     inc=(kc == 7))
                            P.op("act", I("activation", out=SCR[:, UB0 + 512 * tt:UB0 + 512 * tt + 512], in_=ps[:, :], func=AF.Copy),
                                 reads=[(psn, 0, 512)], writes=[("SCR", UB0 + 512 * tt, UB0 + 512 * tt + 512)])
                        for q in range(4):
                            co = CS0 + 512 * q
                            init = 0.0 if q == 0 else SCR[:, co - 1:co]
                            P.op("dve", I("tensor_tensor_scan", out=SCR[:, co:co + 512], data0=SCR[:, UB0 + 512 * q:UB0 + 512 * q + 512],
                                          data1=ZT, initial=init, op0=ALU.add, op1=ALU.add),
                                 reads=[("SCR", UB0 + 512 * q, UB0 + 512 * q + 512), ZT_rg, ("SCR", co - 1, co)],
                                 writes=[("SCR", co, co + 512)])
                        P.op("dve", I("tensor_tensor", out=TM, in0=SCR[:, CS0:CS0 + 16], in1=INVT[:, 0:16], op=ALU.mult),
                             reads=[("SCR", CS0, CS0 + 16), ("INVT", 0, 16)], writes=[TM_rg])
                        P.op("dve", I("tensor_tensor", out=TM, in0=TM, in1=SCR[:, UB0:UB0 + 16], op=ALU.subtract),
                             reads=[TM_rg, ("SCR", UB0, UB0 + 16)], writes=[TM_rg])
                        P.op("dve", I("scalar_tensor_tensor", out=SCR[:, UB0:UB0 + S], in0=SCR[:, CS0:CS0 + S], scalar=1.0 / w,
                                      in1=SCR[:, UB0:UB0 + S], op0=ALU.mult, op1=ALU.subtract),
                             reads=[("SCR", CS0, CS0 + S), ("SCR", UB0, UB0 + S)], writes=[("SCR", UB0, UB0 + S)])
                        pt_, pn_, po = PSETS[gi % 2][cc]
                        P.op("dve", I("scalar_tensor_tensor", out=pt_[:, po:po + S], in0=SCR[:, CS0 - w:CS0 - w + S], scalar=-1.0 / w,
                                      in1=SCR[:, UB0:UB0 + S], op0=ALU.mult, op1=ALU.add),
                             reads=[("SCR", CS0 - w, CS0 - w + S), ("SCR", UB0, UB0 + S)], writes=[(pn_, po, po + S)])
                        P.op("dve", I("tensor_copy", out=pt_[:, po:po + w - 1], in_=HALO[:, 0:w - 1]),
                             reads=[TM_rg], writes=[(pn_, po, po + w - 1)])

                def grp(gi):
                    goff = W.next("pwgrp", gi, 512)
                    for mm in range(2):
                        for tt in range(4):
                            ps, psn = ps_mm()
                            for kc in range(2):
                                lo = goff + kc * 256 + mm * 128
                                pt_, pn_, po = PSETS[gi % 2][kc]
                                io = po + 512 * tt
                                P.op("pe", I("matmul", ps[:, :], lhsT=WB[:, lo:lo + 128], rhs=pt_[:, io:io + 512],
                                             start=(kc == 0), stop=(kc == 1)),
                                     reads=[("WB", lo, lo + 128), (pn_, io, io + 512)], writes=[(psn, 0, 512)],
                                     inc=(kc == 1))
                            ch = 2 * gi + mm
                            yo = ch * S + 512 * tt
                            P.op("act", I("activation", out=BIG[:, yo:yo + 512], in_=ps[:, :], func=AF.Copy, scale=PSC[:, ch:ch + 1]),
                                 reads=[(psn, 0, 512), ("PSC", 0, 8)], writes=[("BIG", yo, yo + 512)])
                inproj(0)
                for gi in range(4):
                    if gi + 1 < 4:
                        inproj(gi + 1)
                    grp(gi)
                proj_postnorm(l, 1, "pwout", 8, 2, 0, S, 0, 0, 4, bg=lambda: prenorm_units(l, 2, 0, 2, sqt=(2, 3, 4)))

            def store_out(h0, h1, strm):
                for c in range(8):
                    P.op("sp", I("dma_start", out=yT[c * 128:(c + 1) * 128, h0:h1], in_=X[:, c * S + h0:c * S + h1]),
                         reads=[("X", c * S + h0, c * S + h1)], stream=strm, amt=16)

            run_layers = EXEC_LAYERS if EXEC_LAYERS is not None else layers
            masks_built = [False]
            for li, l in enumerate(run_layers):
                kind = l % 3
                pre_tiles = 2 if li > 0 else 0
                if kind == 0:
                    attention(l, pre_tiles)
                elif kind == 1:
                    convmix(l, pre_tiles)
                else:
                    poolmix(l, pre_tiles)
                ffn(l, pre_done=True, next_l=(run_layers[li + 1] if li + 1 < len(run_layers) else None))
            store_out(1024, 1536, "ost1")
            store_out(1536, 2048, "ost2")
            P.wait_all("sp", ["ost0", "ost1", "ost2"])

        def t_name(t, nm):
            return {"gvec": "GV", "dwv": "DWV", "cdw": "CDW", "psc": "PSC"}[nm]

        Wd = WMgr(DummyProg(), dram, WB, None)
        construct(DummyProg(), Wd)
        P = Prog()
        P.unordered.update(["xld0", "xld1", "xld2", "cst", "ost0", "ost1", "ost2"])
        W = WMgr(P, dram, WB, Wd.rec)
        construct(P, W)
        assert W.used == len(Wd.rec) and W.issued == len(Wd.rec)
        P.emit(nc, st)
    return nc


def chunkmajor(Wm, cols):
    sub = Wm[:, cols]
    nk = Wm.shape[0] // 128
    return np.ascontiguousarray(sub.reshape(nk, 128, -1).transpose(1, 0, 2).reshape(128, -1))


def host_layout(inp):
    f = lambda a: np.asarray(a, dtype=np.float32)
    out = {}
    ng = f(inp["norm_g"])
    out["gvec"] = np.ascontiguousarray(ng.reshape(16, 8, 128).transpose(2, 0, 1).reshape(128, 128))
    dw = f(inp["ffn_w_dw"])
    out["dwv"] = np.ascontiguousarray(dw.reshape(4, 3, 2, 22, 128).transpose(4, 0, 3, 2, 1).reshape(128, 528))
    cdw = f(inp["conv_w_dw"])[0]
    out["cdw"] = np.ascontiguousarray(cdw.reshape(3, 8, 128).transpose(2, 1, 0).reshape(128, 24))
    out["psc"] = np.ascontiguousarray(f(inp["pool_scale"])[0].reshape(8, 128).T)
    ar = np.arange
    for l in range(4):
        wu = f(inp["ffn_w_up"])[l]
        out["wup%d" % l] = np.stack([chunkmajor(wu, np.concatenate([ar(128 * i, 128 * i + 128), ar(2816 + 128 * i, 2816 + 128 * i + 128)]))
                                     for i in range(22)])
        wd = f(inp["ffn_w_down"])[l]
        out["wdn%d" % l] = np.stack([chunkmajor(wd, ar(128 * m, 128 * m + 128)) for m in range(8)])
    for ia, l in enumerate((0, 3)):
        wq = f(inp["attn_w_qkv"])[ia]
        blks = []
        for g in range(3):
            for c in range(4):
                base = g * 1536 + c * 128
                blks.append(chunkmajor(wq, np.concatenate([ar(base, base + 128), ar(base + 512, base + 640), ar(base + 1024, base + 1152)])))
        out["wqkv%d" % l] = np.stack(blks)
        wo = f(inp["attn_w_o"])[ia]
        out["wo%d" % l] = np.stack([chunkmajor(wo, ar(512 * b, 512 * b + 512)) for b in range(2)])
    cw = f(inp["conv_w_in"])[0]
    out["cwin"] = np.stack([chunkmajor(cw, np.concatenate([ar(128 * i, 128 * i + 128), ar(1024 + 128 * i, 1152 + 128 * i), ar(2048 + 128 * i, 2176 + 128 * i)]))
                            for i in range(8)])
    for nm, key in (("cwout", "conv_w_out"), ("pwin", "pool_w_in"), ("pwout", "pool_w_out")):
        wm = f(inp[key])[0]
        out[nm] = np.stack([chunkmajor(wm, ar(256 * b, 256 * b + 256)) for b in range(4)])
    wg = f(inp["pool_w_grp"])[0]
    out["pwgrp"] = np.stack([chunkmajor(wg[g], ar(0, 256)) for g in range(4)])
    return out


_PROG_CACHE = {}


def _run(layers, xT_list, lay):
    key = tuple(layers)
    if key not in _PROG_CACHE:
        _PROG_CACHE[key] = build_program(list(layers))
    nc = _PROG_CACHE[key]
    names = ["gvec", "dwv", "cdw", "psc"]
    for l in layers:
        names += ["wup%d" % l, "wdn%d" % l]
        kind = l % 3
        if kind == 0:
            names += ["wqkv%d" % l, "wo%d" % l]
        elif kind == 1:
            names += ["cwin", "cwout"]
        else:
            names += ["pwin", "pwgrp", "pwout"]
    in_maps = []
    for b in range(8):
        m = {n: lay[n] for n in names}
        m["xT"] = xT_list[b]
        in_maps.append(m)
    res = run_bass_kernel_spmd(nc, in_maps, core_ids=list(range(8)))
    return [np.asarray(r["yT"]) for r in res.results]


def kernel(**inputs):
    lay = host_layout(inputs)
    x = np.asarray(inputs["x"], dtype=np.float32)
    xT = [np.ascontiguousarray(x[b].T) for b in range(8)]
    if FUSED:
        yT = _run((0, 1, 2, 3), xT, lay)
    else:
        yT = xT
        for l in range(4):
            yT = _run((l,), yT, lay)
    return np.ascontiguousarray(np.stack([y.T for y in yT]).astype(np.float32))
```

```python
import numpy as np
import concourse.bass as bass
import concourse.mybir as mybir
from concourse.bass_utils import run_bass_kernel_spmd
from contextlib import ExitStack

F32 = mybir.dt.float32
BF16 = mybir.dt.bfloat16
AF = mybir.ActivationFunctionType
ALU = mybir.AluOpType

S = 2048
EPS = 1e-6
DIL = (1, 4, 16)
NSLOT = 3
SLOT = 3072
ST = 520
NT = 12
FUSED = True
EXEC_LAYERS = None


class Prog:
    ENG = ("pe", "act", "dve", "pool", "sp")

    def __init__(self):
        self.q = {e: [] for e in self.ENG}
        self.cnt = {}
        self.waited = {e: {} for e in self.ENG}
        self.recs = {}
        self.unordered = set()
        self.pe_pending = False
        self.nops = 0

    def _overlaps(self, name, lo, hi):
        return [r for r in self.recs.get(name, ()) if r[0] < hi and lo < r[1]]

    def op(self, eng, fn, reads=(), writes=(), inc=True, stream=None, amt=1):
        own = stream if stream is not None else eng
        if self.pe_pending and eng != "pe":
            raise RuntimeError("non-PE op constructed inside an open PE group")
        deps = {}

        def add(dep):
            if dep is None:
                return
            s, c = dep
            if s in self.unordered:
                c = self.cnt.get(s, 0)
            if eng == "pe" and s == "pe":
                return
            if deps.get(s, 0) < c:
                deps[s] = c
        for (name, lo, hi) in reads:
            for r in self._overlaps(name, lo, hi):
                add(r[2])
        for (name, lo, hi) in writes:
            for r in self._overlaps(name, lo, hi):
                add(r[2])
                for s, c in r[3].items():
                    add((s, c))
        for s, c in deps.items():
            if self.waited[eng].get(s, 0) < c:
                self.waited[eng][s] = c
                self.q[eng].append(("wait", s, c))
        after = self.cnt.get(own, 0) + amt
        if inc:
            self.cnt[own] = after
            if eng == "pe":
                self.pe_pending = False
        else:
            assert eng == "pe"
            self.pe_pending = True
        self.q[eng].append(("op", fn, own if inc else None, amt))
        self.nops += 1
        for (name, lo, hi) in reads:
            lst = self.recs.setdefault(name, [])
            for r in lst:
                if r[0] == lo and r[1] == hi and r[2] is None:
                    r[3][own] = after
                    break
            else:
                lst.append([lo, hi, None, {own: after}])
        for (name, lo, hi) in writes:
            lst = self.recs.setdefault(name, [])
            lst[:] = [r for r in lst if not (lo <= r[0] and r[1] <= hi)]
            lst.append([lo, hi, (own, after), {}])

    def wait_all(self, eng, streams):
        for s in streams:
            c = self.cnt.get(s, 0)
            if c and self.waited[eng].get(s, 0) < c:
                self.waited[eng][s] = c
                self.q[eng].append(("wait", s, c))

    def emit(self, nc, stack):
        assert not self.pe_pending
        sems = {s: stack.enter_context(nc.semaphore("s_" + s)) for s in self.cnt}
        block = stack.enter_context(nc.Block())

        def replay(e, name):
            for it in self.q[name]:
                if it[0] == "wait":
                    e.wait_ge(sems[it[1]], it[2])
                else:
                    ins = it[1](e)
                    if it[2] is not None:
                        ins.then_inc(sems[it[2]], it[3])

        @block.tensor
        def _(e):
            replay(e, "pe")

        @block.scalar
        def _(e):
            replay(e, "act")

        @block.vector
        def _(e):
            replay(e, "dve")

        @block.gpsimd
        def _(e):
            replay(e, "pool")

        @block.sync
        def _(e):
            replay(e, "sp")


class DummyProg:
    def op(self, *a, **k):
        pass

    def wait_all(self, *a, **k):
        pass


def I(fn, *a, **k):
    return lambda e: getattr(e, fn)(*a, **k)


class WMgr:
    def __init__(self, P, dram, WB, sched):
        self.P, self.dram, self.WB, self.sched = P, dram, WB, sched
        self.rec = []
        self.issued = 0
        self.used = 0

    def next(self, name, blk, nelem):
        if self.sched is None:
            self.rec.append((name, blk, nelem))
            return 0
        i = self.used
        assert self.sched[i] == (name, blk, nelem), (self.sched[i], name, blk, nelem)
        while self.issued < min(i + NSLOT, len(self.sched)):
            self._issue(self.issued)
            self.issued += 1
        self.used += 1
        return (i % NSLOT) * SLOT

    def _issue(self, j):
        name, blk, nelem = self.sched[j]
        slot = j % NSLOT
        off = slot * SLOT
        self.P.op("pool", I("dma_start", out=self.WB[:, off:off + nelem], in_=self.dram[name][blk]),
                  writes=[("WB", off, off + nelem)], stream="w%d" % slot, amt=16)


def build_program(layers):
    nc = bass.Bass("TRN2", target_bir_lowering=False)
    dram = {}

    def din(name, shape):
        dram[name] = nc.dram_tensor(name, list(shape), F32, kind="ExternalInput").ap()
    din("xT", (1024, S))
    din("gvec", (128, 128))
    din("dwv", (128, 528))
    din("cdw", (128, 24))
    din("psc", (128, 8))
    for l in layers:
        din("wup%d" % l, (22, 128, 2048))
        din("wdn%d" % l, (8, 128, 2816))
        kind = l % 3
        if kind == 0:
            din("wqkv%d" % l, (12, 128, 3072))
            din("wo%d" % l, (2, 128, 2048))
        elif kind == 1:
            din("cwin", (8, 128, 3072))
            din("cwout", (4, 128, 2048))
        else:
            din("pwin", (4, 128, 2048))
            din("pwgrp", (4, 128, 512))
            din("pwout", (4, 128, 2048))
    yT = nc.dram_tensor("yT", [1024, S], F32, kind="ExternalOutput").ap()

    with ExitStack() as st:
        def sb(name, n, dt):
            return st.enter_context(nc.sbuf_tensor(name, [128, n], dt))
        X = sb("X", 8 * S, F32)
        XN = sb("XN", 8 * S, BF16)
        BIG = sb("BIG", 22528, BF16)
        SCR = sb("SCR", NT * ST, F32)
        SB16 = sb("SB16", 6 * 512, BF16)
        WB = sb("WB", NSLOT * SLOT, BF16)
        MASK = sb("MASK", 12 * 512, BF16)
        GV = sb("GV", 128, F32)
        DWV = sb("DWV", 528, F32)
        CDW = sb("CDW", 24, F32)
        PSC = sb("PSC", 8, F32)
        ONES = sb("ONES", 128, BF16)
        ONESAB = sb("ONESAB", 256, BF16)
        HALO = sb("HALO", 88, F32)
        INVT = sb("INVT", 16, F32)
        PS = [st.enter_context(nc.psum_tensor("PS%d" % b, [128, 512], F32)) for b in range(8)]

        def construct(P, W):
            mmi = [0]
            auxi = [0]
            sqi = [0]
            rsi = [0]
            ei = [0]

            def ps_mm():
                b = mmi[0] % 5
                mmi[0] += 1
                return PS[b], "PS%d" % b

            upi = [0]

            def ps_up():
                b = upi[0] % 8
                upi[0] += 1
                return PS[b], "PS%d" % b

            def ps_aux():
                b = (6, 7, 5)[auxi[0] % 3]
                auxi[0] += 1
                return PS[b], "PS%d" % b
            SS, SSn = PS[5], "PS5"

            def scr(i, lo=0, n=512):
                o = i * ST + lo
                return SCR[:, o:o + n], ("SCR", o, o + n)

            def sb16(i, n=512):
                return SB16[:, i * 512:i * 512 + n], ("SB16", i * 512, i * 512 + n)

            for (h0, h1, strm) in ((0, 512, "xld0"), (512, 1024, "xld1"), (1024, 2048, "xld2")):
                for c in range(8):
                    P.op("sp",
                         I("dma_start", out=X[:, c * S + h0:c * S + h1], in_=dram["xT"][c * 128:(c + 1) * 128, h0:h1]),
                         writes=[("X", c * S + h0, c * S + h1)], stream=strm, amt=16)
            for (t, nm, n) in ((GV, "gvec", 128), (DWV, "dwv", 528), (CDW, "cdw", 24), (PSC, "psc", 8)):
                P.op("sp", I("dma_start", out=t[:, :], in_=dram[nm][:, :]), writes=[(t_name(t, nm), 0, n)], stream="cst", amt=16)
            P.op("dve", I("memset", ONES[:, :], 1.0), writes=[("ONES", 0, 128)])
            P.op("dve", I("memset", ONESAB[:, :], 0.0), writes=[("ONESAB", 0, 256)])
            P.op("dve", I("memset", ONESAB[:, 0:64], 1.0), writes=[("ONESAB", 0, 64)])
            P.op("dve", I("memset", ONESAB[:, 192:256], 1.0), writes=[("ONESAB", 192, 256)])

            XNf = XN.bitcast(F32)
            MASKf = MASK.bitcast(F32)

            def add_sq(src_ap, src_rg, first, last, SS=SS, SSn=SSn, defer=False):
                k = sqi[0] % 2
                sqi[0] += 1
                sq_ap, sq_rg = sb16(k)
                P.op("act", I("activation", out=sq_ap, in_=src_ap, func=AF.Square), reads=[src_rg], writes=[sq_rg])

                def pe_part():
                    P.op("pe", I("matmul", SS[:, :], lhsT=ONES[:, :], rhs=sq_ap, start=first, stop=last),
                         reads=[sq_rg, ("ONES", 0, 128)], writes=[(SSn, 0, 512)])
                if defer:
                    return pe_part
                pe_part()
                return None

            def stats_to_rs(SS=SS, SSn=SSn):
                r = 10 + rsi[0] % 2
                rsi[0] += 1
                rs_ap, rs_rg = scr(r)
                P.op("act", I("activation", out=rs_ap, in_=SS[:, :], func=AF.Ln, scale=1.0 / 1024.0, bias=EPS),
                     reads=[(SSn, 0, 512)], writes=[rs_rg])
                P.op("act", I("activation", out=rs_ap, in_=rs_ap, func=AF.Exp, scale=-0.5), reads=[rs_rg], writes=[rs_rg])
                return rs_ap, rs_rg

            def prenorm_units(l, j, tok0, ntiles, sqt=(0, 1)):
                go = (l * 4 + j) * 8
                units = []
                for tt in range(ntiles):
                    t0 = tok0 + 512 * tt
                    rs_box = []

                    def sq(c, t0=t0):
                        o = c * S + t0
                        sq_ap, sq_rg = sb16(sqt[c % len(sqt)])
                        P.op("act", I("activation", out=sq_ap, in_=X[:, o:o + 512], func=AF.Square),
                             reads=[("X", o, o + 512)], writes=[sq_rg])

                    def stat(c):
                        sq_ap, sq_rg = sb16(sqt[c % len(sqt)])
                        P.op("pe", I("matmul", SS[:, :], lhsT=ONES[:, :], rhs=sq_ap, start=(c == 0), stop=(c == 7)),
                             reads=[sq_rg, ("ONES", 0, 128)], writes=[(SSn, 0, 512)])

                    def rs(rs_box=rs_box):
                        rs_box.append(stats_to_rs())

                    def apply(c0, c1, t0=t0, rs_box=rs_box):
                        rs_ap, rs_rg = rs_box[0]
                        for c in range(c0, c1):
                            o = c * S + t0
                            P.op("dve", I("scalar_tensor_tensor", out=XN[:, o:o + 512], in0=X[:, o:o + 512],
                                          scalar=GV[:, go + c:go + c + 1], in1=rs_ap, op0=ALU.mult, op1=ALU.mult),
                                 reads=[("X", o, o + 512), rs_rg, ("GV", 0, 128)], writes=[("XN", o, o + 512)])
                    nb_ = len(sqt)
                    units.append(lambda sq=sq: sq(0))
                    if nb_ >= 3:
                        units.append(lambda sq=sq: sq(1))
                        for c in range(2, 8):
                            units.append(lambda c=c, sq=sq, stat=stat: (sq(c), stat(c - 2)))
                        units.append(lambda stat=stat, rs=rs: (stat(6), stat(7), rs()))
                    else:
                        for c in range(1, 8):
                            units.append(lambda c=c, sq=sq, stat=stat: (sq(c), stat(c - 1)))
                        units.append(lambda stat=stat, rs=rs: (stat(7), rs()))
                    units.append(lambda apply=apply: apply(0, 4))
                    units.append(lambda apply=apply: apply(4, 8))
                return units

            def prenorm(l, j, tok0, ntiles):
                for u in prenorm_units(l, j, tok0, ntiles):
                    u()

            def proj_postnorm(l, j, wname, nk, mpb, in_off, in_cs, in_t0, tok0, ntiles, bg=None, bg_after=1):
                go = (l * 4 + j) * 8
                ncols = mpb * 128
                woff = 0
                pend = None
                bg_list = []
                SSp = (PS[6], "PS6")

                def hs(tt, m):
                    if tt % 2 == 0:
                        return scr(m)
                    o = m * S + 1024
                    return XNf[:, o // 2:o // 2 + 512], ("XN", o, o + 1024)
                for tt in range(ntiles):
                    t0 = tok0 + 512 * tt
                    it0 = in_t0 + 512 * tt
                    for m in range(8):
                        if m % mpb == 0:
                            woff = W.next(wname, m // mpb, nk * ncols)
                        ps, psn = ps_mm()
                        for kc in range(nk):
                            lo = woff + kc * ncols + (m % mpb) * 128
                            io = in_off + kc * in_cs + it0
                            P.op("pe", I("matmul", ps[:, :], lhsT=WB[:, lo:lo + 128], rhs=BIG[:, io:io + 512],
                                         start=(kc == 0), stop=(kc == nk - 1)),
                                 reads=[("WB", lo, lo + 128), ("BIG", io, io + 512)], writes=[(psn, 0, 512)],
                                 inc=(kc == nk - 1))
                        if pend is not None:
                            pend()
                        hs_ap, hs_rg = hs(tt, m)
                        P.op("act", I("activation", out=hs_ap, in_=ps[:, :], func=AF.Copy),
                             reads=[(psn, 0, 512)], writes=[hs_rg])
                        pend = add_sq(ps[:, :], (psn, 0, 512), m == 0, m == 7, SSp[0], SSp[1], defer=True)
                        if bg_list and tt > bg_after:
                            for _ in range(2):
                                if bg_list:
                                    bg_list.pop(0)()
                    pend()
                    pend = None
                    rs_ap, rs_rg = stats_to_rs(SSp[0], SSp[1])
                    def res_add(m, tt=tt, t0=t0):
                        hs_ap, hs_rg = hs(tt, m)
                        xo = m * S + t0
                        P.op("dve",
                             I("tensor_tensor", out=X[:, xo:xo + 512], in0=X[:, xo:xo + 512], in1=hs_ap, op=ALU.add),
                             reads=[("X", xo, xo + 512), hs_rg], writes=[("X", xo, xo + 512)])
                    for m in range(8):
                        hs_ap, hs_rg = hs(tt, m)
                        P.op("dve", I("scalar_tensor_tensor", out=hs_ap, in0=hs_ap, scalar=GV[:, go + m:go + m + 1],
                                      in1=rs_ap, op0=ALU.mult, op1=ALU.mult),
                             reads=[hs_rg, rs_rg, ("GV", 0, 128)], writes=[hs_rg])
                        if m > 0:
                            res_add(m - 1)
                    res_add(7)
                    if tt == bg_after and bg is not None:
                        bg_list.extend(bg())
                while bg_list:
                    bg_list.pop(0)()

            def ffn(l, pre_done=False, next_l=None):
                for half in range(2):
                    tok0 = 1024 * half
                    if half == 0 and not pre_done:
                        prenorm(l, 2, 0, 2)
                    if half == 0:
                        P.op("dve", I("memset", HALO[:, :], 0.0), writes=[("HALO", 0, 88)])
                    pend = []
                    ptc = [0]
                    for i in range(22):
                        woff = W.next("wup%d" % l, i, 2048)
                        for tt in range(2):
                            t0 = tok0 + 512 * tt
                            pss = []
                            for wh in range(2):
                                ps, psn = ps_up()
                                pss.append((ps, psn))
                                for kc in range(8):
                                    lo = woff + kc * 256 + wh * 128
                                    xo = kc * S + t0
                                    P.op("pe", I("matmul", ps[:, :], lhsT=WB[:, lo:lo + 128], rhs=XN[:, xo:xo + 512],
                                                 start=(kc == 0), stop=(kc == 7)),
                                         reads=[("WB", lo, lo + 128), ("XN", xo, xo + 512)], writes=[(psn, 0, 512)],
                                         inc=(kc == 7))
                            tos = []
                            for wh in range(2):
                                ps, psn = pss[wh]
                                ch = 2 * i + wh
                                dwo = l * 132 + ch * 3
                                zo = (wh * 3 + (ptc[0] % 3)) * ST
                                to = (6 + wh * 3 + (ptc[0] % 3)) * ST
                                tos.append(to)
                                if tt == 0 and wh == 0:
                                    zv = SCR[:, zo:zo + 6 * ST].rearrange("p (w r) -> p w r", w=2)[:, :, 0:2]
                                    hv = HALO[:, ch * 2:ch * 2 + 4].rearrange("p (w r) -> p w r", w=2)
                                    P.op("act", I("activation", out=zv, in_=hv, func=AF.Copy),
                                         reads=[("HALO", ch * 2, ch * 2 + 4)],
                                         writes=[("SCR", zo, zo + 2), ("SCR", zo + 3 * ST, zo + 3 * ST + 2)])
                                P.op("act", I("activation", out=SCR[:, zo + 2:zo + 514], in_=ps[:, :], func=AF.Copy),
                                     reads=[(psn, 0, 512)], writes=[("SCR", zo + 2, zo + 514)])
                                P.op("act", I("activation", out=SCR[:, to:to + 512], in_=ps[:, :], func=AF.Copy,
                                              scale=DWV[:, dwo + 2:dwo + 3]),
                                     reads=[(psn, 0, 512), ("DWV", 0, 528)], writes=[("SCR", to, to + 512)])
                            pend2 = []
                            for f in pend:
                                pend2.append(f())
                            pend = []
                            ch = 2 * i
                            zo = (ptc[0] % 3) * ST
                            zno = ((ptc[0] + 1) % 3) * ST
                            tv = SCR[:, zo:zo + 6 * ST].rearrange("p (w r) -> p w r", w=2)[:, :, 512:514]
                            t_rg = [("SCR", zo + 512, zo + 514), ("SCR", zo + 3 * ST + 512, zo + 3 * ST + 514)]
                            if tt == 0:
                                nv = SCR[:, zno:zno + 6 * ST].rearrange("p (w r) -> p w r", w=2)[:, :, 0:2]
                                P.op("act", I("activation", out=nv, in_=tv, func=AF.Copy), reads=t_rg,
                                     writes=[("SCR", zno, zno + 2), ("SCR", zno + 3 * ST, zno + 3 * ST + 2)])
                            elif half == 0:
                                hv = HALO[:, ch * 2:ch * 2 + 4].rearrange("p (w r) -> p w r", w=2)
                                P.op("act", I("activation", out=hv, in_=tv, func=AF.Copy), reads=t_rg,
                                     writes=[("HALO", ch * 2, ch * 2 + 4)])
                            for k in (1, 0):
                                for wh in range(2):
                                    ch = 2 * i + wh
                                    dwo = l * 132 + ch * 3
                                    zo = (wh * 3 + (ptc[0] % 3)) * ST
                                    to = tos[wh]
                                    P.op("dve", I("scalar_tensor_tensor", out=SCR[:, to:to + 512], in0=SCR[:, zo + k:zo + k + 512],
                                                  scalar=DWV[:, dwo + k:dwo + k + 1], in1=SCR[:, to:to + 512],
                                                  op0=ALU.mult, op1=ALU.add),
                                         reads=[("SCR", zo + k, zo + k + 512), ("SCR", to, to + 512), ("DWV", 0, 528)],
                                         writes=[("SCR", to, to + 512)])

                            for f2 in pend2:
                                f2()

                            def fin(tg=tos[0], tu=tos[1], ho=i * 1024 + 512 * tt):
                                P.op("act", I("activation", out=SCR[:, tg:tg + 512], in_=SCR[:, tg:tg + 512], func=AF.Silu),
                                     reads=[("SCR", tg, tg + 512)], writes=[("SCR", tg, tg + 512)])

                                def gate():
                                    P.op("dve", I("tensor_tensor", out=BIG[:, ho:ho + 512], in0=SCR[:, tg:tg + 512],
                                                  in1=SCR[:, tu:tu + 512], op=ALU.mult),
                                         reads=[("SCR", tg, tg + 512), ("SCR", tu, tu + 512)], writes=[("BIG", ho, ho + 512)])
                                return gate
                            pend.append(fin)
                            ptc[0] += 1
                    for f in pend:
                        f()()
                    if half == 0:
                        ffn_down(l, 0, bg=prenorm_units(l, 2, 1024, 2, sqt=(2, 3, 4)))
                        if next_l is None:
                            store_out(0, 1024, "ost0")
                    elif next_l is not None:
                        ffn_down(l, 1, bg=prenorm_units(next_l, 0, 0, 2, sqt=(2, 3, 4)))
                    else:
                        ffn_down(l, 1)

            def ffn_down(l, half, bg=()):
                tok0 = 1024 * half
                go = (l * 4 + 3) * 8
                SSb = [(PS[6], "PS6"), (PS[7], "PS7")]
                bg = list(bg)

                def hs(tt, m):
                    if tt == 0:
                        return scr(m)
                    o = m * S + 1024 * half
                    return XNf[:, o // 2:o // 2 + 512], ("XN", o, o + 1024)
                pend = None
                for m in range(8):
                    woff = W.next("wdn%d" % l, m, 2816)
                    for tt in range(2):
                        ps, psn = ps_mm()
                        for kc in range(22):
                            lo = woff + kc * 128
                            io = kc * 1024 + 512 * tt
                            P.op("pe", I("matmul", ps[:, :], lhsT=WB[:, lo:lo + 128], rhs=BIG[:, io:io + 512],
                                         start=(kc == 0), stop=(kc == 21)),
                                 reads=[("WB", lo, lo + 128), ("BIG", io, io + 512)], writes=[(psn, 0, 512)],
                                 inc=(kc == 21))
                        if pend is not None:
                            pend()
                        hs_ap, hs_rg = hs(tt, m)
                        P.op("act", I("activation", out=hs_ap, in_=ps[:, :], func=AF.Copy),
                             reads=[(psn, 0, 512)], writes=[hs_rg])
                        pend = add_sq(ps[:, :], (psn, 0, 512), m == 0, m == 7, SSb[tt][0], SSb[tt][1], defer=True)
                        for _ in range(2):
                            if bg:
                                bg.pop(0)()
                pend()
                while bg:
                    bg.pop(0)()
                for tt in (0, 1):
                    t0 = tok0 + 512 * tt
                    rs_ap, rs_rg = stats_to_rs(SSb[tt][0], SSb[tt][1])
                    def res_add(m, tt=tt, t0=t0):
                        hs_ap, hs_rg = hs(tt, m)
                        xo = m * S + t0
                        P.op("dve", I("tensor_tensor", out=X[:, xo:xo + 512], in0=X[:, xo:xo + 512], in1=hs_ap, op=ALU.add),
                             reads=[("X", xo, xo + 512), hs_rg], writes=[("X", xo, xo + 512)])
                    for m in range(8):
                        hs_ap, hs_rg = hs(tt, m)
                        P.op("dve", I("scalar_tensor_tensor", out=hs_ap, in0=hs_ap, scalar=GV[:, go + m:go + m + 1],
                                      in1=rs_ap, op0=ALU.mult, op1=ALU.mult),
                             reads=[hs_rg, rs_rg, ("GV", 0, 128)], writes=[hs_rg])
                        if m > 0:
                            res_add(m - 1)
                    res_add(7)

            def mask_setup():
                io_ap, io_rg = scr(9, 0, 128)
                P.op("pool", I("iota", io_ap, [[1, 128]], base=0, channel_multiplier=-1,
                               allow_small_or_imprecise_dtypes=True), writes=[io_rg])
                vc, vc_rg = scr(8, 0, 128)
                vp, vp_rg = scr(8, 128, 128)
                dc, dc_rg = scr(8, 256, 128)
                dp, dp_rg = scr(8, 384, 128)
                P.op("dve", I("tensor_single_scalar", out=vc, in_=io_ap, scalar=0.0, op=ALU.is_ge), reads=[io_rg], writes=[vc_rg])
                P.op("dve", I("tensor_single_scalar", out=vp, in_=io_ap, scalar=0.0, op=ALU.is_le), reads=[io_rg], writes=[vp_rg])
                P.op("dve", I("tensor_single_scalar", out=dc, in_=io_ap, scalar=0.0, op=ALU.max), reads=[io_rg], writes=[dc_rg])
                P.op("dve", I("tensor_scalar", out=dp, in0=io_ap, scalar1=0.0, scalar2=128.0, op0=ALU.min, op1=ALU.add),
                     reads=[io_rg], writes=[dp_rg])

            mk = [0]

            def build_mask(g, c):
                vc, vc_rg = scr(8, 0, 128)
                vp, vp_rg = scr(8, 128, 128)
                dc, dc_rg = scr(8, 256, 128)
                dp, dp_rg = scr(8, 384, 128)
                for s_ in range(2):
                    h = 2 * c + s_
                    sd = float(2.0 ** (-8.0 * (g * 8 + h + 1) / 24.0)) * DIL[g]
                    mo = (g * 4 + c) * 512
                    for (d_ap, d_rg, v_ap, v_rg, co) in ((dc, dc_rg, vc, vc_rg, s_ * 128), (dp, dp_rg, vp, vp_rg, 256 + s_ * 128)):
                        t_ap, t_rg = scr(9, 128 + 128 * (mk[0] % 2), 128)
                        mk[0] += 1
                        P.op("act", I("activation", out=t_ap, in_=d_ap, func=AF.Exp, scale=-sd), reads=[d_rg], writes=[t_rg])
                        P.op("dve", I("tensor_tensor", out=MASK[:, mo + co:mo + co + 128], in0=t_ap, in1=v_ap, op=ALU.mult),
                             reads=[t_rg, v_rg], writes=[("MASK", mo + co, mo + co + 128)])

            QO, KAO, KBO, VO, MO = 0, 2048, 4096, 6144, 10240

            def attention(l, pre_tiles=0):
                prenorm(l, 0, 512 * pre_tiles, 4 - pre_tiles)
                first_attn = not masks_built[0]
                if first_attn:
                    mask_setup()
                    masks_built[0] = True
                P.op("dve", I("memset", BIG[:, KAO:MO], 0.0), writes=[("BIG", KAO, MO)])
                Vv = BIG[:, VO:MO].rearrange("p (b s c) -> p b s c", b=16, s=2)
                for c in range(4):
                    for g in range(3):
                        Dl = DIL[g]
                        nb = (S // Dl) // 128
                        woff = W.next("wqkv%d" % l, g * 4 + c, 3072)
                        if first_attn:
                            build_mask(g, c)
                        for wh in range(2):
                            for tt in range(4):
                                ps, psn = ps_mm()
                                for kc in range(8):
                                    lo = woff + kc * 384 + wh * 128
                                    xo = kc * S + 512 * tt
                                    P.op("pe", I("matmul", ps[:, :], lhsT=WB[:, lo:lo + 128], rhs=XN[:, xo:xo + 512],
                                                 start=(kc == 0), stop=(kc == 7)),
                                         reads=[("WB", lo, lo + 128), ("XN", xo, xo + 512)], writes=[(psn, 0, 512)],
                                         inc=(kc == 7))
                                nu = 512 // Dl
                                u0 = tt * nu

                                def views(dst_rows, base):
                                    if Dl == 1:
                                        return (BIG[dst_rows, base + 512 * tt:base + 512 * tt + 512], ps[dst_rows, :])
                                    ov = BIG[dst_rows, base:base + S].rearrange("p (r u) -> p u r", r=Dl)[:, u0:u0 + nu, :]
                                    iv = ps[dst_rows, :].rearrange("p (u r) -> p u r", r=Dl)
                                    return ov, iv
                                if wh == 0:
                                    ov, iv = views(slice(0, 128), QO)
                                    P.op("act", I("activation", out=ov, in_=iv, func=AF.Copy),
                                         reads=[(psn, 0, 512)], writes=[("BIG", QO, QO + S)])
                                else:
                                    ov, iv = views(slice(0, 64), KAO)
                                    P.op("act", I("activation", out=ov, in_=iv, func=AF.Copy), reads=[(psn, 0, 512)], writes=[("BIG", KAO, KAO + S)])
                                    ov, iv = views(slice(64, 128), KBO)
                                    P.op("dve", I("tensor_copy", out=ov, in_=iv), reads=[(psn, 0, 512)], writes=[("BIG", KBO, KBO + S)])
                        for b0 in range(0, 16, 4):
                            ps, psn = ps_mm()
                            for bb in range(4):
                                b = b0 + bb
                                r, n = b // nb, b % nb
                                stok = r + Dl * 128 * n
                                for kc in range(8):
                                    a0 = kc * S + stok
                                    a1 = a0 + Dl * 127 + 1
                                    lo = woff + kc * 384 + 256
                                    P.op("pe", I("matmul", ps[:, bb * 128:(bb + 1) * 128], lhsT=XN[:, a0:a1:Dl], rhs=WB[:, lo:lo + 128],
                                                 start=(kc == 0), stop=(kc == 7)),
                                         reads=[("XN", a0, a1), ("WB", lo, lo + 128)], writes=[(psn, bb * 128, (bb + 1) * 128)],
                                         inc=(bb == 3 and kc == 7))
                            psv = ps[:, :].rearrange("p (b c) -> p b c", b=4)
                            vlo = VO + b0 * 256
                            P.op("act", I("activation", out=Vv[:, b0:b0 + 4, 0, 0:64], in_=psv[:, :, 0:64], func=AF.Copy),
                                 reads=[(psn, 0, 512)], writes=[("BIG", vlo, vlo + 1024)])
                            P.op("dve", I("tensor_copy", out=Vv[:, b0:b0 + 4, 1, 64:128], in_=psv[:, :, 64:128]),
                                 reads=[(psn, 0, 512)], writes=[("BIG", vlo, vlo + 1024)])
                        mo = (g * 4 + c) * 512

                        def stage1(b):
                            r, n = b // nb, b % nb
                            with_prev = (n != 0)
                            ncols = 512 if with_prev else 256
                            ps, psn = ps_mm()
                            mms = [(0, KAO + 128 * b), (128, KBO + 128 * b)]
                            if with_prev:
                                mms += [(256, KAO + 128 * (b - 1)), (384, KBO + 128 * (b - 1))]
                            qo = QO + 128 * b
                            for idx, (co, ko) in enumerate(mms):
                                P.op("pe", I("matmul", ps[:, co:co + 128], lhsT=BIG[:, ko:ko + 128], rhs=BIG[:, qo:qo + 128],
                                             start=True, stop=True),
                                     reads=[("BIG", ko, ko + 128), ("BIG", qo, qo + 128)], writes=[(psn, co, co + 128)],
                                     inc=(idx == len(mms) - 1))
                            k2 = ei[0] % 3
                            ei[0] += 1
                            e_ap, e_rg = sb16(k2, ncols)
                            pt_ap, pt_rg = sb16(3 + k2, ncols)
                            P.op("act", I("activation", out=e_ap, in_=ps[:, 0:ncols], func=AF.Exp, scale=0.125),
                                 reads=[(psn, 0, ncols)], writes=[e_rg])
                            P.op("dve", I("tensor_tensor", out=pt_ap, in0=e_ap, in1=MASK[:, mo:mo + ncols], op=ALU.mult),
                                 reads=[e_rg, ("MASK", mo, mo + ncols)], writes=[pt_rg])
                            return (b, r, n, with_prev, (3 + k2) * 512, pt_rg)

                        def stage2(st1):
                            b, r, n, with_prev, ptb, pt_rg = st1
                            puz, puzn = ps_aux()
                            terms = [(b, 0, 0), (b, 1, 128)]
                            if with_prev:
                                terms += [(b - 1, 0, 256), (b - 1, 1, 384)]
                            for k, (vb, s_, pco) in enumerate(terms):
                                vlo = VO + vb * 256 + s_ * 128
                                P.op("pe", I("matmul", puz[:, 0:128], lhsT=Vv[:, vb, s_, :], rhs=SB16[:, ptb + pco:ptb + pco + 128],
                                             start=(k == 0), stop=(k == len(terms) - 1)),
                                     reads=[("BIG", vlo, vlo + 128), pt_rg], writes=[(puzn, 0, 128)], inc=False)
                            for k, (vb, s_, pco) in enumerate(terms):
                                P.op("pe", I("matmul", puz[:, 128:256], lhsT=ONESAB[:, s_ * 128:(s_ + 1) * 128],
                                             rhs=SB16[:, ptb + pco:ptb + pco + 128],
                                             start=(k == 0), stop=(k == len(terms) - 1)),
                                     reads=[("ONESAB", 0, 256), pt_rg], writes=[(puzn, 128, 256)],
                                     inc=(k == len(terms) - 1))
                            stok = r + Dl * 128 * n
                            e1 = stok + Dl * 127 + 1
                            dst = SCR[:, 0:4096].rearrange("p (z t) -> p z t", z=2)[:, :, stok:e1:Dl]
                            src = puz[:, 0:256].rearrange("p (z q) -> p z q", z=2)
                            rgs = [("SCR", stok, e1), ("SCR", 2048 + stok, 2048 + e1)]
                            if g == 0:
                                P.op("act", I("activation", out=dst, in_=src, func=AF.Copy), reads=[(puzn, 0, 256)], writes=rgs)
                            else:
                                P.op("dve", I("tensor_tensor", out=dst, in0=src, in1=dst, op=ALU.add),
                                     reads=[(puzn, 0, 256)] + rgs, writes=rgs)
                        sts = [stage1(0), stage1(1)]
                        for b in range(2, 16):
                            sts.append(stage1(b))
                            stage2(sts.pop(0))
                        stage2(sts.pop(0))
                        stage2(sts.pop(0))
                    mgo = MO + c * S
                    for q4 in range(4):
                        a0, a1 = 512 * q4, 512 * q4 + 512
                        P.op("act", I("activation", out=SCR[:, 2048 + a0:2048 + a1], in_=SCR[:, 2048 + a0:2048 + a1], func=AF.Ln),
                             reads=[("SCR", 2048 + a0, 2048 + a1)], writes=[("SCR", 2048 + a0, 2048 + a1)])
                        P.op("act", I("activation", out=SCR[:, 2048 + a0:2048 + a1], in_=SCR[:, 2048 + a0:2048 + a1], func=AF.Exp, scale=-1.0),
                             reads=[("SCR", 2048 + a0, 2048 + a1)], writes=[("SCR", 2048 + a0, 2048 + a1)])
                        P.op("dve", I("tensor_tensor", out=BIG[:, mgo + a0:mgo + a1], in0=SCR[:, a0:a1],
                                      in1=SCR[:, 2048 + a0:2048 + a1], op=ALU.mult),
                             reads=[("SCR", a0, a1), ("SCR", 2048 + a0, 2048 + a1)], writes=[("BIG", mgo + a0, mgo + a1)])
                proj_postnorm(l, 1, "wo%d" % l, 4, 4, MO, S, 0, 0, 4, bg=lambda: prenorm_units(l, 2, 0, 2, sqt=(2, 3, 4)))

            def convmix(l, pre_tiles=0):
                prenorm(l, 0, 512 * pre_tiles, 4 - pre_tiles)
                for i in range(8):
                    woff = W.next("cwin", i, 3072)
                    P.op("dve", I("memset", SCR[:, 0:2], 0.0), writes=[("SCR", 0, 2)])
                    for tt in range(4):
                        t0 = 512 * tt
                        pss = []
                        for j in range(3):
                            ps, psn = ps_mm()
                            pss.append((ps, psn))
                            for kc in range(8):
                                lo = woff + kc * 384 + j * 128
                                xo = kc * S + t0
                                P.op("pe", I("matmul", ps[:, :], lhsT=WB[:, lo:lo + 128], rhs=XN[:, xo:xo + 512],
                                             start=(kc == 0), stop=(kc == 7)),
                                     reads=[("WB", lo, lo + 128), ("XN", xo, xo + 512)], writes=[(psn, 0, 512)],
                                     inc=(kc == 7))
                        (pB, pBn), (pC, pCn), (pH, pHn) = pss
                        hb, hb_rg = scr(8)
                        bb_, bb_rg = scr(9)
                        P.op("act", I("activation", out=hb, in_=pH[:, :], func=AF.Copy), reads=[(pHn, 0, 512)], writes=[hb_rg])
                        P.op("act", I("activation", out=bb_, in_=pB[:, :], func=AF.Copy), reads=[(pBn, 0, 512)], writes=[bb_rg])
                        zo = (tt % 2) * ST
                        zno = ((tt + 1) % 2) * ST
                        P.op("dve", I("tensor_tensor", out=SCR[:, zo + 2:zo + 514], in0=pC[:, :], in1=hb, op=ALU.mult),
                             reads=[(pCn, 0, 512), hb_rg], writes=[("SCR", zo + 2, zo + 514)])
                        if tt < 3:
                            P.op("dve", I("tensor_copy", out=SCR[:, zno:zno + 2], in_=SCR[:, zo + 512:zo + 514]),
                                 reads=[("SCR", zo + 512, zo + 514)], writes=[("SCR", zno, zno + 2)])
                        to = (2 + tt % 2) * ST
                        P.op("dve", I("tensor_scalar", out=SCR[:, to:to + 512], in0=SCR[:, zo + 2:zo + 514],
                                      scalar1=CDW[:, i * 3 + 2:i * 3 + 3], scalar2=None, op0=ALU.mult),
                             reads=[("SCR", zo + 2, zo + 514), ("CDW", 0, 24)], writes=[("SCR", to, to + 512)])
                        for k in (1, 0):
                            P.op("dve", I("scalar_tensor_tensor", out=SCR[:, to:to + 512], in0=SCR[:, zo + k:zo + k + 512],
                                          scalar=CDW[:, i * 3 + k:i * 3 + k + 1], in1=SCR[:, to:to + 512],
                                          op0=ALU.mult, op1=ALU.add),
                                 reads=[("SCR", zo + k, zo + k + 512), ("SCR", to, to + 512), ("CDW", 0, 24)],
                                 writes=[("SCR", to, to + 512)])
                        yo = i * S + t0
                        P.op("dve", I("tensor_tensor", out=BIG[:, yo:yo + 512], in0=SCR[:, to:to + 512], in1=bb_, op=ALU.mult),
                             reads=[("SCR", to, to + 512), bb_rg], writes=[("BIG", yo, yo + 512)])
                proj_postnorm(l, 1, "cwout", 8, 2, 0, S, 0, 0, 4, bg=lambda: prenorm_units(l, 2, 0, 2, sqt=(2, 3, 4)))

            def poolmix(l, pre_tiles=0):
                prenorm(l, 0, 512 * pre_tiles, 4 - pre_tiles)
                CS0 = 2096
                PO = 16384
                SBf = SB16.bitcast(F32)
                ZT = SBf[:, 0:512]
                ZT_rg = ("SB16", 0, 1024)
                TM = HALO[:, 0:16]
                TM_rg = ("HALO", 0, 16)
                P.op("dve", I("memset", SCR[:, 2080:2096], 0.0), writes=[("SCR", 2080, 2096)])
                P.op("dve", I("memset", ZT, 0.0), writes=[ZT_rg])
                chunk_i = [0]
                for t in range(16):
                    P.op("dve", I("memset", INVT[:, t:t + 1], 1.0 / (t + 1)), writes=[("INVT", t, t + 1)])
                PSETS = [[(BIG, "BIG", 16384), (BIG, "BIG", 18432)], [(BIG, "BIG", 20480), (SB16, "SB16", 1024)]]

                def inproj(gi):
                    w = 2 ** (gi + 1)
                    woff = W.next("pwin", gi, 2048)
                    for cc in range(2):
                        UB0 = 0 if chunk_i[0] % 2 == 0 else 4160
                        chunk_i[0] += 1
                        for tt in range(4):
                            ps, psn = ps_mm()
                            for kc in range(8):
                                lo = woff + kc * 256 + cc * 128
                                xo = kc * S + 512 * tt
                                P.op("pe", I("matmul", ps[:, :], lhsT=WB[:, lo:lo + 128], rhs=XN[:, xo:xo + 512],
                                             start=(kc == 0), stop=(kc == 7)),
                                     reads=[("WB", lo, lo + 128), ("XN", xo, xo + 512)], writes=[(psn, 0, 512)],
                                     inc=(kc == 7))
                            P.op("act", I("activation", out=SCR[:, UB0 + 512 * tt:UB0 + 512 * tt + 512], in_=ps[:, :], func=AF.Copy),
                                 reads=[(psn, 0, 512)], writes=[("SCR", UB0 + 512 * tt, UB0 + 512 * tt + 512)])
                        for q in range(4):
                            co = CS0 + 512 * q
                            init = 0.0 if q == 0 else SCR[:, co - 1:co]
                            P.op("dve", I("tensor_tensor_scan", out=SCR[:, co:co + 512], data0=SCR[:, UB0 + 512 * q:UB0 + 512 * q + 512],
                                          data1=ZT, initial=init, op0=ALU.add, op1=ALU.add),
                                 reads=[("SCR", UB0 + 512 * q, UB0 + 512 * q + 512), ZT_rg, ("SCR", co - 1, co)],
                                 writes=[("SCR", co, co + 512)])
                        P.op("dve", I("tensor_tensor", out=TM, in0=SCR[:, CS0:CS0 + 16], in1=INVT[:, 0:16], op=ALU.mult),
                             reads=[("SCR", CS0, CS0 + 16), ("INVT", 0, 16)], writes=[TM_rg])
                        P.op("dve", I("tensor_tensor", out=TM, in0=TM, in1=SCR[:, UB0:UB0 + 16], op=ALU.subtract),
                             reads=[TM_rg, ("SCR", UB0, UB0 + 16)], writes=[TM_rg])
                        P.op("dve", I("scalar_tensor_tensor", out=SCR[:, UB0:UB0 + S], in0=SCR[:, CS0:CS0 + S], scalar=1.0 / w,
                                      in1=SCR[:, UB0:UB0 + S], op0=ALU.mult, op1=ALU.subtract),
                             reads=[("SCR", CS0, CS0 + S), ("SCR", UB0, UB0 + S)], writes=[("SCR", UB0, UB0 + S)])
                        pt_, pn_, po = PSETS[gi % 2][cc]
                        P.op("dve", I("scalar_tensor_tensor", out=pt_[:, po:po + S], in0=SCR[:, CS0 - w:CS0 - w + S], scalar=-1.0 / w,
                                      in1=SCR[:, UB0:UB0 + S], op0=ALU.mult, op1=ALU.add),
                             reads=[("SCR", CS0 - w, CS0 - w + S), ("SCR", UB0, UB0 + S)], writes=[(pn_, po, po + S)])
                        P.op("dve", I("tensor_copy", out=pt_[:, po:po + w - 1], in_=HALO[:, 0:w - 1]),
                             reads=[TM_rg], writes=[(pn_, po, po + w - 1)])

                def grp(gi):
                    goff = W.next("pwgrp", gi, 512)
                    for mm in range(2):
                        for tt in range(4):
                            ps, psn = ps_mm()
                            for kc in range(2):
                                lo = goff + kc * 256 + mm * 128
                                pt_, pn_, po = PSETS[gi % 2][kc]
                                io = po + 512 * tt
                                P.op("pe", I("matmul", ps[:, :], lhsT=WB[:, lo:lo + 128], rhs=pt_[:, io:io + 512],
                                             start=(kc == 0), stop=(kc == 1)),
                                     reads=[("WB", lo, lo + 128), (pn_, io, io + 512)], writes=[(psn, 0, 512)],
                                     inc=(kc == 1))
                            ch = 2 * gi + mm
                            yo = ch * S + 512 * tt
                            P.op("act", I("activation", out=BIG[:, yo:yo + 512], in_=ps[:, :], func=AF.Copy, scale=PSC[:, ch:ch + 1]),
                                 reads=[(psn, 0, 512), ("PSC", 0, 8)], writes=[("BIG", yo, yo + 512)])
                inproj(0)
                for gi in range(4):
                    if gi + 1 < 4:
                        inproj(gi + 1)
                    grp(gi)
                proj_postnorm(l, 1, "pwout", 8, 2, 0, S, 0, 0, 4, bg=lambda: prenorm_units(l, 2, 0, 2, sqt=(2, 3, 4)))

            def store_out(h0, h1, strm):
                for c in range(8):
                    P.op("sp", I("dma_start", out=yT[c * 128:(c + 1) * 128, h0:h1], in_=X[:, c * S + h0:c * S + h1]),
                         reads=[("X", c * S + h0, c * S + h1)], stream=strm, amt=16)

            run_layers = EXEC_LAYERS if EXEC_LAYERS is not None else layers
            masks_built = [False]
            for li, l in enumerate(run_layers):
                kind = l % 3
                pre_tiles = 2 if li > 0 else 0
                if kind == 0:
                    attention(l, pre_tiles)
                elif kind == 1:
                    convmix(l, pre_tiles)
                else:
                    poolmix(l, pre_tiles)
                ffn(l, pre_done=True, next_l=(run_layers[li + 1] if li + 1 < len(run_layers) else None))
            store_out(1024, 1536, "ost1")
            store_out(1536, 2048, "ost2")
            P.wait_all("sp", ["ost0", "ost1", "ost2"])

        def t_name(t, nm):
            return {"gvec": "GV", "dwv": "DWV", "cdw": "CDW", "psc": "PSC"}[nm]

        Wd = WMgr(DummyProg(), dram, WB, None)
        construct(DummyProg(), Wd)
        P = Prog()
        P.unordered.update(["xld0", "xld1", "xld2", "cst", "ost0", "ost1", "ost2"])
        W = WMgr(P, dram, WB, Wd.rec)
        construct(P, W)
        assert W.used == len(Wd.rec) and W.issued == len(Wd.rec)
        P.emit(nc, st)
    return nc


def chunkmajor(Wm, cols):
    sub = Wm[:, cols]
    nk = Wm.shape[0] // 128
    return np.ascontiguousarray(sub.reshape(nk, 128, -1).transpose(1, 0, 2).reshape(128, -1))


def host_layout(inp):
    f = lambda a: np.asarray(a, dtype=np.float32)
    out = {}
    ng = f(inp["norm_g"])
    out["gvec"] = np.ascontiguousarray(ng.reshape(16, 8, 128).transpose(2, 0, 1).reshape(128, 128))
    dw = f(inp["ffn_w_dw"])
    out["dwv"] = np.ascontiguousarray(dw.reshape(4, 3, 2, 22, 128).transpose(4, 0, 3, 2, 1).reshape(128, 528))
    cdw = f(inp["conv_w_dw"])[0]
    out["cdw"] = np.ascontiguousarray(cdw.reshape(3, 8, 128).transpose(2, 1, 0).reshape(128, 24))
    out["psc"] = np.ascontiguousarray(f(inp["pool_scale"])[0].reshape(8, 128).T)
    ar = np.arange
    for l in range(4):
        wu = f(inp["ffn_w_up"])[l]
        out["wup%d" % l] = np.stack([chunkmajor(wu, np.concatenate([ar(128 * i, 128 * i + 128), ar(2816 + 128 * i, 2816 + 128 * i + 128)]))
                                     for i in range(22)])
        wd = f(inp["ffn_w_down"])[l]
        out["wdn%d" % l] = np.stack([chunkmajor(wd, ar(128 * m, 128 * m + 128)) for m in range(8)])
    for ia, l in enumerate((0, 3)):
        wq = f(inp["attn_w_qkv"])[ia]
        blks = []
        for g in range(3):
            for c in range(4):
                base = g * 1536 + c * 128
                blks.append(chunkmajor(wq, np.concatenate([ar(base, base + 128), ar(base + 512, base + 640), ar(base + 1024, base + 1152)])))
        out["wqkv%d" % l] = np.stack(blks)
        wo = f(inp["attn_w_o"])[ia]
        out["wo%d" % l] = np.stack([chunkmajor(wo, ar(512 * b, 512 * b + 512)) for b in range(2)])
    cw = f(inp["conv_w_in"])[0]
    out["cwin"] = np.stack([chunkmajor(cw, np.concatenate([ar(128 * i, 128 * i + 128), ar(1024 + 128 * i, 1152 + 128 * i), ar(2048 + 128 * i, 2176 + 128 * i)]))
                            for i in range(8)])
    for nm, key in (("cwout", "conv_w_out"), ("pwin", "pool_w_in"), ("pwout", "pool_w_out")):
        wm = f(inp[key])[0]
        out[nm] = np.stack([chunkmajor(wm, ar(256 * b, 256 * b + 256)) for b in range(4)])
    wg = f(inp["pool_w_grp"])[0]
    out["pwgrp"] = np.stack([chunkmajor(wg[g], ar(0, 256)) for g in range(4)])
    return out


_PROG_CACHE = {}


def _run(layers, xT_list, lay):
    key = tuple(layers)
    if key not in _PROG_CACHE:
        _PROG_CACHE[key] = build_program(list(layers))
    nc = _PROG_CACHE[key]
    names = ["gvec", "dwv", "cdw", "psc"]
    for l in layers:
        names += ["wup%d" % l, "wdn%d" % l]
        kind = l % 3
        if kind == 0:
            names += ["wqkv%d" % l, "wo%d" % l]
        elif kind == 1:
            names += ["cwin", "cwout"]
        else:
            names += ["pwin", "pwgrp", "pwout"]
    in_maps = []
    for b in range(8):
        m = {n: lay[n] for n in names}
        m["xT"] = xT_list[b]
        in_maps.append(m)
    res = run_bass_kernel_spmd(nc, in_maps, core_ids=list(range(8)))
    return [np.asarray(r["yT"]) for r in res.results]


def kernel(**inputs):
    lay = host_layout(inputs)
    x = np.asarray(inputs["x"], dtype=np.float32)
    xT = [np.ascontiguousarray(x[b].T) for b in range(8)]
    if FUSED:
        yT = _run((0, 1, 2, 3), xT, lay)
    else:
        yT = xT
        for l in range(4):
            yT = _run((l,), yT, lay)
    return np.ascontiguousarray(np.stack([y.T for y in yT]).astype(np.float32))
```

```python
import numpy as np
import concourse.bass as bass
import concourse.mybir as mybir
from concourse.bass_utils import run_bass_kernel_spmd
from contextlib import ExitStack

F32 = mybir.dt.float32
BF16 = mybir.dt.bfloat16
AF = mybir.ActivationFunctionType
ALU = mybir.AluOpType

S = 2048
EPS = 1e-6
DIL = (1, 4, 16)
NSLOT = 3
SLOT = 3072
ST = 520
NT = 12
FUSED = True
EXEC_LAYERS = None


class Prog:
    ENG = ("pe", "act", "dve", "pool", "sp")

    def __init__(self):
        self.q = {e: [] for e in self.ENG}
        self.cnt = {}
        self.waited = {e: {} for e in self.ENG}
        self.recs = {}
        self.unordered = set()
        self.pe_pending = False
        self.nops = 0

    def _overlaps(self, name, lo, hi):
        return [r for r in self.recs.get(name, ()) if r[0] < hi and lo < r[1]]

    def op(self, eng, fn, reads=(), writes=(), inc=True, stream=None, amt=1):
        own = stream if stream is not None else eng
        if self.pe_pending and eng != "pe":
            raise RuntimeError("non-PE op constructed inside an open PE group")
        deps = {}

        def add(dep):
            if dep is None:
                return
            s, c = dep
            if s in self.unordered:
                c = self.cnt.get(s, 0)
            if eng == "pe" and s == "pe":
                return
            if deps.get(s, 0) < c:
                deps[s] = c
        for (name, lo, hi) in reads:
            for r in self._overlaps(name, lo, hi):
                add(r[2])
        for (name, lo, hi) in writes:
            for r in self._overlaps(name, lo, hi):
                add(r[2])
                for s, c in r[3].items():
                    add((s, c))
        for s, c in deps.items():
            if self.waited[eng].get(s, 0) < c:
                self.waited[eng][s] = c
                self.q[eng].append(("wait", s, c))
        after = self.cnt.get(own, 0) + amt
        if inc:
            self.cnt[own] = after
            if eng == "pe":
                self.pe_pending = False
        else:
            assert eng == "pe"
            self.pe_pending = True
        self.q[eng].append(("op", fn, own if inc else None, amt))
        self.nops += 1
        for (name, lo, hi) in reads:
            lst = self.recs.setdefault(name, [])
            for r in lst:
                if r[0] == lo and r[1] == hi and r[2] is None:
                    r[3][own] = after
                    break
            else:
                lst.append([lo, hi, None, {own: after}])
        for (name, lo, hi) in writes:
            lst = self.recs.setdefault(name, [])
            lst[:] = [r for r in lst if not (lo <= r[0] and r[1] <= hi)]
            lst.append([lo, hi, (own, after), {}])

    def wait_all(self, eng, streams):
        for s in streams:
            c = self.cnt.get(s, 0)
            if c and self.waited[eng].get(s, 0) < c:
                self.waited[eng][s] = c
                self.q[eng].append(("wait", s, c))

    def emit(self, nc, stack):
        assert not self.pe_pending
        sems = {s: stack.enter_context(nc.semaphore("s_" + s)) for s in self.cnt}
        block = stack.enter_context(nc.Block())

        def replay(e, name):
            for it in self.q[name]:
                if it[0] == "wait":
                    e.wait_ge(sems[it[1]], it[2])
                else:
                    ins = it[1](e)
                    if it[2] is not None:
                        ins.then_inc(sems[it[2]], it[3])

        @block.tensor
        def _(e):
            replay(e, "pe")

        @block.scalar
        def _(e):
            replay(e, "act")

        @block.vector
        def _(e):
            replay(e, "dve")

        @block.gpsimd
        def _(e):
            replay(e, "pool")

        @block.sync
        def _(e):
            replay(e, "sp")


class DummyProg:
    def op(self, *a, **k):
        pass

    def wait_all(self, *a, **k):
        pass


def I(fn, *a, **k):
    return lambda e: getattr(e, fn)(*a, **k)


class WMgr:
    def __init__(self, P, dram, WB, sched):
        self.P, self.dram, self.WB, self.sched = P, dram, WB, sched
        self.rec = []
        self.issued = 0
        self.used = 0

    def next(self, name, blk, nelem):
        if self.sched is None:
            self.rec.append((name, blk, nelem))
            return 0
        i = self.used
        assert self.sched[i] == (name, blk, nelem), (self.sched[i], name, blk, nelem)
        while self.issued < min(i + NSLOT, len(self.sched)):
            self._issue(self.issued)
            self.issued += 1
        self.used += 1
        return (i % NSLOT) * SLOT

    def _issue(self, j):
        name, blk, nelem = self.sched[j]
        slot = j % NSLOT
        off = slot * SLOT
        self.P.op("pool", I("dma_start", out=self.WB[:, off:off + nelem], in_=self.dram[name][blk]),
                  writes=[("WB", off, off + nelem)], stream="w%d" % slot, amt=16)


def build_program(layers):
    nc = bass.Bass("TRN2", target_bir_lowering=False)
    dram = {}

    def din(name, shape):
        dram[name] = nc.dram_tensor(name, list(shape), F32, kind="ExternalInput").ap()
    din("xT", (1024, S))
    din("gvec", (128, 128))
    din("dwv", (128, 528))
    din("cdw", (128, 24))
    din("psc", (128, 8))
    for l in layers:
        din("wup%d" % l, (22, 128, 2048))
        din("wdn%d" % l, (8, 128, 2816))
        kind = l % 3
        if kind == 0:
            din("wqkv%d" % l, (12, 128, 3072))
            din("wo%d" % l, (2, 128, 2048))
        elif kind == 1:
            din("cwin", (8, 128, 3072))
            din("cwout", (4, 128, 2048))
        else:
            din("pwin", (4, 128, 2048))
            din("pwgrp", (4, 128, 512))
            din("pwout", (4, 128, 2048))
    yT = nc.dram_tensor("yT", [1024, S], F32, kind="ExternalOutput").ap()

    with ExitStack() as st:
        def sb(name, n, dt):
            return st.enter_context(nc.sbuf_tensor(name, [128, n], dt))
        X = sb("X", 8 * S, F32)
        XN = sb("XN", 8 * S, BF16)
        BIG = sb("BIG", 22528, BF16)
        SCR = sb("SCR", NT * ST, F32)
        SB16 = sb("SB16", 6 * 512, BF16)
        WB = sb("WB", NSLOT * SLOT, BF16)
        MASK = sb("MASK", 12 * 512, BF16)
        GV = sb("GV", 128, F32)
        DWV = sb("DWV", 528, F32)
        CDW = sb("CDW", 24, F32)
        PSC = sb("PSC", 8, F32)
        ONES = sb("ONES", 128, BF16)
        ONESAB = sb("ONESAB", 256, BF16)
        HALO = sb("HALO", 88, F32)
        INVT = sb("INVT", 16, F32)
        PS = [st.enter_context(nc.psum_tensor("PS%d" % b, [128, 512], F32)) for b in range(8)]

        def construct(P, W):
            mmi = [0]
            auxi = [0]
            sqi = [0]
            rsi = [0]
            ei = [0]

            def ps_mm():
                b = mmi[0] % 5
                mmi[0] += 1
                return PS[b], "PS%d" % b

            upi = [0]

            def ps_up():
                b = upi[0] % 8
                upi[0] += 1
                return PS[b], "PS%d" % b

            def ps_aux():
                b = (6, 7, 5)[auxi[0] % 3]
                auxi[0] += 1
                return PS[b], "PS%d" % b
            SS, SSn = PS[5], "PS5"

            def scr(i, lo=0, n=512):
                o = i * ST + lo
                return SCR[:, o:o + n], ("SCR", o, o + n)

            def sb16(i, n=512):
                return SB16[:, i * 512:i * 512 + n], ("SB16", i * 512, i * 512 + n)

            for (h0, h1, strm) in ((0, 512, "xld0"), (512, 1024, "xld1"), (1024, 2048, "xld2")):
                for c in range(8):
                    P.op("sp",
                         I("dma_start", out=X[:, c * S + h0:c * S + h1], in_=dram["xT"][c * 128:(c + 1) * 128, h0:h1]),
                         writes=[("X", c * S + h0, c * S + h1)], stream=strm, amt=16)
            for (t, nm, n) in ((GV, "gvec", 128), (DWV, "dwv", 528), (CDW, "cdw", 24), (PSC, "psc", 8)):
                P.op("sp", I("dma_start", out=t[:, :], in_=dram[nm][:, :]), writes=[(t_name(t, nm), 0, n)], stream="cst", amt=16)
            P.op("dve", I("memset", ONES[:, :], 1.0), writes=[("ONES", 0, 128)])
            P.op("dve", I("memset", ONESAB[:, :], 0.0), writes=[("ONESAB", 0, 256)])
            P.op("dve", I("memset", ONESAB[:, 0:64], 1.0), writes=[("ONESAB", 0, 64)])
            P.op("dve", I("memset", ONESAB[:, 192:256], 1.0), writes=[("ONESAB", 192, 256)])

            XNf = XN.bitcast(F32)
            MASKf = MASK.bitcast(F32)

            def add_sq(src_ap, src_rg, first, last, SS=SS, SSn=SSn, defer=False):
                k = sqi[0] % 2
                sqi[0] += 1
                sq_ap, sq_rg = sb16(k)
                P.op("act", I("activation", out=sq_ap, in_=src_ap, func=AF.Square), reads=[src_rg], writes=[sq_rg])

                def pe_part():
                    P.op("pe", I("matmul", SS[:, :], lhsT=ONES[:, :], rhs=sq_ap, start=first, stop=last),
                         reads=[sq_rg, ("ONES", 0, 128)], writes=[(SSn, 0, 512)])
                if defer:
                    return pe_part
                pe_part()
                return None

            def stats_to_rs(SS=SS, SSn=SSn):
                r = 10 + rsi[0] % 2
                rsi[0] += 1
                rs_ap, rs_rg = scr(r)
                P.op("act", I("activation", out=rs_ap, in_=SS[:, :], func=AF.Ln, scale=1.0 / 1024.0, bias=EPS),
                     reads=[(SSn, 0, 512)], writes=[rs_rg])
                P.op("act", I("activation", out=rs_ap, in_=rs_ap, func=AF.Exp, scale=-0.5), reads=[rs_rg], writes=[rs_rg])
                return rs_ap, rs_rg

            def prenorm_units(l, j, tok0, ntiles, sqt=(0, 1)):
                go = (l * 4 + j) * 8
                units = []
                for tt in range(ntiles):
                    t0 = tok0 + 512 * tt
                    rs_box = []

                    def sq(c, t0=t0):
                        o = c * S + t0
                        sq_ap, sq_rg = sb16(sqt[c % len(sqt)])
                        P.op("act", I("activation", out=sq_ap, in_=X[:, o:o + 512], func=AF.Square),
                             reads=[("X", o, o + 512)], writes=[sq_rg])

                    def stat(c):
                        sq_ap, sq_rg = sb16(sqt[c % len(sqt)])
                        P.op("pe", I("matmul", SS[:, :], lhsT=ONES[:, :], rhs=sq_ap, start=(c == 0), stop=(c == 7)),
                             reads=[sq_rg, ("ONES", 0, 128)], writes=[(SSn, 0, 512)])

                    def rs(rs_box=rs_box):
                        rs_box.append(stats_to_rs())

                    def apply(c0, c1, t0=t0, rs_box=rs_box):
                        rs_ap, rs_rg = rs_box[0]
                        for c in range(c0, c1):
                            o = c * S + t0
                            P.op("dve", I("scalar_tensor_tensor", out=XN[:, o:o + 512], in0=X[:, o:o + 512],
                                          scalar=GV[:, go + c:go + c + 1], in1=rs_ap, op0=ALU.mult, op1=ALU.mult),
                                 reads=[("X", o, o + 512), rs_rg, ("GV", 0, 128)], writes=[("XN", o, o + 512)])
                    nb_ = len(sqt)
                    units.append(lambda sq=sq: sq(0))
                    if nb_ >= 3:
                        units.append(lambda sq=sq: sq(1))
                        for c in range(2, 8):
                            units.append(lambda c=c, sq=sq, stat=stat: (sq(c), stat(c - 2)))
                        units.append(lambda stat=stat, rs=rs: (stat(6), stat(7), rs()))
                    else:
                        for c in range(1, 8):
                            units.append(lambda c=c, sq=sq, stat=stat: (sq(c), stat(c - 1)))
                        units.append(lambda stat=stat, rs=rs: (stat(7), rs()))
                    units.append(lambda apply=apply: apply(0, 4))
                    units.append(lambda apply=apply: apply(4, 8))
                return units

            def prenorm(l, j, tok0, ntiles):
                for u in prenorm_units(l, j, tok0, ntiles):
                    u()

            def proj_postnorm(l, j, wname, nk, mpb, in_off, in_cs, in_t0, tok0, ntiles, bg=None, bg_after=1):
                go = (l * 4 + j) * 8
                ncols = mpb * 128
                woff = 0
                pend = None
                bg_list = []
                SSp = (PS[6], "PS6")

                def hs(tt, m):
                    if tt % 2 == 0:
                        return scr(m)
                    o = m * S + 1024
                    return XNf[:, o // 2:o // 2 + 512], ("XN", o, o + 1024)
                for tt in range(ntiles):
                    t0 = tok0 + 512 * tt
                    it0 = in_t0 + 512 * tt
                    for m in range(8):
                        if m % mpb == 0:
                            woff = W.next(wname, m // mpb, nk * ncols)
                        ps, psn = ps_mm()
                        for kc in range(nk):
                            lo = woff + kc * ncols + (m % mpb) * 128
                            io = in_off + kc * in_cs + it0
                            P.op("pe", I("matmul", ps[:, :], lhsT=WB[:, lo:lo + 128], rhs=BIG[:, io:io + 512],
                                         start=(kc == 0), stop=(kc == nk - 1)),
                                 reads=[("WB", lo, lo + 128), ("BIG", io, io + 512)], writes=[(psn, 0, 512)],
                                 inc=(kc == nk - 1))
                        if pend is not None:
                            pend()
                        hs_ap, hs_rg = hs(tt, m)
                        P.op("act", I("activation", out=hs_ap, in_=ps[:, :], func=AF.Copy),
                             reads=[(psn, 0, 512)], writes=[hs_rg])
                        pend = add_sq(ps[:, :], (psn, 0, 512), m == 0, m == 7, SSp[0], SSp[1], defer=True)
                        if bg_list and tt > bg_after:
                            for _ in range(2):
                                if bg_list:
                                    bg_list.pop(0)()
                    pend()
                    pend = None
                    rs_ap, rs_rg = stats_to_rs(SSp[0], SSp[1])
                    def res_add(m, tt=tt, t0=t0):
                        hs_ap, hs_rg = hs(tt, m)
                        xo = m * S + t0
                        P.op("dve",
                             I("tensor_tensor", out=X[:, xo:xo + 512], in0=X[:, xo:xo + 512], in1=hs_ap, op=ALU.add),
                             reads=[("X", xo, xo + 512), hs_rg], writes=[("X", xo, xo + 512)])
                    for m in range(8):
                        hs_ap, hs_rg = hs(tt, m)
                        P.op("dve", I("scalar_tensor_tensor", out=hs_ap, in0=hs_ap, scalar=GV[:, go + m:go + m + 1],
                                      in1=rs_ap, op0=ALU.mult, op1=ALU.mult),
                             reads=[hs_rg, rs_rg, ("GV", 0, 128)], writes=[hs_rg])
                        if m > 0:
                            res_add(m - 1)
                    res_add(7)
                    if tt == bg_after and bg is not None:
                        bg_list.extend(bg())
                while bg_list:
                    bg_list.pop(0)()

            def ffn(l, pre_done=False, next_l=None):
                for half in range(2):
                    tok0 = 1024 * half
                    if half == 0 and not pre_done:
                        prenorm(l, 2, 0, 2)
                    if half == 0:
                        P.op("dve", I("memset", HALO[:, :], 0.0), writes=[("HALO", 0, 88)])
                    pend = []
                    ptc = [0]
                    for i in range(22):
                        woff = W.next("wup%d" % l, i, 2048)
                        for tt in range(2):
                            t0 = tok0 + 512 * tt
                            pss = []
                            for wh in range(2):
                                ps, psn = ps_up()
                                pss.append((ps, psn))
                                for kc in range(8):
                                    lo = woff + kc * 256 + wh * 128
                                    xo = kc * S + t0
                                    P.op("pe", I("matmul", ps[:, :], lhsT=WB[:, lo:lo + 128], rhs=XN[:, xo:xo + 512],
                                                 start=(kc == 0), stop=(kc == 7)),
                                         reads=[("WB", lo, lo + 128), ("XN", xo, xo + 512)], writes=[(psn, 0, 512)],
                                         inc=(kc == 7))
                            tos = []
                            for wh in range(2):
                                ps, psn = pss[wh]
                                ch = 2 * i + wh
                                dwo = l * 132 + ch * 3
                                zo = (wh * 3 + (ptc[0] % 3)) * ST
                                to = (6 + wh * 3 + (ptc[0] % 3)) * ST
                                tos.append(to)
                                if tt == 0 and wh == 0:
                                    zv = SCR[:, zo:zo + 6 * ST].rearrange("p (w r) -> p w r", w=2)[:, :, 0:2]
                                    hv = HALO[:, ch * 2:ch * 2 + 4].rearrange("p (w r) -> p w r", w=2)
                                    P.op("act", I("activation", out=zv, in_=hv, func=AF.Copy),
                                         reads=[("HALO", ch * 2, ch * 2 + 4)],
                                         writes=[("SCR", zo, zo + 2), ("SCR", zo + 3 * ST, zo + 3 * ST + 2)])
                                P.op("act", I("activation", out=SCR[:, zo + 2:zo + 514], in_=ps[:, :], func=AF.Copy),
                                     reads=[(psn, 0, 512)], writes=[("SCR", zo + 2, zo + 514)])
                                P.op("act", I("activation", out=SCR[:, to:to + 512], in_=ps[:, :], func=AF.Copy,
                                              scale=DWV[:, dwo + 2:dwo + 3]),
                                     reads=[(psn, 0, 512), ("DWV", 0, 528)], writes=[("SCR", to, to + 512)])
                            pend2 = []
                            for f in pend:
                                pend2.append(f())
                            pend = []
                            ch = 2 * i
                            zo = (ptc[0] % 3) * ST
                            zno = ((ptc[0] + 1) % 3) * ST
                            tv = SCR[:, zo:zo + 6 * ST].rearrange("p (w r) -> p w r", w=2)[:, :, 512:514]
                            t_rg = [("SCR", zo + 512, zo + 514), ("SCR", zo + 3 * ST + 512, zo + 3 * ST + 514)]
                            if tt == 0:
                                nv = SCR[:, zno:zno + 6 * ST].rearrange("p (w r) -> p w r", w=2)[:, :, 0:2]
                                P.op("act", I("activation", out=nv, in_=tv, func=AF.Copy), reads=t_rg,
                                     writes=[("SCR", zno, zno + 2), ("SCR", zno + 3 * ST, zno + 3 * ST + 2)])
                            elif half == 0:
                                hv = HALO[:, ch * 2:ch * 2 + 4].rearrange("p (w r) -> p w r", w=2)
                                P.op("act", I("activation", out=hv, in_=tv, func=AF.Copy), reads=t_rg,
                                     writes=[("HALO", ch * 2, ch * 2 + 4)])
                            for k in (1, 0):
                                for wh in range(2):
                                    ch = 2 * i + wh
                                    dwo = l * 132 + ch * 3
                                    zo = (wh * 3 + (ptc[0] % 3)) * ST
                                    to = tos[wh]
                                    P.op("dve", I("scalar_tensor_tensor", out=SCR[:, to:to + 512], in0=SCR[:, zo + k:zo + k + 512],
                                                  scalar=DWV[:, dwo + k:dwo + k + 1], in1=SCR[:, to:to + 512],
                                                  op0=ALU.mult, op1=ALU.add),
                                         reads=[("SCR", zo + k, zo + k + 512), ("SCR", to, to + 512), ("DWV", 0, 528)],
                                         writes=[("SCR", to, to + 512)])

                            for f2 in pend2:
                                f2()

                            def fin(tg=tos[0], tu=tos[1], ho=i * 1024 + 512 * tt):
                                P.op("act", I("activation", out=SCR[:, tg:tg + 512], in_=SCR[:, tg:tg + 512], func=AF.Silu),
                                     reads=[("SCR", tg, tg + 512)], writes=[("SCR", tg, tg + 512)])

                                def gate():
                                    P.op("dve", I("tensor_tensor", out=BIG[:, ho:ho + 512], in0=SCR[:, tg:tg + 512],
                                                  in1=SCR[:, tu:tu + 512], op=ALU.mult),
                                         reads=[("SCR", tg, tg + 512), ("SCR", tu, tu + 512)], writes=[("BIG", ho, ho + 512)])
                                return gate
                            pend.append(fin)
                            ptc[0] += 1
                    for f in pend:
                        f()()
                    if half == 0:
                        ffn_down(l, 0, bg=prenorm_units(l, 2, 1024, 2, sqt=(2, 3, 4)))
                        if next_l is None:
                            store_out(0, 1024, "ost0")
                    elif next_l is not None:
                        ffn_down(l, 1, bg=prenorm_units(next_l, 0, 0, 2, sqt=(2, 3, 4)))
                    else:
                        ffn_down(l, 1)

            def ffn_down(l, half, bg=()):
                tok0 = 1024 * half
                go = (l * 4 + 3) * 8
                SSb = [(PS[6], "PS6"), (PS[7], "PS7")]
                bg = list(bg)

                def hs(tt, m):
                    if tt == 0:
                        return scr(m)
                    o = m * S + 1024 * half
                    return XNf[:, o // 2:o // 2 + 512], ("XN", o, o + 1024)
                pend = None
                for m in range(8):
                    woff = W.next("wdn%d" % l, m, 2816)
                    for tt in range(2):
                        ps, psn = ps_mm()
                        for kc in range(22):
                            lo = woff + kc * 128
                            io = kc * 1024 + 512 * tt
                            P.op("pe", I("matmul", ps[:, :], lhsT=WB[:, lo:lo + 128], rhs=BIG[:, io:io + 512],
                                         start=(kc == 0), stop=(kc == 21)),
                                 reads=[("WB", lo, lo + 128), ("BIG", io, io + 512)], writes=[(psn, 0, 512)],
                                 inc=(kc == 21))
                        if pend is not None:
                            pend()
                        hs_ap, hs_rg = hs(tt, m)
                        P.op("act", I("activation", out=hs_ap, in_=ps[:, :], func=AF.Copy),
                             reads=[(psn, 0, 512)], writes=[hs_rg])
                        pend = add_sq(ps[:, :], (psn, 0, 512), m == 0, m == 7, SSb[tt][0], SSb[tt][1], defer=True)
                        for _ in range(2):
                            if bg:
                                bg.pop(0)()
                pend()
                while bg:
                    bg.pop(0)()
                for tt in (0, 1):
                    t0 = tok0 + 512 * tt
                    rs_ap, rs_rg = stats_to_rs(SSb[tt][0], SSb[tt][1])
                    def res_add(m, tt=tt, t0=t0):
                        hs_ap, hs_rg = hs(tt, m)
                        xo = m * S + t0
                        P.op("dve", I("tensor_tensor", out=X[:, xo:xo + 512], in0=X[:, xo:xo + 512], in1=hs_ap, op=ALU.add),
                             reads=[("X", xo, xo + 512), hs_rg], writes=[("X", xo, xo + 512)])
                    for m in range(8):
                        hs_ap, hs_rg = hs(tt, m)
                        P.op("dve", I("scalar_tensor_tensor", out=hs_ap, in0=hs_ap, scalar=GV[:, go + m:go + m + 1],
                                      in1=rs_ap, op0=ALU.mult, op1=ALU.mult),
                             reads=[hs_rg, rs_rg, ("GV", 0, 128)], writes=[hs_rg])
                        if m > 0:
                            res_add(m - 1)
                    res_add(7)

            def mask_setup():
                io_ap, io_rg = scr(9, 0, 128)
                P.op("pool", I("iota", io_ap, [[1, 128]], base=0, channel_multiplier=-1,
                               allow_small_or_imprecise_dtypes=True), writes=[io_rg])
                vc, vc_rg = scr(8, 0, 128)
                vp, vp_rg = scr(8, 128, 128)
                dc, dc_rg = scr(8, 256, 128)
                dp, dp_rg = scr(8, 384, 128)
                P.op("dve", I("tensor_single_scalar", out=vc, in_=io_ap, scalar=0.0, op=ALU.is_ge), reads=[io_rg], writes=[vc_rg])
                P.op("dve", I("tensor_single_scalar", out=vp, in_=io_ap, scalar=0.0, op=ALU.is_le), reads=[io_rg], writes=[vp_rg])
                P.op("dve", I("tensor_single_scalar", out=dc, in_=io_ap, scalar=0.0, op=ALU.max), reads=[io_rg], writes=[dc_rg])
                P.op("dve", I("tensor_scalar", out=dp, in0=io_ap, scalar1=0.0, scalar2=128.0, op0=ALU.min, op1=ALU.add),
                     reads=[io_rg], writes=[dp_rg])

            mk = [0]

            def build_mask(g, c):
                vc, vc_rg = scr(8, 0, 128)
                vp, vp_rg = scr(8, 128, 128)
                dc, dc_rg = scr(8, 256, 128)
                dp, dp_rg = scr(8, 384, 128)
                for s_ in range(2):
                    h = 2 * c + s_
                    sd = float(2.0 ** (-8.0 * (g * 8 + h + 1) / 24.0)) * DIL[g]
                    mo = (g * 4 + c) * 512
                    for (d_ap, d_rg, v_ap, v_rg, co) in ((dc, dc_rg, vc, vc_rg, s_ * 128), (dp, dp_rg, vp, vp_rg, 256 + s_ * 128)):
                        t_ap, t_rg = scr(9, 128 + 128 * (mk[0] % 2), 128)
                        mk[0] += 1
                        P.op("act", I("activation", out=t_ap, in_=d_ap, func=AF.Exp, scale=-sd), reads=[d_rg], writes=[t_rg])
                        P.op("dve", I("tensor_tensor", out=MASK[:, mo + co:mo + co + 128], in0=t_ap, in1=v_ap, op=ALU.mult),
                             reads=[t_rg, v_rg], writes=[("MASK", mo + co, mo + co + 128)])

            QO, KAO, KBO, VO, MO = 0, 2048, 4096, 6144, 10240

            def attention(l, pre_tiles=0):
                prenorm(l, 0, 512 * pre_tiles, 4 - pre_tiles)
                first_attn = not masks_built[0]
                if first_attn:
                    mask_setup()
                    masks_built[0] = True
                P.op("dve", I("memset", BIG[:, KAO:MO], 0.0), writes=[("BIG", KAO, MO)])
                Vv = BIG[:, VO:MO].rearrange("p (b s c) -> p b s c", b=16, s=2)
                for c in range(4):
                    for g in range(3):
                        Dl = DIL[g]
                        nb = (S // Dl) // 128
                        woff = W.next("wqkv%d" % l, g * 4 + c, 3072)
                        if first_attn:
                            build_mask(g, c)
                        for wh in range(2):
                            for tt in range(4):
                                ps, psn = ps_mm()
                                for kc in range(8):
                                    lo = woff + kc * 384 + wh * 128
                                    xo = kc * S + 512 * tt
                                    P.op("pe", I("matmul", ps[:, :], lhsT=WB[:, lo:lo + 128], rhs=XN[:, xo:xo + 512],
                                                 start=(kc == 0), stop=(kc == 7)),
                                         reads=[("WB", lo, lo + 128), ("XN", xo, xo + 512)], writes=[(psn, 0, 512)],
                                         inc=(kc == 7))
                                nu = 512 // Dl
                                u0 = tt * nu

                                def views(dst_rows, base):
                                    if Dl == 1:
                                        return (BIG[dst_rows, base + 512 * tt:base + 512 * tt + 512], ps[dst_rows, :])
                                    ov = BIG[dst_rows, base:base + S].rearrange("p (r u) -> p u r", r=Dl)[:, u0:u0 + nu, :]
                                    iv = ps[dst_rows, :].rearrange("p (u r) -> p u r", r=Dl)
                                    return ov, iv
                                if wh == 0:
                                    ov, iv = views(slice(0, 128), QO)
                                    P.op("act", I("activation", out=ov, in_=iv, func=AF.Copy),
                                         reads=[(psn, 0, 512)], writes=[("BIG", QO, QO + S)])
                                else:
                                    ov, iv = views(slice(0, 64), KAO)
                                    P.op("act", I("activation", out=ov, in_=iv, func=AF.Copy), reads=[(psn, 0, 512)], writes=[("BIG", KAO, KAO + S)])
                                    ov, iv = views(slice(64, 128), KBO)
                                    P.op("dve", I("tensor_copy", out=ov, in_=iv), reads=[(psn, 0, 512)], writes=[("BIG", KBO, KBO + S)])
                        for b0 in range(0, 16, 4):
                            ps, psn = ps_mm()
                            for bb in range(4):
                                b = b0 + bb
                                r, n = b // nb, b % nb
                                stok = r + Dl * 128 * n
                                for kc in range(8):
                                    a0 = kc * S + stok
                                    a1 = a0 + Dl * 127 + 1
                                    lo = woff + kc * 384 + 256
                                    P.op("pe", I("matmul", ps[:, bb * 128:(bb + 1) * 128], lhsT=XN[:, a0:a1:Dl], rhs=WB[:, lo:lo + 128],
                                                 start=(kc == 0), stop=(kc == 7)),
                                         reads=[("XN", a0, a1), ("WB", lo, lo + 128)], writes=[(psn, bb * 128, (bb + 1) * 128)],
                                         inc=(bb == 3 and kc == 7))
                            psv = ps[:, :].rearrange("p (b c) -> p b c", b=4)
                            vlo = VO + b0 * 256
                            P.op("act", I("activation", out=Vv[:, b0:b0 + 4, 0, 0:64], in_=psv[:, :, 0:64], func=AF.Copy),
                                 reads=[(psn, 0, 512)], writes=[("BIG", vlo, vlo + 1024)])
                            P.op("dve", I("tensor_copy", out=Vv[:, b0:b0 + 4, 1, 64:128], in_=psv[:, :, 64:128]),
                                 reads=[(psn, 0, 512)], writes=[("BIG", vlo, vlo + 1024)])
                        mo = (g * 4 + c) * 512

                        def stage1(b):
                            r, n = b // nb, b % nb
                            with_prev = (n != 0)
                            ncols = 512 if with_prev else 256
                            ps, psn = ps_mm()
                            mms = [(0, KAO + 128 * b), (128, KBO + 128 * b)]
                            if with_prev:
                                mms += [(256, KAO + 128 * (b - 1)), (384, KBO + 128 * (b - 1))]
                            qo = QO + 128 * b
                            for idx, (co, ko) in enumerate(mms):
                                P.op("pe", I("matmul", ps[:, co:co + 128], lhsT=BIG[:, ko:ko + 128], rhs=BIG[:, qo:qo + 128],
                                             start=True, stop=True),
                                     reads=[("BIG", ko, ko + 128), ("BIG", qo, qo + 128)], writes=[(psn, co, co + 128)],
                                     inc=(idx == len(mms) - 1))
                            k2 = ei[0] % 3
                            ei[0] += 1
                            e_ap, e_rg = sb16(k2, ncols)
                            pt_ap, pt_rg = sb16(3 + k2, ncols)
                            P.op("act", I("activation", out=e_ap, in_=ps[:, 0:ncols], func=AF.Exp, scale=0.125),
                                 reads=[(psn, 0, ncols)], writes=[e_rg])
                            P.op("dve", I("tensor_tensor", out=pt_ap, in0=e_ap, in1=MASK[:, mo:mo + ncols], op=ALU.mult),
                                 reads=[e_rg, ("MASK", mo, mo + ncols)], writes=[pt_rg])
                            return (b, r, n, with_prev, (3 + k2) * 512, pt_rg)

                        def stage2(st1):
                            b, r, n, with_prev, ptb, pt_rg = st1
                            puz, puzn = ps_aux()
                            terms = [(b, 0, 0), (b, 1, 128)]
                            if with_prev:
                                terms += [(b - 1, 0, 256), (b - 1, 1, 384)]
                            for k, (vb, s_, pco) in enumerate(terms):
                                vlo = VO + vb * 256 + s_ * 128
                                P.op("pe", I("matmul", puz[:, 0:128], lhsT=Vv[:, vb, s_, :], rhs=SB16[:, ptb + pco:ptb + pco + 128],
                                             start=(k == 0), stop=(k == len(terms) - 1)),
                                     reads=[("BIG", vlo, vlo + 128), pt_rg], writes=[(puzn, 0, 128)], inc=False)
                            for k, (vb, s_, pco) in enumerate(terms):
                                P.op("pe", I("matmul", puz[:, 128:256], lhsT=ONESAB[:, s_ * 128:(s_ + 1) * 128],
                                             rhs=SB16[:, ptb + pco:ptb + pco + 128],
                                             start=(k == 0), stop=(k == len(terms) - 1)),
                                     reads=[("ONESAB", 0, 256), pt_rg], writes=[(puzn, 128, 256)],
                                     inc=(k == len(terms) - 1))
                            stok = r + Dl * 128 * n
                            e1 = stok + Dl * 127 + 1
                            dst = SCR[:, 0:4096].rearrange("p (z t) -> p z t", z=2)[:, :, stok:e1:Dl]
                            src = puz[:, 0:256].rearrange("p (z q) -> p z q", z=2)
                            rgs = [("SCR", stok, e1), ("SCR", 2048 + stok, 2048 + e1)]
                            if g == 0:
                                P.op("act", I("activation", out=dst, in_=src, func=AF.Copy), reads=[(puzn, 0, 256)], writes=rgs)
                            else:
                                P.op("dve", I("tensor_tensor", out=dst, in0=src, in1=dst, op=ALU.add),
                                     reads=[(puzn, 0, 256)] + rgs, writes=rgs)
                        sts = [stage1(0), stage1(1)]
                        for b in range(2, 16):
                            sts.append(stage1(b))
                            stage2(sts.pop(0))
                        stage2(sts.pop(0))
                        stage2(sts.pop(0))
                    mgo = MO + c * S
                    for q4 in range(4):
                        a0, a1 = 512 * q4, 512 * q4 + 512
                        P.op("act", I("activation", out=SCR[:, 2048 + a0:2048 + a1], in_=SCR[:, 2048 + a0:2048 + a1], func=AF.Ln),
                             reads=[("SCR", 2048 + a0, 2048 + a1)], writes=[("SCR", 2048 + a0, 2048 + a1)])
                    for q4 in range(4):
                        a0, a1 = 512 * q4, 512 * q4 + 512
                        P.op("act", I("activation", out=SCR[:, 2048 + a0:2048 + a1], in_=SCR[:, 2048 + a0:2048 + a1], func=AF.Exp, scale=-1.0),
                             reads=[("SCR", 2048 + a0, 2048 + a1)], writes=[("SCR", 2048 + a0, 2048 + a1)])
                    for q4 in range(4):
                        a0, a1 = 512 * q4, 512 * q4 + 512
                        P.op("dve", I("tensor_tensor", out=BIG[:, mgo + a0:mgo + a1], in0=SCR[:, a0:a1],
                                      in1=SCR[:, 2048 + a0:2048 + a1], op=ALU.mult),
                             reads=[("SCR", a0, a1), ("SCR", 2048 + a0, 2048 + a1)], writes=[("BIG", mgo + a0, mgo + a1)])
                proj_postnorm(l, 1, "wo%d" % l, 4, 4, MO, S, 0, 0, 4, bg=lambda: prenorm_units(l, 2, 0, 2, sqt=(2, 3, 4)))

            def convmix(l, pre_tiles=0):
                prenorm(l, 0, 512 * pre_tiles, 4 - pre_tiles)
                for i in range(8):
                    woff = W.next("cwin", i, 3072)
                    P.op("dve", I("memset", SCR[:, 0:2], 0.0), writes=[("SCR", 0, 2)])
                    for tt in range(4):
                        t0 = 512 * tt
                        pss = []
                        for j in range(3):
                            ps, psn = ps_mm()
                            pss.append((ps, psn))
                            for kc in range(8):
                                lo = woff + kc * 384 + j * 128
                                xo = kc * S + t0
                                P.op("pe", I("matmul", ps[:, :], lhsT=WB[:, lo:lo + 128], rhs=XN[:, xo:xo + 512],
                                             start=(kc == 0), stop=(kc == 7)),
                                     reads=[("WB", lo, lo + 128), ("XN", xo, xo + 512)], writes=[(psn, 0, 512)],
                                     inc=(kc == 7))
                        (pB, pBn), (pC, pCn), (pH, pHn) = pss
                        hb, hb_rg = scr(8)
                        bb_, bb_rg = scr(9)
                        P.op("act", I("activation", out=hb, in_=pH[:, :], func=AF.Copy), reads=[(pHn, 0, 512)], writes=[hb_rg])
                        P.op("act", I("activation", out=bb_, in_=pB[:, :], func=AF.Copy), reads=[(pBn, 0, 512)], writes=[bb_rg])
                        zo = (tt % 2) * ST
                        zno = ((tt + 1) % 2) * ST
                        P.op("dve", I("tensor_tensor", out=SCR[:, zo + 2:zo + 514], in0=pC[:, :], in1=hb, op=ALU.mult),
                             reads=[(pCn, 0, 512), hb_rg], writes=[("SCR", zo + 2, zo + 514)])
                        if tt < 3:
                            P.op("dve", I("tensor_copy", out=SCR[:, zno:zno + 2], in_=SCR[:, zo + 512:zo + 514]),
                                 reads=[("SCR", zo + 512, zo + 514)], writes=[("SCR", zno, zno + 2)])
                        to = (2 + tt % 2) * ST
                        P.op("dve", I("tensor_scalar", out=SCR[:, to:to + 512], in0=SCR[:, zo + 2:zo + 514],
                                      scalar1=CDW[:, i * 3 + 2:i * 3 + 3], scalar2=None, op0=ALU.mult),
                             reads=[("SCR", zo + 2, zo + 514), ("CDW", 0, 24)], writes=[("SCR", to, to + 512)])
                        for k in (1, 0):
                            P.op("dve", I("scalar_tensor_tensor", out=SCR[:, to:to + 512], in0=SCR[:, zo + k:zo + k + 512],
                                          scalar=CDW[:, i * 3 + k:i * 3 + k + 1], in1=SCR[:, to:to + 512],
                                          op0=ALU.mult, op1=ALU.add),
                                 reads=[("SCR", zo + k, zo + k + 512), ("SCR", to, to + 512), ("CDW", 0, 24)],
                                 writes=[("SCR", to, to + 512)])
                        yo = i * S + t0
                        P.op("dve", I("tensor_tensor", out=BIG[:, yo:yo + 512], in0=SCR[:, to:to + 512], in1=bb_, op=ALU.mult),
                             reads=[("SCR", to, to + 512), bb_rg], writes=[("BIG", yo, yo + 512)])
                proj_postnorm(l, 1, "cwout", 8, 2, 0, S, 0, 0, 4, bg=lambda: prenorm_units(l, 2, 0, 2, sqt=(2, 3, 4)))

            def poolmix(l, pre_tiles=0):
                prenorm(l, 0, 512 * pre_tiles, 4 - pre_tiles)
                CS0 = 2096
                PO = 16384
                SBf = SB16.bitcast(F32)
                ZT = SBf[:, 0:512]
                ZT_rg = ("SB16", 0, 1024)
                TM = HALO[:, 0:16]
                TM_rg = ("HALO", 0, 16)
                P.op("dve", I("memset", SCR[:, 2080:2096], 0.0), writes=[("SCR", 2080, 2096)])
                P.op("dve", I("memset", ZT, 0.0), writes=[ZT_rg])
                chunk_i = [0]
                for t in range(16):
                    P.op("dve", I("memset", INVT[:, t:t + 1], 1.0 / (t + 1)), writes=[("INVT", t, t + 1)])
                PSETS = [[(BIG, "BIG", 16384), (BIG, "BIG", 18432)], [(BIG, "BIG", 20480), (SB16, "SB16", 1024)]]

                def inproj(gi):
                    w = 2 ** (gi + 1)
                    woff = W.next("pwin", gi, 2048)
                    for cc in range(2):
                        UB0 = 0 if chunk_i[0] % 2 == 0 else 4160
                        chunk_i[0] += 1
                        for tt in range(4):
                            ps, psn = ps_mm()
                            for kc in range(8):
                                lo = woff + kc * 256 + cc * 128
                                xo = kc * S + 512 * tt
                                P.op("pe", I("matmul", ps[:, :], lhsT=WB[:, lo:lo + 128], rhs=XN[:, xo:xo + 512],
                                             start=(kc == 0), stop=(kc == 7)),
                                     reads=[("WB", lo, lo + 128), ("XN", xo, xo + 512)], writes=[(psn, 0, 512)],
                                     inc=(kc == 7))
                            P.op("act", I("activation", out=SCR[:, UB0 + 512 * tt:UB0 + 512 * tt + 512], in_=ps[:, :], func=AF.Copy),
                                 reads=[(psn, 0, 512)], writes=[("SCR", UB0 + 512 * tt, UB0 + 512 * tt + 512)])
                        for q in range(4):
                            co = CS0 + 512 * q
                            init = 0.0 if q == 0 else SCR[:, co - 1:co]
                            P.op("dve", I("tensor_tensor_scan", out=SCR[:, co:co + 512], data0=SCR[:, UB0 + 512 * q:UB0 + 512 * q + 512],
                                          data1=ZT, initial=init, op0=ALU.add, op1=ALU.add),
                                 reads=[("SCR", UB0 + 512 * q, UB0 + 512 * q + 512), ZT_rg, ("SCR", co - 1, co)],
                                 writes=[("SCR", co, co + 512)])
                        P.op("dve", I("tensor_tensor", out=TM, in0=SCR[:, CS0:CS0 + 16], in1=INVT[:, 0:16], op=ALU.mult),
                             reads=[("SCR", CS0, CS0 + 16), ("INVT", 0, 16)], writes=[TM_rg])
                        P.op("dve", I("tensor_tensor", out=TM, in0=TM, in1=SCR[:, UB0:UB0 + 16], op=ALU.subtract),
                             reads=[TM_rg, ("SCR", UB0, UB0 + 16)], writes=[TM_rg])
                        P.op("dve", I("scalar_tensor_tensor", out=SCR[:, UB0:UB0 + S], in0=SCR[:, CS0:CS0 + S], scalar=1.0 / w,
                                      in1=SCR[:, UB0:UB0 + S], op0=ALU.mult, op1=ALU.subtract),
                             reads=[("SCR", CS0, CS0 + S), ("SCR", UB0, UB0 + S)], writes=[("SCR", UB0, UB0 + S)])
                        pt_, pn_, po = PSETS[gi % 2][cc]
                        P.op("dve", I("scalar_tensor_tensor", out=pt_[:, po:po + S], in0=SCR[:, CS0 - w:CS0 - w + S], scalar=-1.0 / w,
                                      in1=SCR[:, UB0:UB0 + S], op0=ALU.mult, op1=ALU.add),
                             reads=[("SCR", CS0 - w, CS0 - w + S), ("SCR", UB0, UB0 + S)], writes=[(pn_, po, po + S)])
                        P.op("dve", I("tensor_copy", out=pt_[:, po:po + w - 1], in_=HALO[:, 0:w - 1]),
                             reads=[TM_rg], writes=[(pn_, po, po + w - 1)])

                def grp(gi):
                    goff = W.next("pwgrp", gi, 512)
                    for mm in range(2):
                        for tt in range(4):
                            ps, psn = ps_mm()
                            for kc in range(2):
                                lo = goff + kc * 256 + mm * 128
                                pt_, pn_, po = PSETS[gi % 2][kc]
                                io = po + 512 * tt
                                P.op("pe", I("matmul", ps[:, :], lhsT=WB[:, lo:lo + 128], rhs=pt_[:, io:io + 512],
                                             start=(kc == 0), stop=(kc == 1)),
                                     reads=[("WB", lo, lo + 128), (pn_, io, io + 512)], writes=[(psn, 0, 512)],
                                     inc=(kc == 1))
                            ch = 2 * gi + mm
                            yo = ch * S + 512 * tt
                            P.op("act", I("activation", out=BIG[:, yo:yo + 512], in_=ps[:, :], func=AF.Copy, scale=PSC[:, ch:ch + 1]),
                                 reads=[(psn, 0, 512), ("PSC", 0, 8)], writes=[("BIG", yo, yo + 512)])
                inproj(0)
                for gi in range(4):
                    if gi + 1 < 4:
                        inproj(gi + 1)
                    grp(gi)
                proj_postnorm(l, 1, "pwout", 8, 2, 0, S, 0, 0, 4, bg=lambda: prenorm_units(l, 2, 0, 2, sqt=(2, 3, 4)))

            def store_out(h0, h1, strm):
                for c in range(8):
                    P.op("sp", I("dma_start", out=yT[c * 128:(c + 1) * 128, h0:h1], in_=X[:, c * S + h0:c * S + h1]),
                         reads=[("X", c * S + h0, c * S + h1)], stream=strm, amt=16)

            run_layers = EXEC_LAYERS if EXEC_LAYERS is not None else layers
            masks_built = [False]
            for li, l in enumerate(run_layers):
                kind = l % 3
                pre_tiles = 2 if li > 0 else 0
                if kind == 0:
                    attention(l, pre_tiles)
                elif kind == 1:
                    convmix(l, pre_tiles)
                else:
                    poolmix(l, pre_tiles)
                ffn(l, pre_done=True, next_l=(run_layers[li + 1] if li + 1 < len(run_layers) else None))
            store_out(1024, 1536, "ost1")
            store_out(1536, 2048, "ost2")
            P.wait_all("sp", ["ost0", "ost1", "ost2"])

        def t_name(t, nm):
            return {"gvec": "GV", "dwv": "DWV", "cdw": "CDW", "psc": "PSC"}[nm]

        Wd = WMgr(DummyProg(), dram, WB, None)
        construct(DummyProg(), Wd)
        P = Prog()
        P.unordered.update(["xld0", "xld1", "xld2", "cst", "ost0", "ost1", "ost2"])
        W = WMgr(P, dram, WB, Wd.rec)
        construct(P, W)
        assert W.used == len(Wd.rec) and W.issued == len(Wd.rec)
        P.emit(nc, st)
    return nc


def chunkmajor(Wm, cols):
    sub = Wm[:, cols]
    nk = Wm.shape[0] // 128
    return np.ascontiguousarray(sub.reshape(nk, 128, -1).transpose(1, 0, 2).reshape(128, -1))


def host_layout(inp):
    f = lambda a: np.asarray(a, dtype=np.float32)
    out = {}
    ng = f(inp["norm_g"])
    out["gvec"] = np.ascontiguousarray(ng.reshape(16, 8, 128).transpose(2, 0, 1).reshape(128, 128))
    dw = f(inp["ffn_w_dw"])
    out["dwv"] = np.ascontiguousarray(dw.reshape(4, 3, 2, 22, 128).transpose(4, 0, 3, 2, 1).reshape(128, 528))
    cdw = f(inp["conv_w_dw"])[0]
    out["cdw"] = np.ascontiguousarray(cdw.reshape(3, 8, 128).transpose(2, 1, 0).reshape(128, 24))
    out["psc"] = np.ascontiguousarray(f(inp["pool_scale"])[0].reshape(8, 128).T)
    ar = np.arange
    for l in range(4):
        wu = f(inp["ffn_w_up"])[l]
        out["wup%d" % l] = np.stack([chunkmajor(wu, np.concatenate([ar(128 * i, 128 * i + 128), ar(2816 + 128 * i, 2816 + 128 * i + 128)]))
                                     for i in range(22)])
        wd = f(inp["ffn_w_down"])[l]
        out["wdn%d" % l] = np.stack([chunkmajor(wd, ar(128 * m, 128 * m + 128)) for m in range(8)])
    for ia, l in enumerate((0, 3)):
        wq = f(inp["attn_w_qkv"])[ia]
        blks = []
        for g in range(3):
            for c in range(4):
                base = g * 1536 + c * 128
                blks.append(chunkmajor(wq, np.concatenate([ar(base, base + 128), ar(base + 512, base + 640), ar(base + 1024, base + 1152)])))
        out["wqkv%d" % l] = np.stack(blks)
        wo = f(inp["attn_w_o"])[ia]
        out["wo%d" % l] = np.stack([chunkmajor(wo, ar(512 * b, 512 * b + 512)) for b in range(2)])
    cw = f(inp["conv_w_in"])[0]
    out["cwin"] = np.stack([chunkmajor(cw, np.concatenate([ar(128 * i, 128 * i + 128), ar(1024 + 128 * i, 1152 + 128 * i), ar(2048 + 128 * i, 2176 + 128 * i)]))
                            for i in range(8)])
    for nm, key in (("cwout", "conv_w_out"), ("pwin", "pool_w_in"), ("pwout", "pool_w_out")):
        wm = f(inp[key])[0]
        out[nm] = np.stack([chunkmajor(wm, ar(256 * b, 256 * b + 256)) for b in range(4)])
    wg = f(inp["pool_w_grp"])[0]
    out["pwgrp"] = np.stack([chunkmajor(wg[g], ar(0, 256)) for g in range(4)])
    return out


_PROG_CACHE = {}


def _run(layers, xT_list, lay):
    key = tuple(layers)
    if key not in _PROG_CACHE:
        _PROG_CACHE[key] = build_program(list(layers))
    nc = _PROG_CACHE[key]
    names = ["gvec", "dwv", "cdw", "psc"]
    for l in layers:
        names += ["wup%d" % l, "wdn%d" % l]
        kind = l % 3
        if kind == 0:
            names += ["wqkv%d" % l, "wo%d" % l]
        elif kind == 1:
            names += ["cwin", "cwout"]
        else:
            names += ["pwin", "pwgrp", "pwout"]
    in_maps = []
    for b in range(8):
        m = {n: lay[n] for n in names}
        m["xT"] = xT_list[b]
        in_maps.append(m)
    res = run_bass_kernel_spmd(nc, in_maps, core_ids=list(range(8)))
    return [np.asarray(r["yT"]) for r in res.results]


def kernel(**inputs):
    lay = host_layout(inputs)
    x = np.asarray(inputs["x"], dtype=np.float32)
    xT = [np.ascontiguousarray(x[b].T) for b in range(8)]
    if FUSED:
        yT = _run((0, 1, 2, 3), xT, lay)
    else:
        yT = xT
        for l in range(4):
            yT = _run((l,), yT, lay)
    return np.ascontiguousarray(np.stack([y.T for y in yT]).astype(np.float32))
```

```python
import numpy as np
import concourse.bass as bass
import concourse.mybir as mybir
from concourse.bass_utils import run_bass_kernel_spmd
from contextlib import ExitStack

F32 = mybir.dt.float32
BF16 = mybir.dt.bfloat16
AF = mybir.ActivationFunctionType
ALU = mybir.AluOpType

S = 2048
EPS = 1e-6
DIL = (1, 4, 16)
NSLOT = 3
SLOT = 3072
ST = 520
NT = 12
FUSED = True
EXEC_LAYERS = None


class Prog:
    ENG = ("pe", "act", "dve", "pool", "sp")

    def __init__(self):
        self.q = {e: [] for e in self.ENG}
        self.cnt = {}
        self.waited = {e: {} for e in self.ENG}
        self.recs = {}
        self.unordered = set()
        self.pe_pending = False
        self.nops = 0

    def _overlaps(self, name, lo, hi):
        return [r for r in self.recs.get(name, ()) if r[0] < hi and lo < r[1]]

    def op(self, eng, fn, reads=(), writes=(), inc=True, stream=None, amt=1):
        own = stream if stream is not None else eng
        if self.pe_pending and eng != "pe":
            raise RuntimeError("non-PE op constructed inside an open PE group")
        deps = {}

        def add(dep):
            if dep is None:
                return
            s, c = dep
            if s in self.unordered:
                c = self.cnt.get(s, 0)
            if eng == "pe" and s == "pe":
                return
            if deps.get(s, 0) < c:
                deps[s] = c
        for (name, lo, hi) in reads:
            for r in self._overlaps(name, lo, hi):
                add(r[2])
        for (name, lo, hi) in writes:
            for r in self._overlaps(name, lo, hi):
                add(r[2])
                for s, c in r[3].items():
                    add((s, c))
        for s, c in deps.items():
            if self.waited[eng].get(s, 0) < c:
                self.waited[eng][s] = c
                self.q[eng].append(("wait", s, c))
        after = self.cnt.get(own, 0) + amt
        if inc:
            self.cnt[own] = after
            if eng == "pe":
                self.pe_pending = False
        else:
            assert eng == "pe"
            self.pe_pending = True
        self.q[eng].append(("op", fn, own if inc else None, amt))
        self.nops += 1
        for (name, lo, hi) in reads:
            lst = self.recs.setdefault(name, [])
            for r in lst:
                if r[0] == lo and r[1] == hi and r[2] is None:
                    r[3][own] = after
                    break
            else:
                lst.append([lo, hi, None, {own: after}])
        for (name, lo, hi) in writes:
            lst = self.recs.setdefault(name, [])
            lst[:] = [r for r in lst if not (lo <= r[0] and r[1] <= hi)]
            lst.append([lo, hi, (own, after), {}])

    def wait_all(self, eng, streams):
        for s in streams:
            c = self.cnt.get(s, 0)
            if c and self.waited[eng].get(s, 0) < c:
                self.waited[eng][s] = c
                self.q[eng].append(("wait", s, c))

    def emit(self, nc, stack):
        assert not self.pe_pending
        sems = {s: stack.enter_context(nc.semaphore("s_" + s)) for s in self.cnt}
        block = stack.enter_context(nc.Block())

        def replay(e, name):
            for it in self.q[name]:
                if it[0] == "wait":
                    e.wait_ge(sems[it[1]], it[2])
                else:
                    ins = it[1](e)
                    if it[2] is not None:
                        ins.then_inc(sems[it[2]], it[3])

        @block.tensor
        def _(e):
            replay(e, "pe")

        @block.scalar
        def _(e):
            replay(e, "act")

        @block.vector
        def _(e):
            replay(e, "dve")

        @block.gpsimd
        def _(e):
            replay(e, "pool")

        @block.sync
        def _(e):
            replay(e, "sp")


class DummyProg:
    def op(self, *a, **k):
        pass

    def wait_all(self, *a, **k):
        pass


def I(fn, *a, **k):
    return lambda e: getattr(e, fn)(*a, **k)


class WMgr:
    def __init__(self, P, dram, WB, sched):
        self.P, self.dram, self.WB, self.sched = P, dram, WB, sched
        self.rec = []
        self.issued = 0
        self.used = 0

    def next(self, name, blk, nelem):
        if self.sched is None:
            self.rec.append((name, blk, nelem))
            return 0
        i = self.used
        assert self.sched[i] == (name, blk, nelem), (self.sched[i], name, blk, nelem)
        while self.issued < min(i + NSLOT, len(self.sched)):
            self._issue(self.issued)
            self.issued += 1
        self.used += 1
        return (i % NSLOT) * SLOT

    def _issue(self, j):
        name, blk, nelem = self.sched[j]
        slot = j % NSLOT
        off = slot * SLOT
        self.P.op("pool", I("dma_start", out=self.WB[:, off:off + nelem], in_=self.dram[name][blk]),
                  reads=([("X", 0, 512)] if j == 0 else []),
                  writes=[("WB", off, off + nelem)], stream="w%d" % slot, amt=16)


def build_program(layers):
    nc = bass.Bass("TRN2", target_bir_lowering=False)
    dram = {}

    def din(name, shape):
        dram[name] = nc.dram_tensor(name, list(shape), F32, kind="ExternalInput").ap()
    din("xT", (1024, S))
    din("gvec", (128, 128))
    din("dwv", (128, 528))
    din("cdw", (128, 24))
    din("psc", (128, 8))
    for l in layers:
        din("wup%d" % l, (22, 128, 2048))
        din("wdn%d" % l, (8, 128, 2816))
        kind = l % 3
        if kind == 0:
            din("wqkv%d" % l, (12, 128, 3072))
            din("wo%d" % l, (2, 128, 2048))
        elif kind == 1:
            din("cwin", (8, 128, 3072))
            din("cwout", (4, 128, 2048))
        else:
            din("pwin", (4, 128, 2048))
            din("pwgrp", (4, 128, 512))
            din("pwout", (4, 128, 2048))
    yT = nc.dram_tensor("yT", [1024, S], F32, kind="ExternalOutput").ap()

    with ExitStack() as st:
        def sb(name, n, dt):
            return st.enter_context(nc.sbuf_tensor(name, [128, n], dt))
        X = sb("X", 8 * S, F32)
        XN = sb("XN", 8 * S, BF16)
        BIG = sb("BIG", 22528, BF16)
        SCR = sb("SCR", NT * ST, F32)
        SB16 = sb("SB16", 6 * 512, BF16)
        WB = sb("WB", NSLOT * SLOT, BF16)
        MASK = sb("MASK", 12 * 512, BF16)
        GV = sb("GV", 128, F32)
        DWV = sb("DWV", 528, F32)
        CDW = sb("CDW", 24, F32)
        PSC = sb("PSC", 8, F32)
        ONES = sb("ONES", 128, BF16)
        ONESAB = sb("ONESAB", 256, BF16)
        HALO = sb("HALO", 88, F32)
        INVT = sb("INVT", 16, F32)
        PS = [st.enter_context(nc.psum_tensor("PS%d" % b, [128, 512], F32)) for b in range(8)]

        def construct(P, W):
            mmi = [0]
            auxi = [0]
            sqi = [0]
            rsi = [0]
            ei = [0]

            def ps_mm():
                b = mmi[0] % 5
                mmi[0] += 1
                return PS[b], "PS%d" % b

            upi = [0]

            def ps_up():
                b = upi[0] % 8
                upi[0] += 1
                return PS[b], "PS%d" % b

            def ps_aux():
                b = (6, 7, 5)[auxi[0] % 3]
                auxi[0] += 1
                return PS[b], "PS%d" % b
            SS, SSn = PS[5], "PS5"

            def scr(i, lo=0, n=512):
                o = i * ST + lo
                return SCR[:, o:o + n], ("SCR", o, o + n)

            def sb16(i, n=512):
                return SB16[:, i * 512:i * 512 + n], ("SB16", i * 512, i * 512 + n)

            for (t, nm, n) in ((GV, "gvec", 128), (DWV, "dwv", 528), (CDW, "cdw", 24), (PSC, "psc", 8)):
                P.op("sp", I("dma_start", out=t[:, :], in_=dram[nm][:, :]), writes=[(t_name(t, nm), 0, n)], stream="cst", amt=16)
            for (h0, h1, strm) in ((0, 512, "xld0"), (512, 1024, "xld1"), (1024, 2048, "xld2")):
                for c in range(8):
                    P.op("sp",
                         I("dma_start", out=X[:, c * S + h0:c * S + h1], in_=dram["xT"][c * 128:(c + 1) * 128, h0:h1]),
                         writes=[("X", c * S + h0, c * S + h1)], stream=strm, amt=16)
            P.op("dve", I("memset", ONES[:, :], 1.0), writes=[("ONES", 0, 128)])
            P.op("dve", I("memset", ONESAB[:, :], 0.0), writes=[("ONESAB", 0, 256)])
            P.op("dve", I("memset", ONESAB[:, 0:64], 1.0), writes=[("ONESAB", 0, 64)])
            P.op("dve", I("memset", ONESAB[:, 192:256], 1.0), writes=[("ONESAB", 192, 256)])

            XNf = XN.bitcast(F32)
            MASKf = MASK.bitcast(F32)

            def add_sq(src_ap, src_rg, first, last, SS=SS, SSn=SSn, defer=False):
                k = sqi[0] % 2
                sqi[0] += 1
                sq_ap, sq_rg = sb16(k)
                P.op("act", I("activation", out=sq_ap, in_=src_ap, func=AF.Square), reads=[src_rg], writes=[sq_rg])

                def pe_part():
                    P.op("pe", I("matmul", SS[:, :], lhsT=ONES[:, :], rhs=sq_ap, start=first, stop=last),
                         reads=[sq_rg, ("ONES", 0, 128)], writes=[(SSn, 0, 512)])
                if defer:
                    return pe_part
                pe_part()
                return None

            def stats_to_rs(SS=SS, SSn=SSn):
                r = 10 + rsi[0] % 2
                rsi[0] += 1
                rs_ap, rs_rg = scr(r)
                P.op("act", I("activation", out=rs_ap, in_=SS[:, :], func=AF.Ln, scale=1.0 / 1024.0, bias=EPS),
                     reads=[(SSn, 0, 512)], writes=[rs_rg])
                P.op("act", I("activation", out=rs_ap, in_=rs_ap, func=AF.Exp, scale=-0.5), reads=[rs_rg], writes=[rs_rg])
                return rs_ap, rs_rg

            def prenorm_units(l, j, tok0, ntiles, sqt=(0, 1)):
                go = (l * 4 + j) * 8
                units = []
                for tt in range(ntiles):
                    t0 = tok0 + 512 * tt
                    rs_box = []

                    def sq(c, t0=t0):
                        o = c * S + t0
                        sq_ap, sq_rg = sb16(sqt[c % len(sqt)])
                        P.op("act", I("activation", out=sq_ap, in_=X[:, o:o + 512], func=AF.Square),
                             reads=[("X", o, o + 512)], writes=[sq_rg])

                    def stat(c):
                        sq_ap, sq_rg = sb16(sqt[c % len(sqt)])
                        P.op("pe", I("matmul", SS[:, :], lhsT=ONES[:, :], rhs=sq_ap, start=(c == 0), stop=(c == 7)),
                             reads=[sq_rg, ("ONES", 0, 128)], writes=[(SSn, 0, 512)])

                    def rs(rs_box=rs_box):
                        rs_box.append(stats_to_rs())

                    def apply(c0, c1, t0=t0, rs_box=rs_box):
                        rs_ap, rs_rg = rs_box[0]
                        for c in range(c0, c1):
                            o = c * S + t0
                            P.op("dve", I("scalar_tensor_tensor", out=XN[:, o:o + 512], in0=X[:, o:o + 512],
                                          scalar=GV[:, go + c:go + c + 1], in1=rs_ap, op0=ALU.mult, op1=ALU.mult),
                                 reads=[("X", o, o + 512), rs_rg, ("GV", 0, 128)], writes=[("XN", o, o + 512)])
                    nb_ = len(sqt)
                    units.append(lambda sq=sq: sq(0))
                    if nb_ >= 3:
                        units.append(lambda sq=sq: sq(1))
                        for c in range(2, 8):
                            units.append(lambda c=c, sq=sq, stat=stat: (sq(c), stat(c - 2)))
                        units.append(lambda stat=stat, rs=rs: (stat(6), stat(7), rs()))
                    else:
                        for c in range(1, 8):
                            units.append(lambda c=c, sq=sq, stat=stat: (sq(c), stat(c - 1)))
                        units.append(lambda stat=stat, rs=rs: (stat(7), rs()))
                    units.append(lambda apply=apply: apply(0, 4))
                    units.append(lambda apply=apply: apply(4, 8))
                return units

            def prenorm(l, j, tok0, ntiles):
                for u in prenorm_units(l, j, tok0, ntiles):
                    u()

            def proj_postnorm(l, j, wname, nk, mpb, in_off, in_cs, in_t0, tok0, ntiles, bg=None, bg_after=1):
                go = (l * 4 + j) * 8
                ncols = mpb * 128
                woff = 0
                pend = None
                bg_list = []
                SSp = (PS[6], "PS6")

                def hs(tt, m):
                    if tt % 2 == 0:
                        return scr(m)
                    o = m * S + 1024
                    return XNf[:, o // 2:o // 2 + 512], ("XN", o, o + 1024)
                for tt in range(ntiles):
                    t0 = tok0 + 512 * tt
                    it0 = in_t0 + 512 * tt
                    for m in range(8):
                        if m % mpb == 0:
                            woff = W.next(wname, m // mpb, nk * ncols)
                        ps, psn = ps_mm()
                        for kc in range(nk):
                            lo = woff + kc * ncols + (m % mpb) * 128
                            io = in_off + kc * in_cs + it0
                            P.op("pe", I("matmul", ps[:, :], lhsT=WB[:, lo:lo + 128], rhs=BIG[:, io:io + 512],
                                         start=(kc == 0), stop=(kc == nk - 1)),
                                 reads=[("WB", lo, lo + 128), ("BIG", io, io + 512)], writes=[(psn, 0, 512)],
                                 inc=(kc == nk - 1))
                        if pend is not None:
                            pend()
                        hs_ap, hs_rg = hs(tt, m)
                        P.op("act", I("activation", out=hs_ap, in_=ps[:, :], func=AF.Copy),
                             reads=[(psn, 0, 512)], writes=[hs_rg])
                        pend = add_sq(ps[:, :], (psn, 0, 512), m == 0, m == 7, SSp[0], SSp[1], defer=True)
                        if bg_list and tt > bg_after:
                            for _ in range(2):
                                if bg_list:
                                    bg_list.pop(0)()
                    pend()
                    pend = None
                    rs_ap, rs_rg = stats_to_rs(SSp[0], SSp[1])
                    def res_add(m, tt=tt, t0=t0):
                        hs_ap, hs_rg = hs(tt, m)
                        xo = m * S + t0
                        P.op("dve",
                             I("tensor_tensor", out=X[:, xo:xo + 512], in0=X[:, xo:xo + 512], in1=hs_ap, op=ALU.add),
                             reads=[("X", xo, xo + 512), hs_rg], writes=[("X", xo, xo + 512)])
                    for m in range(8):
                        hs_ap, hs_rg = hs(tt, m)
                        P.op("dve", I("scalar_tensor_tensor", out=hs_ap, in0=hs_ap, scalar=GV[:, go + m:go + m + 1],
                                      in1=rs_ap, op0=ALU.mult, op1=ALU.mult),
                             reads=[hs_rg, rs_rg, ("GV", 0, 128)], writes=[hs_rg])
                        if m > 0:
                            res_add(m - 1)
                    res_add(7)
                    if tt == bg_after and bg is not None:
                        bg_list.extend(bg())
                while bg_list:
                    bg_list.pop(0)()

            def ffn(l, pre_done=False, next_l=None):
                for half in range(2):
                    tok0 = 1024 * half
                    if half == 0 and not pre_done:
                        prenorm(l, 2, 0, 2)
                    if half == 0:
                        P.op("dve", I("memset", HALO[:, :], 0.0), writes=[("HALO", 0, 88)])
                    pend = []
                    ptc = [0]
                    for i in range(22):
                        woff = W.next("wup%d" % l, i, 2048)
                        for tt in range(2):
                            t0 = tok0 + 512 * tt
                            pss = []
                            for wh in range(2):
                                ps, psn = ps_up()
                                pss.append((ps, psn))
                                for kc in range(8):
                                    lo = woff + kc * 256 + wh * 128
                                    xo = kc * S + t0
                                    P.op("pe", I("matmul", ps[:, :], lhsT=WB[:, lo:lo + 128], rhs=XN[:, xo:xo + 512],
                                                 start=(kc == 0), stop=(kc == 7)),
                                         reads=[("WB", lo, lo + 128), ("XN", xo, xo + 512)], writes=[(psn, 0, 512)],
                                         inc=(kc == 7))
                            tos = []
                            for wh in range(2):
                                ps, psn = pss[wh]
                                ch = 2 * i + wh
                                dwo = l * 132 + ch * 3
                                zo = (wh * 3 + (ptc[0] % 3)) * ST
                                to = (6 + wh * 3 + (ptc[0] % 3)) * ST
                                tos.append(to)
                                if tt == 0 and wh == 0:
                                    zv = SCR[:, zo:zo + 6 * ST].rearrange("p (w r) -> p w r", w=2)[:, :, 0:2]
                                    hv = HALO[:, ch * 2:ch * 2 + 4].rearrange("p (w r) -> p w r", w=2)
                                    P.op("act", I("activation", out=zv, in_=hv, func=AF.Copy),
                                         reads=[("HALO", ch * 2, ch * 2 + 4)],
                                         writes=[("SCR", zo, zo + 2), ("SCR", zo + 3 * ST, zo + 3 * ST + 2)])
                                P.op("act", I("activation", out=SCR[:, zo + 2:zo + 514], in_=ps[:, :], func=AF.Copy),
                                     reads=[(psn, 0, 512)], writes=[("SCR", zo + 2, zo + 514)])
                                P.op("act", I("activation", out=SCR[:, to:to + 512], in_=ps[:, :], func=AF.Copy,
                                              scale=DWV[:, dwo + 2:dwo + 3]),
                                     reads=[(psn, 0, 512), ("DWV", 0, 528)], writes=[("SCR", to, to + 512)])
                            pend2 = []
                            for f in pend:
                                pend2.append(f())
                            pend = []
                            ch = 2 * i
                            zo = (ptc[0] % 3) * ST
                            zno = ((ptc[0] + 1) % 3) * ST
                            tv = SCR[:, zo:zo + 6 * ST].rearrange("p (w r) -> p w r", w=2)[:, :, 512:514]
                            t_rg = [("SCR", zo + 512, zo + 514), ("SCR", zo + 3 * ST + 512, zo + 3 * ST + 514)]
                            if tt == 0:
                                nv = SCR[:, zno:zno + 6 * ST].rearrange("p (w r) -> p w r", w=2)[:, :, 0:2]
                                P.op("act", I("activation", out=nv, in_=tv, func=AF.Copy), reads=t_rg,
                                     writes=[("SCR", zno, zno + 2), ("SCR", zno + 3 * ST, zno + 3 * ST + 2)])
                            elif half == 0:
                                hv = HALO[:, ch * 2:ch * 2 + 4].rearrange("p (w r) -> p w r", w=2)
                                P.op("act", I("activation", out=hv, in_=tv, func=AF.Copy), reads=t_rg,
                                     writes=[("HALO", ch * 2, ch * 2 + 4)])
                            for k in (1, 0):
                                for wh in range(2):
                                    ch = 2 * i + wh
                                    dwo = l * 132 + ch * 3
                                    zo = (wh * 3 + (ptc[0] % 3)) * ST
                                    to = tos[wh]
                                    P.op("dve", I("scalar_tensor_tensor", out=SCR[:, to:to + 512], in0=SCR[:, zo + k:zo + k + 512],
                                                  scalar=DWV[:, dwo + k:dwo + k + 1], in1=SCR[:, to:to + 512],
                                                  op0=ALU.mult, op1=ALU.add),
                                         reads=[("SCR", zo + k, zo + k + 512), ("SCR", to, to + 512), ("DWV", 0, 528)],
                                         writes=[("SCR", to, to + 512)])

                            for f2 in pend2:
                                f2()

                            def fin(tg=tos[0], tu=tos[1], ho=i * 1024 + 512 * tt):
                                P.op("act", I("activation", out=SCR[:, tg:tg + 512], in_=SCR[:, tg:tg + 512], func=AF.Silu),
                                     reads=[("SCR", tg, tg + 512)], writes=[("SCR", tg, tg + 512)])

                                def gate():
                                    P.op("dve", I("tensor_tensor", out=BIG[:, ho:ho + 512], in0=SCR[:, tg:tg + 512],
                                                  in1=SCR[:, tu:tu + 512], op=ALU.mult),
                                         reads=[("SCR", tg, tg + 512), ("SCR", tu, tu + 512)], writes=[("BIG", ho, ho + 512)])
                                return gate
                            pend.append(fin)
                            ptc[0] += 1
                    for f in pend:
                        f()()
                    if half == 0:
                        ffn_down(l, 0, bg=prenorm_units(l, 2, 1024, 2, sqt=(2, 3, 4)))
                        if next_l is None:
                            store_out(0, 1024, "ost0")
                    elif next_l is not None:
                        ffn_down(l, 1, bg=prenorm_units(next_l, 0, 0, 2, sqt=(2, 3, 4)))
                    else:
                        ffn_down(l, 1)

            def ffn_down(l, half, bg=()):
                tok0 = 1024 * half
                go = (l * 4 + 3) * 8
                SSb = [(PS[6], "PS6"), (PS[7], "PS7")]
                bg = list(bg)

                def hs(tt, m):
                    if tt == 0:
                        return scr(m)
                    o = m * S + 1024 * half
                    return XNf[:, o // 2:o // 2 + 512], ("XN", o, o + 1024)
                pend = None
                for m in range(8):
                    woff = W.next("wdn%d" % l, m, 2816)
                    for tt in range(2):
                        ps, psn = ps_mm()
                        for kc in range(22):
                            lo = woff + kc * 128
                            io = kc * 1024 + 512 * tt
                            P.op("pe", I("matmul", ps[:, :], lhsT=WB[:, lo:lo + 128], rhs=BIG[:, io:io + 512],
                                         start=(kc == 0), stop=(kc == 21)),
                                 reads=[("WB", lo, lo + 128), ("BIG", io, io + 512)], writes=[(psn, 0, 512)],
                                 inc=(kc == 21))
                        if pend is not None:
                            pend()
                        hs_ap, hs_rg = hs(tt, m)
                        P.op("act", I("activation", out=hs_ap, in_=ps[:, :], func=AF.Copy),
                             reads=[(psn, 0, 512)], writes=[hs_rg])
                        pend = add_sq(ps[:, :], (psn, 0, 512), m == 0, m == 7, SSb[tt][0], SSb[tt][1], defer=True)
                        for _ in range(2):
                            if bg:
                                bg.pop(0)()
                pend()
                while bg:
                    bg.pop(0)()
                for tt in (0, 1):
                    t0 = tok0 + 512 * tt
                    rs_ap, rs_rg = stats_to_rs(SSb[tt][0], SSb[tt][1])
                    def res_add(m, tt=tt, t0=t0):
                        hs_ap, hs_rg = hs(tt, m)
                        xo = m * S + t0
                        P.op("dve", I("tensor_tensor", out=X[:, xo:xo + 512], in0=X[:, xo:xo + 512], in1=hs_ap, op=ALU.add),
                             reads=[("X", xo, xo + 512), hs_rg], writes=[("X", xo, xo + 512)])
                    for m in range(8):
                        hs_ap, hs_rg = hs(tt, m)
                        P.op("dve", I("scalar_tensor_tensor", out=hs_ap, in0=hs_ap, scalar=GV[:, go + m:go + m + 1],
                                      in1=rs_ap, op0=ALU.mult, op1=ALU.mult),
                             reads=[hs_rg, rs_rg, ("GV", 0, 128)], writes=[hs_rg])
                        if m > 0:
                            res_add(m - 1)
                    res_add(7)

            def mask_setup():
                io_ap, io_rg = scr(9, 0, 128)
                P.op("pool", I("iota", io_ap, [[1, 128]], base=0, channel_multiplier=-1,
                               allow_small_or_imprecise_dtypes=True), writes=[io_rg])
                vc, vc_rg = scr(8, 0, 128)
                vp, vp_rg = scr(8, 128, 128)
                dc, dc_rg = scr(8, 256, 128)
                dp, dp_rg = scr(8, 384, 128)
                P.op("dve", I("tensor_single_scalar", out=vc, in_=io_ap, scalar=0.0, op=ALU.is_ge), reads=[io_rg], writes=[vc_rg])
                P.op("dve", I("tensor_single_scalar", out=vp, in_=io_ap, scalar=0.0, op=ALU.is_le), reads=[io_rg], writes=[vp_rg])
                P.op("dve", I("tensor_single_scalar", out=dc, in_=io_ap, scalar=0.0, op=ALU.max), reads=[io_rg], writes=[dc_rg])
                P.op("dve", I("tensor_scalar", out=dp, in0=io_ap, scalar1=0.0, scalar2=128.0, op0=ALU.min, op1=ALU.add),
                     reads=[io_rg], writes=[dp_rg])

            mk = [0]

            def build_mask(g, c):
                vc, vc_rg = scr(8, 0, 128)
                vp, vp_rg = scr(8, 128, 128)
                dc, dc_rg = scr(8, 256, 128)
                dp, dp_rg = scr(8, 384, 128)
                for s_ in range(2):
                    h = 2 * c + s_
                    sd = float(2.0 ** (-8.0 * (g * 8 + h + 1) / 24.0)) * DIL[g]
                    mo = (g * 4 + c) * 512
                    for (d_ap, d_rg, v_ap, v_rg, co) in ((dc, dc_rg, vc, vc_rg, s_ * 128), (dp, dp_rg, vp, vp_rg, 256 + s_ * 128)):
                        t_ap, t_rg = scr(9, 128 + 128 * (mk[0] % 2), 128)
                        mk[0] += 1
                        P.op("act", I("activation", out=t_ap, in_=d_ap, func=AF.Exp, scale=-sd), reads=[d_rg], writes=[t_rg])
                        P.op("dve", I("tensor_tensor", out=MASK[:, mo + co:mo + co + 128], in0=t_ap, in1=v_ap, op=ALU.mult),
                             reads=[t_rg, v_rg], writes=[("MASK", mo + co, mo + co + 128)])

            QO, KAO, KBO, VO, MO = 0, 2048, 4096, 6144, 10240

            def attention(l, pre_tiles=0):
                prenorm(l, 0, 512 * pre_tiles, 4 - pre_tiles)
                first_attn = not masks_built[0]
                if first_attn:
                    mask_setup()
                    masks_built[0] = True
                P.op("dve", I("memset", BIG[:, KAO:MO], 0.0), writes=[("BIG", KAO, MO)])
                Vv = BIG[:, VO:MO].rearrange("p (b s c) -> p b s c", b=16, s=2)
                for c in range(4):
                    for g in range(3):
                        Dl = DIL[g]
                        nb = (S // Dl) // 128
                        woff = W.next("wqkv%d" % l, g * 4 + c, 3072)
                        if first_attn:
                            build_mask(g, c)
                        for wh in range(2):
                            for tt in range(4):
                                ps, psn = ps_mm()
                                for kc in range(8):
                                    lo = woff + kc * 384 + wh * 128
                                    xo = kc * S + 512 * tt
                                    P.op("pe", I("matmul", ps[:, :], lhsT=WB[:, lo:lo + 128], rhs=XN[:, xo:xo + 512],
                                                 start=(kc == 0), stop=(kc == 7)),
                                         reads=[("WB", lo, lo + 128), ("XN", xo, xo + 512)], writes=[(psn, 0, 512)],
                                         inc=(kc == 7))
                                nu = 512 // Dl
                                u0 = tt * nu

                                def views(dst_rows, base):
                                    if Dl == 1:
                                        return (BIG[dst_rows, base + 512 * tt:base + 512 * tt + 512], ps[dst_rows, :])
                                    ov = BIG[dst_rows, base:base + S].rearrange("p (r u) -> p u r", r=Dl)[:, u0:u0 + nu, :]
                                    iv = ps[dst_rows, :].rearrange("p (u r) -> p u r", r=Dl)
                                    return ov, iv
                                if wh == 0:
                                    ov, iv = views(slice(0, 128), QO)
                                    P.op("act", I("activation", out=ov, in_=iv, func=AF.Copy),
                                         reads=[(psn, 0, 512)], writes=[("BIG", QO, QO + S)])
                                else:
                                    ov, iv = views(slice(0, 64), KAO)
                                    P.op("act", I("activation", out=ov, in_=iv, func=AF.Copy), reads=[(psn, 0, 512)], writes=[("BIG", KAO, KAO + S)])
                                    ov, iv = views(slice(64, 128), KBO)
                                    P.op("dve", I("tensor_copy", out=ov, in_=iv), reads=[(psn, 0, 512)], writes=[("BIG", KBO, KBO + S)])
                        for b0 in range(0, 16, 4):
                            ps, psn = ps_mm()
                            for bb in range(4):
                                b = b0 + bb
                                r, n = b // nb, b % nb
                                stok = r + Dl * 128 * n
                                for kc in range(8):
                                    a0 = kc * S + stok
                                    a1 = a0 + Dl * 127 + 1
                                    lo = woff + kc * 384 + 256
                                    P.op("pe", I("matmul", ps[:, bb * 128:(bb + 1) * 128], lhsT=XN[:, a0:a1:Dl], rhs=WB[:, lo:lo + 128],
                                                 start=(kc == 0), stop=(kc == 7)),
                                         reads=[("XN", a0, a1), ("WB", lo, lo + 128)], writes=[(psn, bb * 128, (bb + 1) * 128)],
                                         inc=(bb == 3 and kc == 7))
                            psv = ps[:, :].rearrange("p (b c) -> p b c", b=4)
                            vlo = VO + b0 * 256
                            P.op("act", I("activation", out=Vv[:, b0:b0 + 4, 0, 0:64], in_=psv[:, :, 0:64], func=AF.Copy),
                                 reads=[(psn, 0, 512)], writes=[("BIG", vlo, vlo + 1024)])
                            P.op("dve", I("tensor_copy", out=Vv[:, b0:b0 + 4, 1, 64:128], in_=psv[:, :, 64:128]),
                                 reads=[(psn, 0, 512)], writes=[("BIG", vlo, vlo + 1024)])
                        mo = (g * 4 + c) * 512

                        def stage1(b):
                            r, n = b // nb, b % nb
                            with_prev = (n != 0)
                            ncols = 512 if with_prev else 256
                            ps, psn = ps_mm()
                            mms = [(0, KAO + 128 * b), (128, KBO + 128 * b)]
                            if with_prev:
                                mms += [(256, KAO + 128 * (b - 1)), (384, KBO + 128 * (b - 1))]
                            qo = QO + 128 * b
                            for idx, (co, ko) in enumerate(mms):
                                P.op("pe", I("matmul", ps[:, co:co + 128], lhsT=BIG[:, ko:ko + 128], rhs=BIG[:, qo:qo + 128],
                                             start=True, stop=True),
                                     reads=[("BIG", ko, ko + 128), ("BIG", qo, qo + 128)], writes=[(psn, co, co + 128)],
                                     inc=(idx == len(mms) - 1))
                            k2 = ei[0] % 3
                            ei[0] += 1
                            e_ap, e_rg = sb16(k2, ncols)
                            pt_ap, pt_rg = sb16(3 + k2, ncols)
                            P.op("act", I("activation", out=e_ap, in_=ps[:, 0:ncols], func=AF.Exp, scale=0.125),
                                 reads=[(psn, 0, ncols)], writes=[e_rg])
                            P.op("dve", I("tensor_tensor", out=pt_ap, in0=e_ap, in1=MASK[:, mo:mo + ncols], op=ALU.mult),
                                 reads=[e_rg, ("MASK", mo, mo + ncols)], writes=[pt_rg])
                            return (b, r, n, with_prev, (3 + k2) * 512, pt_rg)

                        def stage2(st1):
                            b, r, n, with_prev, ptb, pt_rg = st1
                            puz, puzn = ps_aux()
                            terms = [(b, 0, 0), (b, 1, 128)]
                            if with_prev:
                                terms += [(b - 1, 0, 256), (b - 1, 1, 384)]
                            for k, (vb, s_, pco) in enumerate(terms):
                                vlo = VO + vb * 256 + s_ * 128
                                P.op("pe", I("matmul", puz[:, 0:128], lhsT=Vv[:, vb, s_, :], rhs=SB16[:, ptb + pco:ptb + pco + 128],
                                             start=(k == 0), stop=(k == len(terms) - 1)),
                                     reads=[("BIG", vlo, vlo + 128), pt_rg], writes=[(puzn, 0, 128)], inc=False)
                            for k, (vb, s_, pco) in enumerate(terms):
                                P.op("pe", I("matmul", puz[:, 128:256], lhsT=ONESAB[:, s_ * 128:(s_ + 1) * 128],
                                             rhs=SB16[:, ptb + pco:ptb + pco + 128],
                                             start=(k == 0), stop=(k == len(terms) - 1)),
                                     reads=[("ONESAB", 0, 256), pt_rg], writes=[(puzn, 128, 256)],
                                     inc=(k == len(terms) - 1))
                            stok = r + Dl * 128 * n
                            e1 = stok + Dl * 127 + 1
                            dst = SCR[:, 0:4096].rearrange("p (z t) -> p z t", z=2)[:, :, stok:e1:Dl]
                            src = puz[:, 0:256].rearrange("p (z q) -> p z q", z=2)
                            rgs = [("SCR", stok, e1), ("SCR", 2048 + stok, 2048 + e1)]
                            if g == 0:
                                P.op("act", I("activation", out=dst, in_=src, func=AF.Copy), reads=[(puzn, 0, 256)], writes=rgs)
                            else:
                                P.op("dve", I("tensor_tensor", out=dst, in0=src, in1=dst, op=ALU.add),
                                     reads=[(puzn, 0, 256)] + rgs, writes=rgs)
                        sts = [stage1(0), stage1(1)]
                        for b in range(2, 16):
                            sts.append(stage1(b))
                            stage2(sts.pop(0))
                        stage2(sts.pop(0))
                        stage2(sts.pop(0))
                    mgo = MO + c * S
                    for q4 in range(4):
                        a0, a1 = 512 * q4, 512 * q4 + 512
                        P.op("act", I("activation", out=SCR[:, 2048 + a0:2048 + a1], in_=SCR[:, 2048 + a0:2048 + a1], func=AF.Ln),
                             reads=[("SCR", 2048 + a0, 2048 + a1)], writes=[("SCR", 2048 + a0, 2048 + a1)])
                    for q4 in range(4):
                        a0, a1 = 512 * q4, 512 * q4 + 512
                        P.op("act", I("activation", out=SCR[:, 2048 + a0:2048 + a1], in_=SCR[:, 2048 + a0:2048 + a1], func=AF.Exp, scale=-1.0),
                             reads=[("SCR", 2048 + a0, 2048 + a1)], writes=[("SCR", 2048 + a0, 2048 + a1)])
                    for q4 in range(4):
                        a0, a1 = 512 * q4, 512 * q4 + 512
                        P.op("dve", I("tensor_tensor", out=BIG[:, mgo + a0:mgo + a1], in0=SCR[:, a0:a1],
                                      in1=SCR[:, 2048 + a0:2048 + a1], op=ALU.mult),
                             reads=[("SCR", a0, a1), ("SCR", 2048 + a0, 2048 + a1)], writes=[("BIG", mgo + a0, mgo + a1)])
                proj_postnorm(l, 1, "wo%d" % l, 4, 4, MO, S, 0, 0, 4, bg=lambda: prenorm_units(l, 2, 0, 2, sqt=(2, 3, 4)))

            def convmix(l, pre_tiles=0):
                prenorm(l, 0, 512 * pre_tiles, 4 - pre_tiles)
                for i in range(8):
                    woff = W.next("cwin", i, 3072)
                    P.op("dve", I("memset", SCR[:, 0:2], 0.0), writes=[("SCR", 0, 2)])
                    for tt in range(4):
                        t0 = 512 * tt
                        pss = []
                        for j in range(3):
                            ps, psn = ps_mm()
                            pss.append((ps, psn))
                            for kc in range(8):
                                lo = woff + kc * 384 + j * 128
                                xo = kc * S + t0
                                P.op("pe", I("matmul", ps[:, :], lhsT=WB[:, lo:lo + 128], rhs=XN[:, xo:xo + 512],
                                             start=(kc == 0), stop=(kc == 7)),
                                     reads=[("WB", lo, lo + 128), ("XN", xo, xo + 512)], writes=[(psn, 0, 512)],
                                     inc=(kc == 7))
                        (pB, pBn), (pC, pCn), (pH, pHn) = pss
                        hb, hb_rg = scr(8)
                        bb_, bb_rg = scr(9)
                        P.op("act", I("activation", out=hb, in_=pH[:, :], func=AF.Copy), reads=[(pHn, 0, 512)], writes=[hb_rg])
                        P.op("act", I("activation", out=bb_, in_=pB[:, :], func=AF.Copy), reads=[(pBn, 0, 512)], writes=[bb_rg])
                        zo = (tt % 2) * ST
                        zno = ((tt + 1) % 2) * ST
                        P.op("dve", I("tensor_tensor", out=SCR[:, zo + 2:zo + 514], in0=pC[:, :], in1=hb, op=ALU.mult),
                             reads=[(pCn, 0, 512), hb_rg], writes=[("SCR", zo + 2, zo + 514)])
                        if tt < 3:
                            P.op("dve", I("tensor_copy", out=SCR[:, zno:zno + 2], in_=SCR[:, zo + 512:zo + 514]),
                                 reads=[("SCR", zo + 512, zo + 514)], writes=[("SCR", zno, zno + 2)])
                        to = (2 + tt % 2) * ST
                        P.op("dve", I("tensor_scalar", out=SCR[:, to:to + 512], in0=SCR[:, zo + 2:zo + 514],
                                      scalar1=CDW[:, i * 3 + 2:i * 3 + 3], scalar2=None, op0=ALU.mult),
                             reads=[("SCR", zo + 2, zo + 514), ("CDW", 0, 24)], writes=[("SCR", to, to + 512)])
                        for k in (1, 0):
                            P.op("dve", I("scalar_tensor_tensor", out=SCR[:, to:to + 512], in0=SCR[:, zo + k:zo + k + 512],
                                          scalar=CDW[:, i * 3 + k:i * 3 + k + 1], in1=SCR[:, to:to + 512],
                                          op0=ALU.mult, op1=ALU.add),
                                 reads=[("SCR", zo + k, zo + k + 512), ("SCR", to, to + 512), ("CDW", 0, 24)],
                                 writes=[("SCR", to, to + 512)])
                        yo = i * S + t0
                        P.op("dve", I("tensor_tensor", out=BIG[:, yo:yo + 512], in0=SCR[:, to:to + 512], in1=bb_, op=ALU.mult),
                             reads=[("SCR", to, to + 512), bb_rg], writes=[("BIG", yo, yo + 512)])
                proj_postnorm(l, 1, "cwout", 8, 2, 0, S, 0, 0, 4, bg=lambda: prenorm_units(l, 2, 0, 2, sqt=(2, 3, 4)))

            def poolmix(l, pre_tiles=0):
                prenorm(l, 0, 512 * pre_tiles, 4 - pre_tiles)
                CS0 = 2096
                PO = 16384
                SBf = SB16.bitcast(F32)
                ZT = SBf[:, 0:512]
                ZT_rg = ("SB16", 0, 1024)
                TM = HALO[:, 0:16]
                TM_rg = ("HALO", 0, 16)
                P.op("dve", I("memset", SCR[:, 2080:2096], 0.0), writes=[("SCR", 2080, 2096)])
                P.op("dve", I("memset", ZT, 0.0), writes=[ZT_rg])
                chunk_i = [0]
                for t in range(16):
                    P.op("dve", I("memset", INVT[:, t:t + 1], 1.0 / (t + 1)), writes=[("INVT", t, t + 1)])
                PSETS = [[(BIG, "BIG", 16384), (BIG, "BIG", 18432)], [(BIG, "BIG", 20480), (SB16, "SB16", 1024)]]

                def inproj(gi):
                    w = 2 ** (gi + 1)
                    woff = W.next("pwin", gi, 2048)
                    for cc in range(2):
                        UB0 = 0 if chunk_i[0] % 2 == 0 else 4160
                        chunk_i[0] += 1
                        for tt in range(4):
                            ps, psn = ps_mm()
                            for kc in range(8):
                                lo = woff + kc * 256 + cc * 128
                                xo = kc * S + 512 * tt
                                P.op("pe", I("matmul", ps[:, :], lhsT=WB[:, lo:lo + 128], rhs=XN[:, xo:xo + 512],
                                             start=(kc == 0), stop=(kc == 7)),
                                     reads=[("WB", lo, lo + 128), ("XN", xo, xo + 512)], writes=[(psn, 0, 512)],
                                     inc=(kc == 7))
                            P.op("act", I("activation", out=SCR[:, UB0 + 512 * tt:UB0 + 512 * tt + 512], in_=ps[:, :], func=AF.Copy),
                                 reads=[(psn, 0, 512)], writes=[("SCR", UB0 + 512 * tt, UB0 + 512 * tt + 512)])
                        for q in range(4):
                            co = CS0 + 512 * q
                            init = 0.0 if q == 0 else SCR[:, co - 1:co]
                            P.op("dve", I("tensor_tensor_scan", out=SCR[:, co:co + 512], data0=SCR[:, UB0 + 512 * q:UB0 + 512 * q + 512],
                                          data1=ZT, initial=init, op0=ALU.add, op1=ALU.add),
                                 reads=[("SCR", UB0 + 512 * q, UB0 + 512 * q + 512), ZT_rg, ("SCR", co - 1, co)],
                                 writes=[("SCR", co, co + 512)])
                        P.op("dve", I("tensor_tensor", out=TM, in0=SCR[:, CS0:CS0 + 16], in1=INVT[:, 0:16], op=ALU.mult),
                             reads=[("SCR", CS0, CS0 + 16), ("INVT", 0, 16)], writes=[TM_rg])
                        P.op("dve", I("tensor_tensor", out=TM, in0=TM, in1=SCR[:, UB0:UB0 + 16], op=ALU.subtract),
                             reads=[TM_rg, ("SCR", UB0, UB0 + 16)], writes=[TM_rg])
                        P.op("dve", I("scalar_tensor_tensor", out=SCR[:, UB0:UB0 + S], in0=SCR[:, CS0:CS0 + S], scalar=1.0 / w,
                                      in1=SCR[:, UB0:UB0 + S], op0=ALU.mult, op1=ALU.subtract),
                             reads=[("SCR", CS0, CS0 + S), ("SCR", UB0, UB0 + S)], writes=[("SCR", UB0, UB0 + S)])
                        pt_, pn_, po = PSETS[gi % 2][cc]
                        P.op("dve", I("scalar_tensor_tensor", out=pt_[:, po:po + S], in0=SCR[:, CS0 - w:CS0 - w + S], scalar=-1.0 / w,
                                      in1=SCR[:, UB0:UB0 + S], op0=ALU.mult, op1=ALU.add),
                             reads=[("SCR", CS0 - w, CS0 - w + S), ("SCR", UB0, UB0 + S)], writes=[(pn_, po, po + S)])
                        P.op("dve", I("tensor_copy", out=pt_[:, po:po + w - 1], in_=HALO[:, 0:w - 1]),
                             reads=[TM_rg], writes=[(pn_, po, po + w - 1)])

                def grp(gi):
                    goff = W.next("pwgrp", gi, 512)
                    for mm in range(2):
                        for tt in range(4):
                            ps, psn = ps_mm()
                            for kc in range(2):
                                lo = goff + kc * 256 + mm * 128
                                pt_, pn_, po = PSETS[gi % 2][kc]
                                io = po + 512 * tt
                                P.op("pe", I("matmul", ps[:, :], lhsT=WB[:, lo:lo + 128], rhs=pt_[:, io:io + 512],
                                             start=(kc == 0), stop=(kc == 1)),
                                     reads=[("WB", lo, lo + 128), (pn_, io, io + 512)], writes=[(psn, 0, 512)],
                                     inc=(kc == 1))
                            ch = 2 * gi + mm
                            yo = ch * S + 512 * tt
                            P.op("act", I("activation", out=BIG[:, yo:yo + 512], in_=ps[:, :], func=AF.Copy, scale=PSC[:, ch:ch + 1]),
                                 reads=[(psn, 0, 512), ("PSC", 0, 8)], writes=[("BIG", yo, yo + 512)])
                inproj(0)
                for gi in range(4):
                    if gi + 1 < 4:
                        inproj(gi + 1)
                    grp(gi)
                proj_postnorm(l, 1, "pwout", 8, 2, 0, S, 0, 0, 4, bg=lambda: prenorm_units(l, 2, 0, 2, sqt=(2, 3, 4)))

            def store_out(h0, h1, strm):
                for c in range(8):
                    P.op("sp", I("dma_start", out=yT[c * 128:(c + 1) * 128, h0:h1], in_=X[:, c * S + h0:c * S + h1]),
                         reads=[("X", c * S + h0, c * S + h1)], stream=strm, amt=16)

            run_layers = EXEC_LAYERS if EXEC_LAYERS is not None else layers
            masks_built = [False]
            for li, l in enumerate(run_layers):
                kind = l % 3
                pre_tiles = 2 if li > 0 else 0
                if kind == 0:
                    attention(l, pre_tiles)
                elif kind == 1:
                    convmix(l, pre_tiles)
                else:
                    poolmix(l, pre_tiles)
                ffn(l, pre_done=True, next_l=(run_layers[li + 1] if li + 1 < len(run_layers) else None))
            store_out(1024, 1536, "ost1")
            store_out(1536, 2048, "ost2")
            P.wait_all("sp", ["ost0", "ost1", "ost2"])

        def t_name(t, nm):
            return {"gvec": "GV", "dwv": "DWV", "cdw": "CDW", "psc": "PSC"}[nm]

        Wd = WMgr(DummyProg(), dram, WB, None)
        construct(DummyProg(), Wd)
        P = Prog()
        P.unordered.update(["xld0", "xld1", "xld2", "cst", "ost0", "ost1", "ost2"])
        W = WMgr(P, dram, WB, Wd.rec)
        construct(P, W)
        assert W.used == len(Wd.rec) and W.issued == len(Wd.rec)
        P.emit(nc, st)
    return nc


def chunkmajor(Wm, cols):
    sub = Wm[:, cols]
    nk = Wm.shape[0] // 128
    return np.ascontiguousarray(sub.reshape(nk, 128, -1).transpose(1, 0, 2).reshape(128, -1))


def host_layout(inp):
    f = lambda a: np.asarray(a, dtype=np.float32)
    out = {}
    ng = f(inp["norm_g"])
    out["gvec"] = np.ascontiguousarray(ng.reshape(16, 8, 128).transpose(2, 0, 1).reshape(128, 128))
    dw = f(inp["ffn_w_dw"])
    out["dwv"] = np.ascontiguousarray(dw.reshape(4, 3, 2, 22, 128).transpose(4, 0, 3, 2, 1).reshape(128, 528))
    cdw = f(inp["conv_w_dw"])[0]
    out["cdw"] = np.ascontiguousarray(cdw.reshape(3, 8, 128).transpose(2, 1, 0).reshape(128, 24))
    out["psc"] = np.ascontiguousarray(f(inp["pool_scale"])[0].reshape(8, 128).T)
    ar = np.arange
    for l in range(4):
        wu = f(inp["ffn_w_up"])[l]
        out["wup%d" % l] = np.stack([chunkmajor(wu, np.concatenate([ar(128 * i, 128 * i + 128), ar(2816 + 128 * i, 2816 + 128 * i + 128)]))
                                     for i in range(22)])
        wd = f(inp["ffn_w_down"])[l]
        out["wdn%d" % l] = np.stack([chunkmajor(wd, ar(128 * m, 128 * m + 128)) for m in range(8)])
    for ia, l in enumerate((0, 3)):
        wq = f(inp["attn_w_qkv"])[ia]
        blks = []
        for g in range(3):
            for c in range(4):
                base = g * 1536 + c * 128
                blks.append(chunkmajor(wq, np.concatenate([ar(base, base + 128), ar(base + 512, base + 640), ar(base + 1024, base + 1152)])))
        out["wqkv%d" % l] = np.stack(blks)
        wo = f(inp["attn_w_o"])[ia]
        out["wo%d" % l] = np.stack([chunkmajor(wo, ar(512 * b, 512 * b + 512)) for b in range(2)])
    cw = f(inp["conv_w_in"])[0]
    out["cwin"] = np.stack([chunkmajor(cw, np.concatenate([ar(128 * i, 128 * i + 128), ar(1024 + 128 * i, 1152 + 128 * i), ar(2048 + 128 * i, 2176 + 128 * i)]))
                            for i in range(8)])
    for nm, key in (("cwout", "conv_w_out"), ("pwin", "pool_w_in"), ("pwout", "pool_w_out")):
        wm = f(inp[key])[0]
        out[nm] = np.stack([chunkmajor(wm, ar(256 * b, 256 * b + 256)) for b in range(4)])
    wg = f(inp["pool_w_grp"])[0]
    out["pwgrp"] = np.stack([chunkmajor(wg[g], ar(0, 256)) for g in range(4)])
    return out


_PROG_CACHE = {}


def _run(layers, xT_list, lay):
    key = tuple(layers)
    if key not in _PROG_CACHE:
        _PROG_CACHE[key] = build_program(list(layers))
    nc = _PROG_CACHE[key]
    names = ["gvec", "dwv", "cdw", "psc"]
    for l in layers:
        names += ["wup%d" % l, "wdn%d" % l]
        kind = l % 3
        if kind == 0:
            names += ["wqkv%d" % l, "wo%d" % l]
        elif kind == 1:
            names += ["cwin", "cwout"]
        else:
            names += ["pwin", "pwgrp", "pwout"]
    in_maps = []
    for b in range(8):
        m = {n: lay[n] for n in names}
        m["xT"] = xT_list[b]
        in_maps.append(m)
    res = run_bass_kernel_spmd(nc, in_maps, core_ids=list(range(8)))
    return [np.asarray(r["yT"]) for r in res.results]


def kernel(**inputs):
    lay = host_layout(inputs)
    x = np.asarray(inputs["x"], dtype=np.float32)
    xT = [np.ascontiguousarray(x[b].T) for b in range(8)]
    if FUSED:
        yT = _run((0, 1, 2, 3), xT, lay)
    else:
        yT = xT
        for l in range(4):
            yT = _run((l,), yT, lay)
    return np.ascontiguousarray(np.stack([y.T for y in yT]).astype(np.float32))
```

```python
import numpy as np
import concourse.bass as bass
import concourse.mybir as mybir
from concourse.bass_utils import run_bass_kernel_spmd
from contextlib import ExitStack

F32 = mybir.dt.float32
BF16 = mybir.dt.bfloat16
AF = mybir.ActivationFunctionType
ALU = mybir.AluOpType

S = 2048
EPS = 1e-6
DIL = (1, 4, 16)
NSLOT = 3
SLOT = 3072
ST = 520
NT = 12
FUSED = True
EXEC_LAYERS = None


class Prog:
    ENG = ("pe", "act", "dve", "pool", "sp")

    def __init__(self):
        self.q = {e: [] for e in self.ENG}
        self.cnt = {}
        self.waited = {e: {} for e in self.ENG}
        self.recs = {}
        self.unordered = set()
        self.pe_pending = False
        self.nops = 0

    def _overlaps(self, name, lo, hi):
        return [r for r in self.recs.get(name, ()) if r[0] < hi and lo < r[1]]

    def op(self, eng, fn, reads=(), writes=(), inc=True, stream=None, amt=1):
        own = stream if stream is not None else eng
        if self.pe_pending and eng != "pe":
            raise RuntimeError("non-PE op constructed inside an open PE group")
        deps = {}

        def add(dep):
            if dep is None:
                return
            s, c = dep
            if s in self.unordered:
                c = self.cnt.get(s, 0)
            if eng == "pe" and s == "pe":
                return
            if deps.get(s, 0) < c:
                deps[s] = c
        for (name, lo, hi) in reads:
            for r in self._overlaps(name, lo, hi):
                add(r[2])
        for (name, lo, hi) in writes:
            for r in self._overlaps(name, lo, hi):
                add(r[2])
                for s, c in r[3].items():
                    add((s, c))
        for s, c in deps.items():
            if self.waited[eng].get(s, 0) < c:
                self.waited[eng][s] = c
                self.q[eng].append(("wait", s, c))
        after = self.cnt.get(own, 0) + amt
        if inc:
            self.cnt[own] = after
            if eng == "pe":
                self.pe_pending = False
        else:
            assert eng == "pe"
            self.pe_pending = True
        self.q[eng].append(("op", fn, own if inc else None, amt))
        self.nops += 1
        for (name, lo, hi) in reads:
            lst = self.recs.setdefault(name, [])
            for r in lst:
                if r[0] == lo and r[1] == hi and r[2] is None:
                    r[3][own] = after
                    break
            else:
                lst.append([lo, hi, None, {own: after}])
        for (name, lo, hi) in writes:
            lst = self.recs.setdefault(name, [])
            lst[:] = [r for r in lst if not (lo <= r[0] and r[1] <= hi)]
            lst.append([lo, hi, (own, after), {}])

    def wait_all(self, eng, streams):
        for s in streams:
            c = self.cnt.get(s, 0)
            if c and self.waited[eng].get(s, 0) < c:
                self.waited[eng][s] = c
                self.q[eng].append(("wait", s, c))

    def emit(self, nc, stack):
        assert not self.pe_pending
        sems = {s: stack.enter_context(nc.semaphore("s_" + s)) for s in self.cnt}
        block = stack.enter_context(nc.Block())

        def replay(e, name):
            for it in self.q[name]:
                if it[0] == "wait":
                    e.wait_ge(sems[it[1]], it[2])
                else:
                    ins = it[1](e)
                    if it[2] is not None:
                        ins.then_inc(sems[it[2]], it[3])

        @block.tensor
        def _(e):
            replay(e, "pe")

        @block.scalar
        def _(e):
            replay(e, "act")

        @block.vector
        def _(e):
            replay(e, "dve")

        @block.gpsimd
        def _(e):
            replay(e, "pool")

        @block.sync
        def _(e):
            replay(e, "sp")


class DummyProg:
    def op(self, *a, **k):
        pass

    def wait_all(self, *a, **k):
        pass


def I(fn, *a, **k):
    return lambda e: getattr(e, fn)(*a, **k)


class WMgr:
    def __init__(self, P, dram, WB, sched):
        self.P, self.dram, self.WB, self.sched = P, dram, WB, sched
        self.rec = []
        self.issued = 0
        self.used = 0

    def next(self, name, blk, nelem):
        if self.sched is None:
            self.rec.append((name, blk, nelem))
            return 0
        i = self.used
        assert self.sched[i] == (name, blk, nelem), (self.sched[i], name, blk, nelem)
        while self.issued < min(i + NSLOT, len(self.sched)):
            self._issue(self.issued)
            self.issued += 1
        self.used += 1
        return (i % NSLOT) * SLOT

    def _issue(self, j):
        name, blk, nelem = self.sched[j]
        slot = j % NSLOT
        off = slot * SLOT
        self.P.op("pool", I("dma_start", out=self.WB[:, off:off + nelem], in_=self.dram[name][blk]),
                  reads=([("X", 0, 512)] if j == 0 else []),
                  writes=[("WB", off, off + nelem)], stream="w%d" % slot, amt=16)


def build_program(layers):
    nc = bass.Bass("TRN2", target_bir_lowering=False)
    dram = {}

    def din(name, shape):
        dram[name] = nc.dram_tensor(name, list(shape), F32, kind="ExternalInput").ap()
    din("xT", (1024, S))
    din("gvec", (128, 128))
    din("dwv", (128, 528))
    din("cdw", (128, 24))
    din("psc", (128, 8))
    for l in layers:
        din("wup%d" % l, (22, 128, 2048))
        din("wdn%d" % l, (8, 128, 2816))
        kind = l % 3
        if kind == 0:
            din("wqkv%d" % l, (12, 128, 3072))
            din("wo%d" % l, (2, 128, 2048))
        elif kind == 1:
            din("cwin", (8, 128, 3072))
            din("cwout", (4, 128, 2048))
        else:
            din("pwin", (4, 128, 2048))
            din("pwgrp", (4, 128, 512))
            din("pwout", (4, 128, 2048))
    yT = nc.dram_tensor("yT", [1024, S], F32, kind="ExternalOutput").ap()

    with ExitStack() as st:
        def sb(name, n, dt):
            return st.enter_context(nc.sbuf_tensor(name, [128, n], dt))
        X = sb("X", 8 * S, F32)
        XN = sb("XN", 8 * S, BF16)
        BIG = sb("BIG", 22528, BF16)
        SCR = sb("SCR", NT * ST, F32)
        SB16 = sb("SB16", 6 * 512, BF16)
        WB = sb("WB", NSLOT * SLOT, BF16)
        MASK = sb("MASK", 12 * 512, BF16)
        GV = sb("GV", 128, F32)
        DWV = sb("DWV", 528, F32)
        CDW = sb("CDW", 24, F32)
        PSC = sb("PSC", 8, F32)
        ONES = sb("ONES", 128, BF16)
        ONESAB = sb("ONESAB", 256, BF16)
        HALO = sb("HALO", 88, F32)
        INVT = sb("INVT", 16, F32)
        PS = [st.enter_context(nc.psum_tensor("PS%d" % b, [128, 512], F32)) for b in range(8)]

        def construct(P, W):
            mmi = [0]
            auxi = [0]
            sqi = [0]
            rsi = [0]
            ei = [0]

            def ps_mm():
                b = mmi[0] % 5
                mmi[0] += 1
                return PS[b], "PS%d" % b

            upi = [0]

            def ps_up():
                b = upi[0] % 8
                upi[0] += 1
                return PS[b], "PS%d" % b

            def ps_aux():
                b = (6, 7, 5)[auxi[0] % 3]
                auxi[0] += 1
                return PS[b], "PS%d" % b
            SS, SSn = PS[5], "PS5"

            def scr(i, lo=0, n=512):
                o = i * ST + lo
                return SCR[:, o:o + n], ("SCR", o, o + n)

            def sb16(i, n=512):
                return SB16[:, i * 512:i * 512 + n], ("SB16", i * 512, i * 512 + n)

            for (t, nm, n) in ((GV, "gvec", 128), (DWV, "dwv", 528), (CDW, "cdw", 24), (PSC, "psc", 8)):
                P.op("sp", I("dma_start", out=t[:, :], in_=dram[nm][:, :]), writes=[(t_name(t, nm), 0, n)], stream="cst", amt=16)
            for (h0, h1, strm) in ((0, 512, "xld0"), (512, 1024, "xld1"), (1024, 2048, "xld2")):
                for c in range(8):
                    P.op("sp",
                         I("dma_start", out=X[:, c * S + h0:c * S + h1], in_=dram["xT"][c * 128:(c + 1) * 128, h0:h1]),
                         writes=[("X", c * S + h0, c * S + h1)], stream=strm, amt=16)
            P.op("dve", I("memset", ONES[:, :], 1.0), writes=[("ONES", 0, 128)])
            P.op("dve", I("memset", ONESAB[:, :], 0.0), writes=[("ONESAB", 0, 256)])
            P.op("dve", I("memset", ONESAB[:, 0:64], 1.0), writes=[("ONESAB", 0, 64)])
            P.op("dve", I("memset", ONESAB[:, 192:256], 1.0), writes=[("ONESAB", 192, 256)])

            XNf = XN.bitcast(F32)
            MASKf = MASK.bitcast(F32)

            def add_sq(src_ap, src_rg, first, last, SS=SS, SSn=SSn, defer=False):
                k = sqi[0] % 2
                sqi[0] += 1
                sq_ap, sq_rg = sb16(k)
                P.op("act", I("activation", out=sq_ap, in_=src_ap, func=AF.Square), reads=[src_rg], writes=[sq_rg])

                def pe_part():
                    P.op("pe", I("matmul", SS[:, :], lhsT=ONES[:, :], rhs=sq_ap, start=first, stop=last),
                         reads=[sq_rg, ("ONES", 0, 128)], writes=[(SSn, 0, 512)])
                if defer:
                    return pe_part
                pe_part()
                return None

            def stats_to_rs(SS=SS, SSn=SSn):
                r = 10 + rsi[0] % 2
                rsi[0] += 1
                rs_ap, rs_rg = scr(r)
                P.op("act", I("activation", out=rs_ap, in_=SS[:, :], func=AF.Ln, scale=1.0 / 1024.0, bias=EPS),
                     reads=[(SSn, 0, 512)], writes=[rs_rg])
                P.op("act", I("activation", out=rs_ap, in_=rs_ap, func=AF.Exp, scale=-0.5), reads=[rs_rg], writes=[rs_rg])
                return rs_ap, rs_rg

            def prenorm_units(l, j, tok0, ntiles, sqt=(0, 1)):
                go = (l * 4 + j) * 8
                units = []
                for tt in range(ntiles):
                    t0 = tok0 + 512 * tt
                    rs_box = []

                    def sq(c, t0=t0):
                        o = c * S + t0
                        sq_ap, sq_rg = sb16(sqt[c % len(sqt)])
                        P.op("act", I("activation", out=sq_ap, in_=X[:, o:o + 512], func=AF.Square),
                             reads=[("X", o, o + 512)], writes=[sq_rg])

                    def stat(c):
                        sq_ap, sq_rg = sb16(sqt[c % len(sqt)])
                        P.op("pe", I("matmul", SS[:, :], lhsT=ONES[:, :], rhs=sq_ap, start=(c == 0), stop=(c == 7)),
                             reads=[sq_rg, ("ONES", 0, 128)], writes=[(SSn, 0, 512)])

                    def rs(rs_box=rs_box):
                        rs_box.append(stats_to_rs())

                    def apply(c0, c1, t0=t0, rs_box=rs_box):
                        rs_ap, rs_rg = rs_box[0]
                        for c in range(c0, c1):
                            o = c * S + t0
                            P.op("dve", I("scalar_tensor_tensor", out=XN[:, o:o + 512], in0=X[:, o:o + 512],
                                          scalar=GV[:, go + c:go + c + 1], in1=rs_ap, op0=ALU.mult, op1=ALU.mult),
                                 reads=[("X", o, o + 512), rs_rg, ("GV", 0, 128)], writes=[("XN", o, o + 512)])
                    nb_ = len(sqt)
                    units.append(lambda sq=sq: sq(0))
                    if nb_ >= 3:
                        units.append(lambda sq=sq: sq(1))
                        for c in range(2, 8):
                            units.append(lambda c=c, sq=sq, stat=stat: (sq(c), stat(c - 2)))
                        units.append(lambda stat=stat, rs=rs: (stat(6), stat(7), rs()))
                    else:
                        for c in range(1, 8):
                            units.append(lambda c=c, sq=sq, stat=stat: (sq(c), stat(c - 1)))
                        units.append(lambda stat=stat, rs=rs: (stat(7), rs()))
                    units.append(lambda apply=apply: apply(0, 4))
                    units.append(lambda apply=apply: apply(4, 8))
                return units

            def prenorm(l, j, tok0, ntiles):
                for u in prenorm_units(l, j, tok0, ntiles):
                    u()

            def proj_postnorm(l, j, wname, nk, mpb, in_off, in_cs, in_t0, tok0, ntiles, bg=None, bg_after=1):
                go = (l * 4 + j) * 8
                ncols = mpb * 128
                woff = 0
                pend = None
                bg_list = []
                SSp = (PS[6], "PS6")

                def hs(tt, m):
                    if tt % 2 == 0:
                        return scr(m)
                    o = m * S + 1024
                    return XNf[:, o // 2:o // 2 + 512], ("XN", o, o + 1024)
                for tt in range(ntiles):
                    t0 = tok0 + 512 * tt
                    it0 = in_t0 + 512 * tt
                    for m in range(8):
                        if m % mpb == 0:
                            woff = W.next(wname, m // mpb, nk * ncols)
                        ps, psn = ps_mm()
                        for kc in range(nk):
                            lo = woff + kc * ncols + (m % mpb) * 128
                            io = in_off + kc * in_cs + it0
                            P.op("pe", I("matmul", ps[:, :], lhsT=WB[:, lo:lo + 128], rhs=BIG[:, io:io + 512],
                                         start=(kc == 0), stop=(kc == nk - 1)),
                                 reads=[("WB", lo, lo + 128), ("BIG", io, io + 512)], writes=[(psn, 0, 512)],
                                 inc=(kc == nk - 1))
                        if pend is not None:
                            pend()
                        hs_ap, hs_rg = hs(tt, m)
                        P.op("act", I("activation", out=hs_ap, in_=ps[:, :], func=AF.Copy),
                             reads=[(psn, 0, 512)], writes=[hs_rg])
                        pend = add_sq(ps[:, :], (psn, 0, 512), m == 0, m == 7, SSp[0], SSp[1], defer=True)
                        if bg_list and tt > bg_after:
                            for _ in range(2):
                                if bg_list:
                                    bg_list.pop(0)()
                    pend()
                    pend = None
                    rs_ap, rs_rg = stats_to_rs(SSp[0], SSp[1])
                    def res_add(m, tt=tt, t0=t0):
                        hs_ap, hs_rg = hs(tt, m)
                        xo = m * S + t0
                        P.op("dve",
                             I("tensor_tensor", out=X[:, xo:xo + 512], in0=X[:, xo:xo + 512], in1=hs_ap, op=ALU.add),
                             reads=[("X", xo, xo + 512), hs_rg], writes=[("X", xo, xo + 512)])
                    for m in range(8):
                        hs_ap, hs_rg = hs(tt, m)
                        P.op("dve", I("scalar_tensor_tensor", out=hs_ap, in0=hs_ap, scalar=GV[:, go + m:go + m + 1],
                                      in1=rs_ap, op0=ALU.mult, op1=ALU.mult),
                             reads=[hs_rg, rs_rg, ("GV", 0, 128)], writes=[hs_rg])
                        if m > 0:
                            res_add(m - 1)
                    res_add(7)
                    if tt == bg_after and bg is not None:
                        bg_list.extend(bg())
                while bg_list:
                    bg_list.pop(0)()

            def ffn(l, pre_done=False, next_l=None):
                for half in range(2):
                    tok0 = 1024 * half
                    if half == 0 and not pre_done:
                        prenorm(l, 2, 0, 2)
                    if half == 0:
                        P.op("dve", I("memset", HALO[:, :], 0.0), writes=[("HALO", 0, 88)])
                    pend = []
                    ptc = [0]
                    for i in range(22):
                        woff = W.next("wup%d" % l, i, 2048)
                        for tt in range(2):
                            t0 = tok0 + 512 * tt
                            pss = []
                            for wh in range(2):
                                ps, psn = ps_up()
                                pss.append((ps, psn))
                                for kc in range(8):
                                    lo = woff + kc * 256 + wh * 128
                                    xo = kc * S + t0
                                    P.op("pe", I("matmul", ps[:, :], lhsT=WB[:, lo:lo + 128], rhs=XN[:, xo:xo + 512],
                                                 start=(kc == 0), stop=(kc == 7)),
                                         reads=[("WB", lo, lo + 128), ("XN", xo, xo + 512)], writes=[(psn, 0, 512)],
                                         inc=(kc == 7))
                            tos = []
                            for wh in range(2):
                                ps, psn = pss[wh]
                                ch = 2 * i + wh
                                dwo = l * 132 + ch * 3
                                zo = (wh * 3 + (ptc[0] % 3)) * ST
                                to = (6 + wh * 3 + (ptc[0] % 3)) * ST
                                tos.append(to)
                                if tt == 0 and wh == 0:
                                    zv = SCR[:, zo:zo + 6 * ST].rearrange("p (w r) -> p w r", w=2)[:, :, 0:2]
                                    hv = HALO[:, ch * 2:ch * 2 + 4].rearrange("p (w r) -> p w r", w=2)
                                    P.op("act", I("activation", out=zv, in_=hv, func=AF.Copy),
                                         reads=[("HALO", ch * 2, ch * 2 + 4)],
                                         writes=[("SCR", zo, zo + 2), ("SCR", zo + 3 * ST, zo + 3 * ST + 2)])
                                P.op("act", I("activation", out=SCR[:, zo + 2:zo + 514], in_=ps[:, :], func=AF.Copy),
                                     reads=[(psn, 0, 512)], writes=[("SCR", zo + 2, zo + 514)])
                                P.op("act", I("activation", out=SCR[:, to:to + 512], in_=ps[:, :], func=AF.Copy,
                                              scale=DWV[:, dwo + 2:dwo + 3]),
                                     reads=[(psn, 0, 512), ("DWV", 0, 528)], writes=[("SCR", to, to + 512)])
                            pend2 = []
                            for f in pend:
                                pend2.append(f())
                            pend = []
                            ch = 2 * i
                            zo = (ptc[0] % 3) * ST
                            zno = ((ptc[0] + 1) % 3) * ST
                            tv = SCR[:, zo:zo + 6 * ST].rearrange("p (w r) -> p w r", w=2)[:, :, 512:514]
                            t_rg = [("SCR", zo + 512, zo + 514), ("SCR", zo + 3 * ST + 512, zo + 3 * ST + 514)]
                            if tt == 0:
                                nv = SCR[:, zno:zno + 6 * ST].rearrange("p (w r) -> p w r", w=2)[:, :, 0:2]
                                P.op("act", I("activation", out=nv, in_=tv, func=AF.Copy), reads=t_rg,
                                     writes=[("SCR", zno, zno + 2), ("SCR", zno + 3 * ST, zno + 3 * ST + 2)])
                            elif half == 0:
                                hv = HALO[:, ch * 2:ch * 2 + 4].rearrange("p (w r) -> p w r", w=2)
                                P.op("act", I("activation", out=hv, in_=tv, func=AF.Copy), reads=t_rg,
                                     writes=[("HALO", ch * 2, ch * 2 + 4)])
                            for k in (1, 0):
                                for wh in range(2):
                                    ch = 2 * i + wh
                                    dwo = l * 132 + ch * 3
                                    zo = (wh * 3 + (ptc[0] % 3)) * ST
                                    to = tos[wh]
                                    P.op("dve", I("scalar_tensor_tensor", out=SCR[:, to:to + 512], in0=SCR[:, zo + k:zo + k + 512],
                                                  scalar=DWV[:, dwo + k:dwo + k + 1], in1=SCR[:, to:to + 512],
                                                  op0=ALU.mult, op1=ALU.add),
                                         reads=[("SCR", zo + k, zo + k + 512), ("SCR", to, to + 512), ("DWV", 0, 528)],
                                         writes=[("SCR", to, to + 512)])

                            for f2 in pend2:
                                f2()

                            def fin(tg=tos[0], tu=tos[1], ho=i * 1024 + 512 * tt):
                                P.op("act", I("activation", out=SCR[:, tg:tg + 512], in_=SCR[:, tg:tg + 512], func=AF.Silu),
                                     reads=[("SCR", tg, tg + 512)], writes=[("SCR", tg, tg + 512)])

                                def gate():
                                    P.op("dve", I("tensor_tensor", out=BIG[:, ho:ho + 512], in0=SCR[:, tg:tg + 512],
                                                  in1=SCR[:, tu:tu + 512], op=ALU.mult),
                                         reads=[("SCR", tg, tg + 512), ("SCR", tu, tu + 512)], writes=[("BIG", ho, ho + 512)])
                                return gate
                            pend.append(fin)
                            ptc[0] += 1
                    for f in pend:
                        f()()
                    if half == 0:
                        ffn_down(l, 0, bg=prenorm_units(l, 2, 1024, 2, sqt=(2, 3, 4)))
                        if next_l is None:
                            store_out(0, 1024, "ost0")
                    elif next_l is not None:
                        ffn_down(l, 1, bg=prenorm_units(next_l, 0, 0, 2, sqt=(2, 3, 4)))
                    else:
                        ffn_down(l, 1)

            def ffn_down(l, half, bg=()):
                tok0 = 1024 * half
                go = (l * 4 + 3) * 8
                SSb = [(PS[6], "PS6"), (PS[7], "PS7")]
                bg = list(bg)

                def hs(tt, m):
                    if tt == 0:
                        return scr(m)
                    o = m * S + 1024 * half
                    return XNf[:, o // 2:o // 2 + 512], ("XN", o, o + 1024)
                pend = None
                for m in range(8):
                    woff = W.next("wdn%d" % l, m, 2816)
                    for tt in range(2):
                        ps, psn = ps_mm()
                        for kc in range(22):
                            lo = woff + kc * 128
                            io = kc * 1024 + 512 * tt
                            P.op("pe", I("matmul", ps[:, :], lhsT=WB[:, lo:lo + 128], rhs=BIG[:, io:io + 512],
                                         start=(kc == 0), stop=(kc == 21)),
                                 reads=[("WB", lo, lo + 128), ("BIG", io, io + 512)], writes=[(psn, 0, 512)],
                                 inc=(kc == 21))
                        if pend is not None:
                            pend()
                        hs_ap, hs_rg = hs(tt, m)
                        P.op("act", I("activation", out=hs_ap, in_=ps[:, :], func=AF.Copy),
                             reads=[(psn, 0, 512)], writes=[hs_rg])
                        pend = add_sq(ps[:, :], (psn, 0, 512), m == 0, m == 7, SSb[tt][0], SSb[tt][1], defer=True)
                        for _ in range(2):
                            if bg:
                                bg.pop(0)()
                pend()
                while bg:
                    bg.pop(0)()
                for tt in (0, 1):
                    t0 = tok0 + 512 * tt
                    rs_ap, rs_rg = stats_to_rs(SSb[tt][0], SSb[tt][1])
                    def res_add(m, tt=tt, t0=t0):
                        hs_ap, hs_rg = hs(tt, m)
                        xo = m * S + t0
                        P.op("dve", I("tensor_tensor", out=X[:, xo:xo + 512], in0=X[:, xo:xo + 512], in1=hs_ap, op=ALU.add),
                             reads=[("X", xo, xo + 512), hs_rg], writes=[("X", xo, xo + 512)])
                    for m in range(8):
                        hs_ap, hs_rg = hs(tt, m)
                        P.op("dve", I("scalar_tensor_tensor", out=hs_ap, in0=hs_ap, scalar=GV[:, go + m:go + m + 1],
                                      in1=rs_ap, op0=ALU.mult, op1=ALU.mult),
                             reads=[hs_rg, rs_rg, ("GV", 0, 128)], writes=[hs_rg])
                        if m > 0:
                            res_add(m - 1)
                    res_add(7)

            def mask_setup():
                io_ap, io_rg = scr(9, 0, 128)
                P.op("pool", I("iota", io_ap, [[1, 128]], base=0, channel_multiplier=-1,
                               allow_small_or_imprecise_dtypes=True), writes=[io_rg])
                vc, vc_rg = scr(8, 0, 128)
                vp, vp_rg = scr(8, 128, 128)
                dc, dc_rg = scr(8, 256, 128)
                dp, dp_rg = scr(8, 384, 128)
                P.op("dve", I("tensor_single_scalar", out=vc, in_=io_ap, scalar=0.0, op=ALU.is_ge), reads=[io_rg], writes=[vc_rg])
                P.op("dve", I("tensor_single_scalar", out=vp, in_=io_ap, scalar=0.0, op=ALU.is_le), reads=[io_rg], writes=[vp_rg])
                P.op("dve", I("tensor_single_scalar", out=dc, in_=io_ap, scalar=0.0, op=ALU.max), reads=[io_rg], writes=[dc_rg])
                P.op("dve", I("tensor_scalar", out=dp, in0=io_ap, scalar1=0.0, scalar2=128.0, op0=ALU.min, op1=ALU.add),
                     reads=[io_rg], writes=[dp_rg])

            mk = [0]

            def build_mask(g, c):
                vc, vc_rg = scr(8, 0, 128)
                vp, vp_rg = scr(8, 128, 128)
                dc, dc_rg = scr(8, 256, 128)
                dp, dp_rg = scr(8, 384, 128)
                for s_ in range(2):
                    h = 2 * c + s_
                    sd = float(2.0 ** (-8.0 * (g * 8 + h + 1) / 24.0)) * DIL[g]
                    mo = (g * 4 + c) * 512
                    for (d_ap, d_rg, v_ap, v_rg, co) in ((dc, dc_rg, vc, vc_rg, s_ * 128), (dp, dp_rg, vp, vp_rg, 256 + s_ * 128)):
                        t_ap, t_rg = scr(9, 128 + 128 * (mk[0] % 2), 128)
                        mk[0] += 1
                        P.op("act", I("activation", out=t_ap, in_=d_ap, func=AF.Exp, scale=-sd), reads=[d_rg], writes=[t_rg])
                        P.op("dve", I("tensor_tensor", out=MASK[:, mo + co:mo + co + 128], in0=t_ap, in1=v_ap, op=ALU.mult),
                             reads=[t_rg, v_rg], writes=[("MASK", mo + co, mo + co + 128)])

            QO, KAO, KBO, VO, MO = 0, 2048, 4096, 6144, 10240

            def attention(l, pre_tiles=0):
                prenorm(l, 0, 512 * pre_tiles, 4 - pre_tiles)
                first_attn = not masks_built[0]
                if first_attn:
                    mask_setup()
                    masks_built[0] = True
                P.op("dve", I("memset", BIG[:, KAO:MO], 0.0), writes=[("BIG", KAO, MO)])
                Vv = BIG[:, VO:MO].rearrange("p (b s c) -> p b s c", b=16, s=2)
                for c in range(4):
                    for g in range(3):
                        Dl = DIL[g]
                        nb = (S // Dl) // 128
                        woff = W.next("wqkv%d" % l, g * 4 + c, 3072)
                        if first_attn:
                            build_mask(g, c)
                        for wh in range(2):
                            for tt in range(4):
                                ps, psn = ps_mm()
                                for kc in range(8):
                                    lo = woff + kc * 384 + wh * 128
                                    xo = kc * S + 512 * tt
                                    P.op("pe", I("matmul", ps[:, :], lhsT=WB[:, lo:lo + 128], rhs=XN[:, xo:xo + 512],
                                                 start=(kc == 0), stop=(kc == 7)),
                                         reads=[("WB", lo, lo + 128), ("XN", xo, xo + 512)], writes=[(psn, 0, 512)],
                                         inc=(kc == 7))
                                nu = 512 // Dl
                                u0 = tt * nu

                                def views(dst_rows, base):
                                    if Dl == 1:
                                        return (BIG[dst_rows, base + 512 * tt:base + 512 * tt + 512], ps[dst_rows, :])
                                    ov = BIG[dst_rows, base:base + S].rearrange("p (r u) -> p u r", r=Dl)[:, u0:u0 + nu, :]
                                    iv = ps[dst_rows, :].rearrange("p (u r) -> p u r", r=Dl)
                                    return ov, iv
                                if wh == 0:
                                    ov, iv = views(slice(0, 128), QO)
                                    P.op("act", I("activation", out=ov, in_=iv, func=AF.Copy),
                                         reads=[(psn, 0, 512)], writes=[("BIG", QO, QO + S)])
                                else:
                                    ov, iv = views(slice(0, 64), KAO)
                                    P.op("act", I("activation", out=ov, in_=iv, func=AF.Copy), reads=[(psn, 0, 512)], writes=[("BIG", KAO, KAO + S)])
                                    ov, iv = views(slice(64, 128), KBO)
                                    P.op("dve", I("tensor_copy", out=ov, in_=iv), reads=[(psn, 0, 512)], writes=[("BIG", KBO, KBO + S)])
                        for b0 in range(0, 16, 4):
                            ps, psn = ps_mm()
                            for bb in range(4):
                                b = b0 + bb
                                r, n = b // nb, b % nb
                                stok = r + Dl * 128 * n
                                for kc in range(8):
                                    a0 = kc * S + stok
                                    a1 = a0 + Dl * 127 + 1
                                    lo = woff + kc * 384 + 256
                                    P.op("pe", I("matmul", ps[:, bb * 128:(bb + 1) * 128], lhsT=XN[:, a0:a1:Dl], rhs=WB[:, lo:lo + 128],
                                                 start=(kc == 0), stop=(kc == 7)),
                                         reads=[("XN", a0, a1), ("WB", lo, lo + 128)], writes=[(psn, bb * 128, (bb + 1) * 128)],
                                         inc=(bb == 3 and kc == 7))
                            psv = ps[:, :].rearrange("p (b c) -> p b c", b=4)
                            vlo = VO + b0 * 256
                            P.op("act", I("activation", out=Vv[:, b0:b0 + 4, 0, 0:64], in_=psv[:, :, 0:64], func=AF.Copy),
                                 reads=[(psn, 0, 512)], writes=[("BIG", vlo, vlo + 1024)])
                            P.op("dve", I("tensor_copy", out=Vv[:, b0:b0 + 4, 1, 64:128], in_=psv[:, :, 64:128]),
                                 reads=[(psn, 0, 512)], writes=[("BIG", vlo, vlo + 1024)])
                        mo = (g * 4 + c) * 512

                        def stage1(b):
                            r, n = b // nb, b % nb
                            with_prev = (n != 0)
                            ncols = 512 if with_prev else 256
                            ps, psn = ps_mm()
                            mms = [(0, KAO + 128 * b), (128, KBO + 128 * b)]
                            if with_prev:
                                mms += [(256, KAO + 128 * (b - 1)), (384, KBO + 128 * (b - 1))]
                            qo = QO + 128 * b
                            for idx, (co, ko) in enumerate(mms):
                                P.op("pe", I("matmul", ps[:, co:co + 128], lhsT=BIG[:, ko:ko + 128], rhs=BIG[:, qo:qo + 128],
                                             start=True, stop=True),
                                     reads=[("BIG", ko, ko + 128), ("BIG", qo, qo + 128)], writes=[(psn, co, co + 128)],
                                     inc=(idx == len(mms) - 1))
                            k2 = ei[0] % 3
                            ei[0] += 1
                            e_ap, e_rg = sb16(k2, ncols)
                            pt_ap, pt_rg = sb16(3 + k2, ncols)
                            P.op("act", I("activation", out=e_ap, in_=ps[:, 0:ncols], func=AF.Exp, scale=0.125),
                                 reads=[(psn, 0, ncols)], writes=[e_rg])
                            P.op("dve", I("tensor_tensor", out=pt_ap, in0=e_ap, in1=MASK[:, mo:mo + ncols], op=ALU.mult),
                                 reads=[e_rg, ("MASK", mo, mo + ncols)], writes=[pt_rg])
                            return (b, r, n, with_prev, (3 + k2) * 512, pt_rg)

                        def stage2(st1):
                            b, r, n, with_prev, ptb, pt_rg = st1
                            puz, puzn = ps_aux()
                            terms = [(b, 0, 0), (b, 1, 128)]
                            if with_prev:
                                terms += [(b - 1, 0, 256), (b - 1, 1, 384)]
                            for k, (vb, s_, pco) in enumerate(terms):
                                vlo = VO + vb * 256 + s_ * 128
                                P.op("pe", I("matmul", puz[:, 0:128], lhsT=Vv[:, vb, s_, :], rhs=SB16[:, ptb + pco:ptb + pco + 128],
                                             start=(k == 0), stop=(k == len(terms) - 1)),
                                     reads=[("BIG", vlo, vlo + 128), pt_rg], writes=[(puzn, 0, 128)], inc=False)
                            for k, (vb, s_, pco) in enumerate(terms):
                                P.op("pe", I("matmul", puz[:, 128:256], lhsT=ONESAB[:, s_ * 128:(s_ + 1) * 128],
                                             rhs=SB16[:, ptb + pco:ptb + pco + 128],
                                             start=(k == 0), stop=(k == len(terms) - 1)),
                                     reads=[("ONESAB", 0, 256), pt_rg], writes=[(puzn, 128, 256)],
                                     inc=(k == len(terms) - 1))
                            stok = r + Dl * 128 * n
                            e1 = stok + Dl * 127 + 1
                            dst = SCR[:, 0:4096].rearrange("p (z t) -> p z t", z=2)[:, :, stok:e1:Dl]
                            src = puz[:, 0:256].rearrange("p (z q) -> p z q", z=2)
                            rgs = [("SCR", stok, e1), ("SCR", 2048 + stok, 2048 + e1)]
                            if g == 0:
                                P.op("act", I("activation", out=dst, in_=src, func=AF.Copy), reads=[(puzn, 0, 256)], writes=rgs)
                            else:
                                P.op("dve", I("tensor_tensor", out=dst, in0=src, in1=dst, op=ALU.add),
                                     reads=[(puzn, 0, 256)] + rgs, writes=rgs)
                        sts = [stage1(0), stage1(1)]
                        for b in range(2, 16):
                            sts.append(stage1(b))
                            stage2(sts.pop(0))
                        stage2(sts.pop(0))
                        stage2(sts.pop(0))
                    mgo = MO + c * S
                    for q4 in range(4):
                        a0, a1 = 512 * q4, 512 * q4 + 512
                        P.op("act", I("activation", out=SCR[:, 2048 + a0:2048 + a1], in_=SCR[:, 2048 + a0:2048 + a1], func=AF.Ln),
                             reads=[("SCR", 2048 + a0, 2048 + a1)], writes=[("SCR", 2048 + a0, 2048 + a1)])
                    for q4 in range(4):
                        a0, a1 = 512 * q4, 512 * q4 + 512
                        P.op("act", I("activation", out=SCR[:, 2048 + a0:2048 + a1], in_=SCR[:, 2048 + a0:2048 + a1], func=AF.Exp, scale=-1.0),
                             reads=[("SCR", 2048 + a0, 2048 + a1)], writes=[("SCR", 2048 + a0, 2048 + a1)])
                    for q4 in range(4):
                        a0, a1 = 512 * q4, 512 * q4 + 512
                        P.op("dve", I("tensor_tensor", out=BIG[:, mgo + a0:mgo + a1], in0=SCR[:, a0:a1],
                                      in1=SCR[:, 2048 + a0:2048 + a1], op=ALU.mult),
                             reads=[("SCR", a0, a1), ("SCR", 2048 + a0, 2048 + a1)], writes=[("BIG", mgo + a0, mgo + a1)])
                proj_postnorm(l, 1, "wo%d" % l, 4, 4, MO, S, 0, 0, 4, bg=lambda: prenorm_units(l, 2, 0, 2, sqt=(2, 3, 4)))

            def convmix(l, pre_tiles=0):
                prenorm(l, 0, 512 * pre_tiles, 4 - pre_tiles)
                for i in range(8):
                    woff = W.next("cwin", i, 3072)
                    P.op("dve", I("memset", SCR[:, 0:2], 0.0), writes=[("SCR", 0, 2)])
                    for tt in range(4):
                        t0 = 512 * tt
                        pss = []
                        for j in range(3):
                            ps, psn = ps_mm()
                            pss.append((ps, psn))
                            for kc in range(8):
                                lo = woff + kc * 384 + j * 128
                                xo = kc * S + t0
                                P.op("pe", I("matmul", ps[:, :], lhsT=WB[:, lo:lo + 128], rhs=XN[:, xo:xo + 512],
                                             start=(kc == 0), stop=(kc == 7)),
                                     reads=[("WB", lo, lo + 128), ("XN", xo, xo + 512)], writes=[(psn, 0, 512)],
                                     inc=(kc == 7))
                        (pB, pBn), (pC, pCn), (pH, pHn) = pss
                        hb, hb_rg = scr(8)
                        bb_, bb_rg = scr(9)
                        P.op("act", I("activation", out=hb, in_=pH[:, :], func=AF.Copy), reads=[(pHn, 0, 512)], writes=[hb_rg])
                        P.op("act", I("activation", out=bb_, in_=pB[:, :], func=AF.Copy), reads=[(pBn, 0, 512)], writes=[bb_rg])
                        zo = (tt % 2) * ST
                        zno = ((tt + 1) % 2) * ST
                        P.op("dve", I("tensor_tensor", out=SCR[:, zo + 2:zo + 514], in0=pC[:, :], in1=hb, op=ALU.mult),
                             reads=[(pCn, 0, 512), hb_rg], writes=[("SCR", zo + 2, zo + 514)])
                        if tt < 3:
                            P.op("dve", I("tensor_copy", out=SCR[:, zno:zno + 2], in_=SCR[:, zo + 512:zo + 514]),
                                 reads=[("SCR", zo + 512, zo + 514)], writes=[("SCR", zno, zno + 2)])
                        to = (2 + tt % 2) * ST
                        P.op("dve", I("tensor_scalar", out=SCR[:, to:to + 512], in0=SCR[:, zo + 2:zo + 514],
                                      scalar1=CDW[:, i * 3 + 2:i * 3 + 3], scalar2=None, op0=ALU.mult),
                             reads=[("SCR", zo + 2, zo + 514), ("CDW", 0, 24)], writes=[("SCR", to, to + 512)])
                        for k in (1, 0):
                            P.op("dve", I("scalar_tensor_tensor", out=SCR[:, to:to + 512], in0=SCR[:, zo + k:zo + k + 512],
                                          scalar=CDW[:, i * 3 + k:i * 3 + k + 1], in1=SCR[:, to:to + 512],
                                          op0=ALU.mult, op1=ALU.add),
                                 reads=[("SCR", zo + k, zo + k + 512), ("SCR", to, to + 512), ("CDW", 0, 24)],
                                 writes=[("SCR", to, to + 512)])
                        yo = i * S + t0
                        P.op("dve", I("tensor_tensor", out=BIG[:, yo:yo + 512], in0=SCR[:, to:to + 512], in1=bb_, op=ALU.mult),
                             reads=[("SCR", to, to + 512), bb_rg], writes=[("BIG", yo, yo + 512)])
                proj_postnorm(l, 1, "cwout", 8, 2, 0, S, 0, 0, 4, bg=lambda: prenorm_units(l, 2, 0, 2, sqt=(2, 3, 4)))

            def poolmix(l, pre_tiles=0):
                prenorm(l, 0, 512 * pre_tiles, 4 - pre_tiles)
                CS0 = 2096
                PO = 16384
                SBf = SB16.bitcast(F32)
                ZT = SBf[:, 0:512]
                ZT_rg = ("SB16", 0, 1024)
                TM = HALO[:, 0:16]
                TM_rg = ("HALO", 0, 16)
                P.op("dve", I("memset", SCR[:, 2080:2096], 0.0), writes=[("SCR", 2080, 2096)])
                P.op("dve", I("memset", ZT, 0.0), writes=[ZT_rg])
                chunk_i = [0]
                for t in range(16):
                    P.op("dve", I("memset", INVT[:, t:t + 1], 1.0 / (t + 1)), writes=[("INVT", t, t + 1)])
                PSETS = [[(BIG, "BIG", 16384), (BIG, "BIG", 18432)], [(BIG, "BIG", 20480), (SB16, "SB16", 1024)]]

                def inproj(gi):
                    w = 2 ** (gi + 1)
                    woff = W.next("pwin", gi, 2048)
                    for cc in range(2):
                        UB0 = 0 if chunk_i[0] % 2 == 0 else 4160
                        chunk_i[0] += 1
                        for tt in range(4):
                            ps, psn = ps_mm()
                            for kc in range(8):
                                lo = woff + kc * 256 + cc * 128
                                xo = kc * S + 512 * tt
                                P.op("pe", I("matmul", ps[:, :], lhsT=WB[:, lo:lo + 128], rhs=XN[:, xo:xo + 512],
                                             start=(kc == 0), stop=(kc == 7)),
                                     reads=[("WB", lo, lo + 128), ("XN", xo, xo + 512)], writes=[(psn, 0, 512)],
                                     inc=(kc == 7))
                            P.op("act", I("activation", out=SCR[:, UB0 + 512 * tt:UB0 + 512 * tt + 512], in_=ps[:, :], func=AF.Copy),
                                 reads=[(psn, 0, 512)], writes=[("SCR", UB0 + 512 * tt, UB0 + 512 * tt + 512)])
                        def scan(q):
                            co = CS0 + 512 * q
                            init = 0.0 if q == 0 else SCR[:, co - 1:co]
                            P.op("dve", I("tensor_tensor_scan", out=SCR[:, co:co + 512], data0=SCR[:, UB0 + 512 * q:UB0 + 512 * q + 512],
                                          data1=ZT, initial=init, op0=ALU.add, op1=ALU.add),
                                 reads=[("SCR", UB0 + 512 * q, UB0 + 512 * q + 512), ZT_rg, ("SCR", co - 1, co)],
                                 writes=[("SCR", co, co + 512)])
                        scan(0)
                        scan(1)
                        P.op("dve", I("tensor_tensor", out=TM, in0=SCR[:, CS0:CS0 + 16], in1=INVT[:, 0:16], op=ALU.mult),
                             reads=[("SCR", CS0, CS0 + 16), ("INVT", 0, 16)], writes=[TM_rg])
                        scan(2)
                        P.op("dve", I("tensor_tensor", out=TM, in0=TM, in1=SCR[:, UB0:UB0 + 16], op=ALU.subtract),
                             reads=[TM_rg, ("SCR", UB0, UB0 + 16)], writes=[TM_rg])
                        scan(3)
                        P.op("dve", I("scalar_tensor_tensor", out=SCR[:, UB0:UB0 + S], in0=SCR[:, CS0:CS0 + S], scalar=1.0 / w,
                                      in1=SCR[:, UB0:UB0 + S], op0=ALU.mult, op1=ALU.subtract),
                             reads=[("SCR", CS0, CS0 + S), ("SCR", UB0, UB0 + S)], writes=[("SCR", UB0, UB0 + S)])
                        pt_, pn_, po = PSETS[gi % 2][cc]
                        P.op("dve", I("scalar_tensor_tensor", out=pt_[:, po:po + S], in0=SCR[:, CS0 - w:CS0 - w + S], scalar=-1.0 / w,
                                      in1=SCR[:, UB0:UB0 + S], op0=ALU.mult, op1=ALU.add),
                             reads=[("SCR", CS0 - w, CS0 - w + S), ("SCR", UB0, UB0 + S)], writes=[(pn_, po, po + S)])
                        P.op("dve", I("tensor_copy", out=pt_[:, po:po + w - 1], in_=HALO[:, 0:w - 1]),
                             reads=[TM_rg], writes=[(pn_, po, po + w - 1)])

                def grp(gi):
                    goff = W.next("pwgrp", gi, 512)
                    for mm in range(2):
                        for tt in range(4):
                            ps, psn = ps_mm()
                            for kc in range(2):
                                lo = goff + kc * 256 + mm * 128
                                pt_, pn_, po = PSETS[gi % 2][kc]
                                io = po + 512 * tt
                                P.op("pe", I("matmul", ps[:, :], lhsT=WB[:, lo:lo + 128], rhs=pt_[:, io:io + 512],
                                             start=(kc == 0), stop=(kc == 1)),
                                     reads=[("WB", lo, lo + 128), (pn_, io, io + 512)], writes=[(psn, 0, 512)],
                                     inc=(kc == 1))
                            ch = 2 * gi + mm
                            yo = ch * S + 512 * tt
                            P.op("act", I("activation", out=BIG[:, yo:yo + 512], in_=ps[:, :], func=AF.Copy, scale=PSC[:, ch:ch + 1]),
                                 reads=[(psn, 0, 512), ("PSC", 0, 8)], writes=[("BIG", yo, yo + 512)])
                inproj(0)
                for gi in range(4):
                    if gi + 1 < 4:
                        inproj(gi + 1)
                    grp(gi)
                proj_postnorm(l, 1, "pwout", 8, 2, 0, S, 0, 0, 4, bg=lambda: prenorm_units(l, 2, 0, 2, sqt=(2, 3, 4)))

            def store_out(h0, h1, strm):
                for c in range(8):
                    P.op("sp", I("dma_start", out=yT[c * 128:(c + 1) * 128, h0:h1], in_=X[:, c * S + h0:c * S + h1]),
                         reads=[("X", c * S + h0, c * S + h1)], stream=strm, amt=16)

            run_layers = EXEC_LAYERS if EXEC_LAYERS is not None else layers
            masks_built = [False]
            for li, l in enumerate(run_layers):
                kind = l % 3
                pre_tiles = 2 if li > 0 else 0
                if kind == 0:
                    attention(l, pre_tiles)
                elif kind == 1:
                    convmix(l, pre_tiles)
                else:
                    poolmix(l, pre_tiles)
                ffn(l, pre_done=True, next_l=(run_layers[li + 1] if li + 1 < len(run_layers) else None))
            store_out(1024, 1536, "ost1")
            store_out(1536, 2048, "ost2")
            P.wait_all("sp", ["ost0", "ost1", "ost2"])

        def t_name(t, nm):
            return {"gvec": "GV", "dwv": "DWV", "cdw": "CDW", "psc": "PSC"}[nm]

        Wd = WMgr(DummyProg(), dram, WB, None)
        construct(DummyProg(), Wd)
        P = Prog()
        P.unordered.update(["xld0", "xld1", "xld2", "cst", "ost0", "ost1", "ost2"])
        W = WMgr(P, dram, WB, Wd.rec)
        construct(P, W)
        assert W.used == len(Wd.rec) and W.issued == len(Wd.rec)
        P.emit(nc, st)
    return nc


def chunkmajor(Wm, cols):
    sub = Wm[:, cols]
    nk = Wm.shape[0] // 128
    return np.ascontiguousarray(sub.reshape(nk, 128, -1).transpose(1, 0, 2).reshape(128, -1))


def host_layout(inp):
    f = lambda a: np.asarray(a, dtype=np.float32)
    out = {}
    ng = f(inp["norm_g"])
    out["gvec"] = np.ascontiguousarray(ng.reshape(16, 8, 128).transpose(2, 0, 1).reshape(128, 128))
    dw = f(inp["ffn_w_dw"])
    out["dwv"] = np.ascontiguousarray(dw.reshape(4, 3, 2, 22, 128).transpose(4, 0, 3, 2, 1).reshape(128, 528))
    cdw = f(inp["conv_w_dw"])[0]
    out["cdw"] = np.ascontiguousarray(cdw.reshape(3, 8, 128).transpose(2, 1, 0).reshape(128, 24))
    out["psc"] = np.ascontiguousarray(f(inp["pool_scale"])[0].reshape(8, 128).T)
    ar = np.arange
    for l in range(4):
        wu = f(inp["ffn_w_up"])[l]
        out["wup%d" % l] = np.stack([chunkmajor(wu, np.concatenate([ar(128 * i, 128 * i + 128), ar(2816 + 128 * i, 2816 + 128 * i + 128)]))
                                     for i in range(22)])
        wd = f(inp["ffn_w_down"])[l]
        out["wdn%d" % l] = np.stack([chunkmajor(wd, ar(128 * m, 128 * m + 128)) for m in range(8)])
    for ia, l in enumerate((0, 3)):
        wq = f(inp["attn_w_qkv"])[ia]
        blks = []
        for g in range(3):
            for c in range(4):
                base = g * 1536 + c * 128
                blks.append(chunkmajor(wq, np.concatenate([ar(base, base + 128), ar(base + 512, base + 640), ar(base + 1024, base + 1152)])))
        out["wqkv%d" % l] = np.stack(blks)
        wo = f(inp["attn_w_o"])[ia]
        out["wo%d" % l] = np.stack([chunkmajor(wo, ar(512 * b, 512 * b + 512)) for b in range(2)])
    cw = f(inp["conv_w_in"])[0]
    out["cwin"] = np.stack([chunkmajor(cw, np.concatenate([ar(128 * i, 128 * i + 128), ar(1024 + 128 * i, 1152 + 128 * i), ar(2048 + 128 * i, 2176 + 128 * i)]))
                            for i in range(8)])
    for nm, key in (("cwout", "conv_w_out"), ("pwin", "pool_w_in"), ("pwout", "pool_w_out")):
        wm = f(inp[key])[0]
        out[nm] = np.stack([chunkmajor(wm, ar(256 * b, 256 * b + 256)) for b in range(4)])
    wg = f(inp["pool_w_grp"])[0]
    out["pwgrp"] = np.stack([chunkmajor(wg[g], ar(0, 256)) for g in range(4)])
    return out


_PROG_CACHE = {}


def _run(layers, xT_list, lay):
    key = tuple(layers)
    if key not in _PROG_CACHE:
        _PROG_CACHE[key] = build_program(list(layers))
    nc = _PROG_CACHE[key]
    names = ["gvec", "dwv", "cdw", "psc"]
    for l in layers:
        names += ["wup%d" % l, "wdn%d" % l]
        kind = l % 3
        if kind == 0:
            names += ["wqkv%d" % l, "wo%d" % l]
        elif kind == 1:
            names += ["cwin", "cwout"]
        else:
            names += ["pwin", "pwgrp", "pwout"]
    in_maps = []
    for b in range(8):
        m = {n: lay[n] for n in names}
        m["xT"] = xT_list[b]
        in_maps.append(m)
    res = run_bass_kernel_spmd(nc, in_maps, core_ids=list(range(8)))
    return [np.asarray(r["yT"]) for r in res.results]


def kernel(**inputs):
    lay = host_layout(inputs)
    x = np.asarray(inputs["x"], dtype=np.float32)
    xT = [np.ascontiguousarray(x[b].T) for b in range(8)]
    if FUSED:
        yT = _run((0, 1, 2, 3), xT, lay)
    else:
        yT = xT
        for l in range(4):
            yT = _run((l,), yT, lay)
    return np.ascontiguousarray(np.stack([y.T for y in yT]).astype(np.float32))
```

```python
import numpy as np
import concourse.bass as bass
import concourse.mybir as mybir
from concourse.bass_utils import run_bass_kernel_spmd
from contextlib import ExitStack

F32 = mybir.dt.float32
BF16 = mybir.dt.bfloat16
AF = mybir.ActivationFunctionType
ALU = mybir.AluOpType

S = 2048
EPS = 1e-6
DIL = (1, 4, 16)
NSLOT = 3
SLOT = 3072
ST = 520
NT = 12
FUSED = True
EXEC_LAYERS = None


class Prog:
    ENG = ("pe", "act", "dve", "pool", "sp")

    def __init__(self):
        self.q = {e: [] for e in self.ENG}
        self.cnt = {}
        self.waited = {e: {} for e in self.ENG}
        self.recs = {}
        self.unordered = set()
        self.pe_pending = False
        self.nops = 0

    def _overlaps(self, name, lo, hi):
        return [r for r in self.recs.get(name, ()) if r[0] < hi and lo < r[1]]

    def op(self, eng, fn, reads=(), writes=(), inc=True, stream=None, amt=1):
        own = stream if stream is not None else eng
        if self.pe_pending and eng != "pe":
            raise RuntimeError("non-PE op constructed inside an open PE group")
        deps = {}

        def add(dep):
            if dep is None:
                return
            s, c = dep
            if s in self.unordered:
                c = self.cnt.get(s, 0)
            if eng == "pe" and s == "pe":
                return
            if deps.get(s, 0) < c:
                deps[s] = c
        for (name, lo, hi) in reads:
            for r in self._overlaps(name, lo, hi):
                add(r[2])
        for (name, lo, hi) in writes:
            for r in self._overlaps(name, lo, hi):
                add(r[2])
                for s, c in r[3].items():
                    add((s, c))
        for s, c in deps.items():
            if self.waited[eng].get(s, 0) < c:
                self.waited[eng][s] = c
                self.q[eng].append(("wait", s, c))
        after = self.cnt.get(own, 0) + amt
        if inc:
            self.cnt[own] = after
            if eng == "pe":
                self.pe_pending = False
        else:
            assert eng == "pe"
            self.pe_pending = True
        self.q[eng].append(("op", fn, own if inc else None, amt))
        self.nops += 1
        for (name, lo, hi) in reads:
            lst = self.recs.setdefault(name, [])
            for r in lst:
                if r[0] == lo and r[1] == hi and r[2] is None:
                    r[3][own] = after
                    break
            else:
                lst.append([lo, hi, None, {own: after}])
        for (name, lo, hi) in writes:
            lst = self.recs.setdefault(name, [])
            lst[:] = [r for r in lst if not (lo <= r[0] and r[1] <= hi)]
            lst.append([lo, hi, (own, after), {}])

    def wait_all(self, eng, streams):
        for s in streams:
            c = self.cnt.get(s, 0)
            if c and self.waited[eng].get(s, 0) < c:
                self.waited[eng][s] = c
                self.q[eng].append(("wait", s, c))

    def emit(self, nc, stack):
        assert not self.pe_pending
        sems = {s: stack.enter_context(nc.semaphore("s_" + s)) for s in self.cnt}
        block = stack.enter_context(nc.Block())

        def replay(e, name):
            for it in self.q[name]:
                if it[0] == "wait":
                    e.wait_ge(sems[it[1]], it[2])
                else:
                    ins = it[1](e)
                    if it[2] is not None:
                        ins.then_inc(sems[it[2]], it[3])

        @block.tensor
        def _(e):
            replay(e, "pe")

        @block.scalar
        def _(e):
            replay(e, "act")

        @block.vector
        def _(e):
            replay(e, "dve")

        @block.gpsimd
        def _(e):
            replay(e, "pool")

        @block.sync
        def _(e):
            replay(e, "sp")


class DummyProg:
    def op(self, *a, **k):
        pass

    def wait_all(self, *a, **k):
        pass


def I(fn, *a, **k):
    return lambda e: getattr(e, fn)(*a, **k)


class WMgr:
    def __init__(self, P, dram, WB, sched):
        self.P, self.dram, self.WB, self.sched = P, dram, WB, sched
        self.rec = []
        self.issued = 0
        self.used = 0

    def next(self, name, blk, nelem):
        if self.sched is None:
            self.rec.append((name, blk, nelem))
            return 0
        i = self.used
        assert self.sched[i] == (name, blk, nelem), (self.sched[i], name, blk, nelem)
        while self.issued < min(i + NSLOT, len(self.sched)):
            self._issue(self.issued)
            self.issued += 1
        self.used += 1
        return (i % NSLOT) * SLOT

    def _issue(self, j):
        name, blk, nelem = self.sched[j]
        slot = j % NSLOT
        off = slot * SLOT
        self.P.op("pool", I("dma_start", out=self.WB[:, off:off + nelem], in_=self.dram[name][blk]),
                  reads=([("X", 0, 512)] if j == 0 else []),
                  writes=[("WB", off, off + nelem)], stream="w%d" % slot, amt=16)


def build_program(layers):
    nc = bass.Bass("TRN2", target_bir_lowering=False)
    dram = {}

    def din(name, shape):
        dram[name] = nc.dram_tensor(name, list(shape), F32, kind="ExternalInput").ap()
    din("xT", (1024, S))
    din("gvec", (128, 128))
    din("dwv", (128, 528))
    din("cdw", (128, 24))
    din("psc", (128, 8))
    for l in layers:
        din("wup%d" % l, (22, 128, 2048))
        din("wdn%d" % l, (8, 128, 2816))
        kind = l % 3
        if kind == 0:
            din("wqkv%d" % l, (12, 128, 3072))
            din("wo%d" % l, (2, 128, 2048))
        elif kind == 1:
            din("cwin", (8, 128, 3072))
            din("cwout", (4, 128, 2048))
        else:
            din("pwin", (4, 128, 2048))
            din("pwgrp", (4, 128, 512))
            din("pwout", (4, 128, 2048))
    yT = nc.dram_tensor("yT", [1024, S], F32, kind="ExternalOutput").ap()

    with ExitStack() as st:
        def sb(name, n, dt):
            return st.enter_context(nc.sbuf_tensor(name, [128, n], dt))
        X = sb("X", 8 * S, F32)
        XN = sb("XN", 8 * S, BF16)
        BIG = sb("BIG", 22528, BF16)
        SCR = sb("SCR", NT * ST, F32)
        SB16 = sb("SB16", 6 * 512, BF16)
        WB = sb("WB", NSLOT * SLOT, BF16)
        MASK = sb("MASK", 12 * 512, BF16)
        GV = sb("GV", 128, F32)
        DWV = sb("DWV", 528, F32)
        CDW = sb("CDW", 24, F32)
        PSC = sb("PSC", 8, F32)
        ONES = sb("ONES", 128, BF16)
        ONESAB = sb("ONESAB", 256, BF16)
        HALO = sb("HALO", 88, F32)
        INVT = sb("INVT", 16, F32)
        PS = [st.enter_context(nc.psum_tensor("PS%d" % b, [128, 512], F32)) for b in range(8)]

        def construct(P, W):
            mmi = [0]
            auxi = [0]
            sqi = [0]
            rsi = [0]
            ei = [0]

            def ps_mm():
                b = mmi[0] % 5
                mmi[0] += 1
                return PS[b], "PS%d" % b

            upi = [0]

            def ps_up():
                b = upi[0] % 8
                upi[0] += 1
                return PS[b], "PS%d" % b

            def ps_aux():
                b = (6, 7, 5)[auxi[0] % 3]
                auxi[0] += 1
                return PS[b], "PS%d" % b
            SS, SSn = PS[5], "PS5"

            def scr(i, lo=0, n=512):
                o = i * ST + lo
                return SCR[:, o:o + n], ("SCR", o, o + n)

            def sb16(i, n=512):
                return SB16[:, i * 512:i * 512 + n], ("SB16", i * 512, i * 512 + n)

            for (t, nm, n) in ((GV, "gvec", 128), (DWV, "dwv", 528), (CDW, "cdw", 24), (PSC, "psc", 8)):
                P.op("sp", I("dma_start", out=t[:, :], in_=dram[nm][:, :]), writes=[(t_name(t, nm), 0, n)], stream="cst", amt=16)
            for (h0, h1, strm) in ((0, 512, "xld0"), (512, 1024, "xld1"), (1024, 2048, "xld2")):
                for c in range(8):
                    P.op("act" if strm == "xld1" else "sp",
                         I("dma_start", out=X[:, c * S + h0:c * S + h1], in_=dram["xT"][c * 128:(c + 1) * 128, h0:h1]),
                         writes=[("X", c * S + h0, c * S + h1)], stream=strm, amt=16)
            P.op("dve", I("memset", ONES[:, :], 1.0), writes=[("ONES", 0, 128)])
            P.op("dve", I("memset", ONESAB[:, :], 0.0), writes=[("ONESAB", 0, 256)])
            P.op("dve", I("memset", ONESAB[:, 0:64], 1.0), writes=[("ONESAB", 0, 64)])
            P.op("dve", I("memset", ONESAB[:, 192:256], 1.0), writes=[("ONESAB", 192, 256)])

            XNf = XN.bitcast(F32)
            MASKf = MASK.bitcast(F32)

            def add_sq(src_ap, src_rg, first, last, SS=SS, SSn=SSn, defer=False):
                k = sqi[0] % 2
                sqi[0] += 1
                sq_ap, sq_rg = sb16(k)
                P.op("act", I("activation", out=sq_ap, in_=src_ap, func=AF.Square), reads=[src_rg], writes=[sq_rg])

                def pe_part():
                    P.op("pe", I("matmul", SS[:, :], lhsT=ONES[:, :], rhs=sq_ap, start=first, stop=last),
                         reads=[sq_rg, ("ONES", 0, 128)], writes=[(SSn, 0, 512)])
                if defer:
                    return pe_part
                pe_part()
                return None

            def stats_to_rs(SS=SS, SSn=SSn):
                r = 10 + rsi[0] % 2
                rsi[0] += 1
                rs_ap, rs_rg = scr(r)
                P.op("act", I("activation", out=rs_ap, in_=SS[:, :], func=AF.Ln, scale=1.0 / 1024.0, bias=EPS),
                     reads=[(SSn, 0, 512)], writes=[rs_rg])
                P.op("act", I("activation", out=rs_ap, in_=rs_ap, func=AF.Exp, scale=-0.5), reads=[rs_rg], writes=[rs_rg])
                return rs_ap, rs_rg

            def prenorm_units(l, j, tok0, ntiles, sqt=(0, 1)):
                go = (l * 4 + j) * 8
                units = []
                for tt in range(ntiles):
                    t0 = tok0 + 512 * tt
                    rs_box = []

                    def sq(c, t0=t0):
                        o = c * S + t0
                        sq_ap, sq_rg = sb16(sqt[c % len(sqt)])
                        P.op("act", I("activation", out=sq_ap, in_=X[:, o:o + 512], func=AF.Square),
                             reads=[("X", o, o + 512)], writes=[sq_rg])

                    def stat(c):
                        sq_ap, sq_rg = sb16(sqt[c % len(sqt)])
                        P.op("pe", I("matmul", SS[:, :], lhsT=ONES[:, :], rhs=sq_ap, start=(c == 0), stop=(c == 7)),
                             reads=[sq_rg, ("ONES", 0, 128)], writes=[(SSn, 0, 512)])

                    def rs(rs_box=rs_box):
                        rs_box.append(stats_to_rs())

                    def apply(c0, c1, t0=t0, rs_box=rs_box):
                        rs_ap, rs_rg = rs_box[0]
                        for c in range(c0, c1):
                            o = c * S + t0
                            P.op("dve", I("scalar_tensor_tensor", out=XN[:, o:o + 512], in0=X[:, o:o + 512],
                                          scalar=GV[:, go + c:go + c + 1], in1=rs_ap, op0=ALU.mult, op1=ALU.mult),
                                 reads=[("X", o, o + 512), rs_rg, ("GV", 0, 128)], writes=[("XN", o, o + 512)])
                    nb_ = len(sqt)
                    units.append(lambda sq=sq: sq(0))
                    if nb_ >= 3:
                        units.append(lambda sq=sq: sq(1))
                        for c in range(2, 8):
                            units.append(lambda c=c, sq=sq, stat=stat: (sq(c), stat(c - 2)))
                        units.append(lambda stat=stat, rs=rs: (stat(6), stat(7), rs()))
                    else:
                        for c in range(1, 8):
                            units.append(lambda c=c, sq=sq, stat=stat: (sq(c), stat(c - 1)))
                        units.append(lambda stat=stat, rs=rs: (stat(7), rs()))
                    units.append(lambda apply=apply: apply(0, 4))
                    units.append(lambda apply=apply: apply(4, 8))
                return units

            def prenorm(l, j, tok0, ntiles):
                for u in prenorm_units(l, j, tok0, ntiles):
                    u()

            def proj_postnorm(l, j, wname, nk, mpb, in_off, in_cs, in_t0, tok0, ntiles, bg=None, bg_after=1):
                go = (l * 4 + j) * 8
                ncols = mpb * 128
                woff = 0
                pend = None
                bg_list = []
                SSp = (PS[6], "PS6")

                def hs(tt, m):
                    if tt % 2 == 0:
                        return scr(m)
                    o = m * S + 1024
                    return XNf[:, o // 2:o // 2 + 512], ("XN", o, o + 1024)
                for tt in range(ntiles):
                    t0 = tok0 + 512 * tt
                    it0 = in_t0 + 512 * tt
                    for m in range(8):
                        if m % mpb == 0:
                            woff = W.next(wname, m // mpb, nk * ncols)
                        ps, psn = ps_mm()
                        for kc in range(nk):
                            lo = woff + kc * ncols + (m % mpb) * 128
                            io = in_off + kc * in_cs + it0
                            P.op("pe", I("matmul", ps[:, :], lhsT=WB[:, lo:lo + 128], rhs=BIG[:, io:io + 512],
                                         start=(kc == 0), stop=(kc == nk - 1)),
                                 reads=[("WB", lo, lo + 128), ("BIG", io, io + 512)], writes=[(psn, 0, 512)],
                                 inc=(kc == nk - 1))
                        if pend is not None:
                            pend()
                        hs_ap, hs_rg = hs(tt, m)
                        P.op("act", I("activation", out=hs_ap, in_=ps[:, :], func=AF.Copy),
                             reads=[(psn, 0, 512)], writes=[hs_rg])
                        pend = add_sq(ps[:, :], (psn, 0, 512), m == 0, m == 7, SSp[0], SSp[1], defer=True)
                        if bg_list and tt > bg_after:
                            for _ in range(2):
                                if bg_list:
                                    bg_list.pop(0)()
                    pend()
                    pend = None
                    rs_ap, rs_rg = stats_to_rs(SSp[0], SSp[1])
                    def res_add(m, tt=tt, t0=t0):
                        hs_ap, hs_rg = hs(tt, m)
                        xo = m * S + t0
                        P.op("dve",
                             I("tensor_tensor", out=X[:, xo:xo + 512], in0=X[:, xo:xo + 512], in1=hs_ap, op=ALU.add),
                             reads=[("X", xo, xo + 512), hs_rg], writes=[("X", xo, xo + 512)])
                    for m in range(8):
                        hs_ap, hs_rg = hs(tt, m)
                        P.op("dve", I("scalar_tensor_tensor", out=hs_ap, in0=hs_ap, scalar=GV[:, go + m:go + m + 1],
                                      in1=rs_ap, op0=ALU.mult, op1=ALU.mult),
                             reads=[hs_rg, rs_rg, ("GV", 0, 128)], writes=[hs_rg])
                        if m > 0:
                            res_add(m - 1)
                    res_add(7)
                    if tt == bg_after and bg is not None:
                        bg_list.extend(bg())
                while bg_list:
                    bg_list.pop(0)()

            def ffn(l, pre_done=False, next_l=None):
                for half in range(2):
                    tok0 = 1024 * half
                    if half == 0 and not pre_done:
                        prenorm(l, 2, 0, 2)
                    if half == 0:
                        P.op("dve", I("memset", HALO[:, :], 0.0), writes=[("HALO", 0, 88)])
                    pend = []
                    ptc = [0]
                    for i in range(22):
                        woff = W.next("wup%d" % l, i, 2048)
                        for tt in range(2):
                            t0 = tok0 + 512 * tt
                            pss = []
                            for wh in range(2):
                                ps, psn = ps_up()
                                pss.append((ps, psn))
                                for kc in range(8):
                                    lo = woff + kc * 256 + wh * 128
                                    xo = kc * S + t0
                                    P.op("pe", I("matmul", ps[:, :], lhsT=WB[:, lo:lo + 128], rhs=XN[:, xo:xo + 512],
                                                 start=(kc == 0), stop=(kc == 7)),
                                         reads=[("WB", lo, lo + 128), ("XN", xo, xo + 512)], writes=[(psn, 0, 512)],
                                         inc=(kc == 7))
                            tos = []
                            for wh in range(2):
                                ps, psn = pss[wh]
                                ch = 2 * i + wh
                                dwo = l * 132 + ch * 3
                                zo = (wh * 3 + (ptc[0] % 3)) * ST
                                to = (6 + wh * 3 + (ptc[0] % 3)) * ST
                                tos.append(to)
                                if tt == 0 and wh == 0:
                                    zv = SCR[:, zo:zo + 6 * ST].rearrange("p (w r) -> p w r", w=2)[:, :, 0:2]
                                    hv = HALO[:, ch * 2:ch * 2 + 4].rearrange("p (w r) -> p w r", w=2)
                                    P.op("act", I("activation", out=zv, in_=hv, func=AF.Copy),
                                         reads=[("HALO", ch * 2, ch * 2 + 4)],
                                         writes=[("SCR", zo, zo + 2), ("SCR", zo + 3 * ST, zo + 3 * ST + 2)])
                                P.op("act", I("activation", out=SCR[:, zo + 2:zo + 514], in_=ps[:, :], func=AF.Copy),
                                     reads=[(psn, 0, 512)], writes=[("SCR", zo + 2, zo + 514)])
                                P.op("act", I("activation", out=SCR[:, to:to + 512], in_=ps[:, :], func=AF.Copy,
                                              scale=DWV[:, dwo + 2:dwo + 3]),
                                     reads=[(psn, 0, 512), ("DWV", 0, 528)], writes=[("SCR", to, to + 512)])
                            pend2 = []
                            for f in pend:
                                pend2.append(f())
                            pend = []
                            ch = 2 * i
                            zo = (ptc[0] % 3) * ST
                            zno = ((ptc[0] + 1) % 3) * ST
                            tv = SCR[:, zo:zo + 6 * ST].rearrange("p (w r) -> p w r", w=2)[:, :, 512:514]
                            t_rg = [("SCR", zo + 512, zo + 514), ("SCR", zo + 3 * ST + 512, zo + 3 * ST + 514)]
                            if tt == 0:
                                nv = SCR[:, zno:zno + 6 * ST].rearrange("p (w r) -> p w r", w=2)[:, :, 0:2]
                                P.op("act", I("activation", out=nv, in_=tv, func=AF.Copy), reads=t_rg,
                                     writes=[("SCR", zno, zno + 2), ("SCR", zno + 3 * ST, zno + 3 * ST + 2)])
                            elif half == 0:
                                hv = HALO[:, ch * 2:ch * 2 + 4].rearrange("p (w r) -> p w r", w=2)
                                P.op("act", I("activation", out=hv, in_=tv, func=AF.Copy), reads=t_rg,
                                     writes=[("HALO", ch * 2, ch * 2 + 4)])
                            for k in (1, 0):
                                for wh in range(2):
                                    ch = 2 * i + wh
                                    dwo = l * 132 + ch * 3
                                    zo = (wh * 3 + (ptc[0] % 3)) * ST
                                    to = tos[wh]
                                    P.op("dve", I("scalar_tensor_tensor", out=SCR[:, to:to + 512], in0=SCR[:, zo + k:zo + k + 512],
                                                  scalar=DWV[:, dwo + k:dwo + k + 1], in1=SCR[:, to:to + 512],
                                                  op0=ALU.mult, op1=ALU.add),
                                         reads=[("SCR", zo + k, zo + k + 512), ("SCR", to, to + 512), ("DWV", 0, 528)],
                                         writes=[("SCR", to, to + 512)])

                            for f2 in pend2:
                                f2()

                            def fin(tg=tos[0], tu=tos[1], ho=i * 1024 + 512 * tt):
                                P.op("act", I("activation", out=SCR[:, tg:tg + 512], in_=SCR[:, tg:tg + 512], func=AF.Silu),
                                     reads=[("SCR", tg, tg + 512)], writes=[("SCR", tg, tg + 512)])

                                def gate():
                                    P.op("dve", I("tensor_tensor", out=BIG[:, ho:ho + 512], in0=SCR[:, tg:tg + 512],
                                                  in1=SCR[:, tu:tu + 512], op=ALU.mult),
                                         reads=[("SCR", tg, tg + 512), ("SCR", tu, tu + 512)], writes=[("BIG", ho, ho + 512)])
                                return gate
                            pend.append(fin)
                            ptc[0] += 1
                    for f in pend:
                        f()()
                    if half == 0:
                        ffn_down(l, 0, bg=prenorm_units(l, 2, 1024, 2, sqt=(2, 3, 4)))
                        if next_l is None:
                            store_out(0, 1024, "ost0")
                    elif next_l is not None:
                        ffn_down(l, 1, bg=prenorm_units(next_l, 0, 0, 2, sqt=(2, 3, 4)))
                    else:
                        ffn_down(l, 1)

            def ffn_down(l, half, bg=()):
                tok0 = 1024 * half
                go = (l * 4 + 3) * 8
                SSb = [(PS[6], "PS6"), (PS[7], "PS7")]
                bg = list(bg)

                def hs(tt, m):
                    if tt == 0:
                        return scr(m)
                    o = m * S + 1024 * half
                    return XNf[:, o // 2:o // 2 + 512], ("XN", o, o + 1024)
                pend = None
                for m in range(8):
                    woff = W.next("wdn%d" % l, m, 2816)
                    for tt in range(2):
                        ps, psn = ps_mm()
                        for kc in range(22):
                            lo = woff + kc * 128
                            io = kc * 1024 + 512 * tt
                            P.op("pe", I("matmul", ps[:, :], lhsT=WB[:, lo:lo + 128], rhs=BIG[:, io:io + 512],
                                         start=(kc == 0), stop=(kc == 21)),
                                 reads=[("WB", lo, lo + 128), ("BIG", io, io + 512)], writes=[(psn, 0, 512)],
                                 inc=(kc == 21))
                        if pend is not None:
                            pend()
                        hs_ap, hs_rg = hs(tt, m)
                        P.op("act", I("activation", out=hs_ap, in_=ps[:, :], func=AF.Copy),
                             reads=[(psn, 0, 512)], writes=[hs_rg])
                        pend = add_sq(ps[:, :], (psn, 0, 512), m == 0, m == 7, SSb[tt][0], SSb[tt][1], defer=True)
                        for _ in range(2):
                            if bg:
                                bg.pop(0)()
                pend()
                while bg:
                    bg.pop(0)()
                for tt in (0, 1):
                    t0 = tok0 + 512 * tt
                    rs_ap, rs_rg = stats_to_rs(SSb[tt][0], SSb[tt][1])
                    def res_add(m, tt=tt, t0=t0):
                        hs_ap, hs_rg = hs(tt, m)
                        xo = m * S + t0
                        P.op("dve", I("tensor_tensor", out=X[:, xo:xo + 512], in0=X[:, xo:xo + 512], in1=hs_ap, op=ALU.add),
                             reads=[("X", xo, xo + 512), hs_rg], writes=[("X", xo, xo + 512)])
                    for m in range(8):
                        hs_ap, hs_rg = hs(tt, m)
                        P.op("dve", I("scalar_tensor_tensor", out=hs_ap, in0=hs_ap, scalar=GV[:, go + m:go + m + 1],
                                      in1=rs_ap, op0=ALU.mult, op1=ALU.mult),
                             reads=[hs_rg, rs_rg, ("GV", 0, 128)], writes=[hs_rg])
                        if m > 0:
                            res_add(m - 1)
                    res_add(7)

            def mask_setup():
                io_ap, io_rg = scr(9, 0, 128)
                P.op("pool", I("iota", io_ap, [[1, 128]], base=0, channel_multiplier=-1,
                               allow_small_or_imprecise_dtypes=True), writes=[io_rg])
                vc, vc_rg = scr(8, 0, 128)
                vp, vp_rg = scr(8, 128, 128)
                dc, dc_rg = scr(8, 256, 128)
                dp, dp_rg = scr(8, 384, 128)
                P.op("dve", I("tensor_single_scalar", out=vc, in_=io_ap, scalar=0.0, op=ALU.is_ge), reads=[io_rg], writes=[vc_rg])
                P.op("dve", I("tensor_single_scalar", out=vp, in_=io_ap, scalar=0.0, op=ALU.is_le), reads=[io_rg], writes=[vp_rg])
                P.op("dve", I("tensor_single_scalar", out=dc, in_=io_ap, scalar=0.0, op=ALU.max), reads=[io_rg], writes=[dc_rg])
                P.op("dve", I("tensor_scalar", out=dp, in0=io_ap, scalar1=0.0, scalar2=128.0, op0=ALU.min, op1=ALU.add),
                     reads=[io_rg], writes=[dp_rg])

            mk = [0]

            def build_mask(g, c):
                vc, vc_rg = scr(8, 0, 128)
                vp, vp_rg = scr(8, 128, 128)
                dc, dc_rg = scr(8, 256, 128)
                dp, dp_rg = scr(8, 384, 128)
                for s_ in range(2):
                    h = 2 * c + s_
                    sd = float(2.0 ** (-8.0 * (g * 8 + h + 1) / 24.0)) * DIL[g]
                    mo = (g * 4 + c) * 512
                    for (d_ap, d_rg, v_ap, v_rg, co) in ((dc, dc_rg, vc, vc_rg, s_ * 128), (dp, dp_rg, vp, vp_rg, 256 + s_ * 128)):
                        t_ap, t_rg = scr(9, 128 + 128 * (mk[0] % 2), 128)
                        mk[0] += 1
                        P.op("act", I("activation", out=t_ap, in_=d_ap, func=AF.Exp, scale=-sd), reads=[d_rg], writes=[t_rg])
                        P.op("dve", I("tensor_tensor", out=MASK[:, mo + co:mo + co + 128], in0=t_ap, in1=v_ap, op=ALU.mult),
                             reads=[t_rg, v_rg], writes=[("MASK", mo + co, mo + co + 128)])

            QO, KAO, KBO, VO, MO = 0, 2048, 4096, 6144, 10240

            def attention(l, pre_tiles=0):
                prenorm(l, 0, 512 * pre_tiles, 4 - pre_tiles)
                first_attn = not masks_built[0]
                if first_attn:
                    mask_setup()
                    masks_built[0] = True
                P.op("dve", I("memset", BIG[:, KAO:MO], 0.0), writes=[("BIG", KAO, MO)])
                Vv = BIG[:, VO:MO].rearrange("p (b s c) -> p b s c", b=16, s=2)
                for c in range(4):
                    for g in range(3):
                        Dl = DIL[g]
                        nb = (S // Dl) // 128
                        woff = W.next("wqkv%d" % l, g * 4 + c, 3072)
                        if first_attn:
                            build_mask(g, c)
                        for wh in range(2):
                            for tt in range(4):
                                ps, psn = ps_mm()
                                for kc in range(8):
                                    lo = woff + kc * 384 + wh * 128
                                    xo = kc * S + 512 * tt
                                    P.op("pe", I("matmul", ps[:, :], lhsT=WB[:, lo:lo + 128], rhs=XN[:, xo:xo + 512],
                                                 start=(kc == 0), stop=(kc == 7)),
                                         reads=[("WB", lo, lo + 128), ("XN", xo, xo + 512)], writes=[(psn, 0, 512)],
                                         inc=(kc == 7))
                                nu = 512 // Dl
                                u0 = tt * nu

                                def views(dst_rows, base):
                                    if Dl == 1:
                                        return (BIG[dst_rows, base + 512 * tt:base + 512 * tt + 512], ps[dst_rows, :])
                                    ov = BIG[dst_rows, base:base + S].rearrange("p (r u) -> p u r", r=Dl)[:, u0:u0 + nu, :]
                                    iv = ps[dst_rows, :].rearrange("p (u r) -> p u r", r=Dl)
                                    return ov, iv
                                if wh == 0:
                                    ov, iv = views(slice(0, 128), QO)
                                    P.op("act", I("activation", out=ov, in_=iv, func=AF.Copy),
                                         reads=[(psn, 0, 512)], writes=[("BIG", QO, QO + S)])
                                else:
                                    ov, iv = views(slice(0, 64), KAO)
                                    P.op("act", I("activation", out=ov, in_=iv, func=AF.Copy), reads=[(psn, 0, 512)], writes=[("BIG", KAO, KAO + S)])
                                    ov, iv = views(slice(64, 128), KBO)
                                    P.op("dve", I("tensor_copy", out=ov, in_=iv), reads=[(psn, 0, 512)], writes=[("BIG", KBO, KBO + S)])
                        for b0 in range(0, 16, 4):
                            ps, psn = ps_mm()
                            for bb in range(4):
                                b = b0 + bb
                                r, n = b // nb, b % nb
                                stok = r + Dl * 128 * n
                                for kc in range(8):
                                    a0 = kc * S + stok
                                    a1 = a0 + Dl * 127 + 1
                                    lo = woff + kc * 384 + 256
                                    P.op("pe", I("matmul", ps[:, bb * 128:(bb + 1) * 128], lhsT=XN[:, a0:a1:Dl], rhs=WB[:, lo:lo + 128],
                                                 start=(kc == 0), stop=(kc == 7)),
                                         reads=[("XN", a0, a1), ("WB", lo, lo + 128)], writes=[(psn, bb * 128, (bb + 1) * 128)],
                                         inc=(bb == 3 and kc == 7))
                            psv = ps[:, :].rearrange("p (b c) -> p b c", b=4)
                            vlo = VO + b0 * 256
                            P.op("act", I("activation", out=Vv[:, b0:b0 + 4, 0, 0:64], in_=psv[:, :, 0:64], func=AF.Copy),
                                 reads=[(psn, 0, 512)], writes=[("BIG", vlo, vlo + 1024)])
                            P.op("dve", I("tensor_copy", out=Vv[:, b0:b0 + 4, 1, 64:128], in_=psv[:, :, 64:128]),
                                 reads=[(psn, 0, 512)], writes=[("BIG", vlo, vlo + 1024)])
                        mo = (g * 4 + c) * 512

                        def stage1(b):
                            r, n = b // nb, b % nb
                            with_prev = (n != 0)
                            ncols = 512 if with_prev else 256
                            ps, psn = ps_mm()
                            mms = [(0, KAO + 128 * b), (128, KBO + 128 * b)]
                            if with_prev:
                                mms += [(256, KAO + 128 * (b - 1)), (384, KBO + 128 * (b - 1))]
                            qo = QO + 128 * b
                            for idx, (co, ko) in enumerate(mms):
                                P.op("pe", I("matmul", ps[:, co:co + 128], lhsT=BIG[:, ko:ko + 128], rhs=BIG[:, qo:qo + 128],
                                             start=True, stop=True),
                                     reads=[("BIG", ko, ko + 128), ("BIG", qo, qo + 128)], writes=[(psn, co, co + 128)],
                                     inc=(idx == len(mms) - 1))
                            k2 = ei[0] % 3
                            ei[0] += 1
                            e_ap, e_rg = sb16(k2, ncols)
                            pt_ap, pt_rg = sb16(3 + k2, ncols)
                            P.op("act", I("activation", out=e_ap, in_=ps[:, 0:ncols], func=AF.Exp, scale=0.125),
                                 reads=[(psn, 0, ncols)], writes=[e_rg])
                            P.op("dve", I("tensor_tensor", out=pt_ap, in0=e_ap, in1=MASK[:, mo:mo + ncols], op=ALU.mult),
                                 reads=[e_rg, ("MASK", mo, mo + ncols)], writes=[pt_rg])
                            return (b, r, n, with_prev, (3 + k2) * 512, pt_rg)

                        def stage2(st1):
                            b, r, n, with_prev, ptb, pt_rg = st1
                            puz, puzn = ps_aux()
                            terms = [(b, 0, 0), (b, 1, 128)]
                            if with_prev:
                                terms += [(b - 1, 0, 256), (b - 1, 1, 384)]
                            for k, (vb, s_, pco) in enumerate(terms):
                                vlo = VO + vb * 256 + s_ * 128
                                P.op("pe", I("matmul", puz[:, 0:128], lhsT=Vv[:, vb, s_, :], rhs=SB16[:, ptb + pco:ptb + pco + 128],
                                             start=(k == 0), stop=(k == len(terms) - 1)),
                                     reads=[("BIG", vlo, vlo + 128), pt_rg], writes=[(puzn, 0, 128)], inc=False)
                            for k, (vb, s_, pco) in enumerate(terms):
                                P.op("pe", I("matmul", puz[:, 128:256], lhsT=ONESAB[:, s_ * 128:(s_ + 1) * 128],
                                             rhs=SB16[:, ptb + pco:ptb + pco + 128],
                                             start=(k == 0), stop=(k == len(terms) - 1)),
                                     reads=[("ONESAB", 0, 256), pt_rg], writes=[(puzn, 128, 256)],
                                     inc=(k == len(terms) - 1))
                            stok = r + Dl * 128 * n
                            e1 = stok + Dl * 127 + 1
                            dst = SCR[:, 0:4096].rearrange("p (z t) -> p z t", z=2)[:, :, stok:e1:Dl]
                            src = puz[:, 0:256].rearrange("p (z q) -> p z q", z=2)
                            rgs = [("SCR", stok, e1), ("SCR", 2048 + stok, 2048 + e1)]
                            if g == 0:
                                P.op("act", I("activation", out=dst, in_=src, func=AF.Copy), reads=[(puzn, 0, 256)], writes=rgs)
                            else:
                                P.op("dve", I("tensor_tensor", out=dst, in0=src, in1=dst, op=ALU.add),
                                     reads=[(puzn, 0, 256)] + rgs, writes=rgs)
                        sts = [stage1(0), stage1(1)]
                        for b in range(2, 16):
                            sts.append(stage1(b))
                            stage2(sts.pop(0))
                        stage2(sts.pop(0))
                        stage2(sts.pop(0))
                    mgo = MO + c * S
                    for q4 in range(4):
                        a0, a1 = 512 * q4, 512 * q4 + 512
                        P.op("act", I("activation", out=SCR[:, 2048 + a0:2048 + a1], in_=SCR[:, 2048 + a0:2048 + a1], func=AF.Ln),
                             reads=[("SCR", 2048 + a0, 2048 + a1)], writes=[("SCR", 2048 + a0, 2048 + a1)])
                    for q4 in range(4):
                        a0, a1 = 512 * q4, 512 * q4 + 512
                        P.op("act", I("activation", out=SCR[:, 2048 + a0:2048 + a1], in_=SCR[:, 2048 + a0:2048 + a1], func=AF.Exp, scale=-1.0),
                             reads=[("SCR", 2048 + a0, 2048 + a1)], writes=[("SCR", 2048 + a0, 2048 + a1)])
                    for q4 in range(4):
                        a0, a1 = 512 * q4, 512 * q4 + 512
                        P.op("dve", I("tensor_tensor", out=BIG[:, mgo + a0:mgo + a1], in0=SCR[:, a0:a1],
                                      in1=SCR[:, 2048 + a0:2048 + a1], op=ALU.mult),
                             reads=[("SCR", a0, a1), ("SCR", 2048 + a0, 2048 + a1)], writes=[("BIG", mgo + a0, mgo + a1)])
                proj_postnorm(l, 1, "wo%d" % l, 4, 4, MO, S, 0, 0, 4, bg=lambda: prenorm_units(l, 2, 0, 2, sqt=(2, 3, 4)))

            def convmix(l, pre_tiles=0):
                prenorm(l, 0, 512 * pre_tiles, 4 - pre_tiles)
                for i in range(8):
                    woff = W.next("cwin", i, 3072)
                    P.op("dve", I("memset", SCR[:, 0:2], 0.0), writes=[("SCR", 0, 2)])
                    for tt in range(4):
                        t0 = 512 * tt
                        pss = []
                        for j in range(3):
                            ps, psn = ps_mm()
                            pss.append((ps, psn))
                            for kc in range(8):
                                lo = woff + kc * 384 + j * 128
                                xo = kc * S + t0
                                P.op("pe", I("matmul", ps[:, :], lhsT=WB[:, lo:lo + 128], rhs=XN[:, xo:xo + 512],
                                             start=(kc == 0), stop=(kc == 7)),
                                     reads=[("WB", lo, lo + 128), ("XN", xo, xo + 512)], writes=[(psn, 0, 512)],
                                     inc=(kc == 7))
                        (pB, pBn), (pC, pCn), (pH, pHn) = pss
                        hb, hb_rg = scr(8)
                        bb_, bb_rg = scr(9)
                        P.op("act", I("activation", out=hb, in_=pH[:, :], func=AF.Copy), reads=[(pHn, 0, 512)], writes=[hb_rg])
                        P.op("act", I("activation", out=bb_, in_=pB[:, :], func=AF.Copy), reads=[(pBn, 0, 512)], writes=[bb_rg])
                        zo = (tt % 2) * ST
                        zno = ((tt + 1) % 2) * ST
                        P.op("dve", I("tensor_tensor", out=SCR[:, zo + 2:zo + 514], in0=pC[:, :], in1=hb, op=ALU.mult),
                             reads=[(pCn, 0, 512), hb_rg], writes=[("SCR", zo + 2, zo + 514)])
                        if tt < 3:
                            P.op("dve", I("tensor_copy", out=SCR[:, zno:zno + 2], in_=SCR[:, zo + 512:zo + 514]),
                                 reads=[("SCR", zo + 512, zo + 514)], writes=[("SCR", zno, zno + 2)])
                        to = (2 + tt % 2) * ST
                        P.op("dve", I("tensor_scalar", out=SCR[:, to:to + 512], in0=SCR[:, zo + 2:zo + 514],
                                      scalar1=CDW[:, i * 3 + 2:i * 3 + 3], scalar2=None, op0=ALU.mult),
                             reads=[("SCR", zo + 2, zo + 514), ("CDW", 0, 24)], writes=[("SCR", to, to + 512)])
                        for k in (1, 0):
                            P.op("dve", I("scalar_tensor_tensor", out=SCR[:, to:to + 512], in0=SCR[:, zo + k:zo + k + 512],
                                          scalar=CDW[:, i * 3 + k:i * 3 + k + 1], in1=SCR[:, to:to + 512],
                                          op0=ALU.mult, op1=ALU.add),
                                 reads=[("SCR", zo + k, zo + k + 512), ("SCR", to, to + 512), ("CDW", 0, 24)],
                                 writes=[("SCR", to, to + 512)])
                        yo = i * S + t0
                        P.op("dve", I("tensor_tensor", out=BIG[:, yo:yo + 512], in0=SCR[:, to:to + 512], in1=bb_, op=ALU.mult),
                             reads=[("SCR", to, to + 512), bb_rg], writes=[("BIG", yo, yo + 512)])
                proj_postnorm(l, 1, "cwout", 8, 2, 0, S, 0, 0, 4, bg=lambda: prenorm_units(l, 2, 0, 2, sqt=(2, 3, 4)))

            def poolmix(l, pre_tiles=0):
                prenorm(l, 0, 512 * pre_tiles, 4 - pre_tiles)
                CS0 = 2096
                PO = 16384
                SBf = SB16.bitcast(F32)
                ZT = SBf[:, 0:512]
                ZT_rg = ("SB16", 0, 1024)
                TM = HALO[:, 0:16]
                TM_rg = ("HALO", 0, 16)
                P.op("dve", I("memset", SCR[:, 2080:2096], 0.0), writes=[("SCR", 2080, 2096)])
                P.op("dve", I("memset", ZT, 0.0), writes=[ZT_rg])
                chunk_i = [0]
                for t in range(16):
                    P.op("dve", I("memset", INVT[:, t:t + 1], 1.0 / (t + 1)), writes=[("INVT", t, t + 1)])
                PSETS = [[(BIG, "BIG", 16384), (BIG, "BIG", 18432)], [(BIG, "BIG", 20480), (SB16, "SB16", 1024)]]

                def inproj(gi):
                    w = 2 ** (gi + 1)
                    woff = W.next("pwin", gi, 2048)
                    for cc in range(2):
                        UB0 = 0 if chunk_i[0] % 2 == 0 else 4160
                        chunk_i[0] += 1
                        for tt in range(4):
                            ps, psn = ps_mm()
                            for kc in range(8):
                                lo = woff + kc * 256 + cc * 128
                                xo = kc * S + 512 * tt
                                P.op("pe", I("matmul", ps[:, :], lhsT=WB[:, lo:lo + 128], rhs=XN[:, xo:xo + 512],
                                             start=(kc == 0), stop=(kc == 7)),
                                     reads=[("WB", lo, lo + 128), ("XN", xo, xo + 512)], writes=[(psn, 0, 512)],
                                     inc=(kc == 7))
                            P.op("act", I("activation", out=SCR[:, UB0 + 512 * tt:UB0 + 512 * tt + 512], in_=ps[:, :], func=AF.Copy),
                                 reads=[(psn, 0, 512)], writes=[("SCR", UB0 + 512 * tt, UB0 + 512 * tt + 512)])
                        def scan(q):
                            co = CS0 + 512 * q
                            init = 0.0 if q == 0 else SCR[:, co - 1:co]
                            P.op("dve", I("tensor_tensor_scan", out=SCR[:, co:co + 512], data0=SCR[:, UB0 + 512 * q:UB0 + 512 * q + 512],
                                          data1=ZT, initial=init, op0=ALU.add, op1=ALU.add),
                                 reads=[("SCR", UB0 + 512 * q, UB0 + 512 * q + 512), ZT_rg, ("SCR", co - 1, co)],
                                 writes=[("SCR", co, co + 512)])
                        scan(0)
                        scan(1)
                        P.op("dve", I("tensor_tensor", out=TM, in0=SCR[:, CS0:CS0 + 16], in1=INVT[:, 0:16], op=ALU.mult),
                             reads=[("SCR", CS0, CS0 + 16), ("INVT", 0, 16)], writes=[TM_rg])
                        scan(2)
                        P.op("dve", I("tensor_tensor", out=TM, in0=TM, in1=SCR[:, UB0:UB0 + 16], op=ALU.subtract),
                             reads=[TM_rg, ("SCR", UB0, UB0 + 16)], writes=[TM_rg])
                        scan(3)
                        P.op("dve", I("scalar_tensor_tensor", out=SCR[:, UB0:UB0 + S], in0=SCR[:, CS0:CS0 + S], scalar=1.0 / w,
                                      in1=SCR[:, UB0:UB0 + S], op0=ALU.mult, op1=ALU.subtract),
                             reads=[("SCR", CS0, CS0 + S), ("SCR", UB0, UB0 + S)], writes=[("SCR", UB0, UB0 + S)])
                        pt_, pn_, po = PSETS[gi % 2][cc]
                        P.op("dve", I("scalar_tensor_tensor", out=pt_[:, po:po + S], in0=SCR[:, CS0 - w:CS0 - w + S], scalar=-1.0 / w,
                                      in1=SCR[:, UB0:UB0 + S], op0=ALU.mult, op1=ALU.add),
                             reads=[("SCR", CS0 - w, CS0 - w + S), ("SCR", UB0, UB0 + S)], writes=[(pn_, po, po + S)])
                        P.op("dve", I("tensor_copy", out=pt_[:, po:po + w - 1], in_=HALO[:, 0:w - 1]),
                             reads=[TM_rg], writes=[(pn_, po, po + w - 1)])

                def grp(gi):
                    goff = W.next("pwgrp", gi, 512)
                    for mm in range(2):
                        for tt in range(4):
                            ps, psn = ps_mm()
                            for kc in range(2):
                                lo = goff + kc * 256 + mm * 128
                                pt_, pn_, po = PSETS[gi % 2][kc]
                                io = po + 512 * tt
                                P.op("pe", I("matmul", ps[:, :], lhsT=WB[:, lo:lo + 128], rhs=pt_[:, io:io + 512],
                                             start=(kc == 0), stop=(kc == 1)),
                                     reads=[("WB", lo, lo + 128), (pn_, io, io + 512)], writes=[(psn, 0, 512)],
                                     inc=(kc == 1))
                            ch = 2 * gi + mm
                            yo = ch * S + 512 * tt
                            P.op("act", I("activation", out=BIG[:, yo:yo + 512], in_=ps[:, :], func=AF.Copy, scale=PSC[:, ch:ch + 1]),
                                 reads=[(psn, 0, 512), ("PSC", 0, 8)], writes=[("BIG", yo, yo + 512)])
                inproj(0)
                for gi in range(4):
                    if gi + 1 < 4:
                        inproj(gi + 1)
                    grp(gi)
                proj_postnorm(l, 1, "pwout", 8, 2, 0, S, 0, 0, 4, bg=lambda: prenorm_units(l, 2, 0, 2, sqt=(2, 3, 4)))

            def store_out(h0, h1, strm):
                for c in range(8):
                    P.op("act" if strm == "ost2" else "sp",
                         I("dma_start", out=yT[c * 128:(c + 1) * 128, h0:h1], in_=X[:, c * S + h0:c * S + h1]),
                         reads=[("X", c * S + h0, c * S + h1)], stream=strm, amt=16)

            run_layers = EXEC_LAYERS if EXEC_LAYERS is not None else layers
            masks_built = [False]
            for li, l in enumerate(run_layers):
                kind = l % 3
                pre_tiles = 2 if li > 0 else 0
                if kind == 0:
                    attention(l, pre_tiles)
                elif kind == 1:
                    convmix(l, pre_tiles)
                else:
                    poolmix(l, pre_tiles)
                ffn(l, pre_done=True, next_l=(run_layers[li + 1] if li + 1 < len(run_layers) else None))
            store_out(1024, 1536, "ost1")
            store_out(1536, 2048, "ost2")
            P.wait_all("sp", ["ost0", "ost1", "ost2"])

        def t_name(t, nm):
            return {"gvec": "GV", "dwv": "DWV", "cdw": "CDW", "psc": "PSC"}[nm]

        Wd = WMgr(DummyProg(), dram, WB, None)
        construct(DummyProg(), Wd)
        P = Prog()
        P.unordered.update(["xld0", "xld1", "xld2", "cst", "ost0", "ost1", "ost2"])
        W = WMgr(P, dram, WB, Wd.rec)
        construct(P, W)
        assert W.used == len(Wd.rec) and W.issued == len(Wd.rec)
        P.emit(nc, st)
    return nc


def chunkmajor(Wm, cols):
    sub = Wm[:, cols]
    nk = Wm.shape[0] // 128
    return np.ascontiguousarray(sub.reshape(nk, 128, -1).transpose(1, 0, 2).reshape(128, -1))


def host_layout(inp):
    f = lambda a: np.asarray(a, dtype=np.float32)
    out = {}
    ng = f(inp["norm_g"])
    out["gvec"] = np.ascontiguousarray(ng.reshape(16, 8, 128).transpose(2, 0, 1).reshape(128, 128))
    dw = f(inp["ffn_w_dw"])
    out["dwv"] = np.ascontiguousarray(dw.reshape(4, 3, 2, 22, 128).transpose(4, 0, 3, 2, 1).reshape(128, 528))
    cdw = f(inp["conv_w_dw"])[0]
    out["cdw"] = np.ascontiguousarray(cdw.reshape(3, 8, 128).transpose(2, 1, 0).reshape(128, 24))
    out["psc"] = np.ascontiguousarray(f(inp["pool_scale"])[0].reshape(8, 128).T)
    ar = np.arange
    for l in range(4):
        wu = f(inp["ffn_w_up"])[l]
        out["wup%d" % l] = np.stack([chunkmajor(wu, np.concatenate([ar(128 * i, 128 * i + 128), ar(2816 + 128 * i, 2816 + 128 * i + 128)]))
                                     for i in range(22)])
        wd = f(inp["ffn_w_down"])[l]
        out["wdn%d" % l] = np.stack([chunkmajor(wd, ar(128 * m, 128 * m + 128)) for m in range(8)])
    for ia, l in enumerate((0, 3)):
        wq = f(inp["attn_w_qkv"])[ia]
        blks = []
        for g in range(3):
            for c in range(4):
                base = g * 1536 + c * 128
                blks.append(chunkmajor(wq, np.concatenate([ar(base, base + 128), ar(base + 512, base + 640), ar(base + 1024, base + 1152)])))
        out["wqkv%d" % l] = np.stack(blks)
        wo = f(inp["attn_w_o"])[ia]
        out["wo%d" % l] = np.stack([chunkmajor(wo, ar(512 * b, 512 * b + 512)) for b in range(2)])
    cw = f(inp["conv_w_in"])[0]
    out["cwin"] = np.stack([chunkmajor(cw, np.concatenate([ar(128 * i, 128 * i + 128), ar(1024 + 128 * i, 1152 + 128 * i), ar(2048 + 128 * i, 2176 + 128 * i)]))
                            for i in range(8)])
    for nm, key in (("cwout", "conv_w_out"), ("pwin", "pool_w_in"), ("pwout", "pool_w_out")):
        wm = f(inp[key])[0]
        out[nm] = np.stack([chunkmajor(wm, ar(256 * b, 256 * b + 256)) for b in range(4)])
    wg = f(inp["pool_w_grp"])[0]
    out["pwgrp"] = np.stack([chunkmajor(wg[g], ar(0, 256)) for g in range(4)])
    return out


_PROG_CACHE = {}


def _run(layers, xT_list, lay):
    key = tuple(layers)
    if key not in _PROG_CACHE:
        _PROG_CACHE[key] = build_program(list(layers))
    nc = _PROG_CACHE[key]
    names = ["gvec", "dwv", "cdw", "psc"]
    for l in layers:
        names += ["wup%d" % l, "wdn%d" % l]
        kind = l % 3
        if kind == 0:
            names += ["wqkv%d" % l, "wo%d" % l]
        elif kind == 1:
            names += ["cwin", "cwout"]
        else:
            names += ["pwin", "pwgrp", "pwout"]
    in_maps = []
    for b in range(8):
        m = {n: lay[n] for n in names}
        m["xT"] = xT_list[b]
        in_maps.append(m)
    res = run_bass_kernel_spmd(nc, in_maps, core_ids=list(range(8)))
    return [np.asarray(r["yT"]) for r in res.results]


def kernel(**inputs):
    lay = host_layout(inputs)
    x = np.asarray(inputs["x"], dtype=np.float32)
    xT = [np.ascontiguousarray(x[b].T) for b in range(8)]
    if FUSED:
        yT = _run((0, 1, 2, 3), xT, lay)
    else:
        yT = xT
        for l in range(4):
            yT = _run((l,), yT, lay)
    return np.ascontiguousarray(np.stack([y.T for y in yT]).astype(np.float32))
```
